# Optimizing a Trainium2 kernel written in Bass

```python
import math
import jax, jax.numpy as jnp
from jax import lax
import numpy as np

D_MODEL = 1024
BATCH = 16
SEQ = 4096
DEPTH = 1

PLE_DIM = 256
RMS_EPS = 1e-6
A_HEAD_DIM = 64
A_HEADS = D_MODEL // 128
A_WIDTH = A_HEADS * A_HEAD_DIM
MOBA_BLOCK = 256
MOBA_TOPK = 3
Q_CHUNK = 16
REL_BUCKETS = 32
REL_MAX_EXACT = REL_BUCKETS // 2
REL_MAX_DIST = 128
B_HEAD_DIM = 64
B_HEADS = D_MODEL // 128
B_WIDTH = B_HEADS * B_HEAD_DIM
DECAY_RANK = 64
ICLR_RANK = 64
GN_EPS = 64e-5
A_COLS = 4 * A_WIDTH
RW_COLS = 4 * B_WIDTH + DECAY_RANK + ICLR_RANK
IN_COLS = A_COLS + RW_COLS + 2 * D_MODEL

kernel_name = "hybrid_moba_rwkv7_gated_block"


def _rms_norm(x, g):
    xf = x.astype(jnp.float32)
    y = xf * lax.rsqrt(jnp.mean(xf * xf, axis=-1, keepdims=True) + RMS_EPS)
    return y * g.astype(jnp.float32)


def _t5_bucket(dist):
    n = jnp.maximum(dist, 0)
    nf = jnp.maximum(n, 1).astype(jnp.float32)
    large = REL_MAX_EXACT + (jnp.log(nf / REL_MAX_EXACT) / math.log(REL_MAX_DIST / REL_MAX_EXACT)
                             * (REL_BUCKETS - REL_MAX_EXACT)).astype(jnp.int32)
    large = jnp.minimum(large, REL_BUCKETS - 1)
    return jnp.where(n < REL_MAX_EXACT, n, large)


def _moba_attention(q, k, v, rel_bias):
    bsz, seq, nh, hd = q.shape
    n_blk = -(-seq // MOBA_BLOCK)
    pad = n_blk * MOBA_BLOCK - seq
    topk = min(MOBA_TOPK, n_blk)
    qh = jnp.swapaxes(q, 1, 2).astype(jnp.float32) * (hd ** -0.5)
    kh = jnp.pad(jnp.swapaxes(k, 1, 2).astype(jnp.float32), ((0, 0), (0, 0), (0, pad), (0, 0)))
    vh = jnp.pad(jnp.swapaxes(v, 1, 2).astype(jnp.float32), ((0, 0), (0, 0), (0, pad), (0, 0)))
    k_blk = kh.reshape(bsz, nh, n_blk, MOBA_BLOCK, hd)
    v_blk = vh.reshape(bsz, nh, n_blk, MOBA_BLOCK, hd)
    k_mean = jnp.mean(k_blk, axis=3)
    gate = jnp.einsum('bhsd,bhnd->bhsn', qh, k_mean)
    q_blk = jnp.arange(seq) // MOBA_BLOCK
    past = jnp.arange(n_blk)[None, :] < q_blk[:, None]
    gate = jnp.where(past[None, None], gate, -jnp.inf)
    _, sel = lax.top_k(gate, topk)
    bias_h = rel_bias.T.astype(jnp.float32)
    b_idx = jnp.arange(bsz)[:, None, None, None]
    h_idx = jnp.arange(nh)[None, :, None, None]
    offs = jnp.arange(MOBA_BLOCK)

    def chunk(c):
        t0 = c * Q_CHUNK
        pos = t0 + jnp.arange(Q_CHUNK)
        qb = t0 // MOBA_BLOCK
        qc = lax.dynamic_slice_in_dim(qh, t0, Q_CHUNK, axis=2)
        sc = lax.dynamic_slice_in_dim(sel, t0, Q_CHUNK, axis=2)
        ks = k_blk[b_idx, h_idx, sc]
        vs = v_blk[b_idx, h_idx, sc]
        key_pos = sc[..., None] * MOBA_BLOCK + offs
        dist = pos[None, None, :, None, None] - key_pos
        l_sel = jnp.einsum('bhqd,bhqkjd->bhqkj', qc, ks) + bias_h[h_idx[..., None], _t5_bucket(dist)]
        valid = (sc < qb)[..., None]
        l_sel = jnp.where(valid, l_sel, -jnp.inf).reshape(bsz, nh, Q_CHUNK, topk * MOBA_BLOCK)
        ko = lax.dynamic_index_in_dim(k_blk, qb, axis=2, keepdims=False)
        vo = lax.dynamic_index_in_dim(v_blk, qb, axis=2, keepdims=False)
        dist_o = pos[:, None] - (qb * MOBA_BLOCK + offs)[None, :]
        l_own = jnp.einsum('bhqd,bhjd->bhqj', qc, ko) + bias_h[:, _t5_bucket(dist_o)][None]
        l_own = jnp.where((dist_o >= 0)[None, None], l_own, -jnp.inf)
        probs = jax.nn.softmax(jnp.concatenate([l_sel, l_own], axis=-1), axis=-1)
        p_sel = probs[..., :topk * MOBA_BLOCK].reshape(bsz, nh, Q_CHUNK, topk, MOBA_BLOCK)
        p_own = probs[..., topk * MOBA_BLOCK:]
        return (jnp.einsum('bhqkj,bhqkjd->bhqd', p_sel, vs)
                + jnp.einsum('bhqj,bhjd->bhqd', p_own, vo))

    outs = lax.map(chunk, jnp.arange(seq // Q_CHUNK))
    return outs.transpose(1, 0, 3, 2, 4).reshape(bsz, seq, nh * hd)


def _rwkv7_time_mix(cols, mu, w0, w_up, a0, a_up, k_k, k_a, r_k, ln_w, ln_b):
    bsz, seq, _ = cols.shape
    cols = cols.astype(jnp.float32)
    prev = jnp.pad(cols, ((0, 0), (1, 0), (0, 0)))[:, :-1]
    cols = cols + (prev - cols) * mu.astype(jnp.float32)
    o = np.cumsum([0, B_WIDTH, B_WIDTH, B_WIDTH, B_WIDTH, DECAY_RANK])
    r, k, v, z = (cols[..., o[j]:o[j + 1]] for j in range(4))
    wd = cols[..., o[4]:o[5]]
    ad = cols[..., o[5]:]
    w_log = -jax.nn.softplus(-(w0 + jnp.tanh(wd) @ w_up)) - 0.5
    decay = jnp.exp(-jnp.exp(w_log))
    a = jax.nn.sigmoid(a0 + ad @ a_up)
    heads = lambda t: t.reshape(bsz, seq, B_HEADS, B_HEAD_DIM)
    kk = heads(k * k_k)
    kk = kk / jnp.maximum(jnp.linalg.norm(kk, axis=-1, keepdims=True), 1e-12)
    k = k * (1.0 + (a - 1.0) * k_a)
    rh, kh, vh, ah = heads(r), heads(k), heads(v), heads(a)
    xs = tuple(jnp.moveaxis(t, 1, 0) for t in (rh, heads(decay), kh, vh, -kk, kk * ah))

    def step(state, inp):
        r_t, w_t, k_t, v_t, a_t, b_t = inp
        sa = jnp.einsum('bhvk,bhk->bhv', state, a_t)
        state = (state * w_t[:, :, None, :] + sa[..., None] * b_t[:, :, None, :]
                 + v_t[..., None] * k_t[:, :, None, :])
        return state, jnp.einsum('bhvk,bhk->bhv', state, r_t)

    s0 = jnp.zeros((bsz, B_HEADS, B_HEAD_DIM, B_HEAD_DIM), jnp.float32)
    _, y = lax.scan(step, s0, xs)
    y = jnp.moveaxis(y, 0, 1)
    mean = jnp.mean(y, axis=-1, keepdims=True)
    var = jnp.mean(jnp.square(y - mean), axis=-1, keepdims=True)
    y = ((y - mean) * lax.rsqrt(var + GN_EPS)).reshape(bsz, seq, B_WIDTH) * ln_w + ln_b
    bonus = jnp.sum(rh * kh * r_k, axis=-1, keepdims=True) * vh
    y = y + bonus.reshape(bsz, seq, B_WIDTH)
    return y, z


def setup_inputs(seed: int = 0) -> dict:
    key = jax.random.key(seed)
    ks = jax.random.split(key, 22)
    n = lambda k, s, sc: jax.random.normal(k, s, jnp.float32) * sc
    L = DEPTH
    return {
        "x": n(ks[0], (BATCH, SEQ, D_MODEL), 1.0),
        "p": n(ks[1], (DEPTH, BATCH, SEQ, PLE_DIM), 1.0),
        "g_pre": 1.0 + n(ks[2], (L, D_MODEL), 0.05),
        "w_in": n(ks[3], (L, D_MODEL, IN_COLS), D_MODEL ** -0.5),
        "rel_bias": n(ks[4], (REL_BUCKETS, A_HEADS), 0.5),
        "mu_shift": jax.random.uniform(ks[5], (L, RW_COLS), jnp.float32, 0.0, 1.0),
        "w0": jax.random.uniform(ks[6], (L, B_WIDTH), jnp.float32, -4.0, 1.0),
        "w_up": n(ks[7], (L, DECAY_RANK, B_WIDTH), 0.5 * DECAY_RANK ** -0.5),
        "a0": n(ks[8], (L, B_WIDTH), 0.5),
        "a_up": n(ks[9], (L, ICLR_RANK, B_WIDTH), 0.5 * ICLR_RANK ** -0.5),
        "k_k": 0.85 + n(ks[10], (L, B_WIDTH), 0.05),
        "k_a": 1.0 + n(ks[11], (L, B_WIDTH), 0.05),
        "r_k": n(ks[12], (L, B_HEADS, B_HEAD_DIM), 0.1),
        "ln_x_w": 1.0 + n(ks[13], (L, B_WIDTH), 0.05),
        "ln_x_b": n(ks[14], (L, B_WIDTH), 0.02),
        "p_a": n(ks[15], (L, A_WIDTH, D_MODEL), A_WIDTH ** -0.5),
        "p_b": n(ks[16], (L, B_WIDTH, D_MODEL), B_WIDTH ** -0.5),
        "w_out": n(ks[17], (L, D_MODEL, D_MODEL), D_MODEL ** -0.5),
        "g_post": 1.0 + n(ks[18], (L, D_MODEL), 0.05),
        "w_ple_up": n(ks[19], (L, PLE_DIM, D_MODEL), PLE_DIM ** -0.5),
        "w_ple_gate": n(ks[20], (L, D_MODEL, D_MODEL), D_MODEL ** -0.5),
    }


def reference(x, p, g_pre, w_in, rel_bias, mu_shift, w0, w_up, a0, a_up, k_k, k_a, r_k,
              ln_x_w, ln_x_b, p_a, p_b, w_out, g_post, w_ple_up, w_ple_gate):
    bsz, seq, _ = x.shape
    h = x.astype(jnp.float32)
    for i in range(DEPTH):
        u = _rms_norm(h, g_pre[i]).astype(w_in.dtype)
        cols = u @ w_in[i]
        a_cols = cols[..., :A_COLS]
        b_cols = cols[..., A_COLS:A_COLS + RW_COLS]
        gate_a = cols[..., A_COLS + RW_COLS:A_COLS + RW_COLS + D_MODEL].astype(jnp.float32)
        gate_b = cols[..., A_COLS + RW_COLS + D_MODEL:].astype(jnp.float32)
        qa, ka, va, za = (a_cols[..., j * A_WIDTH:(j + 1) * A_WIDTH].reshape(bsz, seq, A_HEADS, A_HEAD_DIM)
                          for j in range(4))
        y_a = _moba_attention(qa, ka, va, rel_bias) * jax.nn.silu(za.reshape(bsz, seq, A_WIDTH).astype(jnp.float32))
        y_b, z_b = _rwkv7_time_mix(b_cols, mu_shift[i], w0[i], w_up[i], a0[i], a_up[i], k_k[i], k_a[i],
                                   r_k[i], ln_x_w[i], ln_x_b[i])
        y_b = y_b * jax.nn.silu(z_b)
        merged = (jax.nn.sigmoid(gate_a) * (y_a.astype(p_a.dtype) @ p_a[i])
                  + jax.nn.sigmoid(gate_b) * (y_b.astype(p_b.dtype) @ p_b[i]))
        y = merged.astype(w_out.dtype) @ w_out[i]
        h = h + _rms_norm(y, g_post[i])
        e = p[i] @ w_ple_up[i]
        h = h + jax.nn.sigmoid((h.astype(w_ple_gate.dtype) @ w_ple_gate[i]).astype(jnp.float32)) * e
    return h.astype(x.dtype)
```

```python
import math
from contextlib import ExitStack

import numpy as np
import concourse.bass as bass
import concourse.mybir as mybir
from concourse.bass_utils import run_bass_kernel_spmd

F32 = mybir.dt.float32
BF16 = mybir.dt.bfloat16
ALU = mybir.AluOpType
AF = mybir.ActivationFunctionType
AX = mybir.AxisListType

D = 1024
NCORES = 8
A_W = 512
B_W = 512
HD = 64
NH = 8
IN_COLS = 6272
RW_COLS = 2176
PLE = 256
BLK = 256
RMS_EPS = 1e-6
GN_EPS = 64e-5
NEG = -30000.0


class Sched:
    ENGS = ("pe", "act", "dve", "pool", "sp")
    NDMA = 8

    def __init__(self):
        self.streams = {e: [] for e in self.ENGS}
        self.waited = {e: {} for e in self.ENGS}
        self.last_w = {}
        self.readers = {}
        self.dma_cnt = {e: 0 for e in self.ENGS}
        self.dma_val = {}
        self.final_dma = []

    def _add_wait(self, eng, waits, ev):
        if ev is None:
            return
        if ev[0] == "eng":
            _, e2, j = ev
            if e2 == "pe" and eng == "pe":
                return
            key = e2
            val = j
        else:
            _, key, val = ev
        if self.waited[eng].get(key, -1) >= val:
            return
        self.waited[eng][key] = val
        waits.append(ev)
        if ev[0] == "eng":
            self.streams[ev[1]][ev[2]]["sig"] = True

    def _deps(self, eng, reads, writes):
        evs = []
        for k in reads:
            evs.append(self.last_w.get(k))
        for k in writes:
            evs.append(self.last_w.get(k))
            evs.extend(self.readers.get(k, ()))
        best = {}
        for ev in evs:
            if ev is None:
                continue
            key = ev[1]
            if key not in best or ev[2] > best[key][2]:
                best[key] = ev
        waits = []
        for ev in best.values():
            self._add_wait(eng, waits, ev)
        return waits

    def _commit(self, ev, reads, writes):
        for k in reads:
            self.readers.setdefault(k, []).append(ev)
        for k in writes:
            self.last_w[k] = ev
            self.readers[k] = []

    def op(self, eng, fn, reads=(), writes=()):
        waits = self._deps(eng, reads, writes)
        idx = len(self.streams[eng])
        self.streams[eng].append({"fn": fn, "waits": waits, "sig": False, "dma": None})
        self._commit(("eng", eng, idx), reads, writes)

    def dma(self, eng, out, in_, reads=(), writes=(), final=False, **kw):
        slot = self.dma_cnt[eng] % self.NDMA
        self.dma_cnt[eng] += 1
        key = (eng, slot)
        prev = self.dma_val.get(key, 0)
        waits = self._deps(eng, reads, writes)
        if prev > 0:
            self._add_wait(eng, waits, ("dma", key, prev))
        val = prev + 16
        self.dma_val[key] = val
        fn = lambda e, out=out, in_=in_, kw=kw: e.dma_start(out=out, in_=in_, **kw)
        self.streams[eng].append({"fn": fn, "waits": waits, "sig": False, "dma": key})
        ev = ("dma", key, val)
        self._commit(ev, reads, writes)
        if final:
            self.final_dma.append(ev)

    def barrier(self):
        evs = []
        for e in ("pe", "act", "dve", "pool"):
            for j in range(len(self.streams[e]) - 1, -1, -1):
                if self.streams[e][j]["fn"] is not None and self.streams[e][j]["dma"] is None:
                    evs.append(("eng", e, j))
                    break
        for key, val in self.dma_val.items():
            evs.append(("dma", key, val))
        for e in self.ENGS:
            waits = []
            for ev in evs:
                if ev[0] == "eng" and ev[1] == e:
                    continue
                self._add_wait(e, waits, ev)
            self.streams[e].append({"fn": None, "waits": waits, "sig": False, "dma": None})
        self.last_w = {}
        self.readers = {}

    def emit(self, nc, es):
        sems = {e: es.enter_context(nc.semaphore("sem_" + e)) for e in ("pe", "act", "dve", "pool")}
        dsems = {}
        for (e, slot) in self.dma_val:
            dsems[(e, slot)] = es.enter_context(nc.semaphore("dsem_%s%d" % (e, slot)))
        fin_waits = []
        for ev in self.final_dma:
            self._add_wait("sp", fin_waits, ev)
        counts = {}
        for e in ("pe", "act", "dve", "pool"):
            c = 0
            lst = []
            for o in self.streams[e]:
                if o["sig"]:
                    c += 1
                lst.append(c)
            counts[e] = lst
        block = es.enter_context(nc.Block())

        def run(engname, eobj):
            def do_wait(ev):
                if ev[0] == "eng":
                    eobj.wait_ge(sems[ev[1]], counts[ev[1]][ev[2]])
                else:
                    eobj.wait_ge(dsems[ev[1]], ev[2])
            for o in self.streams[engname]:
                for ev in o["waits"]:
                    do_wait(ev)
                if o["fn"] is None:
                    continue
                ins = o["fn"](eobj)
                if o["dma"] is not None:
                    ins.then_inc(dsems[o["dma"]], 16)
                elif o["sig"]:
                    ins.then_inc(sems[engname], 1)
            if engname == "sp":
                for ev in fin_waits:
                    do_wait(ev)

        @block.sync
        def _(e):
            run("sp", e)

        @block.tensor
        def _(e):
            run("pe", e)

        @block.scalar
        def _(e):
            run("act", e)

        @block.vector
        def _(e):
            run("dve", e)

        @block.gpsimd
        def _(e):
            run("pool", e)


class Ring:
    def __init__(self, tiles, name, keys=None):
        self.tiles = tiles
        self.keys = keys if keys is not None else [(name, j) for j in range(len(tiles))]
        self.i = 0

    def next(self):
        j = self.i % len(self.tiles)
        self.i += 1
        return self.tiles[j], self.keys[j]


def build_program(S, NSEQ, dbg=None, lvl=30, sub=99, moba=True, only_even=False, GRPN=4, bar=False):
    dbg = dbg or set()
    nc = bass.Bass("TRN2", target_bir_lowering=False)
    NT = S // 512
    NQT = S // 128
    NBLK = S // BLK
    NTOK = NSEQ * S

    def din(name, shape, dt=F32):
        return nc.dram_tensor(name, list(shape), dt, kind="ExternalInput").ap()

    x_d = din("x", [NTOK, D])
    p_d = din("p", [NTOK, PLE])
    g_pre_d = din("g_pre", [D])
    w_in_d = din("w_in", [D, IN_COLS])
    relb_d = din("rel_bias", [32, NH])
    mu_d = din("mu_shift", [RW_COLS])
    w0_d = din("w0", [B_W])
    w_up_d = din("w_up", [64, B_W])
    a0_d = din("a0", [B_W])
    a_up_d = din("a_up", [64, B_W])
    k_k_d = din("k_k", [B_W])
    k_a_d = din("k_a", [B_W])
    r_k_d = din("r_k", [B_W])
    lnw_d = din("ln_x_w", [B_W])
    lnb_d = din("ln_x_b", [B_W])
    p_a_d = din("p_a", [A_W, D])
    p_b_d = din("p_b", [B_W, D])
    w_out_d = din("w_out", [D, D])
    g_post_d = din("g_post", [D])
    w_pu_d = din("w_ple_up", [PLE, D])
    w_pg_d = din("w_ple_gate", [D, D])
    c_ident_d = din("c_ident", [128, 128])
    c_onehot_d = din("c_onehot", [33, 512])
    c_esel_d = din("c_esel", [128, 16 * 128])
    c_tri_d = din("c_tri", [64, 3 * 64])
    c_bd_d = din("c_bd", [128, 128])
    c_J_d = din("c_J", [128, 128])
    c_pen_d = din("c_pen", [16 * 128])

    out_d = nc.dram_tensor("out", [NTOK, D], F32, kind="ExternalOutput").ap()

    def scratch(name, shape, dt):
        kind = "ExternalOutput" if name in dbg else "Internal"
        return nc.dram_tensor(name, list(shape), dt, kind=kind).ap()

    qT_d = scratch("s_qT", [NSEQ, A_W, S], BF16)
    kT_d = scratch("s_kT", [NSEQ, A_W, S], BF16)
    v_d = scratch("s_v", [NSEQ, S, 2 * A_W], BF16)
    za_d = scratch("s_za", [NSEQ, A_W, S], BF16)
    rw_d = scratch("s_rw", [NSEQ, RW_COLS, S], F32)
    gt_d = scratch("s_gt", [NSEQ, 2 * D, S], BF16)
    ya_d = scratch("s_ya", [NSEQ, A_W, S], BF16)
    yb_d = scratch("s_yb", [NSEQ, B_W, S], BF16)
    fb_d = scratch("s_fb", [NH, 512], F32)

    sc = Sched()
    es = ExitStack()

    def sb(name, shape, dt=F32):
        return es.enter_context(nc.sbuf_tensor(name, list(shape), dt))

    def ps(name, shape, dt=F32):
        return es.enter_context(nc.psum_tensor(name, list(shape), dt))

    with es:
        ident_f = sb("ident_f", [128, 128])
        ident_b = sb("ident_b", [128, 128], BF16)
        sc.dma("sp", ident_f[:], c_ident_d[:, :], writes=[("ident_f",)])
        sc.op("dve", lambda e: e.tensor_copy(ident_b[:], ident_f[:]), reads=[("ident_f",)], writes=[("ident_b",)])
        psb = [ps("psb%d" % i, [128, 512]) for i in range(8)]
        psring = Ring(psb, "psb")

        vstage = sb("vstage", [64, 128])
        vc = sb("vc", [128, 64])
        VOFF = {}
        _r = 0
        sc.op("dve", lambda e: e.memset(vstage[:], 0.0), writes=[("vstage",)])
        for nm, dv, n in (("g_pre", g_pre_d, 8), ("mu", mu_d, 17), ("w0", w0_d, 4), ("a0", a0_d, 4),
                          ("k_k", k_k_d, 4), ("k_a", k_a_d, 4), ("r_k", r_k_d, 4)):
            VOFF[nm] = _r
            sc.dma("sp", vstage[_r:_r + n, :], dv.rearrange("(k p) -> k p", p=128), reads=[("vstage",)], writes=[("vstage", nm)])
            _r += n
        pt, pk = psring.next()
        sc.op("pe", lambda e, pt=pt: e.transpose(pt[:, 0:64], vstage[:], ident_f[0:64, 0:64]),
              reads=[("vstage",), ("ident_f",)] + [("vstage", nm) for nm in VOFF], writes=[pk])
        sc.op("dve", lambda e, pt=pt: e.tensor_copy(vc[:], pt[:, 0:64]), reads=[pk], writes=[("vc",)])
        gpre_c = vc[:, VOFF["g_pre"]:VOFF["g_pre"] + 8]

        kmean = sb("kmean", [128, NSEQ, 4, 16], F32)
        kmean_b = sb("kmean_b", [128, NSEQ, 4, 16], BF16)
        if True:
            es1 = ExitStack()
            with es1:
                def sb1(name, shape, dt=F32):
                    return es1.enter_context(nc.sbuf_tensor(name, list(shape), dt))
                wp = sb1("wp", [128, 8, IN_COLS], BF16)
                wst = [sb1("wst%d" % i, [128, 1568]) for i in range(2)]
                wring = Ring(wst, "wst")
                for kc in range(8):
                    for q4 in range(4):
                        t, tk = wring.next()
                        c0 = q4 * 1568
                        sc.dma("sp", t[:], w_in_d[kc * 128:(kc + 1) * 128, c0:c0 + 1568], writes=[tk])
                        eng = ("dve", "pool")[(kc * 4 + q4) % 2]
                        sc.op(eng, lambda e, t=t, kc=kc, c0=c0: e.tensor_scalar(
                            wp[:, kc, c0:c0 + 1568], t[:], gpre_c[:, kc:kc + 1], None, ALU.mult),
                            reads=[tk, ("vc",)], writes=[("wp", kc, q4)])
                wp_keys = [("wp", kc, q4) for kc in range(8) for q4 in range(4)]

                mu_c = vc[:, VOFF["mu"]:VOFF["mu"] + 17]
                carry = sb1("carry", [128, 17])
                xt = [sb1("xt%d" % i, [128, 4, D]) for i in range(2)]
                xring = Ring(xt, "xt")
                ub = [sb1("ub%d" % i, [128, D], BF16) for i in range(2)]
                ubring = Ring(ub, "ub")
                sqj = sb1("sqj", [128, D], BF16)
                ss = [sb1("ss%d" % i, [128, 4]) for i in range(2)]
                ssring = Ring(ss, "ss")
                rs = [sb1("rs%d" % i, [128, 4]) for i in range(2)]
                rsring = Ring(rs, "rs")
                uT = [sb1("uT%d" % i, [128, 8, 512], BF16) for i in range(2)]
                uTring = Ring(uT, "uT")
                ob = [sb1("ob%d" % i, [128, 512], BF16) for i in range(4)]
                obring = Ring(ob, "ob")
                vo = [sb1("vo%d" % i, [128, NH, 128], BF16) for i in range(2)]
                voring = Ring(vo, "vo")
                for i in range(2):
                    sc.op("dve", lambda e, i=i: e.memset(vo[i][:], 1.0), writes=[("vo", i), ("vo_init",)])
                cb = [sb1("cb%d" % i, [128, 513]) for i in range(3)]
                cbring = Ring(cb, "cb")
                db = [sb1("db%d" % i, [128, 512]) for i in range(2)]
                dbring = Ring(db, "db")
                shb = [sb1("shb%d" % i, [128, 512]) for i in range(3)]
                shring = Ring(shb, "shb")
                sc.op("dve", lambda e: e.memset(kmean[:], 0.0), writes=[("kmean",)])

                xloaded = {}

                def load_x(s, ti):
                    if s >= NSEQ:
                        return
                    tok0 = s * S + ti * 512
                    xtile, xk = xring.next()
                    sc.dma("sp", xtile[:], x_d[tok0:tok0 + 512, :].rearrange("(a p) d -> p a d", p=128),
                           writes=[xk])
                    xloaded[(s, ti)] = (xtile, xk)

                load_x(0, 0)
                for s in range(NSEQ if lvl >= 1 else 0):
                    sc.op("dve", lambda e: e.memset(carry[:], 0.0), writes=[("carry",)], reads=[])
                    for ti in range(NT):
                        nxt = (s, ti + 1) if ti + 1 < NT else (s + 1, 0)
                        load_x(*nxt)
                        xtile, xk = xloaded.pop((s, ti))
                        sst, ssk = ssring.next()
                        rst, rsk = rsring.next()
                        sc.op("dve", lambda e, sst=sst: e.memset(sst[:], 0.0), writes=[(ssk, a) for a in range(4)])
                        for a in range(4):
                            sc.op("act", lambda e, a=a, xtile=xtile, sst=sst: e.activation(
                                sqj[:], xtile[:, a, :], AF.Square, accum_out=sst[:, a:a + 1]),
                                reads=[xk], writes=[("sqj",), (ssk, a)])
                        sc.op("act", lambda e, sst=sst: e.activation(
                            sst[:], sst[:], AF.Sqrt, bias=float(RMS_EPS), scale=1.0 / D),
                            reads=[(ssk, a) for a in range(4)], writes=[(ssk, "q")])
                        sc.op("dve", lambda e, sst=sst, rst=rst: e.reciprocal(rst[:], sst[:]),
                              reads=[(ssk, "q")], writes=[rsk] + [(ssk, a) for a in range(4)])
                        uTt, uTk = uTring.next()
                        for a in range(4):
                            ubt, ubk = ubring.next()
                            sc.op("dve", lambda e, a=a, ubt=ubt, xtile=xtile, rst=rst: e.tensor_scalar(
                                ubt[:], xtile[:, a, :], rst[:, a:a + 1], None, ALU.mult),
                                reads=[xk, rsk], writes=[ubk])
                            pt, pk = psring.next()
                            ptb = pt[:].bitcast(BF16)

                            def tr_fn(e, ubt=ubt, ptb=ptb):
                                ins = None
                                for kc in range(8):
                                    ins = e.transpose(ptb[:, kc * 128:(kc + 1) * 128], ubt[:, kc * 128:(kc + 1) * 128], ident_b[:])
                                return ins
                            sc.op("pe", tr_fn, reads=[ubk, ("ident_b",)], writes=[pk])
                            eng = ("act", "dve")[a % 2]
                            if eng == "act":
                                sc.op("act", lambda e, a=a, uTt=uTt, ptb=ptb: e.copy(
                                    uTt[:, :, a * 128:(a + 1) * 128], ptb.rearrange("p (k t) -> p k t", k=8)),
                                    reads=[pk], writes=[(uTk, a)])
                            else:
                                sc.op("dve", lambda e, a=a, uTt=uTt, ptb=ptb: e.tensor_copy(
                                    uTt[:, :, a * 128:(a + 1) * 128], ptb.rearrange("p (k t) -> p k t", k=8)),
                                    reads=[pk], writes=[(uTk, a)])
                        uT_keys = [(uTk, a) for a in range(4)]

                        def proj(cc, pt, pk, uTt=uTt, uT_keys=uT_keys):
                            def fn(e):
                                ins = None
                                for kc in range(8):
                                    ins = e.matmul(pt[:], wp[:, kc, cc * 128:(cc + 1) * 128], uTt[:, kc, :],
                                                   start=(kc == 0), stop=(kc == 7))
                                return ins
                            sc.op("pe", fn, reads=uT_keys + wp_keys, writes=[pk])

                        for cc in range(49):
                            if 8 <= cc < 12:
                                continue
                            if lvl < 2 or (lvl == 2 and cc >= 4) or (lvl == 3 and cc >= 8) or (lvl == 4 and cc >= 16) or (lvl == 5 and cc >= 33):
                                continue
                            pt, pk = psring.next()
                            proj(cc, pt, pk)
                            tsl = slice(ti * 512, (ti + 1) * 512)
                            rows = slice((cc % 4) * 128, (cc % 4) * 128 + 128)
                            if cc < 4:
                                o, ok = obring.next()
                                sc.op("act", lambda e, o=o, pt=pt: e.mul(o[:], pt[:], 0.125), reads=[pk], writes=[ok])
                                sc.dma("pool", qT_d[s, rows, tsl], o[:], reads=[ok], writes=[("qT", s, cc, ti)])
                            elif cc < 8:
                                o, ok = obring.next()
                                for hb in range(2):
                                    sc.op("act", lambda e, o=o, pt=pt, hb=hb, s=s, cc=cc, ti=ti: e.activation(
                                        o[:, hb * 256:(hb + 1) * 256], pt[:, hb * 256:(hb + 1) * 256], AF.Copy,
                                        accum_out=kmean[:, s, cc - 4, 2 * ti + hb:2 * ti + hb + 1]),
                                        reads=[pk, ("kmean",)], writes=([ok] if hb == 0 else []) + [(ok, hb), ("kmean", s, cc - 4, ti, hb)])
                                sc.dma("pool", kT_d[s, rows, tsl], o[:], reads=[ok, (ok, 0), (ok, 1)], writes=[("kT", s, cc - 4, ti)])
                            elif cc < 16:
                                o, ok = obring.next()
                                sc.op("act", lambda e, o=o, pt=pt: e.activation(o[:], pt[:], AF.Silu), reads=[pk], writes=[ok])
                                sc.dma("pool", za_d[s, rows, tsl], o[:], reads=[ok], writes=[("za", s, cc - 12, ti)])
                            elif cc < 33:
                                j = cc - 16
                                c, ck = cbring.next()
                                sc.op("act", lambda e, c=c, pt=pt: e.copy(c[:, 1:513], pt[:]), reads=[pk], writes=[(ck, 1)])
                                sc.op("dve", lambda e, c=c, j=j: e.tensor_copy(c[:, 0:1], carry[:, j:j + 1]),
                                      reads=[("carry", j), ("carry",)], writes=[(ck, 0)])
                                sc.op("dve", lambda e, c=c, j=j: e.tensor_copy(carry[:, j:j + 1], c[:, 512:513]),
                                      reads=[(ck, 1), ("carry",)], writes=[("carry", j)])
                                dd, dk = dbring.next()
                                sc.op("dve", lambda e, c=c, dd=dd: e.tensor_tensor(dd[:], c[:, 0:512], c[:, 1:513], ALU.subtract),
                                      reads=[(ck, 0), (ck, 1)], writes=[dk])
                                sh, shk = shring.next()
                                sc.op("dve", lambda e, c=c, dd=dd, sh=sh, j=j: e.scalar_tensor_tensor(
                                    sh[:], dd[:], mu_c[:, j:j + 1], c[:, 1:513], ALU.mult, ALU.add),
                                    reads=[dk, (ck, 1), ("vc",)], writes=[shk])
                                sc.dma("pool", rw_d[s, j * 128:(j + 1) * 128, tsl], sh[:], reads=[shk], writes=[("rw", s, j, ti)])
                            else:
                                j = cc - 33
                                o, ok = obring.next()
                                sc.op("act", lambda e, o=o, pt=pt: e.activation(o[:], pt[:], AF.Sigmoid), reads=[pk], writes=[ok])
                                sc.dma("pool", gt_d[s, j * 128:(j + 1) * 128, tsl], o[:], reads=[ok], writes=[("gt", s, j, ti)])
                        for a in range(4 if lvl >= 7 else 0):
                            pt, pk = psring.next()

                            def vfn(e, a=a, pt=pt, uTt=uTt):
                                ins = None
                                for kc in range(8):
                                    ins = e.matmul(pt[:], uTt[:, kc, a * 128:(a + 1) * 128], wp[:, kc, 1024:1536],
                                                   start=(kc == 0), stop=(kc == 7))
                                return ins
                            sc.op("pe", vfn, reads=uT_keys + wp_keys, writes=[pk])
                            o, ok = voring.next()
                            sc.op("act", lambda e, o=o, pt=pt: e.copy(o[:, :, 0:64], pt[:].rearrange("p (h d) -> p h d", h=NH)),
                                  reads=[pk, ("vo_init",)], writes=[ok])
                            t0 = ti * 512 + a * 128
                            sc.dma("pool", v_d[s, t0:t0 + 128, :], o[:].rearrange("p h d -> p (h d)"), reads=[ok], writes=[("v", s, ti * 4 + a)])
                sc.op("dve", lambda e: e.tensor_scalar(kmean_b[:], kmean[:], 1.0 / BLK, None, ALU.mult),
                      reads=[("kmean",)] + [("kmean", s, c, ti, hb) for s in range(NSEQ) for c in range(4) for ti in range(NT) for hb in range(2)],
                      writes=[("kmean_b",)])

        sc.barrier()
        if lvl >= 10 and moba:
            es2 = ExitStack()
            with es2:
                def sb2(name, shape, dt=F32):
                    return es2.enter_context(nc.sbuf_tensor(name, list(shape), dt))
                psS = Ring(psb[0:4], "psb", keys=[("psb", j) for j in range(0, 4)])
                psN = Ring(psb[4:6], "psb", keys=[("psb", j) for j in range(4, 6)])
                psM = Ring(psb[6:8], "psb", keys=[("psb", j) for j in range(6, 8)])
                relb33 = sb2("relb33", [33, NH])
                b31bc = sb2("b31bc", [128, NH])
                sc.dma("sp", b31bc[:], relb_d[31, :].partition_broadcast(128), writes=[("b31bc",)])
                sc.op("dve", lambda e: e.memset(relb33[32:33, :], NEG), writes=[("relb33", 1)])
                sc.dma("sp", relb33[0:32, :], relb_d[:, :], writes=[("relb33", 0)])
                oneh = sb2("oneh", [33, 512])
                sc.dma("sp", oneh[:], c_onehot_d[:, :], writes=[("oneh",)])
                pt, pk = psM.next()
                sc.op("pe", lambda e, pt=pt: e.matmul(pt[0:8, :], relb33[:], oneh[:], start=True, stop=True),
                      reads=[("relb33", 0), ("relb33", 1), ("oneh",)], writes=[pk])
                fbs = sb2("fbs", [8, 512])
                sc.op("dve", lambda e, pt=pt: e.tensor_copy(fbs[:], pt[0:8, :]), reads=[pk], writes=[("fbs",)])
                sc.dma("sp", fb_d[:, :], fbs[:], reads=[("fbs",)], writes=[("fb_d",)])
                Jf = sb2("Jf", [128, 128])
                sc.dma("sp", Jf[:], c_J_d[:, :], writes=[("Jf",)])
                Tt = sb2("Tt", [128, NH, 2, 128], BF16)
                tfl = [sb2("tfl%d" % i, [128, 128]) for i in range(2)]
                tflr = Ring(tfl, "tfl")
                for h in range(NH):
                    for dl in range(2):
                        tf, tfk = tflr.next()
                        src = bass.AP(tensor=fb_d.tensor, offset=h * 512 + 1 + dl * 128, ap=[[1, 128], [1, 128]])
                        sc.dma("sp", tf[:], src, reads=[("fb_d",)], writes=[tfk])
                        pt, pk = psM.next()
                        sc.op("pe", lambda e, pt=pt, tf=tf: e.matmul(pt[:, 0:128], Jf[:], tf[:], start=True, stop=True),
                              reads=[tfk, ("Jf",)], writes=[pk])
                        sc.op("dve", lambda e, pt=pt, h=h, dl=dl: e.tensor_scalar(Tt[:, h, dl, :], pt[:, 0:128], b31bc[:, h:h + 1], None, ALU.subtract),
                              reads=[pk, ("b31bc",)], writes=[("Tt", h, dl)])
                Tt_keys = [("Tt", h, dl) for h in range(NH) for dl in range(2)]
                if "d_Tt" in dbg:
                    dTt = nc.dram_tensor("d_Tt", [128, NH * 2 * 128], BF16, kind="ExternalOutput").ap()
                    sc.dma("sp", dTt[:, :], Tt[:].rearrange("p h d q -> p (h d q)"), reads=Tt_keys)
                eself = sb2("eself", [128, 16 * 128])
                esel = sb2("esel", [128, 16, 128], BF16)
                sc.dma("sp", eself[:], c_esel_d[:, :], writes=[("eself",)])
                sc.op("dve", lambda e: e.tensor_copy(esel[:].rearrange("p a b -> p (a b)"), eself[:]), reads=[("eself",)], writes=[("esel",)])
                ones_b = sb2("ones_b", [128, 64], BF16)
                sc.op("dve", lambda e: e.memset(ones_b[:], 1.0), writes=[("ones_b",)])
                maskT = [sb2("maskT%d" % i, [128, NH, 128], BF16) for i in range(2)]
                for i in range(2):
                    sc.op("dve", lambda e, i=i: e.memset(maskT[i][:, :, :], 0.0), writes=[("maskT", i)])
                mring = Ring(maskT, "maskT")
                pen_sb = sb2("pen_sb", [128, 16 * 128])
                sc.op("dve", lambda e: e.memset(pen_sb[:], 0.0), writes=[("pen_sb", "z")])
                sc.dma("sp", pen_sb[0:1, :], c_pen_d.rearrange("(a n) -> a n", a=1), reads=[("pen_sb", "z")], writes=[("pen_sb", 0)])
                sc.dma("sp", pen_sb[64:65, :], c_pen_d.rearrange("(a n) -> a n", a=1), reads=[("pen_sb", "z")], writes=[("pen_sb", 1)])
                onesq_b = sb2("onesq_b", [128, 128], BF16)
                sc.op("dve", lambda e: e.memset(onesq_b[:], 1.0), writes=[("onesq_b",)])
                penb = sb2("penb", [128, 16 * 128], BF16)
                sc.op("dve", lambda e: e.tensor_copy(penb[:], pen_sb[:]), reads=[("pen_sb", 0), ("pen_sb", 1), ("pen_sb", "z")], writes=[("penb",)])
                kT_sb = sb2("kT_sb", [128, 4, S], BF16)
                qT_sb = sb2("qT_sb", [128, 4, S], BF16)
                v_sb = sb2("v_sb", [128, S // 128, 2 * A_W], BF16)
                gsb = [sb2("gsb%d" % i, [128, NH, 16]) for i in range(2)]
                gring = Ring(gsb, "gsb")
                top8 = [sb2("top8_%d" % i, [128, NH, 8]) for i in range(2)]
                t8ring = Ring(top8, "top8")
                selm = [sb2("selm%d" % i, [128, NH, 16]) for i in range(2)]
                sring = Ring(selm, "selm")
                mvb = [sb2("mvb%d" % i, [128, NH, 16], BF16) for i in range(2)]
                mvring = Ring(mvb, "mvb")
                Pb = [sb2("Pb%d" % i, [128, 512], BF16) for i in range(3)]
                Pring = Ring(Pb, "Pb")
                rden = [sb2("rden%d" % i, [64, 128]) for i in range(2)]
                rdring = Ring(rden, "rden")
                ynorm = [sb2("ynorm%d" % i, [64, 128]) for i in range(2)]
                ynring = Ring(ynorm, "ynorm")
                zat = [sb2("zat%d" % i, [64, NH, 128], BF16) for i in range(2)]
                zring = Ring(zat, "zat")
                yout = [sb2("yout%d" % i, [64, NH, 128], BF16) for i in range(2)]
                yring = Ring(yout, "yout")

                for s in range(NSEQ if lvl >= 11 else 0):
                    for c in range(4):
                        sc.dma("sp", kT_sb[:, c, :], kT_d[s, c * 128:(c + 1) * 128, :],
                               reads=[("kT", s, c, ti) for ti in range(NT)], writes=[("kT_sb", c)])
                        sc.dma("sp", qT_sb[:, c, :], qT_d[s, c * 128:(c + 1) * 128, :],
                               reads=[("qT", s, c, ti) for ti in range(NT)], writes=[("qT_sb", c)])
                    for j4 in range(0, S // 128, 4):
                        n4 = min(4, S // 128 - j4)
                        sc.dma("sp", v_sb[:, j4:j4 + n4, :], v_d[s, j4 * 128:(j4 + n4) * 128, :].rearrange("(a p) d -> p a d", p=128),
                               reads=[("v", s, j) for j in range(j4, j4 + n4)], writes=[("v_sb", j) for j in range(j4, j4 + n4)])
                    for qt in range(NQT if lvl >= 12 else 0):
                        QB = qt // 2
                        qsl = slice(qt * 128, (qt + 1) * 128)
                        mT, mTk = None, None
                        if bar:
                            sc.barrier()
                        if QB > 0:
                            pt, pk = psM.next()

                            def gfn(e, pt=pt, qsl=qsl, s=s, QB=QB):
                                ins = None
                                for h in range(NH):
                                    hp = slice((h % 2) * 64, (h % 2) * 64 + 64)
                                    ins = e.matmul(pt[:, h * 16:(h + 1) * 16], qT_sb[hp, h // 2, qsl], kmean_b[hp, s, h // 2, :],
                                                   start=True, stop=False)
                                    p0 = (h % 2) * 64
                                    ins = e.matmul(pt[:, h * 16:(h + 1) * 16], onesq_b[p0:p0 + 1, :],
                                                   penb[p0:p0 + 1, QB * 128 + h * 16:QB * 128 + (h + 1) * 16], start=False, stop=True)
                                return ins
                            sc.op("pe", gfn, reads=[("qT_sb", c) for c in range(4)] + [("kmean_b",), ("penb",), ("onesq_b",)], writes=[pk])
                            g, gk = gring.next()
                            sc.op("dve", lambda e, g=g, pt=pt: e.tensor_copy(g[:].rearrange("p h n -> p (h n)"), pt[:, 0:NH * 16]),
                                  reads=[pk], writes=[gk, (gk, 1)])
                            t8, t8k = t8ring.next()
                            for h in range(NH if sub >= 3 else 0):
                                sc.op("dve", lambda e, t8=t8, g=g, h=h: e.max(t8[:, h, :], g[:, h, :]),
                                      reads=[gk, (gk, 1)], writes=[(t8k, h)])
                            sm, smk = sring.next()
                            if sub >= 4:
                              sc.op("dve", lambda e, sm=sm, g=g, t8=t8: e.tensor_tensor(
                                sm[:], g[:], t8[:, :, 2:3].to_broadcast([128, NH, 16]), ALU.is_ge),
                                reads=[gk, (gk, 1)] + [(t8k, h) for h in range(NH)], writes=[smk])
                            mv, mvk = mvring.next()
                            if sub >= 5:
                              sc.op("dve", lambda e, mv=mv, sm=sm: e.tensor_scalar(mv[:], sm[:], -NEG, NEG, ALU.mult, ALU.add),
                                  reads=[smk], writes=[mvk])
                            pt2, pk2 = psM.next()
                            ptb2 = pt2[:].bitcast(BF16)

                            def mtr(e, mv=mv, ptb2=ptb2):
                                ins = None
                                for h in range(NH):
                                    ins = e.transpose(ptb2[0:16, h * 128:(h + 1) * 128], mv[:, h, :], ident_b[:])
                                return ins
                            if sub >= 6:
                                sc.op("pe", mtr, reads=[mvk, ("ident_b",)], writes=[pk2])
                            mT, mTk = mring.next()
                            if sub >= 7:
                                sc.op("dve", lambda e, mT=mT, ptb2=ptb2: e.tensor_copy(
                                    mT[0:16, :, :], ptb2[0:16, :].rearrange("p (h q) -> p h q", h=NH)),
                                    reads=[pk2], writes=[mTk])
                                sc.op("dve", lambda e, mT=mT, ptb2=ptb2: e.tensor_copy(
                                    mT[64:80, :, :], ptb2[0:16, :].rearrange("p (h q) -> p h q", h=NH)),
                                    reads=[pk2], writes=[(mTk, "b")])
                        if "d_gate" in dbg and qt == NQT - 1 and s == 0:
                            dg = nc.dram_tensor("d_gate", [128, NH * 16], F32, kind="ExternalOutput").ap()
                            sc.dma("sp", dg[:, :], g[:].rearrange("p h n -> p (h n)"), reads=[gk, (gk, 1)])
                            dsm = nc.dram_tensor("d_sm", [128, NH * 16], F32, kind="ExternalOutput").ap()
                            sc.dma("sp", dsm[:, :], sm[:].rearrange("p h n -> p (h n)"), reads=[smk])
                            dmt = nc.dram_tensor("d_mt", [33, NH * 128], BF16, kind="ExternalOutput").ap()
                            sc.dma("sp", dmt[:, :], mT[:].rearrange("p h n -> p (h n)"), reads=[mTk, (mTk, "c")])
                            dkm = nc.dram_tensor("d_km", [128, NSEQ * 64], F32, kind="ExternalOutput").ap()
                            sc.dma("sp", dkm[:, :], kmean[:].rearrange("p s c n -> p (s c n)"), reads=[("kmean_b",)])
                        if bar:
                            sc.barrier()
                        zt, ztk = zring.next()
                        sc.dma("sp", zt[:], za_d[s].rearrange("(h d) t -> d h t", h=NH)[:, :, qsl],
                               reads=[("za", s, c, qt // 4) for c in range(4)], writes=[ztk])
                        yo, yok = yring.next()
                        pend = []

                        def drain(keep):
                            while len(pend) > keep:
                                pend.pop(0)()
                        for h in range(NH if lvl >= 13 else 0):
                            hp = slice((h % 2) * 64, (h % 2) * 64 + 64)
                            c = h // 2
                            nd, ndk = psN.next()
                            kts = list(range(qt + 1))
                            ngroups = (len(kts) + GRPN - 1) // GRPN
                            for gi, g0 in enumerate(range(0, len(kts), GRPN)):
                                grp = kts[g0:g0 + GRPN]
                                st_, stk = psS.next()

                                def sfn(e, grp=grp, st_=st_, hp=hp, c=c, qsl=qsl, qt=qt, QB=QB, h=h, mT=mT):
                                    ins = None
                                    for j, kt in enumerate(grp):
                                        osl = st_[:, j * 128:(j + 1) * 128]
                                        extra = []
                                        n = kt // 2
                                        if n < QB:
                                            extra.append((esel[hp, n, :], mT[hp, h, :]))
                                        ins = e.matmul(osl, kT_sb[hp, c, kt * 128:(kt + 1) * 128], qT_sb[hp, c, qsl],
                                                       start=True, stop=(len(extra) == 0))
                                        for i2, (l_, r_) in enumerate(extra):
                                            ins = e.matmul(osl, l_, r_, start=False, stop=(i2 == len(extra) - 1))
                                    return ins
                                rd = [("kT_sb", c), ("qT_sb", c), ("esel",)]
                                if mTk is not None:
                                    rd += [mTk, (mTk, "b")]
                                sc.op("pe", sfn, reads=rd, writes=[stk])
                                for j, kt in enumerate(grp):
                                    if kt >= qt - 1:
                                        dl = qt - kt
                                        sc.op("dve", lambda e, st_=st_, j=j, h=h, dl=dl: e.tensor_tensor(
                                            st_[:, j * 128:(j + 1) * 128], st_[:, j * 128:(j + 1) * 128], Tt[:, h, dl, :], ALU.add),
                                            reads=[stk, ("Tt", h, dl)], writes=[stk])
                                P, Pk = Pring.next()
                                ng = len(grp)
                                sc.op("act", lambda e, P=P, st_=st_, ng=ng, h=h: e.activation(P[:, 0:ng * 128], st_[:, 0:ng * 128], AF.Exp, bias=b31bc[:, h:h + 1]),
                                      reads=[stk, ("b31bc",)], writes=[Pk])

                                def emit_pv(grp=grp, P=P, Pk=Pk, nd=nd, ndk=ndk, h=h, qt=qt, last=(gi == ngroups - 1), yo=yo, yok=yok, zt=zt, ztk=ztk):
                                    def pvfn(e):
                                        ins = None
                                        for j, kt in enumerate(grp):
                                            ins = e.matmul(nd[:, 0:128], v_sb[:, kt, h * 128:(h + 1) * 128], P[:, j * 128:(j + 1) * 128],
                                                           start=(kt == 0), stop=(kt == qt))
                                        return ins
                                    sc.op("pe", pvfn, reads=[Pk] + [("v_sb", kt) for kt in grp], writes=[ndk])
                                    if last:
                                        rdn, rdk = rdring.next()
                                        sc.op("dve", lambda e: e.reciprocal(rdn[:], nd[64:128, 0:128]), reads=[ndk], writes=[rdk])
                                        yn, ynk = ynring.next()
                                        sc.op("dve", lambda e: e.tensor_tensor(yn[:], nd[0:64, 0:128], rdn[:], ALU.mult),
                                              reads=[ndk, rdk], writes=[ynk])
                                        sc.op("pool", lambda e: e.tensor_tensor(yo[:, h, :], yn[:], zt[:, h, :], ALU.mult),
                                              reads=[ynk, ztk], writes=[(yok, h)])
                                pend.append(emit_pv)
                                drain(1)
                        drain(0)
                        sc.dma("pool", ya_d[s].rearrange("(h d) t -> d h t", h=NH)[:, :, qsl], yo[:],
                               reads=[(yok, h) for h in range(NH)], writes=[("ya", s, qt), yok])

        if lvl >= 20 and (lvl < 40 or not moba):
            zt_ = sb("zstub", [128, 512], BF16)
            sc.op("dve", lambda e: e.memset(zt_[:], 0.0), writes=[("zstub",)])
            for s in range(NSEQ):
                for ti in range(NT):
                    for c in range(4):
                        if lvl < 40:
                            sc.dma("sp", yb_d[s, c * 128:(c + 1) * 128, ti * 512:(ti + 1) * 512], zt_[:], reads=[("zstub",)],
                                   writes=[("yb", s, ti)] if c == 3 else [("yb_part", s, ti, c)])
                        if not moba:
                            sc.dma("sp", ya_d[s, c * 128:(c + 1) * 128, ti * 512:(ti + 1) * 512], zt_[:], reads=[("zstub",)],
                                   writes=[("ya", s, ti * 4 + c)])

        sc.barrier()
        if lvl >= 40:
            es4 = ExitStack()
            with es4:
                def sb4(name, shape, dt=F32):
                    return es4.enter_context(nc.sbuf_tensor(name, list(shape), dt))
                psW = Ring(psb, "psb")
                NC_ = S // 64
                NG = 4
                HS = [64, NG, 64]
                vs2 = sb4("vs2", [64, 64])
                sc.op("dve", lambda e: e.memset(vs2[:], 0.0), writes=[("vs2",)])
                VO2 = {}
                for i, (nm, dv) in enumerate((("w0", w0_d), ("a0", a0_d), ("k_k", k_k_d), ("k_a", k_a_d), ("r_k", r_k_d))):
                    VO2[nm] = i * 8
                    sc.dma("sp", vs2[i * 8:(i + 1) * 8, :], dv.rearrange("(h d) -> h d", d=64), reads=[("vs2",)], writes=[("vs2", nm)])
                pt, pk = psW.next()
                sc.op("pe", lambda e, pt=pt: e.transpose(pt[0:64, 0:64], vs2[:], ident_f[0:64, 0:64]),
                      reads=[("vs2",), ("ident_f",)] + [("vs2", nm) for nm in VO2], writes=[pk])
                vh = sb4("vh", [64, 64])
                sc.op("dve", lambda e, pt=pt: e.tensor_copy(vh[:], pt[0:64, 0:64]), reads=[pk], writes=[("vh",)])
                omk = sb4("omk", [64, NH])
                sc.op("dve", lambda e: e.tensor_scalar(omk[:], vh[:, VO2["k_a"]:VO2["k_a"] + 8], -1.0, 1.0, ALU.mult, ALU.add),
                      reads=[("vh",)], writes=[("omk",)])

                def vb(nm):
                    o = VO2[nm] + CUR["hg"] * NG
                    return vh[:, o:o + NG].rearrange("p (h o) -> p h o", o=1).to_broadcast(HS)
                wup = sb4("wup", [64, B_W])
                aup = sb4("aup", [64, B_W])
                sc.dma("sp", wup[:], w_up_d[:, :], writes=[("wup",)])
                sc.dma("sp", aup[:], a_up_d[:, :], writes=[("aup",)])
                lnw = sb4("lnw", [64, B_W])
                lnb = sb4("lnb", [64, B_W])
                sc.dma("sp", lnw[:], lnw_d.partition_broadcast(64), writes=[("lnw",)])
                sc.dma("sp", lnb[:], lnb_d.partition_broadcast(64), writes=[("lnb",)])
                tri = sb4("tri", [64, 3, 64])
                sc.dma("sp", tri[:].rearrange("p a b -> p (a b)"), c_tri_d[:, :], writes=[("tri",)])
                ones64 = sb4("ones64", [64, 64])
                sc.op("dve", lambda e: e.memset(ones64[:], 1.0), writes=[("ones64",)])
                smask = sb4("smask", HS)
                sc.op("dve", lambda e: e.memset(smask[:], 1.0), writes=[("smask",)])
                sc.op("dve", lambda e: e.memset(smask[:, :, 0:1], 0.0), reads=[("smask",)], writes=[("smask", 1)])
                identb8 = ident_f[0:64, 0:64].rearrange("p (o d) -> p o d", o=1).to_broadcast(HS)

                def trib(i):
                    return tri[:, i:i + 1, :].to_broadcast(HS)

                T_ = {}
                CUR = {"set": 0, "list": None, "hg": 0}

                class Defer:
                    def op(self, eng, fn, reads=(), writes=()):
                        CUR["list"].append(("op", eng, fn, list(reads), list(writes), {}))

                    def dma(self, eng, out, in_, reads=(), writes=(), **kw):
                        CUR["list"].append(("dma", eng, (out, in_), list(reads), list(writes), kw))
                cur = Defer()

                def tile(name, shape=None):
                    nm = "r%d_%s" % (CUR["set"], name)
                    if nm not in T_:
                        T_[nm] = sb4(nm, shape or HS)
                    return T_[nm], (nm,)
                HstA = [[sb4("Hst%d_%d" % (q, i), HS) for i in range(2)] for q in range(4)]

                def ew(eng, fn, reads, writes):
                    cur.op(eng, fn, reads=reads, writes=writes)

                def headmm(out_fn, l_fn, r_fn, reads, pk, extra=None):
                    items = []
                    for h in range(NG):
                        pairs = [(l_fn(h), r_fn(h))] + ([(a(h), b(h)) for a, b in extra] if extra else [])
                        for i, (l_, r_) in enumerate(pairs):
                            items.append((out_fn(h), l_, r_, i == 0, i == len(pairs) - 1))

                    def fn(e, items=items):
                        ins = None
                        for (o_, l_, r_, st, sp) in items:
                            ins = e.matmul(o_, l_, r_, start=st, stop=sp)
                        return ins
                    cur.op("pe", fn, reads=reads, writes=[pk])

                def flat(t):
                    return t[:].rearrange("p h t -> p (h t)")

                for s in range(NSEQ):
                    for hg in range(2):
                        sc.op("dve", lambda e, q=(s % 2) * 2 + hg: e.memset(HstA[q][0][:], 0.0), writes=[("Hst", (s % 2) * 2 + hg, 0)])

                psSets = [Ring(psb[2 * q:2 * q + 2], "psb", keys=[("psb", j) for j in range(2 * q, 2 * q + 2)]) for q in range(4)]

                def body(s, ci, hg):
                    chain = (s % 2) * 2 + hg
                    psW = psSets[chain]
                    G0 = hg * NG
                    VB = {nm: vb(nm) for nm in VO2}
                    if True:
                        Hst = HstA[chain]
                        csl = slice(ci * 64, (ci + 1) * 64)
                        ti = ci // 8
                        hcur, hck = Hst[ci % 2], ("Hst", chain, ci % 2)
                        hnxt, hnk = Hst[(ci + 1) % 2], ("Hst", chain, (ci + 1) % 2)
                        fm = {}
                        for qi, nm in enumerate(("r", "k", "v", "z")):
                            t, tk = tile("in_" + nm)
                            cur.dma("sp", t[:], rw_d[s, qi * 512 + G0 * 64:qi * 512 + (G0 + NG) * 64, csl].rearrange("(h d) t -> d h t", h=NG),
                                   reads=[("rw", s, qi * 4 + j, ti) for j in range(4)], writes=[tk])
                            fm[nm] = (t, tk)
                        wd, wdk = tile("wd", [64, 64])
                        ad, adk = tile("ad", [64, 64])
                        cur.dma("sp", wd[:], rw_d[s, 2048:2112, csl], reads=[("rw", s, 16, ti)], writes=[wdk])
                        cur.dma("sp", ad[:], rw_d[s, 2112:2176, csl], reads=[("rw", s, 16, ti)], writes=[adk])
                        r_, rk_ = fm["r"]; k_, kk_ = fm["k"]; v_, vk_ = fm["v"]; z_, zk_ = fm["z"]
                        tw, twk = tile("tw", [64, 64])
                        ew("act", lambda e: e.activation(tw[:], wd[:], AF.Tanh), [wdk], [twk])
                        pW, pWk = psW.next()
                        headmm(lambda h: pW[0:64, h * 64:(h + 1) * 64], lambda h: wup[:, (G0 + h) * 64:(G0 + h + 1) * 64], lambda h: tw[:],
                               [twk, ("wup",)], pWk)
                        pA, pAk = psW.next()
                        headmm(lambda h: pA[0:64, h * 64:(h + 1) * 64], lambda h: aup[:, (G0 + h) * 64:(G0 + h + 1) * 64], lambda h: ad[:],
                               [adk, ("aup",)], pAk)
                        pv3 = lambda p: p[0:64, 0:NG * 64].rearrange("p (h t) -> p h t", h=NG)
                        PW = NG * 64
                        lw, lwk = tile("lw")
                        ew("dve", lambda e, pW=pW: e.tensor_tensor(lw[:], pv3(pW), VB["w0"], ALU.add), [pWk, ("vh",)], [lwk])
                        ew("act", lambda e: e.activation(flat(lw), flat(lw), AF.Sigmoid), [lwk], [lwk])
                        ew("dve", lambda e: e.tensor_scalar(flat(lw), flat(lw), -math.exp(-0.5), None, ALU.mult), [lwk], [lwk])
                        av, avk = tile("av")
                        ew("dve", lambda e, pA=pA: e.tensor_tensor(av[:], pv3(pA), VB["a0"], ALU.add), [pAk, ("vh",)], [avk])
                        ew("act", lambda e: e.activation(flat(av), flat(av), AF.Sigmoid), [avk], [avk])
                        kr, krk = tile("kr")
                        ew("dve", lambda e: e.tensor_tensor(kr[:], k_[:], VB["k_k"], ALU.mult), [kk_, ("vh",)], [krk])
                        sq, sqk = tile("sq")
                        ew("dve", lambda e: e.tensor_tensor(flat(sq), flat(kr), flat(kr), ALU.mult), [krk], [sqk])
                        pS, pSk = psW.next()
                        cur.op("pe", lambda e, pS=pS: e.matmul(pS[0:64, 0:PW], ones64[:], flat(sq), start=True, stop=True),
                              reads=[sqk, ("ones64",)], writes=[pSk])
                        rn, rnk = sq, sqk
                        ew("act", lambda e, pS=pS: e.activation(flat(rn), pS[0:64, 0:PW], AF.Sqrt, bias=1e-24, scale=1.0), [pSk], [rnk])
                        ew("dve", lambda e: e.reciprocal(flat(rn), flat(rn)), [rnk], [rnk])
                        kkn, kknk = kr, krk
                        ew("dve", lambda e: e.tensor_tensor(flat(kkn), flat(kr), flat(rn), ALU.mult), [krk, rnk], [kknk])
                        k2, k2k = tile("k2")
                        ew("dve", lambda e: e.tensor_tensor(k2[:], av[:], VB["k_a"], ALU.mult), [avk, ("vh",)], [k2k])
                        ew("dve", lambda e: e.tensor_tensor(k2[:], k2[:], omk[:, G0:G0 + NG].rearrange("p (h o) -> p h o", o=1).to_broadcast(HS), ALU.add),
                           [k2k, ("omk",)], [k2k])
                        ew("dve", lambda e: e.tensor_tensor(flat(k2), flat(k2), flat(k_), ALU.mult), [k2k, kk_], [k2k])
                        bv, bvk = tile("bv")
                        ew("dve", lambda e: e.tensor_tensor(flat(bv), flat(kkn), flat(av), ALU.mult), [kknk, avk], [bvk])
                        cs, csk = tile("cs")
                        ew("dve", lambda e: e.tensor_tensor_scan(flat(cs), flat(smask), flat(lw), 0.0, ALU.mult, ALU.add),
                           [lwk, ("smask",), ("smask", 1)], [csk])
                        ecs, ecsk = tile("ecs")
                        ew("act", lambda e: e.activation(flat(ecs), flat(cs), AF.Exp), [csk], [ecsk])
                        csx, csxk = lw, lwk
                        ew("dve", lambda e: e.tensor_tensor(flat(csx), flat(cs), flat(lw), ALU.subtract), [csk, lwk], [csxk])
                        ew("act", lambda e: e.activation(flat(csx), flat(csx), AF.Exp), [csxk], [csxk])
                        encs, encsk = cs, csk
                        ew("act", lambda e: e.activation(flat(encs), flat(cs), AF.Exp, scale=-1.0), [csk], [encsk])
                        dte, dtek = tile("dte")
                        ew("dve", lambda e: e.tensor_tensor(dte[:], encs[:], ecs[:, :, 63:64].to_broadcast(HS), ALU.mult), [encsk, ecsk], [dtek])
                        At, Atk = csx, csxk
                        ew("dve", lambda e: e.scalar_tensor_tensor(flat(At), flat(kkn), -1.0, flat(csx), ALU.mult, ALU.mult), [kknk, csxk], [Atk])
                        Bt, Btk = tile("Bt")
                        ew("dve", lambda e: e.tensor_tensor(flat(Bt), flat(bv), flat(encs), ALU.mult), [bvk, encsk], [Btk])
                        Kt, Ktk = tile("Kt")
                        ew("dve", lambda e: e.tensor_tensor(flat(Kt), flat(k2), flat(encs), ALU.mult), [k2k, encsk], [Ktk])
                        Rt, Rtk = tile("Rt")
                        ew("dve", lambda e: e.tensor_tensor(flat(Rt), flat(r_), flat(ecs), ALU.mult), [rk_, ecsk], [Rtk])
                        bc, bck = bv, bvk
                        ew("dve", lambda e: e.tensor_tensor(flat(bc), flat(bv), flat(dte), ALU.mult), [bvk, dtek], [bck])
                        kc, kck = tile("kc")
                        ew("dve", lambda e: e.tensor_tensor(flat(kc), flat(k2), flat(dte), ALU.mult), [k2k, dtek], [kck])
                        tm = {}
                        for nm, (src, srck) in (("V", (v_, vk_)), ("bc", (bc, bck)), ("kc", (kc, kck)), ("At", (At, Atk))):
                            pT_, pTk_ = psW.next()

                            def trf(e, pT_=pT_, src=src):
                                ins = None
                                for h in range(NG):
                                    ins = e.transpose(pT_[0:64, h * 64:(h + 1) * 64], src[:, h, :], ident_f[0:64, 0:64])
                                return ins
                            cur.op("pe", trf, reads=[srck, ("ident_f",)], writes=[pTk_])
                            if nm == "At":
                                X, Xk = tile("X", [64, NG, 128])
                                ew("act", lambda e, pT_=pT_: e.copy(X[:, :, 64:128], pv3(pT_)), [pTk_], [(Xk, 1)])
                            else:
                                d, dk = tile("tm_" + nm)
                                ew("act", lambda e, pT_=pT_, d=d: e.copy(flat(d), pT_[0:64, 0:PW]), [pTk_], [dk])
                                tm[nm] = (d, dk)
                        Vt, Vtk = tm["V"]; bct, bctk = tm["bc"]; kct, kctk = tm["kc"]
                        def mm_mask(name, L, Lk, Rr, Rk, mi):
                            p_, pk_ = psW.next()
                            headmm(lambda h: p_[0:64, h * 64:(h + 1) * 64], lambda h: L[:, h, :], lambda h: Rr[:, h, :], [Lk, Rk], pk_)
                            d, dk = tile(name)
                            ew("dve", lambda e, p_=p_, d=d: e.tensor_tensor(d[:], pv3(p_), trib(mi), ALU.mult), [pk_, ("tri",)], [dk])
                            return d, dk
                        Q, Qk = mm_mask("Q0", Bt, Btk, At, Atk, 0)
                        Pm, Pmk = mm_mask("P0", At, Atk, Bt, Btk, 2)
                        AakT, AakTk = mm_mask("AakT", Kt, Ktk, At, Atk, 0)
                        ArbT, ArbTk = mm_mask("ArbT", Bt, Btk, Rt, Rtk, 1)
                        ArkT, ArkTk = mm_mask("ArkT", Kt, Ktk, Rt, Rtk, 1)
                        pX, pXk = psW.next()
                        headmm(lambda h: pX[0:64, h * 64:(h + 1) * 64], lambda h: AakT[:, h, :], lambda h: Vt[:, h, :], [AakTk, Vtk], pXk)
                        ew("act", lambda e, pX=pX: e.copy(X[:, :, 0:64], pv3(pX)), [pXk], [(Xk, 0)])
                        Xkeys = [(Xk, 0), (Xk, 1)]
                        for lv in range(6):
                            pa_, pak_ = psW.next()

                            def apf(e, pa_=pa_, Q=Q):
                                ins = None
                                for h in range(NG):
                                    ins = e.matmul(pa_[0:64, h * 128:(h + 1) * 128], Q[:, h, :], X[:, h, :], start=True, stop=True)
                                return ins
                            cur.op("pe", apf, reads=[Qk] + Xkeys, writes=[pak_])
                            ew("dve", lambda e, pa_=pa_: e.tensor_tensor(X[:], X[:], pa_[0:64, 0:NG * 128].rearrange("p (h t) -> p h t", h=NG), ALU.add),
                               [pak_] + Xkeys, Xkeys)
                            if lv < 5:
                                pq_, pqk_ = psW.next()
                                headmm(lambda h, pq_=pq_: pq_[0:64, h * 64:(h + 1) * 64], lambda h, Pm=Pm: Pm[:, h, :], lambda h, Q=Q: Q[:, h, :], [Pmk, Qk], pqk_)
                                Q2, Q2k = tile("Q%d" % ((lv + 1) % 2))
                                if lv < 4:
                                    pp_, ppk_ = psW.next()
                                    headmm(lambda h, pp_=pp_: pp_[0:64, h * 64:(h + 1) * 64], lambda h, Q=Q: Q[:, h, :], lambda h, Pm=Pm: Pm[:, h, :], [Pmk, Qk], ppk_)
                                    P2, P2k = tile("P%d" % ((lv + 1) % 2))
                                    ew("act", lambda e, pp_=pp_, P2=P2: e.copy(flat(P2), pp_[0:64, 0:PW]), [ppk_], [P2k])
                                ew("dve", lambda e, pq_=pq_, Q2=Q2: e.tensor_copy(flat(Q2), pq_[0:64, 0:PW]), [pqk_], [Q2k])
                                Q, Qk = Q2, Q2k
                                if lv < 4:
                                    Pm, Pmk = P2, P2k
                        U0 = lambda h: X[:, h, 0:64]
                        Ah = lambda h: X[:, h, 64:128]
                        pM, pMk = psW.next()
                        headmm(lambda h: pM[0:64, h * 64:(h + 1) * 64], Ah, lambda h: bct[:, h, :], Xkeys + [bctk], pMk)
                        Mc, Mck = tile("Mc")
                        ew("dve", lambda e: e.tensor_tensor(Mc[:], identb8, ecs[:, :, 63:64].to_broadcast(HS), ALU.mult), [ecsk, ("ident_f",)], [Mck])
                        ew("dve", lambda e, pM=pM: e.tensor_tensor(Mc[:], Mc[:], pv3(pM), ALU.add), [pMk, Mck], [Mck])
                        pG, pGk = psW.next()
                        headmm(lambda h: pG[0:64, h * 64:(h + 1) * 64], lambda h: bct[:, h, :], U0, Xkeys + [bctk, kctk, Vtk], pGk,
                               extra=[(lambda h: kct[:, h, :], lambda h: Vt[:, h, :])])
                        G, Gk = tile("G")
                        ew("act", lambda e, pG=pG: e.copy(flat(G), pG[0:64, 0:PW]), [pGk], [Gk])
                        pR, pRk = psW.next()
                        headmm(lambda h: pR[0:64, h * 64:(h + 1) * 64], Ah, lambda h: ArbT[:, h, :], Xkeys + [ArbTk], pRk)
                        Rh, Rhk = tile("Rh")
                        ew("dve", lambda e, pR=pR: e.tensor_tensor(Rh[:], Rt[:], pv3(pR), ALU.add), [pRk, Rtk], [Rhk])
                        pO, pOk = psW.next()
                        headmm(lambda h: pO[0:64, h * 64:(h + 1) * 64], lambda h: ArbT[:, h, :], U0, Xkeys + [ArbTk, ArkTk, Vtk, Rhk, hck], pOk,
                               extra=[(lambda h: ArkT[:, h, :], lambda h: Vt[:, h, :]), (lambda h: Rh[:, h, :], lambda h, hcur=hcur: hcur[:, h, :])])
                        pH, pHk = psW.next()
                        headmm(lambda h: pH[0:64, h * 64:(h + 1) * 64], lambda h: Mc[:, h, :], lambda h, hcur=hcur: hcur[:, h, :], [Mck, hck], pHk)
                        ew("dve", lambda e, pH=pH, hnxt=hnxt: e.tensor_tensor(hnxt[:], G[:], pv3(pH), ALU.add), [pHk, Gk], [hnk])
                        Ot, Otk = tile("Ot")
                        ew("act", lambda e, pO=pO: e.copy(flat(Ot), pO[0:64, 0:PW]), [pOk], [Otk])
                        st1, st1k = tile("st1", [64, NG])
                        st2, st2k = tile("st2", [64, NG])
                        junk, junkk = tile("junk", [64, 64])
                        for h in range(NG):
                            ew("act", lambda e, h=h: e.activation(junk[:], Ot[:, h, :], AF.Copy, accum_out=st1[:, h:h + 1]), [Otk], [junkk, (st1k, h)])
                            ew("act", lambda e, h=h: e.activation(junk[:], Ot[:, h, :], AF.Square, accum_out=st2[:, h:h + 1]), [Otk], [junkk, (st2k, h)])
                        st1a = [(st1k, h) for h in range(NG)]
                        st2a = [(st2k, h) for h in range(NG)]
                        ew("dve", lambda e: e.tensor_scalar(st1[:], st1[:], 1.0 / 64, None, ALU.mult), st1a, st1a)
                        msq, msqk = tile("msq", [64, NG])
                        ew("dve", lambda e: e.tensor_tensor(msq[:], st1[:], st1[:], ALU.mult), st1a, [msqk])
                        ew("dve", lambda e: e.scalar_tensor_tensor(st2[:], st2[:], 1.0 / 64, msq[:], ALU.mult, ALU.subtract), st2a + [msqk], st2a)
                        ew("act", lambda e: e.activation(st2[:], st2[:], AF.Sqrt, bias=float(GN_EPS), scale=1.0), st2a, st2a)
                        ew("dve", lambda e: e.reciprocal(st2[:], st2[:]), st2a, st2a)
                        b3 = lambda t: t[:].rearrange("p (h o) -> p h o", o=1).to_broadcast(HS)
                        ew("dve", lambda e: e.tensor_tensor(Ot[:], Ot[:], b3(st1), ALU.subtract), [Otk] + st1a, [Otk])
                        ew("dve", lambda e: e.tensor_tensor(Ot[:], Ot[:], b3(st2), ALU.mult), [Otk] + st2a, [Otk])
                        ew("dve", lambda e: e.tensor_tensor(flat(Ot), flat(Ot), lnw[:, G0 * 64:(G0 + NG) * 64], ALU.mult), [Otk, ("lnw",)], [Otk])
                        ew("dve", lambda e: e.tensor_tensor(flat(Ot), flat(Ot), lnb[:, G0 * 64:(G0 + NG) * 64], ALU.add), [Otk, ("lnb",)], [Otk])
                        rk3, rk3k = tile("rk3")
                        ew("dve", lambda e: e.tensor_tensor(flat(rk3), flat(r_), flat(k2), ALU.mult), [rk_, k2k], [rk3k])
                        ew("dve", lambda e: e.tensor_tensor(rk3[:], rk3[:], VB["r_k"], ALU.mult), [rk3k, ("vh",)], [rk3k])
                        pBn, pBnk = psW.next()
                        headmm(lambda h: pBn[0:64, h:h + 1], lambda h: rk3[:, h, :], lambda h: ones64[:, 0:1], [rk3k, ("ones64",)], pBnk)
                        sbn, sbnk = tile("sbn", [64, NG])
                        ew("dve", lambda e, pBn=pBn: e.tensor_copy(sbn[:], pBn[0:64, 0:NG]), [pBnk], [sbnk])
                        bon, bonk = tile("bon")
                        ew("dve", lambda e: e.tensor_tensor(bon[:], Vt[:], b3(sbn), ALU.mult), [Vtk, sbnk], [bonk])
                        ew("dve", lambda e: e.tensor_tensor(flat(Ot), flat(Ot), flat(bon), ALU.add), [Otk, bonk], [Otk])
                        pY, pYk = psW.next()

                        def tyf(e, pY=pY):
                            ins = None
                            for h in range(NG):
                                ins = e.transpose(pY[0:64, h * 64:(h + 1) * 64], Ot[:, h, :], ident_f[0:64, 0:64])
                            return ins
                        cur.op("pe", tyf, reads=[Otk, ("ident_f",)], writes=[pYk])
                        zs, zsk = tile("zs")
                        ew("act", lambda e: e.activation(flat(zs), flat(z_), AF.Silu), [zk_], [zsk])
                        ybn = "ybb%d" % CUR["set"]
                        if ybn not in T_:
                            T_[ybn] = es4.enter_context(nc.sbuf_tensor(ybn, [64, NG, 64], BF16))
                        ybb, ybbk = T_[ybn], (ybn,)
                        ew("dve", lambda e, pY=pY: e.tensor_tensor(ybb[:], zs[:], pv3(pY), ALU.mult), [pYk, zsk], [ybbk])
                        cur.dma("pool", yb_d[s, G0 * 64:(G0 + NG) * 64, :].rearrange("(h d) t -> d h t", h=NG)[:, :, csl], ybb[:], reads=[ybbk],
                               writes=[("yb_c", s, ci, hg)] + ([("yb", s, ti, hg)] if ci % 8 == 7 else []))


                def flush(lists):
                    n = max(len(l) for l in lists)
                    for i in range(n):
                        for l in lists:
                            if i < len(l):
                                kind, eng, a_, rd, wr, kw = l[i]
                                if kind == "op":
                                    sc.op(eng, a_, reads=rd, writes=wr)
                                else:
                                    sc.dma(eng, a_[0], a_[1], reads=rd, writes=wr, **kw)

                for s0 in range(0, NSEQ, 2):
                    for ci in range(NC_):
                        lists = []
                        for s in range(s0, min(s0 + 2, NSEQ)):
                            for hg in range(2):
                                CUR["set"] = (s % 2) * 2 + hg
                                CUR["hg"] = hg
                                CUR["list"] = []
                                body(s, ci, hg)
                                lists.append(CUR["list"])
                        flush(lists)

        sc.barrier()
        if lvl >= 30:
            es3 = ExitStack()
            with es3:
                def sb3(name, shape, dt=F32):
                    return es3.enter_context(nc.sbuf_tensor(name, list(shape), dt))
                psR = Ring(psb, "psb")
                wstg = [sb3("wstg%d" % i, [128, D]) for i in range(2)]
                wsr = Ring(wstg, "wstg")

                def load_w(name, dram, nk):
                    t = sb3(name, [128, nk, D], BF16)
                    for kc in range(nk):
                        st, stk = wsr.next()
                        sc.dma("sp", st[:], dram[kc * 128:(kc + 1) * 128, :], writes=[stk])
                        sc.op(("dve", "pool")[kc % 2], lambda e, st=st, kc=kc, t=t: e.tensor_copy(t[:, kc, :], st[:]),
                              reads=[stk], writes=[(name, kc)])
                    return t, [(name, kc) for kc in range(nk)]
                pa_sb, pa_k = load_w("pa_sb", p_a_d, 4)
                pb_sb, pb_k = load_w("pb_sb", p_b_d, 4)
                wo_sb, wo_k = load_w("wo_sb", w_out_d, 8)
                wg_sb, wg_k = load_w("wg_sb", w_pg_d, 8)
                wu_sb, wu_k = load_w("wu_sb", w_pu_d, 2)
                gpost = sb3("gpost", [128, D])
                sc.dma("sp", gpost[:], g_post_d.partition_broadcast(128), writes=[("gpost",)])
                yaT = [sb3("yaT%d" % i, [128, 4, 512], BF16) for i in range(2)]
                ybT = [sb3("ybT%d" % i, [128, 4, 512], BF16) for i in range(2)]
                gtT = [sb3("gtT%d" % i, [128, 16, 512], BF16) for i in range(2)]
                yar, ybr, gtr = Ring(yaT, "yaT"), Ring(ybT, "ybT"), Ring(gtT, "gtT")
                t1b = [sb3("t1b%d" % i, [128, 512]) for i in range(2)]
                t1r = Ring(t1b, "t1b")
                mgT = [sb3("mgT%d" % i, [128, 8, 512], BF16) for i in range(2)]
                mgr = Ring(mgT, "mgT")
                x3 = [sb3("x3_%d" % i, [128, D]) for i in range(2)]
                x3r = Ring(x3, "x3")
                p3 = [sb3("p3_%d" % i, [128, PLE]) for i in range(2)]
                p3r = Ring(p3, "p3")
                p3b = [sb3("p3b_%d" % i, [128, PLE], BF16) for i in range(2)]
                p3br = Ring(p3b, "p3b")
                ysb = [sb3("ysb%d" % i, [128, D]) for i in range(2)]
                ysr = Ring(ysb, "ysb")
                sq3 = sb3("sq3", [128, D], BF16)
                st3 = [sb3("st3_%d" % i, [128, 2]) for i in range(2)]
                st3r = Ring(st3, "st3")
                hsb = [sb3("hsb%d" % i, [128, D]) for i in range(2)]
                hsr = Ring(hsb, "hsb")
                hbb = [sb3("hbb%d" % i, [128, D], BF16) for i in range(2)]
                hbr = Ring(hbb, "hbb")
                hT = [sb3("hT%d" % i, [128, 8, 128], BF16) for i in range(2)]
                hTr = Ring(hT, "hT")
                pT = [sb3("pT%d" % i, [128, 2, 128], BF16) for i in range(2)]
                pTr = Ring(pT, "pT")
                sg = [sb3("sg%d" % i, [128, D]) for i in range(2)]
                sgr = Ring(sg, "sg")
                osb = [sb3("osb%d" % i, [128, D]) for i in range(2)]
                osr = Ring(osb, "osb")

                for s in range(NSEQ):
                    for ti in range(NT):
                        tsl = slice(ti * 512, (ti + 1) * 512)
                        ya, yak = yar.next()
                        yb, ybk = ybr.next()
                        gt, gtk = gtr.next()
                        sc.dma("sp", ya[:], ya_d[s, :, tsl].rearrange("(c p) t -> p c t", p=128),
                               reads=[("ya", s, qt) for qt in range(ti * 4, ti * 4 + 4)], writes=[yak])
                        sc.dma("sp", yb[:], yb_d[s, :, tsl].rearrange("(c p) t -> p c t", p=128),
                               reads=[("yb", s, ti), ("yb", s, ti, 0), ("yb", s, ti, 1)], writes=[ybk])
                        sc.dma("sp", gt[:], gt_d[s, :, tsl].rearrange("(c p) t -> p c t", p=128),
                               reads=[("gt", s, j, ti) for j in range(16)], writes=[gtk])
                        mg, mgk = mgr.next()
                        for m in range(8):
                            pA, pAk = psR.next()
                            pB, pBk = psR.next()

                            def abfn(e, pA=pA, pB=pB, m=m, ya=ya, yb=yb):
                                ins = None
                                for c in range(4):
                                    ins = e.matmul(pA[:], pa_sb[:, c, m * 128:(m + 1) * 128], ya[:, c, :], start=(c == 0), stop=(c == 3))
                                for c in range(4):
                                    ins = e.matmul(pB[:], pb_sb[:, c, m * 128:(m + 1) * 128], yb[:, c, :], start=(c == 0), stop=(c == 3))
                                return ins
                            sc.op("pe", abfn, reads=[yak, ybk] + pa_k + pb_k, writes=[pAk, pBk])
                            t1, t1k = t1r.next()
                            sc.op("dve", lambda e, t1=t1, pA=pA, gt=gt, m=m: e.tensor_tensor(t1[:], pA[:], gt[:, m, :], ALU.mult),
                                  reads=[pAk, gtk], writes=[t1k])
                            t2, t2k = t1r.next()
                            sc.op("dve", lambda e, t2=t2, pB=pB, gt=gt, m=m: e.tensor_tensor(t2[:], pB[:], gt[:, 8 + m, :], ALU.mult),
                                  reads=[pBk, gtk], writes=[t2k])
                            sc.op("pool", lambda e, mg=mg, t1=t1, t2=t2, m=m: e.tensor_tensor(mg[:, m, :], t1[:], t2[:], ALU.add),
                                  reads=[t1k, t2k], writes=[(mgk, m)])
                        mg_keys = [(mgk, m) for m in range(8)]
                        for a in range(4):
                            tok0 = s * S + ti * 512 + a * 128
                            xt3, x3k = x3r.next()
                            sc.dma("sp", xt3[:], x_d[tok0:tok0 + 128, :], writes=[x3k])
                            pt3, p3k = p3r.next()
                            sc.dma("sp", pt3[:], p_d[tok0:tok0 + 128, :], writes=[p3k])
                            yps = []
                            for half in range(2):
                                pY, pYk = psR.next()

                                def yfn(e, pY=pY, half=half, mg=mg, a=a):
                                    ins = None
                                    for m in range(8):
                                        ins = e.matmul(pY[:], mg[:, m, a * 128:(a + 1) * 128], wo_sb[:, m, half * 512:(half + 1) * 512],
                                                       start=(m == 0), stop=(m == 7))
                                    return ins
                                sc.op("pe", yfn, reads=mg_keys + wo_k, writes=[pYk])
                                yps.append((pY, pYk))
                            ys, ysk = ysr.next()
                            stt, sttk = st3r.next()
                            for half in range(2):
                                pY, pYk = yps[half]
                                sc.op("act", lambda e, ys=ys, pY=pY, half=half: e.copy(ys[:, half * 512:(half + 1) * 512], pY[:]),
                                      reads=[pYk], writes=[(ysk, half)])
                            sc.op("act", lambda e, ys=ys, stt=stt: e.activation(sq3[:], ys[:], AF.Square, accum_out=stt[:, 0:1]),
                                  reads=[(ysk, 0), (ysk, 1)], writes=[("sq3",), (sttk, 0)])
                            sc.op("act", lambda e, stt=stt: e.activation(stt[:, 0:1], stt[:, 0:1], AF.Sqrt, bias=float(RMS_EPS), scale=1.0 / D),
                                  reads=[(sttk, 0)], writes=[(sttk, 0)])
                            sc.op("dve", lambda e, stt=stt: e.reciprocal(stt[:, 1:2], stt[:, 0:1]), reads=[(sttk, 0)], writes=[(sttk, 1)])
                            hs, hsk = hsr.next()
                            sc.op("dve", lambda e, hs=hs, ys=ys, stt=stt: e.scalar_tensor_tensor(
                                hs[:], ys[:], stt[:, 1:2], gpost[:], ALU.mult, ALU.mult),
                                reads=[(ysk, 0), (ysk, 1), (sttk, 1), ("gpost",)], writes=[hsk])
                            sc.op("pool", lambda e, hs=hs, xt3=xt3: e.tensor_tensor(hs[:], hs[:], xt3[:], ALU.add),
                                  reads=[hsk, x3k], writes=[hsk])
                            hb, hbk = hbr.next()
                            sc.op("act", lambda e, hb=hb, hs=hs: e.copy(hb[:], hs[:]), reads=[hsk], writes=[hbk])
                            pb3, p3bk = p3br.next()
                            sc.op("pool", lambda e, pb3=pb3, pt3=pt3: e.tensor_copy(pb3[:], pt3[:]), reads=[p3k], writes=[p3bk])
                            pTp, pTpk = psR.next()
                            ptb = pTp[:].bitcast(BF16)

                            def trfn(e, ptb=ptb, hb=hb):
                                ins = None
                                for m in range(8):
                                    ins = e.transpose(ptb[:, m * 128:(m + 1) * 128], hb[:, m * 128:(m + 1) * 128], ident_b[:])
                                return ins
                            sc.op("pe", trfn, reads=[hbk, ("ident_b",)], writes=[pTpk])
                            hTt, hTk = hTr.next()
                            sc.op("dve", lambda e, hTt=hTt, ptb=ptb: e.tensor_copy(hTt[:], ptb.rearrange("p (m t) -> p m t", m=8)),
                                  reads=[pTpk], writes=[hTk])
                            pP, pPk = psR.next()
                            ppb = pP[:].bitcast(BF16)

                            def trp(e, ppb=ppb, pb3=pb3):
                                ins = None
                                for j in range(2):
                                    ins = e.transpose(ppb[:, j * 128:(j + 1) * 128], pb3[:, j * 128:(j + 1) * 128], ident_b[:])
                                return ins
                            sc.op("pe", trp, reads=[p3bk, ("ident_b",)], writes=[pPk])
                            pTt, pTk = pTr.next()
                            sc.op("act", lambda e, pTt=pTt, ppb=ppb: e.copy(pTt[:], ppb[:, 0:256].rearrange("p (m t) -> p m t", m=2)),
                                  reads=[pPk], writes=[pTk])
                            sgt, sgk = sgr.next()
                            ot, otk = osr.next()
                            for half in range(2):
                                pG, pGk = psR.next()

                                def gfn3(e, pG=pG, half=half, hTt=hTt):
                                    ins = None
                                    for m in range(8):
                                        ins = e.matmul(pG[:], hTt[:, m, :], wg_sb[:, m, half * 512:(half + 1) * 512], start=(m == 0), stop=(m == 7))
                                    return ins
                                sc.op("pe", gfn3, reads=[hTk] + wg_k, writes=[pGk])
                                sc.op("act", lambda e, sgt=sgt, pG=pG, half=half: e.activation(sgt[:, half * 512:(half + 1) * 512], pG[:], AF.Sigmoid),
                                      reads=[pGk], writes=[(sgk, half)])
                                pE, pEk = psR.next()

                                def efn(e, pE=pE, half=half, pTt=pTt):
                                    ins = None
                                    for j in range(2):
                                        ins = e.matmul(pE[:], pTt[:, j, :], wu_sb[:, j, half * 512:(half + 1) * 512], start=(j == 0), stop=(j == 1))
                                    return ins
                                sc.op("pe", efn, reads=[pTk] + wu_k, writes=[pEk])
                                sc.op("dve", lambda e, sgt=sgt, pE=pE, half=half: e.tensor_tensor(
                                    sgt[:, half * 512:(half + 1) * 512], pE[:], sgt[:, half * 512:(half + 1) * 512], ALU.mult),
                                    reads=[pEk, (sgk, half)], writes=[(sgk, half)])
                            sc.op("pool", lambda e, ot=ot, sgt=sgt, hs=hs: e.tensor_tensor(ot[:], sgt[:], hs[:], ALU.add),
                                  reads=[(sgk, 0), (sgk, 1), hsk], writes=[otk])
                            sc.dma("pool", out_d[tok0:tok0 + 128, :], ot[:], reads=[otk], writes=[("out", tok0)], final=True)

        sc.emit(nc, es)
    return nc


def t5_bucket_np(n):
    n = np.maximum(n, 0)
    nf = np.maximum(n, 1).astype(np.float32)
    large = 16 + (np.log(nf / np.float32(16)) / np.float32(math.log(128 / 16)) * np.float32(16)).astype(np.int32)
    large = np.minimum(large, 31)
    return np.where(n < 16, n, large)


def make_consts():
    c = {}
    c["c_ident"] = np.eye(128, dtype=np.float32)
    oh = np.zeros((33, 512), np.float32)
    d = np.arange(512) - 128
    bk = t5_bucket_np(d)
    for j in range(512):
        if d[j] >= 0:
            oh[bk[j], j] = 1.0
        else:
            oh[32, j] = 1.0
    c["c_onehot"] = oh
    es_ = np.zeros((128, 16, 128), np.float32)
    for n in range(16):
        es_[n, n, :] = 1.0
        es_[64 + n, n, :] = 1.0
    c["c_esel"] = es_.reshape(128, 16 * 128)
    tri = np.zeros((64, 3, 64), np.float32)
    i = np.arange(64)
    tri[:, 0, :] = (i[:, None] < i[None, :])
    tri[:, 1, :] = (i[:, None] <= i[None, :])
    tri[:, 2, :] = (i[:, None] > i[None, :])
    c["c_tri"] = tri.reshape(64, 192)
    bd = np.zeros((128, 128), np.float32)
    bd[:64, :64] = 1.0
    bd[64:, 64:] = 1.0
    c["c_bd"] = bd
    c["c_J"] = np.ascontiguousarray(np.eye(128, dtype=np.float32)[::-1])
    pen = np.zeros((16, NH, 16), np.float32)
    for qb in range(16):
        pen[qb, :, qb:] = -30000.0
    c["c_pen"] = pen.reshape(-1)
    return c


_WNAMES = ["g_pre", "w_in", "mu_shift", "w0", "w_up", "a0", "a_up", "k_k", "k_a", "r_k", "ln_x_w", "ln_x_b",
           "p_a", "p_b", "w_out", "g_post", "w_ple_up", "w_ple_gate"]


def make_in_maps(inputs, ncores, nseq, S):
    consts = make_consts()
    maps = []
    for c in range(ncores):
        m = dict(consts)
        m["x"] = np.ascontiguousarray(inputs["x"][c * nseq:(c + 1) * nseq].reshape(nseq * S, D))
        m["p"] = np.ascontiguousarray(inputs["p"][0, c * nseq:(c + 1) * nseq].reshape(nseq * S, PLE))
        m["rel_bias"] = np.ascontiguousarray(inputs["rel_bias"])
        for n in _WNAMES:
            a = np.asarray(inputs[n])[0]
            m[n] = np.ascontiguousarray(a.reshape(-1) if n == "r_k" else a)
        maps.append(m)
    return maps


def kernel(**inputs):
    inputs = {k: np.asarray(v) for k, v in inputs.items()}
    B, S, _ = inputs["x"].shape
    nseq = B // NCORES
    nc = build_program(S, nseq, lvl=99, moba=True)
    maps = make_in_maps(inputs, NCORES, nseq, S)
    res = run_bass_kernel_spmd(nc, maps, core_ids=list(range(NCORES)))
    outs = [r["out"].reshape(nseq, S, D) for r in res.results]
    return np.concatenate(outs, axis=0).astype(np.float32)
```

```python
import math
from contextlib import ExitStack

import numpy as np
import concourse.bass as bass
import concourse.mybir as mybir
from concourse.bass_utils import run_bass_kernel_spmd

F32 = mybir.dt.float32
BF16 = mybir.dt.bfloat16
F32R = mybir.dt.float32r
ALU = mybir.AluOpType
AF = mybir.ActivationFunctionType
AX = mybir.AxisListType

D = 1024
NCORES = 8
A_W = 512
B_W = 512
HD = 64
NH = 8
IN_COLS = 6272
RW_COLS = 2176
PLE = 256
BLK = 256
RMS_EPS = 1e-6
GN_EPS = 64e-5
NEG = -30000.0


class Sched:
    ENGS = ("pe", "act", "dve", "pool", "sp")
    NDMA = 8

    def __init__(self):
        self.streams = {e: [] for e in self.ENGS}
        self.waited = {e: {} for e in self.ENGS}
        self.last_w = {}
        self.readers = {}
        self.dma_cnt = {e: 0 for e in self.ENGS}
        self.dma_val = {}
        self.final_dma = []

    def _add_wait(self, eng, waits, ev):
        if ev is None:
            return
        if ev[0] == "eng":
            _, e2, j = ev
            if e2 == "pe" and eng == "pe":
                return
            key = e2
            val = j
        else:
            _, key, val = ev
        if self.waited[eng].get(key, -1) >= val:
            return
        self.waited[eng][key] = val
        waits.append(ev)
        if ev[0] == "eng":
            self.streams[ev[1]][ev[2]]["sig"] = True

    def _deps(self, eng, reads, writes):
        evs = []
        for k in reads:
            evs.append(self.last_w.get(k))
        for k in writes:
            evs.append(self.last_w.get(k))
            evs.extend(self.readers.get(k, ()))
        best = {}
        for ev in evs:
            if ev is None:
                continue
            key = ev[1]
            if key not in best or ev[2] > best[key][2]:
                best[key] = ev
        waits = []
        for ev in best.values():
            self._add_wait(eng, waits, ev)
        return waits

    def _commit(self, ev, reads, writes):
        for k in reads:
            self.readers.setdefault(k, []).append(ev)
        for k in writes:
            self.last_w[k] = ev
            self.readers[k] = []

    def op(self, eng, fn, reads=(), writes=()):
        waits = self._deps(eng, reads, writes)
        idx = len(self.streams[eng])
        self.streams[eng].append({"fn": fn, "waits": waits, "sig": False, "dma": None})
        self._commit(("eng", eng, idx), reads, writes)

    def dma(self, eng, out, in_, reads=(), writes=(), final=False, **kw):
        slot = self.dma_cnt[eng] % self.NDMA
        self.dma_cnt[eng] += 1
        key = (eng, slot)
        prev = self.dma_val.get(key, 0)
        waits = self._deps(eng, reads, writes)
        if prev > 0:
            self._add_wait(eng, waits, ("dma", key, prev))
        val = prev + 16
        self.dma_val[key] = val
        fn = lambda e, out=out, in_=in_, kw=kw: e.dma_start(out=out, in_=in_, **kw)
        self.streams[eng].append({"fn": fn, "waits": waits, "sig": False, "dma": key})
        ev = ("dma", key, val)
        self._commit(ev, reads, writes)
        if final:
            self.final_dma.append(ev)

    def barrier(self):
        evs = []
        for e in ("pe", "act", "dve", "pool"):
            for j in range(len(self.streams[e]) - 1, -1, -1):
                if self.streams[e][j]["fn"] is not None and self.streams[e][j]["dma"] is None:
                    evs.append(("eng", e, j))
                    break
        for key, val in self.dma_val.items():
            evs.append(("dma", key, val))
        for e in self.ENGS:
            waits = []
            for ev in evs:
                if ev[0] == "eng" and ev[1] == e:
                    continue
                self._add_wait(e, waits, ev)
            self.streams[e].append({"fn": None, "waits": waits, "sig": False, "dma": None})
        self.last_w = {}
        self.readers = {}

    def emit(self, nc, es):
        sems = {e: es.enter_context(nc.semaphore("sem_" + e)) for e in ("pe", "act", "dve", "pool")}
        dsems = {}
        for (e, slot) in self.dma_val:
            dsems[(e, slot)] = es.enter_context(nc.semaphore("dsem_%s%d" % (e, slot)))
        fin_waits = []
        for ev in self.final_dma:
            self._add_wait("sp", fin_waits, ev)
        counts = {}
        for e in ("pe", "act", "dve", "pool"):
            c = 0
            lst = []
            for o in self.streams[e]:
                if o["sig"]:
                    c += 1
                lst.append(c)
            counts[e] = lst
        block = es.enter_context(nc.Block())

        def run(engname, eobj):
            def do_wait(ev):
                if ev[0] == "eng":
                    eobj.wait_ge(sems[ev[1]], counts[ev[1]][ev[2]])
                else:
                    eobj.wait_ge(dsems[ev[1]], ev[2])
            for o in self.streams[engname]:
                for ev in o["waits"]:
                    do_wait(ev)
                if o["fn"] is None:
                    continue
                ins = o["fn"](eobj)
                if o["dma"] is not None:
                    ins.then_inc(dsems[o["dma"]], 16)
                elif o["sig"]:
                    ins.then_inc(sems[engname], 1)
            if engname == "sp":
                for ev in fin_waits:
                    do_wait(ev)

        @block.sync
        def _(e):
            run("sp", e)

        @block.tensor
        def _(e):
            run("pe", e)

        @block.scalar
        def _(e):
            run("act", e)

        @block.vector
        def _(e):
            run("dve", e)

        @block.gpsimd
        def _(e):
            run("pool", e)


class Ring:
    def __init__(self, tiles, name, keys=None):
        self.tiles = tiles
        self.keys = keys if keys is not None else [(name, j) for j in range(len(tiles))]
        self.i = 0

    def next(self):
        j = self.i % len(self.tiles)
        self.i += 1
        return self.tiles[j], self.keys[j]


def build_program(S, NSEQ, dbg=None, lvl=30, sub=99, moba=True, only_even=False, GRPN=4, bar=False):
    dbg = dbg or set()
    nc = bass.Bass("TRN2", target_bir_lowering=False)
    NT = S // 512
    NQT = S // 128
    NBLK = S // BLK
    NTOK = NSEQ * S

    def din(name, shape, dt=F32):
        return nc.dram_tensor(name, list(shape), dt, kind="ExternalInput").ap()

    x_d = din("x", [NTOK, D])
    p_d = din("p", [NTOK, PLE])
    g_pre_d = din("g_pre", [D])
    w_in_d = din("w_in", [D, IN_COLS])
    relb_d = din("rel_bias", [32, NH])
    mu_d = din("mu_shift", [RW_COLS])
    w0_d = din("w0", [B_W])
    w_up_d = din("w_up", [64, B_W])
    a0_d = din("a0", [B_W])
    a_up_d = din("a_up", [64, B_W])
    k_k_d = din("k_k", [B_W])
    k_a_d = din("k_a", [B_W])
    r_k_d = din("r_k", [B_W])
    lnw_d = din("ln_x_w", [B_W])
    lnb_d = din("ln_x_b", [B_W])
    p_a_d = din("p_a", [A_W, D])
    p_b_d = din("p_b", [B_W, D])
    w_out_d = din("w_out", [D, D])
    g_post_d = din("g_post", [D])
    w_pu_d = din("w_ple_up", [PLE, D])
    w_pg_d = din("w_ple_gate", [D, D])
    c_ident_d = din("c_ident", [128, 128])
    c_onehot_d = din("c_onehot", [33, 512])
    c_esel_d = din("c_esel", [128, 16 * 128])
    c_tri_d = din("c_tri", [64, 3 * 64])
    c_bd_d = din("c_bd", [128, 128])
    c_J_d = din("c_J", [128, 128])
    c_pen_d = din("c_pen", [16 * 128])

    out_d = nc.dram_tensor("out", [NTOK, D], F32, kind="ExternalOutput").ap()

    def scratch(name, shape, dt):
        kind = "ExternalOutput" if name in dbg else "Internal"
        return nc.dram_tensor(name, list(shape), dt, kind=kind).ap()

    qT_d = scratch("s_qT", [NSEQ, A_W, S], BF16)
    kT_d = scratch("s_kT", [NSEQ, A_W, S], BF16)
    v_d = scratch("s_v", [NSEQ, S, 2 * A_W], BF16)
    za_d = scratch("s_za", [NSEQ, A_W, S], BF16)
    rw_d = scratch("s_rw", [NSEQ, RW_COLS, S], F32)
    gt_d = scratch("s_gt", [NSEQ, 2 * D, S], BF16)
    ya_d = scratch("s_ya", [NSEQ, A_W, S], BF16)
    yb_d = scratch("s_yb", [NSEQ, B_W, S], BF16)
    fb_d = scratch("s_fb", [NH, 512], F32)

    sc = Sched()
    es = ExitStack()

    def sb(name, shape, dt=F32):
        return es.enter_context(nc.sbuf_tensor(name, list(shape), dt))

    def ps(name, shape, dt=F32):
        return es.enter_context(nc.psum_tensor(name, list(shape), dt))

    with es:
        ident_f = sb("ident_f", [128, 128])
        ident_b = sb("ident_b", [128, 128], BF16)
        sc.dma("sp", ident_f[:], c_ident_d[:, :], writes=[("ident_f",)])
        sc.op("dve", lambda e: e.tensor_copy(ident_b[:], ident_f[:]), reads=[("ident_f",)], writes=[("ident_b",)])
        psb = [ps("psb%d" % i, [128, 512]) for i in range(8)]
        psring = Ring(psb, "psb")

        vstage = sb("vstage", [64, 128])
        vc = sb("vc", [128, 64])
        VOFF = {}
        _r = 0
        sc.op("dve", lambda e: e.memset(vstage[:], 0.0), writes=[("vstage",)])
        for nm, dv, n in (("g_pre", g_pre_d, 8), ("mu", mu_d, 17), ("w0", w0_d, 4), ("a0", a0_d, 4),
                          ("k_k", k_k_d, 4), ("k_a", k_a_d, 4), ("r_k", r_k_d, 4)):
            VOFF[nm] = _r
            sc.dma("sp", vstage[_r:_r + n, :], dv.rearrange("(k p) -> k p", p=128), reads=[("vstage",)], writes=[("vstage", nm)])
            _r += n
        pt, pk = psring.next()
        sc.op("pe", lambda e, pt=pt: e.transpose(pt[:, 0:64], vstage[:], ident_f[0:64, 0:64]),
              reads=[("vstage",), ("ident_f",)] + [("vstage", nm) for nm in VOFF], writes=[pk])
        sc.op("dve", lambda e, pt=pt: e.tensor_copy(vc[:], pt[:, 0:64]), reads=[pk], writes=[("vc",)])
        gpre_c = vc[:, VOFF["g_pre"]:VOFF["g_pre"] + 8]

        kmean = sb("kmean", [128, NSEQ, 4, 16], F32)
        kmean_b = sb("kmean_b", [128, NSEQ, 4, 16], BF16)
        if True:
            es1 = ExitStack()
            with es1:
                def sb1(name, shape, dt=F32):
                    return es1.enter_context(nc.sbuf_tensor(name, list(shape), dt))
                wp = sb1("wp", [128, 8, IN_COLS], BF16)
                wst = [sb1("wst%d" % i, [128, 1568]) for i in range(2)]
                wring = Ring(wst, "wst")
                for kc in range(8):
                    for q4 in range(4):
                        t, tk = wring.next()
                        c0 = q4 * 1568
                        sc.dma("sp", t[:], w_in_d[kc * 128:(kc + 1) * 128, c0:c0 + 1568], writes=[tk])
                        eng = ("dve", "pool")[(kc * 4 + q4) % 2]
                        sc.op(eng, lambda e, t=t, kc=kc, c0=c0: e.tensor_scalar(
                            wp[:, kc, c0:c0 + 1568], t[:], gpre_c[:, kc:kc + 1], None, ALU.mult),
                            reads=[tk, ("vc",)], writes=[("wp", kc, q4)])
                wp_keys = [("wp", kc, q4) for kc in range(8) for q4 in range(4)]

                mu_c = vc[:, VOFF["mu"]:VOFF["mu"] + 17]
                carry = sb1("carry", [128, 17])
                xt = [sb1("xt%d" % i, [128, 4, D]) for i in range(2)]
                xring = Ring(xt, "xt")
                ub = [sb1("ub%d" % i, [128, D], BF16) for i in range(2)]
                ubring = Ring(ub, "ub")
                sqj = sb1("sqj", [128, D], BF16)
                ss = [sb1("ss%d" % i, [128, 4]) for i in range(2)]
                ssring = Ring(ss, "ss")
                rs = [sb1("rs%d" % i, [128, 4]) for i in range(2)]
                rsring = Ring(rs, "rs")
                uT = [sb1("uT%d" % i, [128, 8, 512], BF16) for i in range(2)]
                uTring = Ring(uT, "uT")
                ob = [sb1("ob%d" % i, [128, 512], BF16) for i in range(4)]
                obring = Ring(ob, "ob")
                vo = [sb1("vo%d" % i, [128, NH, 128], BF16) for i in range(2)]
                voring = Ring(vo, "vo")
                for i in range(2):
                    sc.op("dve", lambda e, i=i: e.memset(vo[i][:], 1.0), writes=[("vo", i), ("vo_init",)])
                cb = [sb1("cb%d" % i, [128, 513]) for i in range(3)]
                cbring = Ring(cb, "cb")
                db = [sb1("db%d" % i, [128, 512]) for i in range(2)]
                dbring = Ring(db, "db")
                shb = [sb1("shb%d" % i, [128, 512]) for i in range(3)]
                shring = Ring(shb, "shb")
                sc.op("dve", lambda e: e.memset(kmean[:], 0.0), writes=[("kmean",)])

                xloaded = {}

                def load_x(s, ti):
                    if s >= NSEQ:
                        return
                    tok0 = s * S + ti * 512
                    xtile, xk = xring.next()
                    sc.dma("sp", xtile[:], x_d[tok0:tok0 + 512, :].rearrange("(a p) d -> p a d", p=128),
                           writes=[xk])
                    xloaded[(s, ti)] = (xtile, xk)

                load_x(0, 0)
                for s in range(NSEQ if lvl >= 1 else 0):
                    sc.op("dve", lambda e: e.memset(carry[:], 0.0), writes=[("carry",)], reads=[])
                    for ti in range(NT):
                        nxt = (s, ti + 1) if ti + 1 < NT else (s + 1, 0)
                        load_x(*nxt)
                        xtile, xk = xloaded.pop((s, ti))
                        sst, ssk = ssring.next()
                        rst, rsk = rsring.next()
                        sc.op("dve", lambda e, sst=sst: e.memset(sst[:], 0.0), writes=[(ssk, a) for a in range(4)])
                        for a in range(4):
                            sc.op("act", lambda e, a=a, xtile=xtile, sst=sst: e.activation(
                                sqj[:], xtile[:, a, :], AF.Square, accum_out=sst[:, a:a + 1]),
                                reads=[xk], writes=[("sqj",), (ssk, a)])
                        sc.op("act", lambda e, sst=sst: e.activation(
                            sst[:], sst[:], AF.Sqrt, bias=float(RMS_EPS), scale=1.0 / D),
                            reads=[(ssk, a) for a in range(4)], writes=[(ssk, "q")])
                        sc.op("dve", lambda e, sst=sst, rst=rst: e.reciprocal(rst[:], sst[:]),
                              reads=[(ssk, "q")], writes=[rsk] + [(ssk, a) for a in range(4)])
                        uTt, uTk = uTring.next()
                        for a in range(4):
                            ubt, ubk = ubring.next()
                            sc.op("dve", lambda e, a=a, ubt=ubt, xtile=xtile, rst=rst: e.tensor_scalar(
                                ubt[:], xtile[:, a, :], rst[:, a:a + 1], None, ALU.mult),
                                reads=[xk, rsk], writes=[ubk])
                            pt, pk = psring.next()
                            ptb = pt[:].bitcast(BF16)

                            def tr_fn(e, ubt=ubt, ptb=ptb):
                                ins = None
                                for kc in range(8):
                                    ins = e.transpose(ptb[:, kc * 128:(kc + 1) * 128], ubt[:, kc * 128:(kc + 1) * 128], ident_b[:])
                                return ins
                            sc.op("pe", tr_fn, reads=[ubk, ("ident_b",)], writes=[pk])
                            eng = ("act", "dve")[a % 2]
                            if eng == "act":
                                sc.op("act", lambda e, a=a, uTt=uTt, ptb=ptb: e.copy(
                                    uTt[:, :, a * 128:(a + 1) * 128], ptb.rearrange("p (k t) -> p k t", k=8)),
                                    reads=[pk], writes=[(uTk, a)])
                            else:
                                sc.op("dve", lambda e, a=a, uTt=uTt, ptb=ptb: e.tensor_copy(
                                    uTt[:, :, a * 128:(a + 1) * 128], ptb.rearrange("p (k t) -> p k t", k=8)),
                                    reads=[pk], writes=[(uTk, a)])
                        uT_keys = [(uTk, a) for a in range(4)]

                        def proj(cc, pt, pk, uTt=uTt, uT_keys=uT_keys):
                            def fn(e):
                                ins = None
                                for kc in range(8):
                                    ins = e.matmul(pt[:], wp[:, kc, cc * 128:(cc + 1) * 128], uTt[:, kc, :],
                                                   start=(kc == 0), stop=(kc == 7))
                                return ins
                            sc.op("pe", fn, reads=uT_keys + wp_keys, writes=[pk])

                        for cc in range(49):
                            if 8 <= cc < 12:
                                continue
                            if lvl < 2 or (lvl == 2 and cc >= 4) or (lvl == 3 and cc >= 8) or (lvl == 4 and cc >= 16) or (lvl == 5 and cc >= 33):
                                continue
                            pt, pk = psring.next()
                            proj(cc, pt, pk)
                            tsl = slice(ti * 512, (ti + 1) * 512)
                            rows = slice((cc % 4) * 128, (cc % 4) * 128 + 128)
                            if cc < 4:
                                o, ok = obring.next()
                                sc.op("act", lambda e, o=o, pt=pt: e.mul(o[:], pt[:], 0.125), reads=[pk], writes=[ok])
                                sc.dma("pool", qT_d[s, rows, tsl], o[:], reads=[ok], writes=[("qT", s, cc, ti)])
                            elif cc < 8:
                                o, ok = obring.next()
                                for hb in range(2):
                                    sc.op("act", lambda e, o=o, pt=pt, hb=hb, s=s, cc=cc, ti=ti: e.activation(
                                        o[:, hb * 256:(hb + 1) * 256], pt[:, hb * 256:(hb + 1) * 256], AF.Copy,
                                        accum_out=kmean[:, s, cc - 4, 2 * ti + hb:2 * ti + hb + 1]),
                                        reads=[pk, ("kmean",)], writes=([ok] if hb == 0 else []) + [(ok, hb), ("kmean", s, cc - 4, ti, hb)])
                                sc.dma("pool", kT_d[s, rows, tsl], o[:], reads=[ok, (ok, 0), (ok, 1)], writes=[("kT", s, cc - 4, ti)])
                            elif cc < 16:
                                o, ok = obring.next()
                                sc.op("act", lambda e, o=o, pt=pt: e.activation(o[:], pt[:], AF.Silu), reads=[pk], writes=[ok])
                                sc.dma("pool", za_d[s, rows, tsl], o[:], reads=[ok], writes=[("za", s, cc - 12, ti)])
                            elif cc < 33:
                                j = cc - 16
                                c, ck = cbring.next()
                                sc.op("act", lambda e, c=c, pt=pt: e.copy(c[:, 1:513], pt[:]), reads=[pk], writes=[(ck, 1)])
                                sc.op("dve", lambda e, c=c, j=j: e.tensor_copy(c[:, 0:1], carry[:, j:j + 1]),
                                      reads=[("carry", j), ("carry",)], writes=[(ck, 0)])
                                sc.op("dve", lambda e, c=c, j=j: e.tensor_copy(carry[:, j:j + 1], c[:, 512:513]),
                                      reads=[(ck, 1), ("carry",)], writes=[("carry", j)])
                                dd, dk = dbring.next()
                                sc.op("dve", lambda e, c=c, dd=dd: e.tensor_tensor(dd[:], c[:, 0:512], c[:, 1:513], ALU.subtract),
                                      reads=[(ck, 0), (ck, 1)], writes=[dk])
                                sh, shk = shring.next()
                                sc.op("dve", lambda e, c=c, dd=dd, sh=sh, j=j: e.scalar_tensor_tensor(
                                    sh[:], dd[:], mu_c[:, j:j + 1], c[:, 1:513], ALU.mult, ALU.add),
                                    reads=[dk, (ck, 1), ("vc",)], writes=[shk])
                                sc.dma("pool", rw_d[s, j * 128:(j + 1) * 128, tsl], sh[:], reads=[shk], writes=[("rw", s, j, ti)])
                            else:
                                j = cc - 33
                                o, ok = obring.next()
                                sc.op("act", lambda e, o=o, pt=pt: e.activation(o[:], pt[:], AF.Sigmoid), reads=[pk], writes=[ok])
                                sc.dma("pool", gt_d[s, j * 128:(j + 1) * 128, tsl], o[:], reads=[ok], writes=[("gt", s, j, ti)])
                        for a in range(4 if lvl >= 7 else 0):
                            pt, pk = psring.next()

                            def vfn(e, a=a, pt=pt, uTt=uTt):
                                ins = None
                                for kc in range(8):
                                    ins = e.matmul(pt[:], uTt[:, kc, a * 128:(a + 1) * 128], wp[:, kc, 1024:1536],
                                                   start=(kc == 0), stop=(kc == 7))
                                return ins
                            sc.op("pe", vfn, reads=uT_keys + wp_keys, writes=[pk])
                            o, ok = voring.next()
                            sc.op("act", lambda e, o=o, pt=pt: e.copy(o[:, :, 0:64], pt[:].rearrange("p (h d) -> p h d", h=NH)),
                                  reads=[pk, ("vo_init",)], writes=[ok])
                            t0 = ti * 512 + a * 128
                            sc.dma("pool", v_d[s, t0:t0 + 128, :], o[:].rearrange("p h d -> p (h d)"), reads=[ok], writes=[("v", s, ti * 4 + a)])
                sc.op("dve", lambda e: e.tensor_scalar(kmean_b[:], kmean[:], 1.0 / BLK, None, ALU.mult),
                      reads=[("kmean",)] + [("kmean", s, c, ti, hb) for s in range(NSEQ) for c in range(4) for ti in range(NT) for hb in range(2)],
                      writes=[("kmean_b",)])

        sc.barrier()
        if lvl >= 10 and moba:
            es2 = ExitStack()
            with es2:
                def sb2(name, shape, dt=F32):
                    return es2.enter_context(nc.sbuf_tensor(name, list(shape), dt))
                psS = Ring(psb[0:4], "psb", keys=[("psb", j) for j in range(0, 4)])
                psN = Ring(psb[4:6], "psb", keys=[("psb", j) for j in range(4, 6)])
                psM = Ring(psb[6:8], "psb", keys=[("psb", j) for j in range(6, 8)])
                relb33 = sb2("relb33", [33, NH])
                b31bc = sb2("b31bc", [128, NH])
                sc.dma("sp", b31bc[:], relb_d[31, :].partition_broadcast(128), writes=[("b31bc",)])
                sc.op("dve", lambda e: e.memset(relb33[32:33, :], NEG), writes=[("relb33", 1)])
                sc.dma("sp", relb33[0:32, :], relb_d[:, :], writes=[("relb33", 0)])
                oneh = sb2("oneh", [33, 512])
                sc.dma("sp", oneh[:], c_onehot_d[:, :], writes=[("oneh",)])
                pt, pk = psM.next()
                sc.op("pe", lambda e, pt=pt: e.matmul(pt[0:8, :], relb33[:], oneh[:], start=True, stop=True),
                      reads=[("relb33", 0), ("relb33", 1), ("oneh",)], writes=[pk])
                fbs = sb2("fbs", [8, 512])
                sc.op("dve", lambda e, pt=pt: e.tensor_copy(fbs[:], pt[0:8, :]), reads=[pk], writes=[("fbs",)])
                sc.dma("sp", fb_d[:, :], fbs[:], reads=[("fbs",)], writes=[("fb_d",)])
                Jf = sb2("Jf", [128, 128])
                sc.dma("sp", Jf[:], c_J_d[:, :], writes=[("Jf",)])
                Tt = sb2("Tt", [128, NH, 2, 128], BF16)
                tfl = [sb2("tfl%d" % i, [128, 128]) for i in range(2)]
                tflr = Ring(tfl, "tfl")
                for h in range(NH):
                    for dl in range(2):
                        tf, tfk = tflr.next()
                        src = bass.AP(tensor=fb_d.tensor, offset=h * 512 + 1 + dl * 128, ap=[[1, 128], [1, 128]])
                        sc.dma("sp", tf[:], src, reads=[("fb_d",)], writes=[tfk])
                        pt, pk = psM.next()
                        sc.op("pe", lambda e, pt=pt, tf=tf: e.matmul(pt[:, 0:128], Jf[:], tf[:], start=True, stop=True),
                              reads=[tfk, ("Jf",)], writes=[pk])
                        sc.op("dve", lambda e, pt=pt, h=h, dl=dl: e.tensor_scalar(Tt[:, h, dl, :], pt[:, 0:128], b31bc[:, h:h + 1], None, ALU.subtract),
                              reads=[pk, ("b31bc",)], writes=[("Tt", h, dl)])
                Tt_keys = [("Tt", h, dl) for h in range(NH) for dl in range(2)]
                if "d_Tt" in dbg:
                    dTt = nc.dram_tensor("d_Tt", [128, NH * 2 * 128], BF16, kind="ExternalOutput").ap()
                    sc.dma("sp", dTt[:, :], Tt[:].rearrange("p h d q -> p (h d q)"), reads=Tt_keys)
                eself = sb2("eself", [128, 16 * 128])
                esel = sb2("esel", [128, 16, 128], BF16)
                sc.dma("sp", eself[:], c_esel_d[:, :], writes=[("eself",)])
                sc.op("dve", lambda e: e.tensor_copy(esel[:].rearrange("p a b -> p (a b)"), eself[:]), reads=[("eself",)], writes=[("esel",)])
                ones_b = sb2("ones_b", [128, 64], BF16)
                sc.op("dve", lambda e: e.memset(ones_b[:], 1.0), writes=[("ones_b",)])
                maskT = [sb2("maskT%d" % i, [128, NH, 128], BF16) for i in range(2)]
                for i in range(2):
                    sc.op("dve", lambda e, i=i: e.memset(maskT[i][:, :, :], 0.0), writes=[("maskT", i)])
                mring = Ring(maskT, "maskT")
                pen_sb = sb2("pen_sb", [128, 16 * 128])
                sc.op("dve", lambda e: e.memset(pen_sb[:], 0.0), writes=[("pen_sb", "z")])
                sc.dma("sp", pen_sb[0:1, :], c_pen_d.rearrange("(a n) -> a n", a=1), reads=[("pen_sb", "z")], writes=[("pen_sb", 0)])
                sc.dma("sp", pen_sb[64:65, :], c_pen_d.rearrange("(a n) -> a n", a=1), reads=[("pen_sb", "z")], writes=[("pen_sb", 1)])
                onesq_b = sb2("onesq_b", [128, 128], BF16)
                sc.op("dve", lambda e: e.memset(onesq_b[:], 1.0), writes=[("onesq_b",)])
                penb = sb2("penb", [128, 16 * 128], BF16)
                sc.op("dve", lambda e: e.tensor_copy(penb[:], pen_sb[:]), reads=[("pen_sb", 0), ("pen_sb", 1), ("pen_sb", "z")], writes=[("penb",)])
                kT_sb = sb2("kT_sb", [128, 4, S], BF16)
                qT_sb = sb2("qT_sb", [128, 4, S], BF16)
                v_sb = sb2("v_sb", [128, S // 128, 2 * A_W], BF16)
                gsb = [sb2("gsb%d" % i, [128, NH, 16]) for i in range(2)]
                gring = Ring(gsb, "gsb")
                top8 = [sb2("top8_%d" % i, [128, NH, 8]) for i in range(2)]
                t8ring = Ring(top8, "top8")
                selm = [sb2("selm%d" % i, [128, NH, 16]) for i in range(2)]
                sring = Ring(selm, "selm")
                mvb = [sb2("mvb%d" % i, [128, NH, 16], BF16) for i in range(2)]
                mvring = Ring(mvb, "mvb")
                Pb = [sb2("Pb%d" % i, [128, 512], BF16) for i in range(3)]
                Pring = Ring(Pb, "Pb")
                rden = [sb2("rden%d" % i, [64, 128]) for i in range(2)]
                rdring = Ring(rden, "rden")
                ynorm = [sb2("ynorm%d" % i, [64, 128]) for i in range(2)]
                ynring = Ring(ynorm, "ynorm")
                zat = [sb2("zat%d" % i, [64, NH, 128], BF16) for i in range(2)]
                zring = Ring(zat, "zat")
                yout = [sb2("yout%d" % i, [64, NH, 128], BF16) for i in range(2)]
                yring = Ring(yout, "yout")

                for s in range(NSEQ if lvl >= 11 else 0):
                    for c in range(4):
                        sc.dma("sp", kT_sb[:, c, :], kT_d[s, c * 128:(c + 1) * 128, :],
                               reads=[("kT", s, c, ti) for ti in range(NT)], writes=[("kT_sb", c)])
                        sc.dma("sp", qT_sb[:, c, :], qT_d[s, c * 128:(c + 1) * 128, :],
                               reads=[("qT", s, c, ti) for ti in range(NT)], writes=[("qT_sb", c)])
                    for j4 in range(0, S // 128, 4):
                        n4 = min(4, S // 128 - j4)
                        sc.dma("sp", v_sb[:, j4:j4 + n4, :], v_d[s, j4 * 128:(j4 + n4) * 128, :].rearrange("(a p) d -> p a d", p=128),
                               reads=[("v", s, j) for j in range(j4, j4 + n4)], writes=[("v_sb", j) for j in range(j4, j4 + n4)])
                    for qt in range(NQT if lvl >= 12 else 0):
                        QB = qt // 2
                        qsl = slice(qt * 128, (qt + 1) * 128)
                        mT, mTk = None, None
                        if bar:
                            sc.barrier()
                        if QB > 0:
                            pt, pk = psM.next()

                            def gfn(e, pt=pt, qsl=qsl, s=s, QB=QB):
                                ins = None
                                for h in range(NH):
                                    hp = slice((h % 2) * 64, (h % 2) * 64 + 64)
                                    ins = e.matmul(pt[:, h * 16:(h + 1) * 16], qT_sb[hp, h // 2, qsl], kmean_b[hp, s, h // 2, :],
                                                   start=True, stop=False)
                                    p0 = (h % 2) * 64
                                    ins = e.matmul(pt[:, h * 16:(h + 1) * 16], onesq_b[p0:p0 + 1, :],
                                                   penb[p0:p0 + 1, QB * 128 + h * 16:QB * 128 + (h + 1) * 16], start=False, stop=True)
                                return ins
                            sc.op("pe", gfn, reads=[("qT_sb", c) for c in range(4)] + [("kmean_b",), ("penb",), ("onesq_b",)], writes=[pk])
                            g, gk = gring.next()
                            sc.op("dve", lambda e, g=g, pt=pt: e.tensor_copy(g[:].rearrange("p h n -> p (h n)"), pt[:, 0:NH * 16]),
                                  reads=[pk], writes=[gk, (gk, 1)])
                            t8, t8k = t8ring.next()
                            for h in range(NH if sub >= 3 else 0):
                                sc.op("dve", lambda e, t8=t8, g=g, h=h: e.max(t8[:, h, :], g[:, h, :]),
                                      reads=[gk, (gk, 1)], writes=[(t8k, h)])
                            sm, smk = sring.next()
                            if sub >= 4:
                              sc.op("dve", lambda e, sm=sm, g=g, t8=t8: e.tensor_tensor(
                                sm[:], g[:], t8[:, :, 2:3].to_broadcast([128, NH, 16]), ALU.is_ge),
                                reads=[gk, (gk, 1)] + [(t8k, h) for h in range(NH)], writes=[smk])
                            mv, mvk = mvring.next()
                            if sub >= 5:
                              sc.op("dve", lambda e, mv=mv, sm=sm: e.tensor_scalar(mv[:], sm[:], -NEG, NEG, ALU.mult, ALU.add),
                                  reads=[smk], writes=[mvk])
                            pt2, pk2 = psM.next()
                            ptb2 = pt2[:].bitcast(BF16)

                            def mtr(e, mv=mv, ptb2=ptb2):
                                ins = None
                                for h in range(NH):
                                    ins = e.transpose(ptb2[0:16, h * 128:(h + 1) * 128], mv[:, h, :], ident_b[:])
                                return ins
                            if sub >= 6:
                                sc.op("pe", mtr, reads=[mvk, ("ident_b",)], writes=[pk2])
                            mT, mTk = mring.next()
                            if sub >= 7:
                                sc.op("dve", lambda e, mT=mT, ptb2=ptb2: e.tensor_copy(
                                    mT[0:16, :, :], ptb2[0:16, :].rearrange("p (h q) -> p h q", h=NH)),
                                    reads=[pk2], writes=[mTk])
                                sc.op("dve", lambda e, mT=mT, ptb2=ptb2: e.tensor_copy(
                                    mT[64:80, :, :], ptb2[0:16, :].rearrange("p (h q) -> p h q", h=NH)),
                                    reads=[pk2], writes=[(mTk, "b")])
                        if "d_gate" in dbg and qt == NQT - 1 and s == 0:
                            dg = nc.dram_tensor("d_gate", [128, NH * 16], F32, kind="ExternalOutput").ap()
                            sc.dma("sp", dg[:, :], g[:].rearrange("p h n -> p (h n)"), reads=[gk, (gk, 1)])
                            dsm = nc.dram_tensor("d_sm", [128, NH * 16], F32, kind="ExternalOutput").ap()
                            sc.dma("sp", dsm[:, :], sm[:].rearrange("p h n -> p (h n)"), reads=[smk])
                            dmt = nc.dram_tensor("d_mt", [33, NH * 128], BF16, kind="ExternalOutput").ap()
                            sc.dma("sp", dmt[:, :], mT[:].rearrange("p h n -> p (h n)"), reads=[mTk, (mTk, "c")])
                            dkm = nc.dram_tensor("d_km", [128, NSEQ * 64], F32, kind="ExternalOutput").ap()
                            sc.dma("sp", dkm[:, :], kmean[:].rearrange("p s c n -> p (s c n)"), reads=[("kmean_b",)])
                        if bar:
                            sc.barrier()
                        zt, ztk = zring.next()
                        sc.dma("sp", zt[:], za_d[s].rearrange("(h d) t -> d h t", h=NH)[:, :, qsl],
                               reads=[("za", s, c, qt // 4) for c in range(4)], writes=[ztk])
                        yo, yok = yring.next()
                        pend = []

                        def drain(keep):
                            while len(pend) > keep:
                                pend.pop(0)()
                        for h in range(NH if lvl >= 13 else 0):
                            hp = slice((h % 2) * 64, (h % 2) * 64 + 64)
                            c = h // 2
                            nd, ndk = psN.next()
                            kts = list(range(qt + 1))
                            ngroups = (len(kts) + GRPN - 1) // GRPN
                            for gi, g0 in enumerate(range(0, len(kts), GRPN)):
                                grp = kts[g0:g0 + GRPN]
                                st_, stk = psS.next()

                                def sfn(e, grp=grp, st_=st_, hp=hp, c=c, qsl=qsl, qt=qt, QB=QB, h=h, mT=mT):
                                    ins = None
                                    for j, kt in enumerate(grp):
                                        osl = st_[:, j * 128:(j + 1) * 128]
                                        extra = []
                                        n = kt // 2
                                        if n < QB:
                                            extra.append((esel[hp, n, :], mT[hp, h, :]))
                                        ins = e.matmul(osl, kT_sb[hp, c, kt * 128:(kt + 1) * 128], qT_sb[hp, c, qsl],
                                                       start=True, stop=(len(extra) == 0))
                                        for i2, (l_, r_) in enumerate(extra):
                                            ins = e.matmul(osl, l_, r_, start=False, stop=(i2 == len(extra) - 1))
                                    return ins
                                rd = [("kT_sb", c), ("qT_sb", c), ("esel",)]
                                if mTk is not None:
                                    rd += [mTk, (mTk, "b")]
                                sc.op("pe", sfn, reads=rd, writes=[stk])
                                for j, kt in enumerate(grp):
                                    if kt >= qt - 1:
                                        dl = qt - kt
                                        sc.op("dve", lambda e, st_=st_, j=j, h=h, dl=dl: e.tensor_tensor(
                                            st_[:, j * 128:(j + 1) * 128], st_[:, j * 128:(j + 1) * 128], Tt[:, h, dl, :], ALU.add),
                                            reads=[stk, ("Tt", h, dl)], writes=[stk])
                                P, Pk = Pring.next()
                                ng = len(grp)
                                sc.op("act", lambda e, P=P, st_=st_, ng=ng, h=h: e.activation(P[:, 0:ng * 128], st_[:, 0:ng * 128], AF.Exp, bias=b31bc[:, h:h + 1]),
                                      reads=[stk, ("b31bc",)], writes=[Pk])

                                def emit_pv(grp=grp, P=P, Pk=Pk, nd=nd, ndk=ndk, h=h, qt=qt, last=(gi == ngroups - 1), yo=yo, yok=yok, zt=zt, ztk=ztk):
                                    def pvfn(e):
                                        ins = None
                                        for j, kt in enumerate(grp):
                                            ins = e.matmul(nd[:, 0:128], v_sb[:, kt, h * 128:(h + 1) * 128], P[:, j * 128:(j + 1) * 128],
                                                           start=(kt == 0), stop=(kt == qt))
                                        return ins
                                    sc.op("pe", pvfn, reads=[Pk] + [("v_sb", kt) for kt in grp], writes=[ndk])
                                    if last:
                                        rdn, rdk = rdring.next()
                                        sc.op("dve", lambda e: e.reciprocal(rdn[:], nd[64:128, 0:128]), reads=[ndk], writes=[rdk])
                                        yn, ynk = ynring.next()
                                        sc.op("dve", lambda e: e.tensor_tensor(yn[:], nd[0:64, 0:128], rdn[:], ALU.mult),
                                              reads=[ndk, rdk], writes=[ynk])
                                        sc.op("pool", lambda e: e.tensor_tensor(yo[:, h, :], yn[:], zt[:, h, :], ALU.mult),
                                              reads=[ynk, ztk], writes=[(yok, h)])
                                pend.append(emit_pv)
                                drain(1)
                        drain(0)
                        sc.dma("pool", ya_d[s].rearrange("(h d) t -> d h t", h=NH)[:, :, qsl], yo[:],
                               reads=[(yok, h) for h in range(NH)], writes=[("ya", s, qt), yok])

        if lvl >= 20 and (lvl < 40 or not moba):
            zt_ = sb("zstub", [128, 512], BF16)
            sc.op("dve", lambda e: e.memset(zt_[:], 0.0), writes=[("zstub",)])
            for s in range(NSEQ):
                for ti in range(NT):
                    for c in range(4):
                        if lvl < 40:
                            sc.dma("sp", yb_d[s, c * 128:(c + 1) * 128, ti * 512:(ti + 1) * 512], zt_[:], reads=[("zstub",)],
                                   writes=[("yb", s, ti)] if c == 3 else [("yb_part", s, ti, c)])
                        if not moba:
                            sc.dma("sp", ya_d[s, c * 128:(c + 1) * 128, ti * 512:(ti + 1) * 512], zt_[:], reads=[("zstub",)],
                                   writes=[("ya", s, ti * 4 + c)])

        sc.barrier()
        if lvl >= 40:
            es4 = ExitStack()
            with es4:
                def sb4(name, shape, dt=F32):
                    return es4.enter_context(nc.sbuf_tensor(name, list(shape), dt))
                psW = Ring(psb, "psb")
                NC_ = S // 64
                NG = 4
                HS = [64, NG, 64]
                vs2 = sb4("vs2", [64, 64])
                sc.op("dve", lambda e: e.memset(vs2[:], 0.0), writes=[("vs2",)])
                VO2 = {}
                for i, (nm, dv) in enumerate((("w0", w0_d), ("a0", a0_d), ("k_k", k_k_d), ("k_a", k_a_d), ("r_k", r_k_d))):
                    VO2[nm] = i * 8
                    sc.dma("sp", vs2[i * 8:(i + 1) * 8, :], dv.rearrange("(h d) -> h d", d=64), reads=[("vs2",)], writes=[("vs2", nm)])
                pt, pk = psW.next()
                sc.op("pe", lambda e, pt=pt: e.transpose(pt[0:64, 0:64], vs2[:], ident_f[0:64, 0:64]),
                      reads=[("vs2",), ("ident_f",)] + [("vs2", nm) for nm in VO2], writes=[pk])
                vh = sb4("vh", [64, 64])
                sc.op("dve", lambda e, pt=pt: e.tensor_copy(vh[:], pt[0:64, 0:64]), reads=[pk], writes=[("vh",)])
                omk = sb4("omk", [64, NH])
                sc.op("dve", lambda e: e.tensor_scalar(omk[:], vh[:, VO2["k_a"]:VO2["k_a"] + 8], -1.0, 1.0, ALU.mult, ALU.add),
                      reads=[("vh",)], writes=[("omk",)])

                def vb(nm):
                    o = VO2[nm] + CUR["hg"] * NG
                    return vh[:, o:o + NG].rearrange("p (h o) -> p h o", o=1).to_broadcast(HS)
                wup = sb4("wup", [64, B_W])
                aup = sb4("aup", [64, B_W])
                sc.dma("sp", wup[:], w_up_d[:, :], writes=[("wup0",)])
                sc.dma("sp", aup[:], a_up_d[:, :], writes=[("aup0",)])
                wup_r = sb4("wup_r", [64, B_W], F32R)
                aup_r = sb4("aup_r", [64, B_W], F32R)
                sc.op("dve", lambda e: e.tensor_copy(wup_r[:], wup[:]), reads=[("wup0",)], writes=[("wup",)])
                sc.op("dve", lambda e: e.tensor_copy(aup_r[:], aup[:]), reads=[("aup0",)], writes=[("aup",)])
                lnw = sb4("lnw", [64, B_W])
                lnb = sb4("lnb", [64, B_W])
                sc.dma("sp", lnw[:], lnw_d.partition_broadcast(64), writes=[("lnw",)])
                sc.dma("sp", lnb[:], lnb_d.partition_broadcast(64), writes=[("lnb",)])
                tri = sb4("tri", [64, 3, 64])
                sc.dma("sp", tri[:].rearrange("p a b -> p (a b)"), c_tri_d[:, :], writes=[("tri",)])
                ones64 = sb4("ones64", [64, 64], F32R)
                ones64f = sb4("ones64f", [64, 2])
                sc.op("dve", lambda e: e.memset(ones64f[:], 1.0), writes=[("ones64f",)])
                ones_t = sb4("ones_t", [64, 64])
                sc.op("dve", lambda e: e.memset(ones_t[:], 1.0), writes=[("ones_t",)])
                sc.op("dve", lambda e: e.tensor_copy(ones64[:], ones_t[:]), reads=[("ones_t",)], writes=[("ones64",)])
                zeros_t = sb4("zeros_t", [64, 4, 64])
                sc.op("dve", lambda e: e.memset(zeros_t[:], 0.0), writes=[("zeros_t",)])
                smask = sb4("smask", HS)
                sc.op("dve", lambda e: e.memset(smask[:], 1.0), writes=[("smask",)])
                sc.op("dve", lambda e: e.memset(smask[:, :, 0:1], 0.0), reads=[("smask",)], writes=[("smask", 1)])
                identb8 = ident_f[0:64, 0:64].rearrange("p (o d) -> p o d", o=1).to_broadcast(HS)

                def trib(i):
                    return tri[:, i:i + 1, :].to_broadcast(HS)

                T_ = {}
                CUR = {"set": 0, "list": None, "hg": 0}

                class Defer:
                    def op(self, eng, fn, reads=(), writes=()):
                        CUR["list"].append(("op", eng, fn, list(reads), list(writes), {}))

                    def dma(self, eng, out, in_, reads=(), writes=(), **kw):
                        CUR["list"].append(("dma", eng, (out, in_), list(reads), list(writes), kw))
                cur = Defer()

                RNAMES = {"tw", "ad_r", "sq", "At", "Bt", "Kt", "Rt", "tm_V", "tm_bc", "tm_kc", "X", "Q0", "Q1", "P0", "P1",
                          "AakT", "ArbT", "ArkT", "Mc", "Rh"}

                def tile(name, shape=None):
                    nm = "r%d_%s" % (CUR["set"], name)
                    if nm not in T_:
                        T_[nm] = sb4(nm, shape or HS, F32R if name in RNAMES else F32)
                    return T_[nm], (nm,)
                HstA = [[sb4("Hst%d_%d" % (q, i), HS, F32R) for i in range(2)] for q in range(4)]

                def ew(eng, fn, reads, writes):
                    cur.op(eng, fn, reads=reads, writes=writes)

                def headmm(out_fn, l_fn, r_fn, reads, pk, extra=None):
                    items = []
                    for h in range(NG):
                        pairs = [(l_fn(h), r_fn(h))] + ([(a(h), b(h)) for a, b in extra] if extra else [])
                        for i, (l_, r_) in enumerate(pairs):
                            items.append((out_fn(h), l_, r_, i == 0, i == len(pairs) - 1))

                    def fn(e, items=items):
                        ins = None
                        for (o_, l_, r_, st, sp) in items:
                            ins = e.matmul(o_, l_, r_, start=st, stop=sp)
                        return ins
                    cur.op("pe", fn, reads=reads, writes=[pk])

                def flat(t):
                    return t[:].rearrange("p h t -> p (h t)")

                for s in range(NSEQ):
                    for hg in range(2):
                        sc.op("dve", lambda e, q=(s % 2) * 2 + hg: e.tensor_copy(HstA[q][0][:], zeros_t[:]), reads=[("zeros_t",)], writes=[("Hst", (s % 2) * 2 + hg, 0)])

                psSets = [Ring(psb[2 * q:2 * q + 2], "psb", keys=[("psb", j) for j in range(2 * q, 2 * q + 2)]) for q in range(4)]

                def body(s, ci, hg):
                    chain = (s % 2) * 2 + hg
                    psW = psSets[chain]
                    G0 = hg * NG
                    VB = {nm: vb(nm) for nm in VO2}
                    if True:
                        Hst = HstA[chain]
                        csl = slice(ci * 64, (ci + 1) * 64)
                        ti = ci // 8
                        hcur, hck = Hst[ci % 2], ("Hst", chain, ci % 2)
                        hnxt, hnk = Hst[(ci + 1) % 2], ("Hst", chain, (ci + 1) % 2)
                        fm = {}
                        for qi, nm in enumerate(("r", "k", "v", "z")):
                            t, tk = tile("in_" + nm)
                            cur.dma("sp", t[:], rw_d[s, qi * 512 + G0 * 64:qi * 512 + (G0 + NG) * 64, csl].rearrange("(h d) t -> d h t", h=NG),
                                   reads=[("rw", s, qi * 4 + j, ti) for j in range(4)], writes=[tk])
                            fm[nm] = (t, tk)
                        wd, wdk = tile("wd", [64, 64])
                        ad, adk = tile("ad", [64, 64])
                        cur.dma("sp", wd[:], rw_d[s, 2048:2112, csl], reads=[("rw", s, 16, ti)], writes=[wdk])
                        cur.dma("sp", ad[:], rw_d[s, 2112:2176, csl], reads=[("rw", s, 16, ti)], writes=[adk])
                        r_, rk_ = fm["r"]; k_, kk_ = fm["k"]; v_, vk_ = fm["v"]; z_, zk_ = fm["z"]
                        tw, twk = tile("tw", [64, 64])
                        adr, adrk = tile("ad_r", [64, 64])
                        ew("act", lambda e: e.copy(adr[:], ad[:]), [adk], [adrk])
                        ew("act", lambda e: e.activation(tw[:], wd[:], AF.Tanh), [wdk], [twk])
                        pW, pWk = psW.next()
                        headmm(lambda h: pW[0:64, h * 64:(h + 1) * 64], lambda h: wup_r[:, (G0 + h) * 64:(G0 + h + 1) * 64], lambda h: tw[:],
                               [twk, ("wup",)], pWk)
                        pA, pAk = psW.next()
                        headmm(lambda h: pA[0:64, h * 64:(h + 1) * 64], lambda h: aup_r[:, (G0 + h) * 64:(G0 + h + 1) * 64], lambda h: adr[:],
                               [adrk, ("aup",)], pAk)
                        pv3 = lambda p: p[0:64, 0:NG * 64].rearrange("p (h t) -> p h t", h=NG)
                        PW = NG * 64
                        lw, lwk = tile("lw")
                        ew("dve", lambda e, pW=pW: e.tensor_tensor(lw[:], pv3(pW), VB["w0"], ALU.add), [pWk, ("vh",)], [lwk])
                        ew("act", lambda e: e.activation(flat(lw), flat(lw), AF.Sigmoid), [lwk], [lwk])
                        ew("dve", lambda e: e.tensor_scalar(flat(lw), flat(lw), -math.exp(-0.5), None, ALU.mult), [lwk], [lwk])
                        av, avk = tile("av")
                        ew("dve", lambda e, pA=pA: e.tensor_tensor(av[:], pv3(pA), VB["a0"], ALU.add), [pAk, ("vh",)], [avk])
                        ew("act", lambda e: e.activation(flat(av), flat(av), AF.Sigmoid), [avk], [avk])
                        kr, krk = tile("kr")
                        ew("dve", lambda e: e.tensor_tensor(kr[:], k_[:], VB["k_k"], ALU.mult), [kk_, ("vh",)], [krk])
                        sq, sqk = tile("sq")
                        ew("dve", lambda e: e.tensor_tensor(flat(sq), flat(kr), flat(kr), ALU.mult), [krk], [sqk])
                        pS, pSk = psW.next()
                        cur.op("pe", lambda e, pS=pS: e.matmul(pS[0:64, 0:PW], ones64[:], flat(sq), start=True, stop=True),
                              reads=[sqk, ("ones64",)], writes=[pSk])
                        rn, rnk = tile("rn")
                        ew("act", lambda e, pS=pS: e.activation(flat(rn), pS[0:64, 0:PW], AF.Sqrt, bias=1e-24, scale=1.0), [pSk], [rnk])
                        ew("dve", lambda e: e.reciprocal(flat(rn), flat(rn)), [rnk], [rnk])
                        kkn, kknk = kr, krk
                        ew("dve", lambda e: e.tensor_tensor(flat(kkn), flat(kr), flat(rn), ALU.mult), [krk, rnk], [kknk])
                        k2, k2k = tile("k2")
                        ew("dve", lambda e: e.tensor_tensor(k2[:], av[:], VB["k_a"], ALU.mult), [avk, ("vh",)], [k2k])
                        ew("dve", lambda e: e.tensor_tensor(k2[:], k2[:], omk[:, G0:G0 + NG].rearrange("p (h o) -> p h o", o=1).to_broadcast(HS), ALU.add),
                           [k2k, ("omk",)], [k2k])
                        ew("dve", lambda e: e.tensor_tensor(flat(k2), flat(k2), flat(k_), ALU.mult), [k2k, kk_], [k2k])
                        bv, bvk = tile("bv")
                        ew("dve", lambda e: e.tensor_tensor(flat(bv), flat(kkn), flat(av), ALU.mult), [kknk, avk], [bvk])
                        cs, csk = tile("cs")
                        ew("dve", lambda e: e.tensor_tensor_scan(flat(cs), flat(smask), flat(lw), 0.0, ALU.mult, ALU.add),
                           [lwk, ("smask",), ("smask", 1)], [csk])
                        ecs, ecsk = tile("ecs")
                        ew("act", lambda e: e.activation(flat(ecs), flat(cs), AF.Exp), [csk], [ecsk])
                        csx, csxk = lw, lwk
                        ew("dve", lambda e: e.tensor_tensor(flat(csx), flat(cs), flat(lw), ALU.subtract), [csk, lwk], [csxk])
                        ew("act", lambda e: e.activation(flat(csx), flat(csx), AF.Exp), [csxk], [csxk])
                        encs, encsk = cs, csk
                        ew("act", lambda e: e.activation(flat(encs), flat(cs), AF.Exp, scale=-1.0), [csk], [encsk])
                        dte, dtek = tile("dte")
                        ew("dve", lambda e: e.tensor_tensor(dte[:], encs[:], ecs[:, :, 63:64].to_broadcast(HS), ALU.mult), [encsk, ecsk], [dtek])
                        At, Atk = tile("At")
                        ew("dve", lambda e: e.scalar_tensor_tensor(flat(At), flat(kkn), -1.0, flat(csx), ALU.mult, ALU.mult), [kknk, csxk], [Atk])
                        Bt, Btk = tile("Bt")
                        ew("dve", lambda e: e.tensor_tensor(flat(Bt), flat(bv), flat(encs), ALU.mult), [bvk, encsk], [Btk])
                        Kt, Ktk = tile("Kt")
                        ew("dve", lambda e: e.tensor_tensor(flat(Kt), flat(k2), flat(encs), ALU.mult), [k2k, encsk], [Ktk])
                        Rt, Rtk = tile("Rt")
                        ew("dve", lambda e: e.tensor_tensor(flat(Rt), flat(r_), flat(ecs), ALU.mult), [rk_, ecsk], [Rtk])
                        bc, bck = bv, bvk
                        ew("dve", lambda e: e.tensor_tensor(flat(bc), flat(bv), flat(dte), ALU.mult), [bvk, dtek], [bck])
                        kc, kck = tile("kc")
                        ew("dve", lambda e: e.tensor_tensor(flat(kc), flat(k2), flat(dte), ALU.mult), [k2k, dtek], [kck])
                        tm = {}
                        for nm, (src, srck) in (("V", (v_, vk_)), ("bc", (bc, bck)), ("kc", (kc, kck)), ("At", (At, Atk))):
                            pT_, pTk_ = psW.next()

                            def trf(e, pT_=pT_, src=src):
                                ins = None
                                for h in range(NG):
                                    ins = e.transpose(pT_[0:64, h * 64:(h + 1) * 64], src[:, h, :].bitcast(F32), ident_f[0:64, 0:64])
                                return ins
                            cur.op("pe", trf, reads=[srck, ("ident_f",)], writes=[pTk_])
                            if nm == "At":
                                X, Xk = tile("X", [64, NG, 128])
                                ew("act", lambda e, pT_=pT_: e.copy(X[:, :, 64:128], pv3(pT_)), [pTk_], [(Xk, 1)])
                            else:
                                d, dk = tile("tm_" + nm)
                                ew("act", lambda e, pT_=pT_, d=d: e.copy(flat(d), pT_[0:64, 0:PW]), [pTk_], [dk])
                                tm[nm] = (d, dk)
                        Vt, Vtk = tm["V"]; bct, bctk = tm["bc"]; kct, kctk = tm["kc"]
                        def mm_mask(name, L, Lk, Rr, Rk, mi):
                            p_, pk_ = psW.next()
                            headmm(lambda h: p_[0:64, h * 64:(h + 1) * 64], lambda h: L[:, h, :], lambda h: Rr[:, h, :], [Lk, Rk], pk_)
                            d, dk = tile(name)
                            ew("dve", lambda e, p_=p_, d=d: e.tensor_tensor(d[:], pv3(p_), trib(mi), ALU.mult), [pk_, ("tri",)], [dk])
                            return d, dk
                        Q, Qk = mm_mask("Q0", Bt, Btk, At, Atk, 0)
                        Pm, Pmk = mm_mask("P0", At, Atk, Bt, Btk, 2)
                        AakT, AakTk = mm_mask("AakT", Kt, Ktk, At, Atk, 0)
                        ArbT, ArbTk = mm_mask("ArbT", Bt, Btk, Rt, Rtk, 1)
                        ArkT, ArkTk = mm_mask("ArkT", Kt, Ktk, Rt, Rtk, 1)
                        pX, pXk = psW.next()
                        headmm(lambda h: pX[0:64, h * 64:(h + 1) * 64], lambda h: AakT[:, h, :], lambda h: Vt[:, h, :], [AakTk, Vtk], pXk)
                        ew("act", lambda e, pX=pX: e.copy(X[:, :, 0:64], pv3(pX)), [pXk], [(Xk, 0)])
                        Xkeys = [(Xk, 0), (Xk, 1)]
                        for lv in range(6):
                            pa_, pak_ = psW.next()

                            def apf(e, pa_=pa_, Q=Q):
                                ins = None
                                for h in range(NG):
                                    ins = e.matmul(pa_[0:64, h * 128:(h + 1) * 128], Q[:, h, :], X[:, h, :], start=True, stop=True)
                                return ins
                            cur.op("pe", apf, reads=[Qk] + Xkeys, writes=[pak_])
                            ew("dve", lambda e, pa_=pa_: e.tensor_tensor(X[:], X[:], pa_[0:64, 0:NG * 128].rearrange("p (h t) -> p h t", h=NG), ALU.add),
                               [pak_] + Xkeys, Xkeys)
                            if lv < 5:
                                pq_, pqk_ = psW.next()
                                headmm(lambda h, pq_=pq_: pq_[0:64, h * 64:(h + 1) * 64], lambda h, Pm=Pm: Pm[:, h, :], lambda h, Q=Q: Q[:, h, :], [Pmk, Qk], pqk_)
                                Q2, Q2k = tile("Q%d" % ((lv + 1) % 2))
                                if lv < 4:
                                    pp_, ppk_ = psW.next()
                                    headmm(lambda h, pp_=pp_: pp_[0:64, h * 64:(h + 1) * 64], lambda h, Q=Q: Q[:, h, :], lambda h, Pm=Pm: Pm[:, h, :], [Pmk, Qk], ppk_)
                                    P2, P2k = tile("P%d" % ((lv + 1) % 2))
                                    ew("act", lambda e, pp_=pp_, P2=P2: e.copy(flat(P2), pp_[0:64, 0:PW]), [ppk_], [P2k])
                                ew("dve", lambda e, pq_=pq_, Q2=Q2: e.tensor_copy(flat(Q2), pq_[0:64, 0:PW]), [pqk_], [Q2k])
                                Q, Qk = Q2, Q2k
                                if lv < 4:
                                    Pm, Pmk = P2, P2k
                        U0 = lambda h: X[:, h, 0:64]
                        Ah = lambda h: X[:, h, 64:128]
                        pM, pMk = psW.next()
                        headmm(lambda h: pM[0:64, h * 64:(h + 1) * 64], Ah, lambda h: bct[:, h, :], Xkeys + [bctk], pMk)
                        Mc, Mck = tile("Mc")
                        ew("dve", lambda e: e.tensor_tensor(Mc[:], identb8, ecs[:, :, 63:64].to_broadcast(HS), ALU.mult), [ecsk, ("ident_f",)], [Mck])
                        ew("dve", lambda e, pM=pM: e.tensor_tensor(Mc[:], Mc[:], pv3(pM), ALU.add), [pMk, Mck], [Mck])
                        pG, pGk = psW.next()
                        headmm(lambda h: pG[0:64, h * 64:(h + 1) * 64], lambda h: bct[:, h, :], U0, Xkeys + [bctk, kctk, Vtk], pGk,
                               extra=[(lambda h: kct[:, h, :], lambda h: Vt[:, h, :])])
                        G, Gk = tile("G")
                        ew("act", lambda e, pG=pG: e.copy(flat(G), pG[0:64, 0:PW]), [pGk], [Gk])
                        pR, pRk = psW.next()
                        headmm(lambda h: pR[0:64, h * 64:(h + 1) * 64], Ah, lambda h: ArbT[:, h, :], Xkeys + [ArbTk], pRk)
                        Rh, Rhk = tile("Rh")
                        ew("dve", lambda e, pR=pR: e.tensor_tensor(Rh[:], Rt[:], pv3(pR), ALU.add), [pRk, Rtk], [Rhk])
                        pO, pOk = psW.next()
                        headmm(lambda h: pO[0:64, h * 64:(h + 1) * 64], lambda h: ArbT[:, h, :], U0, Xkeys + [ArbTk, ArkTk, Vtk, Rhk, hck], pOk,
                               extra=[(lambda h: ArkT[:, h, :], lambda h: Vt[:, h, :]), (lambda h: Rh[:, h, :], lambda h, hcur=hcur: hcur[:, h, :])])
                        pH, pHk = psW.next()
                        headmm(lambda h: pH[0:64, h * 64:(h + 1) * 64], lambda h: Mc[:, h, :], lambda h, hcur=hcur: hcur[:, h, :], [Mck, hck], pHk)
                        ew("dve", lambda e, pH=pH, hnxt=hnxt: e.tensor_tensor(hnxt[:], G[:], pv3(pH), ALU.add), [pHk, Gk], [hnk])
                        Ot, Otk = tile("Ot")
                        ew("act", lambda e, pO=pO: e.copy(flat(Ot), pO[0:64, 0:PW]), [pOk], [Otk])
                        st1, st1k = tile("st1", [64, NG])
                        st2, st2k = tile("st2", [64, NG])
                        junk, junkk = tile("junk", [64, 64])
                        for h in range(NG):
                            ew("act", lambda e, h=h: e.activation(junk[:], Ot[:, h, :], AF.Copy, accum_out=st1[:, h:h + 1]), [Otk], [junkk, (st1k, h)])
                            ew("act", lambda e, h=h: e.activation(junk[:], Ot[:, h, :], AF.Square, accum_out=st2[:, h:h + 1]), [Otk], [junkk, (st2k, h)])
                        st1a = [(st1k, h) for h in range(NG)]
                        st2a = [(st2k, h) for h in range(NG)]
                        ew("dve", lambda e: e.tensor_scalar(st1[:], st1[:], 1.0 / 64, None, ALU.mult), st1a, st1a)
                        msq, msqk = tile("msq", [64, NG])
                        ew("dve", lambda e: e.tensor_tensor(msq[:], st1[:], st1[:], ALU.mult), st1a, [msqk])
                        ew("dve", lambda e: e.scalar_tensor_tensor(st2[:], st2[:], 1.0 / 64, msq[:], ALU.mult, ALU.subtract), st2a + [msqk], st2a)
                        ew("act", lambda e: e.activation(st2[:], st2[:], AF.Sqrt, bias=float(GN_EPS), scale=1.0), st2a, st2a)
                        ew("dve", lambda e: e.reciprocal(st2[:], st2[:]), st2a, st2a)
                        b3 = lambda t: t[:].rearrange("p (h o) -> p h o", o=1).to_broadcast(HS)
                        ew("dve", lambda e: e.tensor_tensor(Ot[:], Ot[:], b3(st1), ALU.subtract), [Otk] + st1a, [Otk])
                        ew("dve", lambda e: e.tensor_tensor(Ot[:], Ot[:], b3(st2), ALU.mult), [Otk] + st2a, [Otk])
                        ew("dve", lambda e: e.tensor_tensor(flat(Ot), flat(Ot), lnw[:, G0 * 64:(G0 + NG) * 64], ALU.mult), [Otk, ("lnw",)], [Otk])
                        ew("dve", lambda e: e.tensor_tensor(flat(Ot), flat(Ot), lnb[:, G0 * 64:(G0 + NG) * 64], ALU.add), [Otk, ("lnb",)], [Otk])
                        rk3, rk3k = tile("rk3")
                        ew("dve", lambda e: e.tensor_tensor(flat(rk3), flat(r_), flat(k2), ALU.mult), [rk_, k2k], [rk3k])
                        ew("dve", lambda e: e.tensor_tensor(rk3[:], rk3[:], VB["r_k"], ALU.mult), [rk3k, ("vh",)], [rk3k])
                        pBn, pBnk = psW.next()
                        headmm(lambda h: pBn[0:64, h:h + 1], lambda h: rk3[:, h, :], lambda h: ones64f[:, 0:1], [rk3k, ("ones64f",)], pBnk)
                        sbn, sbnk = tile("sbn", [64, NG])
                        ew("dve", lambda e, pBn=pBn: e.tensor_copy(sbn[:], pBn[0:64, 0:NG]), [pBnk], [sbnk])
                        bon, bonk = tile("bon")
                        ew("dve", lambda e: e.tensor_tensor(bon[:], Vt[:], b3(sbn), ALU.mult), [Vtk, sbnk], [bonk])
                        ew("dve", lambda e: e.tensor_tensor(flat(Ot), flat(Ot), flat(bon), ALU.add), [Otk, bonk], [Otk])
                        pY, pYk = psW.next()

                        def tyf(e, pY=pY):
                            ins = None
                            for h in range(NG):
                                ins = e.transpose(pY[0:64, h * 64:(h + 1) * 64], Ot[:, h, :], ident_f[0:64, 0:64])
                            return ins
                        cur.op("pe", tyf, reads=[Otk, ("ident_f",)], writes=[pYk])
                        zs, zsk = tile("zs")
                        ew("act", lambda e: e.activation(flat(zs), flat(z_), AF.Silu), [zk_], [zsk])
                        ybn = "ybb%d" % CUR["set"]
                        if ybn not in T_:
                            T_[ybn] = es4.enter_context(nc.sbuf_tensor(ybn, [64, NG, 64], BF16))
                        ybb, ybbk = T_[ybn], (ybn,)
                        ew("dve", lambda e, pY=pY: e.tensor_tensor(ybb[:], zs[:], pv3(pY), ALU.mult), [pYk, zsk], [ybbk])
                        cur.dma("pool", yb_d[s, G0 * 64:(G0 + NG) * 64, :].rearrange("(h d) t -> d h t", h=NG)[:, :, csl], ybb[:], reads=[ybbk],
                               writes=[("yb_c", s, ci, hg)] + ([("yb", s, ti, hg)] if ci % 8 == 7 else []))


                def flush(lists):
                    n = max(len(l) for l in lists)
                    for i in range(n):
                        for l in lists:
                            if i < len(l):
                                kind, eng, a_, rd, wr, kw = l[i]
                                if kind == "op":
                                    sc.op(eng, a_, reads=rd, writes=wr)
                                else:
                                    sc.dma(eng, a_[0], a_[1], reads=rd, writes=wr, **kw)

                for s0 in range(0, NSEQ, 2):
                    for ci in range(NC_):
                        lists = []
                        for s in range(s0, min(s0 + 2, NSEQ)):
                            for hg in range(2):
                                CUR["set"] = (s % 2) * 2 + hg
                                CUR["hg"] = hg
                                CUR["list"] = []
                                body(s, ci, hg)
                                lists.append(CUR["list"])
                        flush(lists)

        sc.barrier()
        if lvl >= 30:
            es3 = ExitStack()
            with es3:
                def sb3(name, shape, dt=F32):
                    return es3.enter_context(nc.sbuf_tensor(name, list(shape), dt))
                psR = Ring(psb, "psb")
                wstg = [sb3("wstg%d" % i, [128, D]) for i in range(2)]
                wsr = Ring(wstg, "wstg")

                def load_w(name, dram, nk):
                    t = sb3(name, [128, nk, D], BF16)
                    for kc in range(nk):
                        st, stk = wsr.next()
                        sc.dma("sp", st[:], dram[kc * 128:(kc + 1) * 128, :], writes=[stk])
                        sc.op(("dve", "pool")[kc % 2], lambda e, st=st, kc=kc, t=t: e.tensor_copy(t[:, kc, :], st[:]),
                              reads=[stk], writes=[(name, kc)])
                    return t, [(name, kc) for kc in range(nk)]
                pa_sb, pa_k = load_w("pa_sb", p_a_d, 4)
                pb_sb, pb_k = load_w("pb_sb", p_b_d, 4)
                wo_sb, wo_k = load_w("wo_sb", w_out_d, 8)
                wg_sb, wg_k = load_w("wg_sb", w_pg_d, 8)
                wu_sb, wu_k = load_w("wu_sb", w_pu_d, 2)
                gpost = sb3("gpost", [128, D])
                sc.dma("sp", gpost[:], g_post_d.partition_broadcast(128), writes=[("gpost",)])
                yaT = [sb3("yaT%d" % i, [128, 4, 512], BF16) for i in range(2)]
                ybT = [sb3("ybT%d" % i, [128, 4, 512], BF16) for i in range(2)]
                gtT = [sb3("gtT%d" % i, [128, 16, 512], BF16) for i in range(2)]
                yar, ybr, gtr = Ring(yaT, "yaT"), Ring(ybT, "ybT"), Ring(gtT, "gtT")
                t1b = [sb3("t1b%d" % i, [128, 512]) for i in range(2)]
                t1r = Ring(t1b, "t1b")
                mgT = [sb3("mgT%d" % i, [128, 8, 512], BF16) for i in range(2)]
                mgr = Ring(mgT, "mgT")
                x3 = [sb3("x3_%d" % i, [128, D]) for i in range(2)]
                x3r = Ring(x3, "x3")
                p3 = [sb3("p3_%d" % i, [128, PLE]) for i in range(2)]
                p3r = Ring(p3, "p3")
                p3b = [sb3("p3b_%d" % i, [128, PLE], BF16) for i in range(2)]
                p3br = Ring(p3b, "p3b")
                ysb = [sb3("ysb%d" % i, [128, D]) for i in range(2)]
                ysr = Ring(ysb, "ysb")
                sq3 = sb3("sq3", [128, D], BF16)
                st3 = [sb3("st3_%d" % i, [128, 2]) for i in range(2)]
                st3r = Ring(st3, "st3")
                hsb = [sb3("hsb%d" % i, [128, D]) for i in range(2)]
                hsr = Ring(hsb, "hsb")
                hbb = [sb3("hbb%d" % i, [128, D], BF16) for i in range(2)]
                hbr = Ring(hbb, "hbb")
                hT = [sb3("hT%d" % i, [128, 8, 128], BF16) for i in range(2)]
                hTr = Ring(hT, "hT")
                pT = [sb3("pT%d" % i, [128, 2, 128], BF16) for i in range(2)]
                pTr = Ring(pT, "pT")
                sg = [sb3("sg%d" % i, [128, D]) for i in range(2)]
                sgr = Ring(sg, "sg")
                osb = [sb3("osb%d" % i, [128, D]) for i in range(2)]
                osr = Ring(osb, "osb")

                for s in range(NSEQ):
                    for ti in range(NT):
                        tsl = slice(ti * 512, (ti + 1) * 512)
                        ya, yak = yar.next()
                        yb, ybk = ybr.next()
                        gt, gtk = gtr.next()
                        sc.dma("sp", ya[:], ya_d[s, :, tsl].rearrange("(c p) t -> p c t", p=128),
                               reads=[("ya", s, qt) for qt in range(ti * 4, ti * 4 + 4)], writes=[yak])
                        sc.dma("sp", yb[:], yb_d[s, :, tsl].rearrange("(c p) t -> p c t", p=128),
                               reads=[("yb", s, ti), ("yb", s, ti, 0), ("yb", s, ti, 1)], writes=[ybk])
                        sc.dma("sp", gt[:], gt_d[s, :, tsl].rearrange("(c p) t -> p c t", p=128),
                               reads=[("gt", s, j, ti) for j in range(16)], writes=[gtk])
                        mg, mgk = mgr.next()
                        for m in range(8):
                            pA, pAk = psR.next()
                            pB, pBk = psR.next()

                            def abfn(e, pA=pA, pB=pB, m=m, ya=ya, yb=yb):
                                ins = None
                                for c in range(4):
                                    ins = e.matmul(pA[:], pa_sb[:, c, m * 128:(m + 1) * 128], ya[:, c, :], start=(c == 0), stop=(c == 3))
                                for c in range(4):
                                    ins = e.matmul(pB[:], pb_sb[:, c, m * 128:(m + 1) * 128], yb[:, c, :], start=(c == 0), stop=(c == 3))
                                return ins
                            sc.op("pe", abfn, reads=[yak, ybk] + pa_k + pb_k, writes=[pAk, pBk])
                            t1, t1k = t1r.next()
                            sc.op("dve", lambda e, t1=t1, pA=pA, gt=gt, m=m: e.tensor_tensor(t1[:], pA[:], gt[:, m, :], ALU.mult),
                                  reads=[pAk, gtk], writes=[t1k])
                            t2, t2k = t1r.next()
                            sc.op("dve", lambda e, t2=t2, pB=pB, gt=gt, m=m: e.tensor_tensor(t2[:], pB[:], gt[:, 8 + m, :], ALU.mult),
                                  reads=[pBk, gtk], writes=[t2k])
                            sc.op("pool", lambda e, mg=mg, t1=t1, t2=t2, m=m: e.tensor_tensor(mg[:, m, :], t1[:], t2[:], ALU.add),
                                  reads=[t1k, t2k], writes=[(mgk, m)])
                        mg_keys = [(mgk, m) for m in range(8)]
                        for a in range(4):
                            tok0 = s * S + ti * 512 + a * 128
                            xt3, x3k = x3r.next()
                            sc.dma("sp", xt3[:], x_d[tok0:tok0 + 128, :], writes=[x3k])
                            pt3, p3k = p3r.next()
                            sc.dma("sp", pt3[:], p_d[tok0:tok0 + 128, :], writes=[p3k])
                            yps = []
                            for half in range(2):
                                pY, pYk = psR.next()

                                def yfn(e, pY=pY, half=half, mg=mg, a=a):
                                    ins = None
                                    for m in range(8):
                                        ins = e.matmul(pY[:], mg[:, m, a * 128:(a + 1) * 128], wo_sb[:, m, half * 512:(half + 1) * 512],
                                                       start=(m == 0), stop=(m == 7))
                                    return ins
                                sc.op("pe", yfn, reads=mg_keys + wo_k, writes=[pYk])
                                yps.append((pY, pYk))
                            ys, ysk = ysr.next()
                            stt, sttk = st3r.next()
                            for half in range(2):
                                pY, pYk = yps[half]
                                sc.op("act", lambda e, ys=ys, pY=pY, half=half: e.copy(ys[:, half * 512:(half + 1) * 512], pY[:]),
                                      reads=[pYk], writes=[(ysk, half)])
                            sc.op("act", lambda e, ys=ys, stt=stt: e.activation(sq3[:], ys[:], AF.Square, accum_out=stt[:, 0:1]),
                                  reads=[(ysk, 0), (ysk, 1)], writes=[("sq3",), (sttk, 0)])
                            sc.op("act", lambda e, stt=stt: e.activation(stt[:, 0:1], stt[:, 0:1], AF.Sqrt, bias=float(RMS_EPS), scale=1.0 / D),
                                  reads=[(sttk, 0)], writes=[(sttk, 0)])
                            sc.op("dve", lambda e, stt=stt: e.reciprocal(stt[:, 1:2], stt[:, 0:1]), reads=[(sttk, 0)], writes=[(sttk, 1)])
                            hs, hsk = hsr.next()
                            sc.op("dve", lambda e, hs=hs, ys=ys, stt=stt: e.scalar_tensor_tensor(
                                hs[:], ys[:], stt[:, 1:2], gpost[:], ALU.mult, ALU.mult),
                                reads=[(ysk, 0), (ysk, 1), (sttk, 1), ("gpost",)], writes=[hsk])
                            sc.op("pool", lambda e, hs=hs, xt3=xt3: e.tensor_tensor(hs[:], hs[:], xt3[:], ALU.add),
                                  reads=[hsk, x3k], writes=[hsk])
                            hb, hbk = hbr.next()
                            sc.op("act", lambda e, hb=hb, hs=hs: e.copy(hb[:], hs[:]), reads=[hsk], writes=[hbk])
                            pb3, p3bk = p3br.next()
                            sc.op("pool", lambda e, pb3=pb3, pt3=pt3: e.tensor_copy(pb3[:], pt3[:]), reads=[p3k], writes=[p3bk])
                            pTp, pTpk = psR.next()
                            ptb = pTp[:].bitcast(BF16)

                            def trfn(e, ptb=ptb, hb=hb):
                                ins = None
                                for m in range(8):
                                    ins = e.transpose(ptb[:, m * 128:(m + 1) * 128], hb[:, m * 128:(m + 1) * 128], ident_b[:])
                                return ins
                            sc.op("pe", trfn, reads=[hbk, ("ident_b",)], writes=[pTpk])
                            hTt, hTk = hTr.next()
                            sc.op("dve", lambda e, hTt=hTt, ptb=ptb: e.tensor_copy(hTt[:], ptb.rearrange("p (m t) -> p m t", m=8)),
                                  reads=[pTpk], writes=[hTk])
                            pP, pPk = psR.next()
                            ppb = pP[:].bitcast(BF16)

                            def trp(e, ppb=ppb, pb3=pb3):
                                ins = None
                                for j in range(2):
                                    ins = e.transpose(ppb[:, j * 128:(j + 1) * 128], pb3[:, j * 128:(j + 1) * 128], ident_b[:])
                                return ins
                            sc.op("pe", trp, reads=[p3bk, ("ident_b",)], writes=[pPk])
                            pTt, pTk = pTr.next()
                            sc.op("act", lambda e, pTt=pTt, ppb=ppb: e.copy(pTt[:], ppb[:, 0:256].rearrange("p (m t) -> p m t", m=2)),
                                  reads=[pPk], writes=[pTk])
                            sgt, sgk = sgr.next()
                            ot, otk = osr.next()
                            for half in range(2):
                                pG, pGk = psR.next()

                                def gfn3(e, pG=pG, half=half, hTt=hTt):
                                    ins = None
                                    for m in range(8):
                                        ins = e.matmul(pG[:], hTt[:, m, :], wg_sb[:, m, half * 512:(half + 1) * 512], start=(m == 0), stop=(m == 7))
                                    return ins
                                sc.op("pe", gfn3, reads=[hTk] + wg_k, writes=[pGk])
                                sc.op("act", lambda e, sgt=sgt, pG=pG, half=half: e.activation(sgt[:, half * 512:(half + 1) * 512], pG[:], AF.Sigmoid),
                                      reads=[pGk], writes=[(sgk, half)])
                                pE, pEk = psR.next()

                                def efn(e, pE=pE, half=half, pTt=pTt):
                                    ins = None
                                    for j in range(2):
                                        ins = e.matmul(pE[:], pTt[:, j, :], wu_sb[:, j, half * 512:(half + 1) * 512], start=(j == 0), stop=(j == 1))
                                    return ins
                                sc.op("pe", efn, reads=[pTk] + wu_k, writes=[pEk])
                                sc.op("dve", lambda e, sgt=sgt, pE=pE, half=half: e.tensor_tensor(
                                    sgt[:, half * 512:(half + 1) * 512], pE[:], sgt[:, half * 512:(half + 1) * 512], ALU.mult),
                                    reads=[pEk, (sgk, half)], writes=[(sgk, half)])
                            sc.op("pool", lambda e, ot=ot, sgt=sgt, hs=hs: e.tensor_tensor(ot[:], sgt[:], hs[:], ALU.add),
                                  reads=[(sgk, 0), (sgk, 1), hsk], writes=[otk])
                            sc.dma("pool", out_d[tok0:tok0 + 128, :], ot[:], reads=[otk], writes=[("out", tok0)], final=True)

        sc.emit(nc, es)
    return nc


def t5_bucket_np(n):
    n = np.maximum(n, 0)
    nf = np.maximum(n, 1).astype(np.float32)
    large = 16 + (np.log(nf / np.float32(16)) / np.float32(math.log(128 / 16)) * np.float32(16)).astype(np.int32)
    large = np.minimum(large, 31)
    return np.where(n < 16, n, large)


def make_consts():
    c = {}
    c["c_ident"] = np.eye(128, dtype=np.float32)
    oh = np.zeros((33, 512), np.float32)
    d = np.arange(512) - 128
    bk = t5_bucket_np(d)
    for j in range(512):
        if d[j] >= 0:
            oh[bk[j], j] = 1.0
        else:
            oh[32, j] = 1.0
    c["c_onehot"] = oh
    es_ = np.zeros((128, 16, 128), np.float32)
    for n in range(16):
        es_[n, n, :] = 1.0
        es_[64 + n, n, :] = 1.0
    c["c_esel"] = es_.reshape(128, 16 * 128)
    tri = np.zeros((64, 3, 64), np.float32)
    i = np.arange(64)
    tri[:, 0, :] = (i[:, None] < i[None, :])
    tri[:, 1, :] = (i[:, None] <= i[None, :])
    tri[:, 2, :] = (i[:, None] > i[None, :])
    c["c_tri"] = tri.reshape(64, 192)
    bd = np.zeros((128, 128), np.float32)
    bd[:64, :64] = 1.0
    bd[64:, 64:] = 1.0
    c["c_bd"] = bd
    c["c_J"] = np.ascontiguousarray(np.eye(128, dtype=np.float32)[::-1])
    pen = np.zeros((16, NH, 16), np.float32)
    for qb in range(16):
        pen[qb, :, qb:] = -30000.0
    c["c_pen"] = pen.reshape(-1)
    return c


_WNAMES = ["g_pre", "w_in", "mu_shift", "w0", "w_up", "a0", "a_up", "k_k", "k_a", "r_k", "ln_x_w", "ln_x_b",
           "p_a", "p_b", "w_out", "g_post", "w_ple_up", "w_ple_gate"]


def make_in_maps(inputs, ncores, nseq, S):
    consts = make_consts()
    maps = []
    for c in range(ncores):
        m = dict(consts)
        m["x"] = np.ascontiguousarray(inputs["x"][c * nseq:(c + 1) * nseq].reshape(nseq * S, D))
        m["p"] = np.ascontiguousarray(inputs["p"][0, c * nseq:(c + 1) * nseq].reshape(nseq * S, PLE))
        m["rel_bias"] = np.ascontiguousarray(inputs["rel_bias"])
        for n in _WNAMES:
            a = np.asarray(inputs[n])[0]
            m[n] = np.ascontiguousarray(a.reshape(-1) if n == "r_k" else a)
        maps.append(m)
    return maps


def kernel(**inputs):
    inputs = {k: np.asarray(v) for k, v in inputs.items()}
    B, S, _ = inputs["x"].shape
    nseq = B // NCORES
    nc = build_program(S, nseq, lvl=99, moba=True)
    maps = make_in_maps(inputs, NCORES, nseq, S)
    res = run_bass_kernel_spmd(nc, maps, core_ids=list(range(NCORES)))
    outs = [r["out"].reshape(nseq, S, D) for r in res.results]
    return np.concatenate(outs, axis=0).astype(np.float32)
```

```python
import math
from contextlib import ExitStack

import numpy as np
import concourse.bass as bass
import concourse.mybir as mybir
from concourse.bass_utils import run_bass_kernel_spmd

F32 = mybir.dt.float32
BF16 = mybir.dt.bfloat16
F32R = mybir.dt.float32r
ALU = mybir.AluOpType
AF = mybir.ActivationFunctionType
AX = mybir.AxisListType

D = 1024
NCORES = 8
A_W = 512
B_W = 512
HD = 64
NH = 8
IN_COLS = 6272
RW_COLS = 2176
PLE = 256
BLK = 256
RMS_EPS = 1e-6
GN_EPS = 64e-5
NEG = -30000.0


class Sched:
    ENGS = ("pe", "act", "dve", "pool", "sp")
    NDMA = 8

    def __init__(self):
        self.streams = {e: [] for e in self.ENGS}
        self.waited = {e: {} for e in self.ENGS}
        self.last_w = {}
        self.readers = {}
        self.dma_cnt = {e: 0 for e in self.ENGS}
        self.dma_val = {}
        self.final_dma = []

    def _add_wait(self, eng, waits, ev):
        if ev is None:
            return
        if ev[0] == "eng":
            _, e2, j = ev
            if e2 == "pe" and eng == "pe":
                return
            key = e2
            val = j
        else:
            _, key, val = ev
        if self.waited[eng].get(key, -1) >= val:
            return
        self.waited[eng][key] = val
        waits.append(ev)
        if ev[0] == "eng":
            self.streams[ev[1]][ev[2]]["sig"] = True

    def _deps(self, eng, reads, writes):
        evs = []
        for k in reads:
            evs.append(self.last_w.get(k))
        for k in writes:
            evs.append(self.last_w.get(k))
            evs.extend(self.readers.get(k, ()))
        best = {}
        for ev in evs:
            if ev is None:
                continue
            key = ev[1]
            if key not in best or ev[2] > best[key][2]:
                best[key] = ev
        waits = []
        for ev in best.values():
            self._add_wait(eng, waits, ev)
        return waits

    def _commit(self, ev, reads, writes):
        for k in reads:
            self.readers.setdefault(k, []).append(ev)
        for k in writes:
            self.last_w[k] = ev
            self.readers[k] = []

    def op(self, eng, fn, reads=(), writes=()):
        waits = self._deps(eng, reads, writes)
        idx = len(self.streams[eng])
        self.streams[eng].append({"fn": fn, "waits": waits, "sig": False, "dma": None})
        self._commit(("eng", eng, idx), reads, writes)

    def dma(self, eng, out, in_, reads=(), writes=(), final=False, **kw):
        slot = self.dma_cnt[eng] % self.NDMA
        self.dma_cnt[eng] += 1
        key = (eng, slot)
        prev = self.dma_val.get(key, 0)
        waits = self._deps(eng, reads, writes)
        if prev > 0:
            self._add_wait(eng, waits, ("dma", key, prev))
        val = prev + 16
        self.dma_val[key] = val
        fn = lambda e, out=out, in_=in_, kw=kw: e.dma_start(out=out, in_=in_, **kw)
        self.streams[eng].append({"fn": fn, "waits": waits, "sig": False, "dma": key})
        ev = ("dma", key, val)
        self._commit(ev, reads, writes)
        if final:
            self.final_dma.append(ev)

    def barrier(self):
        evs = []
        for e in ("pe", "act", "dve", "pool"):
            for j in range(len(self.streams[e]) - 1, -1, -1):
                if self.streams[e][j]["fn"] is not None and self.streams[e][j]["dma"] is None:
                    evs.append(("eng", e, j))
                    break
        for key, val in self.dma_val.items():
            evs.append(("dma", key, val))
        for e in self.ENGS:
            waits = []
            for ev in evs:
                if ev[0] == "eng" and ev[1] == e:
                    continue
                self._add_wait(e, waits, ev)
            self.streams[e].append({"fn": None, "waits": waits, "sig": False, "dma": None})
        self.last_w = {}
        self.readers = {}

    def emit(self, nc, es):
        sems = {e: es.enter_context(nc.semaphore("sem_" + e)) for e in ("pe", "act", "dve", "pool")}
        dsems = {}
        for (e, slot) in self.dma_val:
            dsems[(e, slot)] = es.enter_context(nc.semaphore("dsem_%s%d" % (e, slot)))
        fin_waits = []
        for ev in self.final_dma:
            self._add_wait("sp", fin_waits, ev)
        counts = {}
        for e in ("pe", "act", "dve", "pool"):
            c = 0
            lst = []
            for o in self.streams[e]:
                if o["sig"]:
                    c += 1
                lst.append(c)
            counts[e] = lst
        block = es.enter_context(nc.Block())

        def run(engname, eobj):
            def do_wait(ev):
                if ev[0] == "eng":
                    eobj.wait_ge(sems[ev[1]], counts[ev[1]][ev[2]])
                else:
                    eobj.wait_ge(dsems[ev[1]], ev[2])
            for o in self.streams[engname]:
                for ev in o["waits"]:
                    do_wait(ev)
                if o["fn"] is None:
                    continue
                ins = o["fn"](eobj)
                if o["dma"] is not None:
                    ins.then_inc(dsems[o["dma"]], 16)
                elif o["sig"]:
                    ins.then_inc(sems[engname], 1)
            if engname == "sp":
                for ev in fin_waits:
                    do_wait(ev)

        @block.sync
        def _(e):
            run("sp", e)

        @block.tensor
        def _(e):
            run("pe", e)

        @block.scalar
        def _(e):
            run("act", e)

        @block.vector
        def _(e):
            run("dve", e)

        @block.gpsimd
        def _(e):
            run("pool", e)


class Ring:
    def __init__(self, tiles, name, keys=None):
        self.tiles = tiles
        self.keys = keys if keys is not None else [(name, j) for j in range(len(tiles))]
        self.i = 0

    def next(self):
        j = self.i % len(self.tiles)
        self.i += 1
        return self.tiles[j], self.keys[j]


def build_program(S, NSEQ, dbg=None, lvl=30, sub=99, moba=True, only_even=False, GRPN=4, bar=False):
    dbg = dbg or set()
    nc = bass.Bass("TRN2", target_bir_lowering=False)
    NT = S // 512
    NQT = S // 128
    NBLK = S // BLK
    NTOK = NSEQ * S

    def din(name, shape, dt=F32):
        return nc.dram_tensor(name, list(shape), dt, kind="ExternalInput").ap()

    x_d = din("x", [NTOK, D])
    p_d = din("p", [NTOK, PLE])
    g_pre_d = din("g_pre", [D])
    w_in_d = din("w_in", [D, IN_COLS])
    relb_d = din("rel_bias", [32, NH])
    mu_d = din("mu_shift", [RW_COLS])
    w0_d = din("w0", [B_W])
    w_up_d = din("w_up", [64, B_W])
    a0_d = din("a0", [B_W])
    a_up_d = din("a_up", [64, B_W])
    k_k_d = din("k_k", [B_W])
    k_a_d = din("k_a", [B_W])
    r_k_d = din("r_k", [B_W])
    lnw_d = din("ln_x_w", [B_W])
    lnb_d = din("ln_x_b", [B_W])
    p_a_d = din("p_a", [A_W, D])
    p_b_d = din("p_b", [B_W, D])
    w_out_d = din("w_out", [D, D])
    g_post_d = din("g_post", [D])
    w_pu_d = din("w_ple_up", [PLE, D])
    w_pg_d = din("w_ple_gate", [D, D])
    c_ident_d = din("c_ident", [128, 128])
    c_onehot_d = din("c_onehot", [33, 512])
    c_esel_d = din("c_esel", [128, 16 * 128])
    c_tri_d = din("c_tri", [64, 3 * 64])
    c_bd_d = din("c_bd", [128, 128])
    c_J_d = din("c_J", [128, 128])
    c_pen_d = din("c_pen", [16 * 128])

    out_d = nc.dram_tensor("out", [NTOK, D], F32, kind="ExternalOutput").ap()

    def scratch(name, shape, dt):
        kind = "ExternalOutput" if name in dbg else "Internal"
        return nc.dram_tensor(name, list(shape), dt, kind=kind).ap()

    qT_d = scratch("s_qT", [NSEQ, A_W, S], BF16)
    kT_d = scratch("s_kT", [NSEQ, A_W, S], BF16)
    v_d = scratch("s_v", [NSEQ, S, 2 * A_W], BF16)
    za_d = scratch("s_za", [NSEQ, A_W, S], BF16)
    rw_d = scratch("s_rw", [NSEQ, RW_COLS, S], F32)
    gt_d = scratch("s_gt", [NSEQ, 2 * D, S], BF16)
    ya_d = scratch("s_ya", [NSEQ, A_W, S], BF16)
    yb_d = scratch("s_yb", [NSEQ, B_W, S], BF16)
    fb_d = scratch("s_fb", [NH, 512], F32)

    sc = Sched()
    es = ExitStack()

    def sb(name, shape, dt=F32):
        return es.enter_context(nc.sbuf_tensor(name, list(shape), dt))

    def ps(name, shape, dt=F32):
        return es.enter_context(nc.psum_tensor(name, list(shape), dt))

    with es:
        ident_f = sb("ident_f", [128, 128])
        ident_b = sb("ident_b", [128, 128], BF16)
        sc.dma("sp", ident_f[:], c_ident_d[:, :], writes=[("ident_f",)])
        sc.op("dve", lambda e: e.tensor_copy(ident_b[:], ident_f[:]), reads=[("ident_f",)], writes=[("ident_b",)])
        psb = [ps("psb%d" % i, [128, 512]) for i in range(8)]
        psring = Ring(psb, "psb")

        vstage = sb("vstage", [64, 128])
        vc = sb("vc", [128, 64])
        VOFF = {}
        _r = 0
        sc.op("dve", lambda e: e.memset(vstage[:], 0.0), writes=[("vstage",)])
        for nm, dv, n in (("g_pre", g_pre_d, 8), ("mu", mu_d, 17), ("w0", w0_d, 4), ("a0", a0_d, 4),
                          ("k_k", k_k_d, 4), ("k_a", k_a_d, 4), ("r_k", r_k_d, 4)):
            VOFF[nm] = _r
            sc.dma("sp", vstage[_r:_r + n, :], dv.rearrange("(k p) -> k p", p=128), reads=[("vstage",)], writes=[("vstage", nm)])
            _r += n
        pt, pk = psring.next()
        sc.op("pe", lambda e, pt=pt: e.transpose(pt[:, 0:64], vstage[:], ident_f[0:64, 0:64]),
              reads=[("vstage",), ("ident_f",)] + [("vstage", nm) for nm in VOFF], writes=[pk])
        sc.op("dve", lambda e, pt=pt: e.tensor_copy(vc[:], pt[:, 0:64]), reads=[pk], writes=[("vc",)])
        gpre_c = vc[:, VOFF["g_pre"]:VOFF["g_pre"] + 8]

        kmean = sb("kmean", [128, NSEQ, 4, 16], F32)
        kmean_b = sb("kmean_b", [128, NSEQ, 4, 16], BF16)
        if True:
            es1 = ExitStack()
            with es1:
                def sb1(name, shape, dt=F32):
                    return es1.enter_context(nc.sbuf_tensor(name, list(shape), dt))
                wp = sb1("wp", [128, 8, IN_COLS], BF16)
                wst = [sb1("wst%d" % i, [128, 1568]) for i in range(2)]
                wring = Ring(wst, "wst")
                for kc in range(8):
                    for q4 in range(4):
                        t, tk = wring.next()
                        c0 = q4 * 1568
                        sc.dma("sp", t[:], w_in_d[kc * 128:(kc + 1) * 128, c0:c0 + 1568], writes=[tk])
                        eng = ("dve", "pool")[(kc * 4 + q4) % 2]
                        sc.op(eng, lambda e, t=t, kc=kc, c0=c0: e.tensor_scalar(
                            wp[:, kc, c0:c0 + 1568], t[:], gpre_c[:, kc:kc + 1], None, ALU.mult),
                            reads=[tk, ("vc",)], writes=[("wp", kc, q4)])
                wp_keys = [("wp", kc, q4) for kc in range(8) for q4 in range(4)]

                mu_c = vc[:, VOFF["mu"]:VOFF["mu"] + 17]
                carry = sb1("carry", [128, 17])
                xt = [sb1("xt%d" % i, [128, 4, D]) for i in range(2)]
                xring = Ring(xt, "xt")
                ub = [sb1("ub%d" % i, [128, D], BF16) for i in range(2)]
                ubring = Ring(ub, "ub")
                sqj = sb1("sqj", [128, D], BF16)
                ss = [sb1("ss%d" % i, [128, 4]) for i in range(2)]
                ssring = Ring(ss, "ss")
                rs = [sb1("rs%d" % i, [128, 4]) for i in range(2)]
                rsring = Ring(rs, "rs")
                uT = [sb1("uT%d" % i, [128, 8, 512], BF16) for i in range(2)]
                uTring = Ring(uT, "uT")
                ob = [sb1("ob%d" % i, [128, 512], BF16) for i in range(4)]
                obring = Ring(ob, "ob")
                vo = [sb1("vo%d" % i, [128, NH, 128], BF16) for i in range(2)]
                voring = Ring(vo, "vo")
                for i in range(2):
                    sc.op("dve", lambda e, i=i: e.memset(vo[i][:], 1.0), writes=[("vo", i), ("vo_init",)])
                cb = [sb1("cb%d" % i, [128, 513]) for i in range(3)]
                cbring = Ring(cb, "cb")
                db = [sb1("db%d" % i, [128, 512]) for i in range(2)]
                dbring = Ring(db, "db")
                shb = [sb1("shb%d" % i, [128, 512]) for i in range(3)]
                shring = Ring(shb, "shb")
                sc.op("dve", lambda e: e.memset(kmean[:], 0.0), writes=[("kmean",)])

                xloaded = {}

                def load_x(s, ti):
                    if s >= NSEQ:
                        return
                    tok0 = s * S + ti * 512
                    xtile, xk = xring.next()
                    sc.dma("sp", xtile[:], x_d[tok0:tok0 + 512, :].rearrange("(a p) d -> p a d", p=128),
                           writes=[xk])
                    xloaded[(s, ti)] = (xtile, xk)

                load_x(0, 0)
                for s in range(NSEQ if lvl >= 1 else 0):
                    sc.op("dve", lambda e: e.memset(carry[:], 0.0), writes=[("carry",)], reads=[])
                    for ti in range(NT):
                        nxt = (s, ti + 1) if ti + 1 < NT else (s + 1, 0)
                        load_x(*nxt)
                        xtile, xk = xloaded.pop((s, ti))
                        sst, ssk = ssring.next()
                        rst, rsk = rsring.next()
                        sc.op("dve", lambda e, sst=sst: e.memset(sst[:], 0.0), writes=[(ssk, a) for a in range(4)])
                        for a in range(4):
                            sc.op("act", lambda e, a=a, xtile=xtile, sst=sst: e.activation(
                                sqj[:], xtile[:, a, :], AF.Square, accum_out=sst[:, a:a + 1]),
                                reads=[xk], writes=[("sqj",), (ssk, a)])
                        sc.op("act", lambda e, sst=sst: e.activation(
                            sst[:], sst[:], AF.Sqrt, bias=float(RMS_EPS), scale=1.0 / D),
                            reads=[(ssk, a) for a in range(4)], writes=[(ssk, "q")])
                        sc.op("dve", lambda e, sst=sst, rst=rst: e.reciprocal(rst[:], sst[:]),
                              reads=[(ssk, "q")], writes=[rsk] + [(ssk, a) for a in range(4)])
                        uTt, uTk = uTring.next()
                        for a in range(4):
                            ubt, ubk = ubring.next()
                            sc.op("dve", lambda e, a=a, ubt=ubt, xtile=xtile, rst=rst: e.tensor_scalar(
                                ubt[:], xtile[:, a, :], rst[:, a:a + 1], None, ALU.mult),
                                reads=[xk, rsk], writes=[ubk])
                            pt, pk = psring.next()
                            ptb = pt[:].bitcast(BF16)

                            def tr_fn(e, ubt=ubt, ptb=ptb):
                                ins = None
                                for kc in range(8):
                                    ins = e.transpose(ptb[:, kc * 128:(kc + 1) * 128], ubt[:, kc * 128:(kc + 1) * 128], ident_b[:])
                                return ins
                            sc.op("pe", tr_fn, reads=[ubk, ("ident_b",)], writes=[pk])
                            eng = ("act", "dve")[a % 2]
                            if eng == "act":
                                sc.op("act", lambda e, a=a, uTt=uTt, ptb=ptb: e.copy(
                                    uTt[:, :, a * 128:(a + 1) * 128], ptb.rearrange("p (k t) -> p k t", k=8)),
                                    reads=[pk], writes=[(uTk, a)])
                            else:
                                sc.op("dve", lambda e, a=a, uTt=uTt, ptb=ptb: e.tensor_copy(
                                    uTt[:, :, a * 128:(a + 1) * 128], ptb.rearrange("p (k t) -> p k t", k=8)),
                                    reads=[pk], writes=[(uTk, a)])
                        uT_keys = [(uTk, a) for a in range(4)]

                        def proj(cc, pt, pk, uTt=uTt, uT_keys=uT_keys):
                            def fn(e):
                                ins = None
                                for kc in range(8):
                                    ins = e.matmul(pt[:], wp[:, kc, cc * 128:(cc + 1) * 128], uTt[:, kc, :],
                                                   start=(kc == 0), stop=(kc == 7))
                                return ins
                            sc.op("pe", fn, reads=uT_keys + wp_keys, writes=[pk])

                        for cc in range(49):
                            if 8 <= cc < 12:
                                continue
                            if lvl < 2 or (lvl == 2 and cc >= 4) or (lvl == 3 and cc >= 8) or (lvl == 4 and cc >= 16) or (lvl == 5 and cc >= 33):
                                continue
                            pt, pk = psring.next()
                            proj(cc, pt, pk)
                            tsl = slice(ti * 512, (ti + 1) * 512)
                            rows = slice((cc % 4) * 128, (cc % 4) * 128 + 128)
                            if cc < 4:
                                o, ok = obring.next()
                                sc.op("act", lambda e, o=o, pt=pt: e.mul(o[:], pt[:], 0.125), reads=[pk], writes=[ok])
                                sc.dma("pool", qT_d[s, rows, tsl], o[:], reads=[ok], writes=[("qT", s, cc, ti)])
                            elif cc < 8:
                                o, ok = obring.next()
                                for hb in range(2):
                                    sc.op("act", lambda e, o=o, pt=pt, hb=hb, s=s, cc=cc, ti=ti: e.activation(
                                        o[:, hb * 256:(hb + 1) * 256], pt[:, hb * 256:(hb + 1) * 256], AF.Copy,
                                        accum_out=kmean[:, s, cc - 4, 2 * ti + hb:2 * ti + hb + 1]),
                                        reads=[pk, ("kmean",)], writes=([ok] if hb == 0 else []) + [(ok, hb), ("kmean", s, cc - 4, ti, hb)])
                                sc.dma("pool", kT_d[s, rows, tsl], o[:], reads=[ok, (ok, 0), (ok, 1)], writes=[("kT", s, cc - 4, ti)])
                            elif cc < 16:
                                o, ok = obring.next()
                                sc.op("act", lambda e, o=o, pt=pt: e.activation(o[:], pt[:], AF.Silu), reads=[pk], writes=[ok])
                                sc.dma("pool", za_d[s, rows, tsl], o[:], reads=[ok], writes=[("za", s, cc - 12, ti)])
                            elif cc < 33:
                                j = cc - 16
                                c, ck = cbring.next()
                                sc.op("act", lambda e, c=c, pt=pt: e.copy(c[:, 1:513], pt[:]), reads=[pk], writes=[(ck, 1)])
                                sc.op("dve", lambda e, c=c, j=j: e.tensor_copy(c[:, 0:1], carry[:, j:j + 1]),
                                      reads=[("carry", j), ("carry",)], writes=[(ck, 0)])
                                sc.op("dve", lambda e, c=c, j=j: e.tensor_copy(carry[:, j:j + 1], c[:, 512:513]),
                                      reads=[(ck, 1), ("carry",)], writes=[("carry", j)])
                                dd, dk = dbring.next()
                                sc.op("dve", lambda e, c=c, dd=dd: e.tensor_tensor(dd[:], c[:, 0:512], c[:, 1:513], ALU.subtract),
                                      reads=[(ck, 0), (ck, 1)], writes=[dk])
                                sh, shk = shring.next()
                                sc.op("dve", lambda e, c=c, dd=dd, sh=sh, j=j: e.scalar_tensor_tensor(
                                    sh[:], dd[:], mu_c[:, j:j + 1], c[:, 1:513], ALU.mult, ALU.add),
                                    reads=[dk, (ck, 1), ("vc",)], writes=[shk])
                                sc.dma("pool", rw_d[s, j * 128:(j + 1) * 128, tsl], sh[:], reads=[shk], writes=[("rw", s, j, ti)])
                            else:
                                j = cc - 33
                                o, ok = obring.next()
                                sc.op("act", lambda e, o=o, pt=pt: e.activation(o[:], pt[:], AF.Sigmoid), reads=[pk], writes=[ok])
                                sc.dma("pool", gt_d[s, j * 128:(j + 1) * 128, tsl], o[:], reads=[ok], writes=[("gt", s, j, ti)])
                        for a in range(4 if lvl >= 7 else 0):
                            pt, pk = psring.next()

                            def vfn(e, a=a, pt=pt, uTt=uTt):
                                ins = None
                                for kc in range(8):
                                    ins = e.matmul(pt[:], uTt[:, kc, a * 128:(a + 1) * 128], wp[:, kc, 1024:1536],
                                                   start=(kc == 0), stop=(kc == 7))
                                return ins
                            sc.op("pe", vfn, reads=uT_keys + wp_keys, writes=[pk])
                            o, ok = voring.next()
                            sc.op("act", lambda e, o=o, pt=pt: e.copy(o[:, :, 0:64], pt[:].rearrange("p (h d) -> p h d", h=NH)),
                                  reads=[pk, ("vo_init",)], writes=[ok])
                            t0 = ti * 512 + a * 128
                            sc.dma("pool", v_d[s, t0:t0 + 128, :], o[:].rearrange("p h d -> p (h d)"), reads=[ok], writes=[("v", s, ti * 4 + a)])
                sc.op("dve", lambda e: e.tensor_scalar(kmean_b[:], kmean[:], 1.0 / BLK, None, ALU.mult),
                      reads=[("kmean",)] + [("kmean", s, c, ti, hb) for s in range(NSEQ) for c in range(4) for ti in range(NT) for hb in range(2)],
                      writes=[("kmean_b",)])

        sc.barrier()
        if lvl >= 10 and moba:
            es2 = ExitStack()
            with es2:
                def sb2(name, shape, dt=F32):
                    return es2.enter_context(nc.sbuf_tensor(name, list(shape), dt))
                psS = Ring(psb[0:4], "psb", keys=[("psb", j) for j in range(0, 4)])
                psN = Ring(psb[4:6], "psb", keys=[("psb", j) for j in range(4, 6)])
                psM = Ring(psb[6:8], "psb", keys=[("psb", j) for j in range(6, 8)])
                relb33 = sb2("relb33", [33, NH])
                b31bc = sb2("b31bc", [128, NH])
                sc.dma("sp", b31bc[:], relb_d[31, :].partition_broadcast(128), writes=[("b31bc",)])
                sc.op("dve", lambda e: e.memset(relb33[32:33, :], NEG), writes=[("relb33", 1)])
                sc.dma("sp", relb33[0:32, :], relb_d[:, :], writes=[("relb33", 0)])
                oneh = sb2("oneh", [33, 512])
                sc.dma("sp", oneh[:], c_onehot_d[:, :], writes=[("oneh",)])
                pt, pk = psM.next()
                sc.op("pe", lambda e, pt=pt: e.matmul(pt[0:8, :], relb33[:], oneh[:], start=True, stop=True),
                      reads=[("relb33", 0), ("relb33", 1), ("oneh",)], writes=[pk])
                fbs = sb2("fbs", [8, 512])
                sc.op("dve", lambda e, pt=pt: e.tensor_copy(fbs[:], pt[0:8, :]), reads=[pk], writes=[("fbs",)])
                sc.dma("sp", fb_d[:, :], fbs[:], reads=[("fbs",)], writes=[("fb_d",)])
                Jf = sb2("Jf", [128, 128])
                sc.dma("sp", Jf[:], c_J_d[:, :], writes=[("Jf",)])
                Tt = sb2("Tt", [128, NH, 2, 128], BF16)
                tfl = [sb2("tfl%d" % i, [128, 128]) for i in range(2)]
                tflr = Ring(tfl, "tfl")
                for h in range(NH):
                    for dl in range(2):
                        tf, tfk = tflr.next()
                        src = bass.AP(tensor=fb_d.tensor, offset=h * 512 + 1 + dl * 128, ap=[[1, 128], [1, 128]])
                        sc.dma("sp", tf[:], src, reads=[("fb_d",)], writes=[tfk])
                        pt, pk = psM.next()
                        sc.op("pe", lambda e, pt=pt, tf=tf: e.matmul(pt[:, 0:128], Jf[:], tf[:], start=True, stop=True),
                              reads=[tfk, ("Jf",)], writes=[pk])
                        sc.op("dve", lambda e, pt=pt, h=h, dl=dl: e.tensor_scalar(Tt[:, h, dl, :], pt[:, 0:128], b31bc[:, h:h + 1], None, ALU.subtract),
                              reads=[pk, ("b31bc",)], writes=[("Tt", h, dl)])
                Tt_keys = [("Tt", h, dl) for h in range(NH) for dl in range(2)]
                if "d_Tt" in dbg:
                    dTt = nc.dram_tensor("d_Tt", [128, NH * 2 * 128], BF16, kind="ExternalOutput").ap()
                    sc.dma("sp", dTt[:, :], Tt[:].rearrange("p h d q -> p (h d q)"), reads=Tt_keys)
                eself = sb2("eself", [128, 16 * 128])
                esel = sb2("esel", [128, 16, 128], BF16)
                sc.dma("sp", eself[:], c_esel_d[:, :], writes=[("eself",)])
                sc.op("dve", lambda e: e.tensor_copy(esel[:].rearrange("p a b -> p (a b)"), eself[:]), reads=[("eself",)], writes=[("esel",)])
                ones_b = sb2("ones_b", [128, 64], BF16)
                sc.op("dve", lambda e: e.memset(ones_b[:], 1.0), writes=[("ones_b",)])
                maskT = [sb2("maskT%d" % i, [128, NH, 128], BF16) for i in range(2)]
                for i in range(2):
                    sc.op("dve", lambda e, i=i: e.memset(maskT[i][:, :, :], 0.0), writes=[("maskT", i)])
                mring = Ring(maskT, "maskT")
                pen_sb = sb2("pen_sb", [128, 16 * 128])
                sc.op("dve", lambda e: e.memset(pen_sb[:], 0.0), writes=[("pen_sb", "z")])
                sc.dma("sp", pen_sb[0:1, :], c_pen_d.rearrange("(a n) -> a n", a=1), reads=[("pen_sb", "z")], writes=[("pen_sb", 0)])
                sc.dma("sp", pen_sb[64:65, :], c_pen_d.rearrange("(a n) -> a n", a=1), reads=[("pen_sb", "z")], writes=[("pen_sb", 1)])
                onesq_b = sb2("onesq_b", [128, 128], BF16)
                sc.op("dve", lambda e: e.memset(onesq_b[:], 1.0), writes=[("onesq_b",)])
                penb = sb2("penb", [128, 16 * 128], BF16)
                sc.op("dve", lambda e: e.tensor_copy(penb[:], pen_sb[:]), reads=[("pen_sb", 0), ("pen_sb", 1), ("pen_sb", "z")], writes=[("penb",)])
                kT_sb = sb2("kT_sb", [128, 4, S], BF16)
                qT_sb = sb2("qT_sb", [128, 4, S], BF16)
                v_sb = sb2("v_sb", [128, S // 128, 2 * A_W], BF16)
                gsb = [sb2("gsb%d" % i, [128, NH, 16]) for i in range(2)]
                gring = Ring(gsb, "gsb")
                top8 = [sb2("top8_%d" % i, [128, NH, 8]) for i in range(2)]
                t8ring = Ring(top8, "top8")
                selm = [sb2("selm%d" % i, [128, NH, 16]) for i in range(2)]
                sring = Ring(selm, "selm")
                mvb = [sb2("mvb%d" % i, [128, NH, 16], BF16) for i in range(2)]
                mvring = Ring(mvb, "mvb")
                Pb = [sb2("Pb%d" % i, [128, 512], BF16) for i in range(3)]
                Pring = Ring(Pb, "Pb")
                rden = [sb2("rden%d" % i, [64, 128]) for i in range(2)]
                rdring = Ring(rden, "rden")
                ynorm = [sb2("ynorm%d" % i, [64, 128]) for i in range(2)]
                ynring = Ring(ynorm, "ynorm")
                zat = [sb2("zat%d" % i, [64, NH, 128], BF16) for i in range(2)]
                zring = Ring(zat, "zat")
                yout = [sb2("yout%d" % i, [64, NH, 128], BF16) for i in range(2)]
                yring = Ring(yout, "yout")

                for s in range(NSEQ if lvl >= 11 else 0):
                    for c in range(4):
                        sc.dma("sp", kT_sb[:, c, :], kT_d[s, c * 128:(c + 1) * 128, :],
                               reads=[("kT", s, c, ti) for ti in range(NT)], writes=[("kT_sb", c)])
                        sc.dma("sp", qT_sb[:, c, :], qT_d[s, c * 128:(c + 1) * 128, :],
                               reads=[("qT", s, c, ti) for ti in range(NT)], writes=[("qT_sb", c)])
                    for j4 in range(0, S // 128, 4):
                        n4 = min(4, S // 128 - j4)
                        sc.dma("sp", v_sb[:, j4:j4 + n4, :], v_d[s, j4 * 128:(j4 + n4) * 128, :].rearrange("(a p) d -> p a d", p=128),
                               reads=[("v", s, j) for j in range(j4, j4 + n4)], writes=[("v_sb", j) for j in range(j4, j4 + n4)])
                    for qt in range(NQT if lvl >= 12 else 0):
                        QB = qt // 2
                        qsl = slice(qt * 128, (qt + 1) * 128)
                        mT, mTk = None, None
                        if bar:
                            sc.barrier()
                        if QB > 0:
                            pt, pk = psM.next()

                            def gfn(e, pt=pt, qsl=qsl, s=s, QB=QB):
                                ins = None
                                for h in range(NH):
                                    hp = slice((h % 2) * 64, (h % 2) * 64 + 64)
                                    ins = e.matmul(pt[:, h * 16:(h + 1) * 16], qT_sb[hp, h // 2, qsl], kmean_b[hp, s, h // 2, :],
                                                   start=True, stop=False)
                                    p0 = (h % 2) * 64
                                    ins = e.matmul(pt[:, h * 16:(h + 1) * 16], onesq_b[p0:p0 + 1, :],
                                                   penb[p0:p0 + 1, QB * 128 + h * 16:QB * 128 + (h + 1) * 16], start=False, stop=True)
                                return ins
                            sc.op("pe", gfn, reads=[("qT_sb", c) for c in range(4)] + [("kmean_b",), ("penb",), ("onesq_b",)], writes=[pk])
                            g, gk = gring.next()
                            sc.op("dve", lambda e, g=g, pt=pt: e.tensor_copy(g[:].rearrange("p h n -> p (h n)"), pt[:, 0:NH * 16]),
                                  reads=[pk], writes=[gk, (gk, 1)])
                            t8, t8k = t8ring.next()
                            for h in range(NH if sub >= 3 else 0):
                                sc.op("dve", lambda e, t8=t8, g=g, h=h: e.max(t8[:, h, :], g[:, h, :]),
                                      reads=[gk, (gk, 1)], writes=[(t8k, h)])
                            sm, smk = sring.next()
                            if sub >= 4:
                              sc.op("dve", lambda e, sm=sm, g=g, t8=t8: e.tensor_tensor(
                                sm[:], g[:], t8[:, :, 2:3].to_broadcast([128, NH, 16]), ALU.is_ge),
                                reads=[gk, (gk, 1)] + [(t8k, h) for h in range(NH)], writes=[smk])
                            mv, mvk = mvring.next()
                            if sub >= 5:
                              sc.op("dve", lambda e, mv=mv, sm=sm: e.tensor_scalar(mv[:], sm[:], -NEG, NEG, ALU.mult, ALU.add),
                                  reads=[smk], writes=[mvk])
                            pt2, pk2 = psM.next()
                            ptb2 = pt2[:].bitcast(BF16)

                            def mtr(e, mv=mv, ptb2=ptb2):
                                ins = None
                                for h in range(NH):
                                    ins = e.transpose(ptb2[0:16, h * 128:(h + 1) * 128], mv[:, h, :], ident_b[:])
                                return ins
                            if sub >= 6:
                                sc.op("pe", mtr, reads=[mvk, ("ident_b",)], writes=[pk2])
                            mT, mTk = mring.next()
                            if sub >= 7:
                                sc.op("dve", lambda e, mT=mT, ptb2=ptb2: e.tensor_copy(
                                    mT[0:16, :, :], ptb2[0:16, :].rearrange("p (h q) -> p h q", h=NH)),
                                    reads=[pk2], writes=[mTk])
                                sc.op("dve", lambda e, mT=mT, ptb2=ptb2: e.tensor_copy(
                                    mT[64:80, :, :], ptb2[0:16, :].rearrange("p (h q) -> p h q", h=NH)),
                                    reads=[pk2], writes=[(mTk, "b")])
                        if "d_gate" in dbg and qt == NQT - 1 and s == 0:
                            dg = nc.dram_tensor("d_gate", [128, NH * 16], F32, kind="ExternalOutput").ap()
                            sc.dma("sp", dg[:, :], g[:].rearrange("p h n -> p (h n)"), reads=[gk, (gk, 1)])
                            dsm = nc.dram_tensor("d_sm", [128, NH * 16], F32, kind="ExternalOutput").ap()
                            sc.dma("sp", dsm[:, :], sm[:].rearrange("p h n -> p (h n)"), reads=[smk])
                            dmt = nc.dram_tensor("d_mt", [33, NH * 128], BF16, kind="ExternalOutput").ap()
                            sc.dma("sp", dmt[:, :], mT[:].rearrange("p h n -> p (h n)"), reads=[mTk, (mTk, "c")])
                            dkm = nc.dram_tensor("d_km", [128, NSEQ * 64], F32, kind="ExternalOutput").ap()
                            sc.dma("sp", dkm[:, :], kmean[:].rearrange("p s c n -> p (s c n)"), reads=[("kmean_b",)])
                        if bar:
                            sc.barrier()
                        zt, ztk = zring.next()
                        sc.dma("sp", zt[:], za_d[s].rearrange("(h d) t -> d h t", h=NH)[:, :, qsl],
                               reads=[("za", s, c, qt // 4) for c in range(4)], writes=[ztk])
                        yo, yok = yring.next()
                        pend = []

                        def drain(keep):
                            while len(pend) > keep:
                                pend.pop(0)()
                        for h in range(NH if lvl >= 13 else 0):
                            hp = slice((h % 2) * 64, (h % 2) * 64 + 64)
                            c = h // 2
                            nd, ndk = psN.next()
                            kts = list(range(qt + 1))
                            ngroups = (len(kts) + GRPN - 1) // GRPN
                            for gi, g0 in enumerate(range(0, len(kts), GRPN)):
                                grp = kts[g0:g0 + GRPN]
                                st_, stk = psS.next()

                                def sfn(e, grp=grp, st_=st_, hp=hp, c=c, qsl=qsl, qt=qt, QB=QB, h=h, mT=mT):
                                    ins = None
                                    for j, kt in enumerate(grp):
                                        osl = st_[:, j * 128:(j + 1) * 128]
                                        extra = []
                                        n = kt // 2
                                        if n < QB:
                                            extra.append((esel[hp, n, :], mT[hp, h, :]))
                                        ins = e.matmul(osl, kT_sb[hp, c, kt * 128:(kt + 1) * 128], qT_sb[hp, c, qsl],
                                                       start=True, stop=(len(extra) == 0))
                                        for i2, (l_, r_) in enumerate(extra):
                                            ins = e.matmul(osl, l_, r_, start=False, stop=(i2 == len(extra) - 1))
                                    return ins
                                rd = [("kT_sb", c), ("qT_sb", c), ("esel",)]
                                if mTk is not None:
                                    rd += [mTk, (mTk, "b")]
                                sc.op("pe", sfn, reads=rd, writes=[stk])
                                for j, kt in enumerate(grp):
                                    if kt >= qt - 1:
                                        dl = qt - kt
                                        sc.op("dve", lambda e, st_=st_, j=j, h=h, dl=dl: e.tensor_tensor(
                                            st_[:, j * 128:(j + 1) * 128], st_[:, j * 128:(j + 1) * 128], Tt[:, h, dl, :], ALU.add),
                                            reads=[stk, ("Tt", h, dl)], writes=[stk])
                                P, Pk = Pring.next()
                                ng = len(grp)
                                sc.op("act", lambda e, P=P, st_=st_, ng=ng, h=h: e.activation(P[:, 0:ng * 128], st_[:, 0:ng * 128], AF.Exp, bias=b31bc[:, h:h + 1]),
                                      reads=[stk, ("b31bc",)], writes=[Pk])

                                def emit_pv(grp=grp, P=P, Pk=Pk, nd=nd, ndk=ndk, h=h, qt=qt, last=(gi == ngroups - 1), yo=yo, yok=yok, zt=zt, ztk=ztk):
                                    def pvfn(e):
                                        ins = None
                                        for j, kt in enumerate(grp):
                                            ins = e.matmul(nd[:, 0:128], v_sb[:, kt, h * 128:(h + 1) * 128], P[:, j * 128:(j + 1) * 128],
                                                           start=(kt == 0), stop=(kt == qt))
                                        return ins
                                    sc.op("pe", pvfn, reads=[Pk] + [("v_sb", kt) for kt in grp], writes=[ndk])
                                    if last:
                                        rdn, rdk = rdring.next()
                                        sc.op("dve", lambda e: e.reciprocal(rdn[:], nd[64:128, 0:128]), reads=[ndk], writes=[rdk])
                                        yn, ynk = ynring.next()
                                        sc.op("dve", lambda e: e.tensor_tensor(yn[:], nd[0:64, 0:128], rdn[:], ALU.mult),
                                              reads=[ndk, rdk], writes=[ynk])
                                        sc.op("pool", lambda e: e.tensor_tensor(yo[:, h, :], yn[:], zt[:, h, :], ALU.mult),
                                              reads=[ynk, ztk], writes=[(yok, h)])
                                pend.append(emit_pv)
                                drain(1)
                        drain(0)
                        sc.dma("pool", ya_d[s].rearrange("(h d) t -> d h t", h=NH)[:, :, qsl], yo[:],
                               reads=[(yok, h) for h in range(NH)], writes=[("ya", s, qt), yok])

        if lvl >= 20 and (lvl < 40 or not moba):
            zt_ = sb("zstub", [128, 512], BF16)
            sc.op("dve", lambda e: e.memset(zt_[:], 0.0), writes=[("zstub",)])
            for s in range(NSEQ):
                for ti in range(NT):
                    for c in range(4):
                        if lvl < 40:
                            sc.dma("sp", yb_d[s, c * 128:(c + 1) * 128, ti * 512:(ti + 1) * 512], zt_[:], reads=[("zstub",)],
                                   writes=[("yb", s, ti)] if c == 3 else [("yb_part", s, ti, c)])
                        if not moba:
                            sc.dma("sp", ya_d[s, c * 128:(c + 1) * 128, ti * 512:(ti + 1) * 512], zt_[:], reads=[("zstub",)],
                                   writes=[("ya", s, ti * 4 + c)])

        sc.barrier()
        if lvl >= 40:
            es4 = ExitStack()
            with es4:
                def sb4(name, shape, dt=F32):
                    return es4.enter_context(nc.sbuf_tensor(name, list(shape), dt))
                psW = Ring(psb, "psb")
                NC_ = S // 64
                NG = 4
                HS = [64, NG, 64]
                vs2 = sb4("vs2", [64, 64])
                sc.op("dve", lambda e: e.memset(vs2[:], 0.0), writes=[("vs2",)])
                VO2 = {}
                for i, (nm, dv) in enumerate((("w0", w0_d), ("a0", a0_d), ("k_k", k_k_d), ("k_a", k_a_d), ("r_k", r_k_d))):
                    VO2[nm] = i * 8
                    sc.dma("sp", vs2[i * 8:(i + 1) * 8, :], dv.rearrange("(h d) -> h d", d=64), reads=[("vs2",)], writes=[("vs2", nm)])
                pt, pk = psW.next()
                sc.op("pe", lambda e, pt=pt: e.transpose(pt[0:64, 0:64], vs2[:], ident_f[0:64, 0:64]),
                      reads=[("vs2",), ("ident_f",)] + [("vs2", nm) for nm in VO2], writes=[pk])
                vh = sb4("vh", [64, 64])
                sc.op("dve", lambda e, pt=pt: e.tensor_copy(vh[:], pt[0:64, 0:64]), reads=[pk], writes=[("vh",)])
                omk = sb4("omk", [64, NH])
                sc.op("dve", lambda e: e.tensor_scalar(omk[:], vh[:, VO2["k_a"]:VO2["k_a"] + 8], -1.0, 1.0, ALU.mult, ALU.add),
                      reads=[("vh",)], writes=[("omk",)])

                def vb(nm):
                    o = VO2[nm] + CUR["hg"] * NG
                    return vh[:, o:o + NG].rearrange("p (h o) -> p h o", o=1).to_broadcast(HS)
                wup = sb4("wup", [64, B_W])
                aup = sb4("aup", [64, B_W])
                sc.dma("sp", wup[:], w_up_d[:, :], writes=[("wup0",)])
                sc.dma("sp", aup[:], a_up_d[:, :], writes=[("aup0",)])
                wup_r = sb4("wup_r", [64, B_W], F32R)
                aup_r = sb4("aup_r", [64, B_W], F32R)
                sc.op("dve", lambda e: e.tensor_copy(wup_r[:], wup[:]), reads=[("wup0",)], writes=[("wup",)])
                sc.op("dve", lambda e: e.tensor_copy(aup_r[:], aup[:]), reads=[("aup0",)], writes=[("aup",)])
                lnw = sb4("lnw", [64, B_W])
                lnb = sb4("lnb", [64, B_W])
                sc.dma("sp", lnw[:], lnw_d.partition_broadcast(64), writes=[("lnw",)])
                sc.dma("sp", lnb[:], lnb_d.partition_broadcast(64), writes=[("lnb",)])
                tri = sb4("tri", [64, 3, 64])
                sc.dma("sp", tri[:].rearrange("p a b -> p (a b)"), c_tri_d[:, :], writes=[("tri",)])
                ones64 = sb4("ones64", [64, 64], F32R)
                ones64f = sb4("ones64f", [64, 2])
                sc.op("dve", lambda e: e.memset(ones64f[:], 1.0), writes=[("ones64f",)])
                ones_t = sb4("ones_t", [64, 64])
                sc.op("dve", lambda e: e.memset(ones_t[:], 1.0), writes=[("ones_t",)])
                sc.op("dve", lambda e: e.tensor_copy(ones64[:], ones_t[:]), reads=[("ones_t",)], writes=[("ones64",)])
                zeros_t = sb4("zeros_t", [64, 4, 64])
                sc.op("dve", lambda e: e.memset(zeros_t[:], 0.0), writes=[("zeros_t",)])
                smask = sb4("smask", HS)
                sc.op("dve", lambda e: e.memset(smask[:], 1.0), writes=[("smask",)])
                sc.op("dve", lambda e: e.memset(smask[:, :, 0:1], 0.0), reads=[("smask",)], writes=[("smask", 1)])
                identb8 = ident_f[0:64, 0:64].rearrange("p (o d) -> p o d", o=1).to_broadcast(HS)

                def trib(i):
                    return tri[:, i:i + 1, :].to_broadcast(HS)

                T_ = {}
                CUR = {"set": 0, "list": None, "hg": 0}

                class Defer:
                    def op(self, eng, fn, reads=(), writes=()):
                        CUR["list"].append(("op", eng, fn, list(reads), list(writes), {}))

                    def dma(self, eng, out, in_, reads=(), writes=(), **kw):
                        CUR["list"].append(("dma", eng, (out, in_), list(reads), list(writes), kw))
                cur = Defer()

                RNAMES = {"tw", "ad_r", "sq", "At", "Bt", "Kt", "Rt", "tm_V", "tm_bc", "tm_kc", "X", "Q0", "Q1", "P0", "P1",
                          "AakT", "ArbT", "ArkT", "Mc", "Rh"}

                def tile(name, shape=None):
                    nm = "r%d_%s" % (CUR["set"], name)
                    if nm not in T_:
                        T_[nm] = sb4(nm, shape or HS, F32R if name in RNAMES else F32)
                    return T_[nm], (nm,)
                HstA = [[sb4("Hst%d_%d" % (q, i), HS, F32R) for i in range(2)] for q in range(4)]

                def ew(eng, fn, reads, writes):
                    cur.op(eng, fn, reads=reads, writes=writes)

                def headmm(out_fn, l_fn, r_fn, reads, pk, extra=None):
                    items = []
                    for h in range(NG):
                        pairs = [(l_fn(h), r_fn(h))] + ([(a(h), b(h)) for a, b in extra] if extra else [])
                        for i, (l_, r_) in enumerate(pairs):
                            items.append((out_fn(h), l_, r_, i == 0, i == len(pairs) - 1))

                    def fn(e, items=items):
                        ins = None
                        for (o_, l_, r_, st, sp) in items:
                            ins = e.matmul(o_, l_, r_, start=st, stop=sp)
                        return ins
                    cur.op("pe", fn, reads=reads, writes=[pk])

                def flat(t):
                    return t[:].rearrange("p h t -> p (h t)")

                for s in range(NSEQ):
                    for hg in range(2):
                        sc.op("dve", lambda e, q=(s % 2) * 2 + hg: e.tensor_copy(HstA[q][0][:], zeros_t[:]), reads=[("zeros_t",)], writes=[("Hst", (s % 2) * 2 + hg, 0)])

                psSets = [Ring(psb[2 * q:2 * q + 2], "psb", keys=[("psb", j) for j in range(2 * q, 2 * q + 2)]) for q in range(4)]

                def body(s, ci, hg):
                    chain = (s % 2) * 2 + hg
                    psW = psSets[chain]
                    G0 = hg * NG
                    VB = {nm: vb(nm) for nm in VO2}
                    if True:
                        Hst = HstA[chain]
                        csl = slice(ci * 64, (ci + 1) * 64)
                        ti = ci // 8
                        hcur, hck = Hst[ci % 2], ("Hst", chain, ci % 2)
                        hnxt, hnk = Hst[(ci + 1) % 2], ("Hst", chain, (ci + 1) % 2)
                        fm = {}
                        for qi, nm in enumerate(("r", "k", "v", "z")):
                            t, tk = tile("in_" + nm)
                            cur.dma("sp", t[:], rw_d[s, qi * 512 + G0 * 64:qi * 512 + (G0 + NG) * 64, csl].rearrange("(h d) t -> d h t", h=NG),
                                   reads=[("rw", s, qi * 4 + j, ti) for j in range(4)], writes=[tk])
                            fm[nm] = (t, tk)
                        wd, wdk = tile("wd", [64, 64])
                        ad, adk = tile("ad", [64, 64])
                        cur.dma("sp", wd[:], rw_d[s, 2048:2112, csl], reads=[("rw", s, 16, ti)], writes=[wdk])
                        cur.dma("sp", ad[:], rw_d[s, 2112:2176, csl], reads=[("rw", s, 16, ti)], writes=[adk])
                        r_, rk_ = fm["r"]; k_, kk_ = fm["k"]; v_, vk_ = fm["v"]; z_, zk_ = fm["z"]
                        tw, twk = tile("tw", [64, 64])
                        adr, adrk = tile("ad_r", [64, 64])
                        ew("act", lambda e: e.copy(adr[:], ad[:]), [adk], [adrk])
                        ew("act", lambda e: e.activation(tw[:], wd[:], AF.Tanh), [wdk], [twk])
                        pW, pWk = psW.next()
                        headmm(lambda h: pW[0:64, h * 64:(h + 1) * 64], lambda h: wup_r[:, (G0 + h) * 64:(G0 + h + 1) * 64], lambda h: tw[:],
                               [twk, ("wup",)], pWk)
                        pA, pAk = psW.next()
                        headmm(lambda h: pA[0:64, h * 64:(h + 1) * 64], lambda h: aup_r[:, (G0 + h) * 64:(G0 + h + 1) * 64], lambda h: adr[:],
                               [adrk, ("aup",)], pAk)
                        pv3 = lambda p: p[0:64, 0:NG * 64].rearrange("p (h t) -> p h t", h=NG)
                        PW = NG * 64
                        lw, lwk = tile("lw")
                        ew("dve", lambda e, pW=pW: e.tensor_tensor(lw[:], pv3(pW), VB["w0"], ALU.add), [pWk, ("vh",)], [lwk])
                        ew("act", lambda e: e.activation(flat(lw), flat(lw), AF.Sigmoid), [lwk], [lwk])
                        av, avk = tile("av")
                        ew("dve", lambda e, pA=pA: e.tensor_tensor(av[:], pv3(pA), VB["a0"], ALU.add), [pAk, ("vh",)], [avk])
                        ew("act", lambda e: e.activation(flat(av), flat(av), AF.Sigmoid), [avk], [avk])
                        kr, krk = tile("kr")
                        ew("dve", lambda e: e.tensor_tensor(kr[:], k_[:], VB["k_k"], ALU.mult), [kk_, ("vh",)], [krk])
                        sq, sqk = tile("sq")
                        ew("dve", lambda e: e.tensor_tensor(flat(sq), flat(kr), flat(kr), ALU.mult), [krk], [sqk])
                        pS, pSk = psW.next()
                        cur.op("pe", lambda e, pS=pS: e.matmul(pS[0:64, 0:PW], ones64[:], flat(sq), start=True, stop=True),
                              reads=[sqk, ("ones64",)], writes=[pSk])
                        rn, rnk = tile("rn")
                        ew("act", lambda e, pS=pS: e.activation(flat(rn), pS[0:64, 0:PW], AF.Sqrt, bias=1e-24, scale=1.0), [pSk], [rnk])
                        ew("dve", lambda e: e.reciprocal(flat(rn), flat(rn)), [rnk], [rnk])
                        kkn, kknk = kr, krk
                        ew("dve", lambda e: e.tensor_tensor(flat(kkn), flat(kr), flat(rn), ALU.mult), [krk, rnk], [kknk])
                        k2, k2k = tile("k2")
                        ew("dve", lambda e: e.tensor_tensor(k2[:], av[:], VB["k_a"], ALU.mult), [avk, ("vh",)], [k2k])
                        ew("dve", lambda e: e.tensor_tensor(k2[:], k2[:], omk[:, G0:G0 + NG].rearrange("p (h o) -> p h o", o=1).to_broadcast(HS), ALU.add),
                           [k2k, ("omk",)], [k2k])
                        ew("dve", lambda e: e.tensor_tensor(flat(k2), flat(k2), flat(k_), ALU.mult), [k2k, kk_], [k2k])
                        bv, bvk = tile("bv")
                        ew("pool", lambda e: e.tensor_tensor(flat(bv), flat(kkn), flat(av), ALU.mult), [kknk, avk], [bvk])
                        cs, csk = tile("cs")
                        ew("dve", lambda e: e.tensor_tensor_scan(flat(cs), flat(smask), flat(lw), 0.0, ALU.mult, ALU.add),
                           [lwk, ("smask",), ("smask", 1)], [csk])
                        ecs, ecsk = tile("ecs")
                        ew("act", lambda e: e.activation(flat(ecs), flat(cs), AF.Exp, scale=-math.exp(-0.5)), [csk], [ecsk])
                        csx, csxk = lw, lwk
                        ew("dve", lambda e: e.tensor_tensor(flat(csx), flat(cs), flat(lw), ALU.subtract), [csk, lwk], [csxk])
                        ew("act", lambda e: e.activation(flat(csx), flat(csx), AF.Exp, scale=-math.exp(-0.5)), [csxk], [csxk])
                        encs, encsk = cs, csk
                        ew("act", lambda e: e.activation(flat(encs), flat(cs), AF.Exp, scale=math.exp(-0.5)), [csk], [encsk])
                        dte, dtek = tile("dte")
                        ew("dve", lambda e: e.tensor_tensor(dte[:], encs[:], ecs[:, :, 63:64].to_broadcast(HS), ALU.mult), [encsk, ecsk], [dtek])
                        At, Atk = tile("At")
                        ew("dve", lambda e: e.scalar_tensor_tensor(flat(At), flat(kkn), -1.0, flat(csx), ALU.mult, ALU.mult), [kknk, csxk], [Atk])
                        Bt, Btk = tile("Bt")
                        ew("dve", lambda e: e.tensor_tensor(flat(Bt), flat(bv), flat(encs), ALU.mult), [bvk, encsk], [Btk])
                        Kt, Ktk = tile("Kt")
                        ew("dve", lambda e: e.tensor_tensor(flat(Kt), flat(k2), flat(encs), ALU.mult), [k2k, encsk], [Ktk])
                        Rt, Rtk = tile("Rt")
                        ew("dve", lambda e: e.tensor_tensor(flat(Rt), flat(r_), flat(ecs), ALU.mult), [rk_, ecsk], [Rtk])
                        bc, bck = bv, bvk
                        ew("pool", lambda e: e.tensor_tensor(flat(bc), flat(bv), flat(dte), ALU.mult), [bvk, dtek], [bck])
                        kc, kck = tile("kc")
                        ew("pool", lambda e: e.tensor_tensor(flat(kc), flat(k2), flat(dte), ALU.mult), [k2k, dtek], [kck])
                        tm = {}
                        for nm, (src, srck) in (("V", (v_, vk_)), ("bc", (bc, bck)), ("kc", (kc, kck)), ("At", (At, Atk))):
                            pT_, pTk_ = psW.next()

                            def trf(e, pT_=pT_, src=src):
                                ins = None
                                for h in range(NG):
                                    ins = e.transpose(pT_[0:64, h * 64:(h + 1) * 64], src[:, h, :].bitcast(F32), ident_f[0:64, 0:64])
                                return ins
                            cur.op("pe", trf, reads=[srck, ("ident_f",)], writes=[pTk_])
                            if nm == "At":
                                X, Xk = tile("X", [64, NG, 128])
                                ew("act", lambda e, pT_=pT_: e.copy(X[:, :, 64:128], pv3(pT_)), [pTk_], [(Xk, 1)])
                            else:
                                d, dk = tile("tm_" + nm)
                                ew("act", lambda e, pT_=pT_, d=d: e.copy(flat(d), pT_[0:64, 0:PW]), [pTk_], [dk])
                                tm[nm] = (d, dk)
                        Vt, Vtk = tm["V"]; bct, bctk = tm["bc"]; kct, kctk = tm["kc"]
                        def mm_mask(name, L, Lk, Rr, Rk, mi):
                            p_, pk_ = psW.next()
                            headmm(lambda h: p_[0:64, h * 64:(h + 1) * 64], lambda h: L[:, h, :], lambda h: Rr[:, h, :], [Lk, Rk], pk_)
                            d, dk = tile(name)
                            ew("dve", lambda e, p_=p_, d=d: e.tensor_tensor(d[:], pv3(p_), trib(mi), ALU.mult), [pk_, ("tri",)], [dk])
                            return d, dk
                        Q, Qk = mm_mask("Q0", Bt, Btk, At, Atk, 0)
                        Pm, Pmk = mm_mask("P0", At, Atk, Bt, Btk, 2)
                        AakT, AakTk = mm_mask("AakT", Kt, Ktk, At, Atk, 0)
                        ArbT, ArbTk = mm_mask("ArbT", Bt, Btk, Rt, Rtk, 1)
                        ArkT, ArkTk = mm_mask("ArkT", Kt, Ktk, Rt, Rtk, 1)
                        pX, pXk = psW.next()
                        headmm(lambda h: pX[0:64, h * 64:(h + 1) * 64], lambda h: AakT[:, h, :], lambda h: Vt[:, h, :], [AakTk, Vtk], pXk)
                        ew("act", lambda e, pX=pX: e.copy(X[:, :, 0:64], pv3(pX)), [pXk], [(Xk, 0)])
                        Xkeys = [(Xk, 0), (Xk, 1)]
                        for lv in range(6):
                            pa_, pak_ = psW.next()

                            def apf(e, pa_=pa_, Q=Q):
                                ins = None
                                for h in range(NG):
                                    ins = e.matmul(pa_[0:64, h * 128:(h + 1) * 128], Q[:, h, :], X[:, h, :], start=True, stop=True)
                                return ins
                            cur.op("pe", apf, reads=[Qk] + Xkeys, writes=[pak_])
                            ew("dve", lambda e, pa_=pa_: e.tensor_tensor(X[:], X[:], pa_[0:64, 0:NG * 128].rearrange("p (h t) -> p h t", h=NG), ALU.add),
                               [pak_] + Xkeys, Xkeys)
                            if lv < 5:
                                pq_, pqk_ = psW.next()
                                headmm(lambda h, pq_=pq_: pq_[0:64, h * 64:(h + 1) * 64], lambda h, Pm=Pm: Pm[:, h, :], lambda h, Q=Q: Q[:, h, :], [Pmk, Qk], pqk_)
                                Q2, Q2k = tile("Q%d" % ((lv + 1) % 2))
                                if lv < 4:
                                    pp_, ppk_ = psW.next()
                                    headmm(lambda h, pp_=pp_: pp_[0:64, h * 64:(h + 1) * 64], lambda h, Q=Q: Q[:, h, :], lambda h, Pm=Pm: Pm[:, h, :], [Pmk, Qk], ppk_)
                                    P2, P2k = tile("P%d" % ((lv + 1) % 2))
                                    ew("act", lambda e, pp_=pp_, P2=P2: e.copy(flat(P2), pp_[0:64, 0:PW]), [ppk_], [P2k])
                                ew("act", lambda e, pq_=pq_, Q2=Q2: e.copy(flat(Q2), pq_[0:64, 0:PW]), [pqk_], [Q2k])
                                Q, Qk = Q2, Q2k
                                if lv < 4:
                                    Pm, Pmk = P2, P2k
                        U0 = lambda h: X[:, h, 0:64]
                        Ah = lambda h: X[:, h, 64:128]
                        pM, pMk = psW.next()
                        headmm(lambda h: pM[0:64, h * 64:(h + 1) * 64], Ah, lambda h: bct[:, h, :], Xkeys + [bctk], pMk)
                        Mc, Mck = tile("Mc")
                        ew("dve", lambda e: e.tensor_tensor(Mc[:], identb8, ecs[:, :, 63:64].to_broadcast(HS), ALU.mult), [ecsk, ("ident_f",)], [Mck])
                        ew("dve", lambda e, pM=pM: e.tensor_tensor(Mc[:], Mc[:], pv3(pM), ALU.add), [pMk, Mck], [Mck])
                        pG, pGk = psW.next()
                        headmm(lambda h: pG[0:64, h * 64:(h + 1) * 64], lambda h: bct[:, h, :], U0, Xkeys + [bctk, kctk, Vtk], pGk,
                               extra=[(lambda h: kct[:, h, :], lambda h: Vt[:, h, :])])
                        G, Gk = tile("G")
                        ew("act", lambda e, pG=pG: e.copy(flat(G), pG[0:64, 0:PW]), [pGk], [Gk])
                        pR, pRk = psW.next()
                        headmm(lambda h: pR[0:64, h * 64:(h + 1) * 64], Ah, lambda h: ArbT[:, h, :], Xkeys + [ArbTk], pRk)
                        Rh, Rhk = tile("Rh")
                        ew("dve", lambda e, pR=pR: e.tensor_tensor(Rh[:], Rt[:], pv3(pR), ALU.add), [pRk, Rtk], [Rhk])
                        pO, pOk = psW.next()
                        headmm(lambda h: pO[0:64, h * 64:(h + 1) * 64], lambda h: ArbT[:, h, :], U0, Xkeys + [ArbTk, ArkTk, Vtk, Rhk, hck], pOk,
                               extra=[(lambda h: ArkT[:, h, :], lambda h: Vt[:, h, :]), (lambda h: Rh[:, h, :], lambda h, hcur=hcur: hcur[:, h, :])])
                        pH, pHk = psW.next()
                        headmm(lambda h: pH[0:64, h * 64:(h + 1) * 64], lambda h: Mc[:, h, :], lambda h, hcur=hcur: hcur[:, h, :], [Mck, hck], pHk)
                        ew("dve", lambda e, pH=pH, hnxt=hnxt: e.tensor_tensor(hnxt[:], G[:], pv3(pH), ALU.add), [pHk, Gk], [hnk])
                        Ot, Otk = tile("Ot")
                        ew("act", lambda e, pO=pO: e.copy(flat(Ot), pO[0:64, 0:PW]), [pOk], [Otk])
                        st1, st1k = tile("st1", [64, NG])
                        st2, st2k = tile("st2", [64, NG])
                        junk, junkk = tile("junk", [64, 64])
                        for h in range(NG):
                            ew("act", lambda e, h=h: e.activation(junk[:], Ot[:, h, :], AF.Copy, accum_out=st1[:, h:h + 1]), [Otk], [junkk, (st1k, h)])
                            ew("act", lambda e, h=h: e.activation(junk[:], Ot[:, h, :], AF.Square, accum_out=st2[:, h:h + 1]), [Otk], [junkk, (st2k, h)])
                        st1a = [(st1k, h) for h in range(NG)]
                        st2a = [(st2k, h) for h in range(NG)]
                        ew("dve", lambda e: e.tensor_scalar(st1[:], st1[:], 1.0 / 64, None, ALU.mult), st1a, st1a)
                        msq, msqk = tile("msq", [64, NG])
                        ew("dve", lambda e: e.tensor_tensor(msq[:], st1[:], st1[:], ALU.mult), st1a, [msqk])
                        ew("dve", lambda e: e.scalar_tensor_tensor(st2[:], st2[:], 1.0 / 64, msq[:], ALU.mult, ALU.subtract), st2a + [msqk], st2a)
                        ew("act", lambda e: e.activation(st2[:], st2[:], AF.Sqrt, bias=float(GN_EPS), scale=1.0), st2a, st2a)
                        ew("dve", lambda e: e.reciprocal(st2[:], st2[:]), st2a, st2a)
                        b3 = lambda t: t[:].rearrange("p (h o) -> p h o", o=1).to_broadcast(HS)
                        ew("dve", lambda e: e.tensor_tensor(Ot[:], Ot[:], b3(st1), ALU.subtract), [Otk] + st1a, [Otk])
                        ew("dve", lambda e: e.tensor_tensor(Ot[:], Ot[:], b3(st2), ALU.mult), [Otk] + st2a, [Otk])
                        ew("dve", lambda e: e.tensor_tensor(flat(Ot), flat(Ot), lnw[:, G0 * 64:(G0 + NG) * 64], ALU.mult), [Otk, ("lnw",)], [Otk])
                        ew("dve", lambda e: e.tensor_tensor(flat(Ot), flat(Ot), lnb[:, G0 * 64:(G0 + NG) * 64], ALU.add), [Otk, ("lnb",)], [Otk])
                        rk3, rk3k = tile("rk3")
                        ew("pool", lambda e: e.tensor_tensor(flat(rk3), flat(r_), flat(k2), ALU.mult), [rk_, k2k], [rk3k])
                        ew("dve", lambda e: e.tensor_tensor(rk3[:], rk3[:], VB["r_k"], ALU.mult), [rk3k, ("vh",)], [rk3k])
                        pBn, pBnk = psW.next()
                        headmm(lambda h: pBn[0:64, h:h + 1], lambda h: rk3[:, h, :], lambda h: ones64f[:, 0:1], [rk3k, ("ones64f",)], pBnk)
                        sbn, sbnk = tile("sbn", [64, NG])
                        ew("dve", lambda e, pBn=pBn: e.tensor_copy(sbn[:], pBn[0:64, 0:NG]), [pBnk], [sbnk])
                        bon, bonk = tile("bon")
                        ew("dve", lambda e: e.tensor_tensor(bon[:], Vt[:], b3(sbn), ALU.mult), [Vtk, sbnk], [bonk])
                        ew("dve", lambda e: e.tensor_tensor(flat(Ot), flat(Ot), flat(bon), ALU.add), [Otk, bonk], [Otk])
                        pY, pYk = psW.next()

                        def tyf(e, pY=pY):
                            ins = None
                            for h in range(NG):
                                ins = e.transpose(pY[0:64, h * 64:(h + 1) * 64], Ot[:, h, :], ident_f[0:64, 0:64])
                            return ins
                        cur.op("pe", tyf, reads=[Otk, ("ident_f",)], writes=[pYk])
                        zs, zsk = tile("zs")
                        ew("act", lambda e: e.activation(flat(zs), flat(z_), AF.Silu), [zk_], [zsk])
                        ybn = "ybb%d" % CUR["set"]
                        if ybn not in T_:
                            T_[ybn] = es4.enter_context(nc.sbuf_tensor(ybn, [64, NG, 64], BF16))
                        ybb, ybbk = T_[ybn], (ybn,)
                        ew("dve", lambda e, pY=pY: e.tensor_tensor(ybb[:], zs[:], pv3(pY), ALU.mult), [pYk, zsk], [ybbk])
                        cur.dma("pool", yb_d[s, G0 * 64:(G0 + NG) * 64, :].rearrange("(h d) t -> d h t", h=NG)[:, :, csl], ybb[:], reads=[ybbk],
                               writes=[("yb_c", s, ci, hg)] + ([("yb", s, ti, hg)] if ci % 8 == 7 else []))


                def flush(lists):
                    n = max(len(l) for l in lists)
                    for i in range(n):
                        for l in lists:
                            if i < len(l):
                                kind, eng, a_, rd, wr, kw = l[i]
                                if kind == "op":
                                    sc.op(eng, a_, reads=rd, writes=wr)
                                else:
                                    sc.dma(eng, a_[0], a_[1], reads=rd, writes=wr, **kw)

                for s0 in range(0, NSEQ, 2):
                    for ci in range(NC_):
                        lists = []
                        for s in range(s0, min(s0 + 2, NSEQ)):
                            for hg in range(2):
                                CUR["set"] = (s % 2) * 2 + hg
                                CUR["hg"] = hg
                                CUR["list"] = []
                                body(s, ci, hg)
                                lists.append(CUR["list"])
                        flush(lists)

        sc.barrier()
        if lvl >= 30:
            es3 = ExitStack()
            with es3:
                def sb3(name, shape, dt=F32):
                    return es3.enter_context(nc.sbuf_tensor(name, list(shape), dt))
                psR = Ring(psb, "psb")
                wstg = [sb3("wstg%d" % i, [128, D]) for i in range(2)]
                wsr = Ring(wstg, "wstg")

                def load_w(name, dram, nk):
                    t = sb3(name, [128, nk, D], BF16)
                    for kc in range(nk):
                        st, stk = wsr.next()
                        sc.dma("sp", st[:], dram[kc * 128:(kc + 1) * 128, :], writes=[stk])
                        sc.op(("dve", "pool")[kc % 2], lambda e, st=st, kc=kc, t=t: e.tensor_copy(t[:, kc, :], st[:]),
                              reads=[stk], writes=[(name, kc)])
                    return t, [(name, kc) for kc in range(nk)]
                pa_sb, pa_k = load_w("pa_sb", p_a_d, 4)
                pb_sb, pb_k = load_w("pb_sb", p_b_d, 4)
                wo_sb, wo_k = load_w("wo_sb", w_out_d, 8)
                wg_sb, wg_k = load_w("wg_sb", w_pg_d, 8)
                wu_sb, wu_k = load_w("wu_sb", w_pu_d, 2)
                gpost = sb3("gpost", [128, D])
                sc.dma("sp", gpost[:], g_post_d.partition_broadcast(128), writes=[("gpost",)])
                yaT = [sb3("yaT%d" % i, [128, 4, 512], BF16) for i in range(2)]
                ybT = [sb3("ybT%d" % i, [128, 4, 512], BF16) for i in range(2)]
                gtT = [sb3("gtT%d" % i, [128, 16, 512], BF16) for i in range(2)]
                yar, ybr, gtr = Ring(yaT, "yaT"), Ring(ybT, "ybT"), Ring(gtT, "gtT")
                t1b = [sb3("t1b%d" % i, [128, 512]) for i in range(2)]
                t1r = Ring(t1b, "t1b")
                mgT = [sb3("mgT%d" % i, [128, 8, 512], BF16) for i in range(2)]
                mgr = Ring(mgT, "mgT")
                x3 = [sb3("x3_%d" % i, [128, D]) for i in range(2)]
                x3r = Ring(x3, "x3")
                p3 = [sb3("p3_%d" % i, [128, PLE]) for i in range(2)]
                p3r = Ring(p3, "p3")
                p3b = [sb3("p3b_%d" % i, [128, PLE], BF16) for i in range(2)]
                p3br = Ring(p3b, "p3b")
                ysb = [sb3("ysb%d" % i, [128, D]) for i in range(2)]
                ysr = Ring(ysb, "ysb")
                sq3 = sb3("sq3", [128, D], BF16)
                st3 = [sb3("st3_%d" % i, [128, 2]) for i in range(2)]
                st3r = Ring(st3, "st3")
                hsb = [sb3("hsb%d" % i, [128, D]) for i in range(2)]
                hsr = Ring(hsb, "hsb")
                hbb = [sb3("hbb%d" % i, [128, D], BF16) for i in range(2)]
                hbr = Ring(hbb, "hbb")
                hT = [sb3("hT%d" % i, [128, 8, 128], BF16) for i in range(2)]
                hTr = Ring(hT, "hT")
                pT = [sb3("pT%d" % i, [128, 2, 128], BF16) for i in range(2)]
                pTr = Ring(pT, "pT")
                sg = [sb3("sg%d" % i, [128, D]) for i in range(2)]
                sgr = Ring(sg, "sg")
                osb = [sb3("osb%d" % i, [128, D]) for i in range(2)]
                osr = Ring(osb, "osb")

                for s in range(NSEQ):
                    for ti in range(NT):
                        tsl = slice(ti * 512, (ti + 1) * 512)
                        ya, yak = yar.next()
                        yb, ybk = ybr.next()
                        gt, gtk = gtr.next()
                        sc.dma("sp", ya[:], ya_d[s, :, tsl].rearrange("(c p) t -> p c t", p=128),
                               reads=[("ya", s, qt) for qt in range(ti * 4, ti * 4 + 4)], writes=[yak])
                        sc.dma("sp", yb[:], yb_d[s, :, tsl].rearrange("(c p) t -> p c t", p=128),
                               reads=[("yb", s, ti), ("yb", s, ti, 0), ("yb", s, ti, 1)], writes=[ybk])
                        sc.dma("sp", gt[:], gt_d[s, :, tsl].rearrange("(c p) t -> p c t", p=128),
                               reads=[("gt", s, j, ti) for j in range(16)], writes=[gtk])
                        mg, mgk = mgr.next()
                        for m in range(8):
                            pA, pAk = psR.next()
                            pB, pBk = psR.next()

                            def abfn(e, pA=pA, pB=pB, m=m, ya=ya, yb=yb):
                                ins = None
                                for c in range(4):
                                    ins = e.matmul(pA[:], pa_sb[:, c, m * 128:(m + 1) * 128], ya[:, c, :], start=(c == 0), stop=(c == 3))
                                for c in range(4):
                                    ins = e.matmul(pB[:], pb_sb[:, c, m * 128:(m + 1) * 128], yb[:, c, :], start=(c == 0), stop=(c == 3))
                                return ins
                            sc.op("pe", abfn, reads=[yak, ybk] + pa_k + pb_k, writes=[pAk, pBk])
                            t1, t1k = t1r.next()
                            sc.op("dve", lambda e, t1=t1, pA=pA, gt=gt, m=m: e.tensor_tensor(t1[:], pA[:], gt[:, m, :], ALU.mult),
                                  reads=[pAk, gtk], writes=[t1k])
                            t2, t2k = t1r.next()
                            sc.op("dve", lambda e, t2=t2, pB=pB, gt=gt, m=m: e.tensor_tensor(t2[:], pB[:], gt[:, 8 + m, :], ALU.mult),
                                  reads=[pBk, gtk], writes=[t2k])
                            sc.op("pool", lambda e, mg=mg, t1=t1, t2=t2, m=m: e.tensor_tensor(mg[:, m, :], t1[:], t2[:], ALU.add),
                                  reads=[t1k, t2k], writes=[(mgk, m)])
                        mg_keys = [(mgk, m) for m in range(8)]
                        for a in range(4):
                            tok0 = s * S + ti * 512 + a * 128
                            xt3, x3k = x3r.next()
                            sc.dma("sp", xt3[:], x_d[tok0:tok0 + 128, :], writes=[x3k])
                            pt3, p3k = p3r.next()
                            sc.dma("sp", pt3[:], p_d[tok0:tok0 + 128, :], writes=[p3k])
                            yps = []
                            for half in range(2):
                                pY, pYk = psR.next()

                                def yfn(e, pY=pY, half=half, mg=mg, a=a):
                                    ins = None
                                    for m in range(8):
                                        ins = e.matmul(pY[:], mg[:, m, a * 128:(a + 1) * 128], wo_sb[:, m, half * 512:(half + 1) * 512],
                                                       start=(m == 0), stop=(m == 7))
                                    return ins
                                sc.op("pe", yfn, reads=mg_keys + wo_k, writes=[pYk])
                                yps.append((pY, pYk))
                            ys, ysk = ysr.next()
                            stt, sttk = st3r.next()
                            for half in range(2):
                                pY, pYk = yps[half]
                                sc.op("act", lambda e, ys=ys, pY=pY, half=half: e.copy(ys[:, half * 512:(half + 1) * 512], pY[:]),
                                      reads=[pYk], writes=[(ysk, half)])
                            sc.op("act", lambda e, ys=ys, stt=stt: e.activation(sq3[:], ys[:], AF.Square, accum_out=stt[:, 0:1]),
                                  reads=[(ysk, 0), (ysk, 1)], writes=[("sq3",), (sttk, 0)])
                            sc.op("act", lambda e, stt=stt: e.activation(stt[:, 0:1], stt[:, 0:1], AF.Sqrt, bias=float(RMS_EPS), scale=1.0 / D),
                                  reads=[(sttk, 0)], writes=[(sttk, 0)])
                            sc.op("dve", lambda e, stt=stt: e.reciprocal(stt[:, 1:2], stt[:, 0:1]), reads=[(sttk, 0)], writes=[(sttk, 1)])
                            hs, hsk = hsr.next()
                            sc.op("dve", lambda e, hs=hs, ys=ys, stt=stt: e.scalar_tensor_tensor(
                                hs[:], ys[:], stt[:, 1:2], gpost[:], ALU.mult, ALU.mult),
                                reads=[(ysk, 0), (ysk, 1), (sttk, 1), ("gpost",)], writes=[hsk])
                            sc.op("pool", lambda e, hs=hs, xt3=xt3: e.tensor_tensor(hs[:], hs[:], xt3[:], ALU.add),
                                  reads=[hsk, x3k], writes=[hsk])
                            hb, hbk = hbr.next()
                            sc.op("act", lambda e, hb=hb, hs=hs: e.copy(hb[:], hs[:]), reads=[hsk], writes=[hbk])
                            pb3, p3bk = p3br.next()
                            sc.op("pool", lambda e, pb3=pb3, pt3=pt3: e.tensor_copy(pb3[:], pt3[:]), reads=[p3k], writes=[p3bk])
                            pTp, pTpk = psR.next()
                            ptb = pTp[:].bitcast(BF16)

                            def trfn(e, ptb=ptb, hb=hb):
                                ins = None
                                for m in range(8):
                                    ins = e.transpose(ptb[:, m * 128:(m + 1) * 128], hb[:, m * 128:(m + 1) * 128], ident_b[:])
                                return ins
                            sc.op("pe", trfn, reads=[hbk, ("ident_b",)], writes=[pTpk])
                            hTt, hTk = hTr.next()
                            sc.op("dve", lambda e, hTt=hTt, ptb=ptb: e.tensor_copy(hTt[:], ptb.rearrange("p (m t) -> p m t", m=8)),
                                  reads=[pTpk], writes=[hTk])
                            pP, pPk = psR.next()
                            ppb = pP[:].bitcast(BF16)

                            def trp(e, ppb=ppb, pb3=pb3):
                                ins = None
                                for j in range(2):
                                    ins = e.transpose(ppb[:, j * 128:(j + 1) * 128], pb3[:, j * 128:(j + 1) * 128], ident_b[:])
                                return ins
                            sc.op("pe", trp, reads=[p3bk, ("ident_b",)], writes=[pPk])
                            pTt, pTk = pTr.next()
                            sc.op("act", lambda e, pTt=pTt, ppb=ppb: e.copy(pTt[:], ppb[:, 0:256].rearrange("p (m t) -> p m t", m=2)),
                                  reads=[pPk], writes=[pTk])
                            sgt, sgk = sgr.next()
                            ot, otk = osr.next()
                            for half in range(2):
                                pG, pGk = psR.next()

                                def gfn3(e, pG=pG, half=half, hTt=hTt):
                                    ins = None
                                    for m in range(8):
                                        ins = e.matmul(pG[:], hTt[:, m, :], wg_sb[:, m, half * 512:(half + 1) * 512], start=(m == 0), stop=(m == 7))
                                    return ins
                                sc.op("pe", gfn3, reads=[hTk] + wg_k, writes=[pGk])
                                sc.op("act", lambda e, sgt=sgt, pG=pG, half=half: e.activation(sgt[:, half * 512:(half + 1) * 512], pG[:], AF.Sigmoid),
                                      reads=[pGk], writes=[(sgk, half)])
                                pE, pEk = psR.next()

                                def efn(e, pE=pE, half=half, pTt=pTt):
                                    ins = None
                                    for j in range(2):
                                        ins = e.matmul(pE[:], pTt[:, j, :], wu_sb[:, j, half * 512:(half + 1) * 512], start=(j == 0), stop=(j == 1))
                                    return ins
                                sc.op("pe", efn, reads=[pTk] + wu_k, writes=[pEk])
                                sc.op("dve", lambda e, sgt=sgt, pE=pE, half=half: e.tensor_tensor(
                                    sgt[:, half * 512:(half + 1) * 512], pE[:], sgt[:, half * 512:(half + 1) * 512], ALU.mult),
                                    reads=[pEk, (sgk, half)], writes=[(sgk, half)])
                            sc.op("pool", lambda e, ot=ot, sgt=sgt, hs=hs: e.tensor_tensor(ot[:], sgt[:], hs[:], ALU.add),
                                  reads=[(sgk, 0), (sgk, 1), hsk], writes=[otk])
                            sc.dma("pool", out_d[tok0:tok0 + 128, :], ot[:], reads=[otk], writes=[("out", tok0)], final=True)

        sc.emit(nc, es)
    return nc


def t5_bucket_np(n):
    n = np.maximum(n, 0)
    nf = np.maximum(n, 1).astype(np.float32)
    large = 16 + (np.log(nf / np.float32(16)) / np.float32(math.log(128 / 16)) * np.float32(16)).astype(np.int32)
    large = np.minimum(large, 31)
    return np.where(n < 16, n, large)


def make_consts():
    c = {}
    c["c_ident"] = np.eye(128, dtype=np.float32)
    oh = np.zeros((33, 512), np.float32)
    d = np.arange(512) - 128
    bk = t5_bucket_np(d)
    for j in range(512):
        if d[j] >= 0:
            oh[bk[j], j] = 1.0
        else:
            oh[32, j] = 1.0
    c["c_onehot"] = oh
    es_ = np.zeros((128, 16, 128), np.float32)
    for n in range(16):
        es_[n, n, :] = 1.0
        es_[64 + n, n, :] = 1.0
    c["c_esel"] = es_.reshape(128, 16 * 128)
    tri = np.zeros((64, 3, 64), np.float32)
    i = np.arange(64)
    tri[:, 0, :] = (i[:, None] < i[None, :])
    tri[:, 1, :] = (i[:, None] <= i[None, :])
    tri[:, 2, :] = (i[:, None] > i[None, :])
    c["c_tri"] = tri.reshape(64, 192)
    bd = np.zeros((128, 128), np.float32)
    bd[:64, :64] = 1.0
    bd[64:, 64:] = 1.0
    c["c_bd"] = bd
    c["c_J"] = np.ascontiguousarray(np.eye(128, dtype=np.float32)[::-1])
    pen = np.zeros((16, NH, 16), np.float32)
    for qb in range(16):
        pen[qb, :, qb:] = -30000.0
    c["c_pen"] = pen.reshape(-1)
    return c


_WNAMES = ["g_pre", "w_in", "mu_shift", "w0", "w_up", "a0", "a_up", "k_k", "k_a", "r_k", "ln_x_w", "ln_x_b",
           "p_a", "p_b", "w_out", "g_post", "w_ple_up", "w_ple_gate"]


def make_in_maps(inputs, ncores, nseq, S):
    consts = make_consts()
    maps = []
    for c in range(ncores):
        m = dict(consts)
        m["x"] = np.ascontiguousarray(inputs["x"][c * nseq:(c + 1) * nseq].reshape(nseq * S, D))
        m["p"] = np.ascontiguousarray(inputs["p"][0, c * nseq:(c + 1) * nseq].reshape(nseq * S, PLE))
        m["rel_bias"] = np.ascontiguousarray(inputs["rel_bias"])
        for n in _WNAMES:
            a = np.asarray(inputs[n])[0]
            m[n] = np.ascontiguousarray(a.reshape(-1) if n == "r_k" else a)
        maps.append(m)
    return maps


def kernel(**inputs):
    inputs = {k: np.asarray(v) for k, v in inputs.items()}
    B, S, _ = inputs["x"].shape
    nseq = B // NCORES
    nc = build_program(S, nseq, lvl=99, moba=True)
    maps = make_in_maps(inputs, NCORES, nseq, S)
    res = run_bass_kernel_spmd(nc, maps, core_ids=list(range(NCORES)))
    outs = [r["out"].reshape(nseq, S, D) for r in res.results]
    return np.concatenate(outs, axis=0).astype(np.float32)
```

```python
import math
from contextlib import ExitStack

import numpy as np
import concourse.bass as bass
import concourse.mybir as mybir
from concourse.bass_utils import run_bass_kernel_spmd

F32 = mybir.dt.float32
BF16 = mybir.dt.bfloat16
F32R = mybir.dt.float32r
ALU = mybir.AluOpType
AF = mybir.ActivationFunctionType
AX = mybir.AxisListType

D = 1024
NCORES = 8
A_W = 512
B_W = 512
HD = 64
NH = 8
IN_COLS = 6272
RW_COLS = 2176
PLE = 256
BLK = 256
RMS_EPS = 1e-6
GN_EPS = 64e-5
NEG = -30000.0


class Sched:
    ENGS = ("pe", "act", "dve", "pool", "sp")
    NDMA = 8

    def __init__(self):
        self.streams = {e: [] for e in self.ENGS}
        self.waited = {e: {} for e in self.ENGS}
        self.last_w = {}
        self.readers = {}
        self.dma_cnt = {e: 0 for e in self.ENGS}
        self.dma_val = {}
        self.final_dma = []

    def _add_wait(self, eng, waits, ev):
        if ev is None:
            return
        if ev[0] == "eng":
            _, e2, j = ev
            if e2 == "pe" and eng == "pe":
                return
            key = e2
            val = j
        else:
            _, key, val = ev
        if self.waited[eng].get(key, -1) >= val:
            return
        self.waited[eng][key] = val
        waits.append(ev)
        if ev[0] == "eng":
            self.streams[ev[1]][ev[2]]["sig"] = True

    def _deps(self, eng, reads, writes):
        evs = []
        for k in reads:
            evs.append(self.last_w.get(k))
        for k in writes:
            evs.append(self.last_w.get(k))
            evs.extend(self.readers.get(k, ()))
        best = {}
        for ev in evs:
            if ev is None:
                continue
            key = ev[1]
            if key not in best or ev[2] > best[key][2]:
                best[key] = ev
        waits = []
        for ev in best.values():
            self._add_wait(eng, waits, ev)
        return waits

    def _commit(self, ev, reads, writes):
        for k in reads:
            self.readers.setdefault(k, []).append(ev)
        for k in writes:
            self.last_w[k] = ev
            self.readers[k] = []

    def op(self, eng, fn, reads=(), writes=()):
        waits = self._deps(eng, reads, writes)
        idx = len(self.streams[eng])
        self.streams[eng].append({"fn": fn, "waits": waits, "sig": False, "dma": None})
        self._commit(("eng", eng, idx), reads, writes)

    def dma(self, eng, out, in_, reads=(), writes=(), final=False, **kw):
        slot = self.dma_cnt[eng] % self.NDMA
        self.dma_cnt[eng] += 1
        key = (eng, slot)
        prev = self.dma_val.get(key, 0)
        waits = self._deps(eng, reads, writes)
        if prev > 0:
            self._add_wait(eng, waits, ("dma", key, prev))
        val = prev + 16
        self.dma_val[key] = val
        fn = lambda e, out=out, in_=in_, kw=kw: e.dma_start(out=out, in_=in_, **kw)
        self.streams[eng].append({"fn": fn, "waits": waits, "sig": False, "dma": key})
        ev = ("dma", key, val)
        self._commit(ev, reads, writes)
        if final:
            self.final_dma.append(ev)

    def barrier(self):
        evs = []
        for e in ("pe", "act", "dve", "pool"):
            for j in range(len(self.streams[e]) - 1, -1, -1):
                if self.streams[e][j]["fn"] is not None and self.streams[e][j]["dma"] is None:
                    evs.append(("eng", e, j))
                    break
        for key, val in self.dma_val.items():
            evs.append(("dma", key, val))
        for e in self.ENGS:
            waits = []
            for ev in evs:
                if ev[0] == "eng" and ev[1] == e:
                    continue
                self._add_wait(e, waits, ev)
            self.streams[e].append({"fn": None, "waits": waits, "sig": False, "dma": None})
        self.last_w = {}
        self.readers = {}

    def emit(self, nc, es):
        sems = {e: es.enter_context(nc.semaphore("sem_" + e)) for e in ("pe", "act", "dve", "pool")}
        dsems = {}
        for (e, slot) in self.dma_val:
            dsems[(e, slot)] = es.enter_context(nc.semaphore("dsem_%s%d" % (e, slot)))
        fin_waits = []
        for ev in self.final_dma:
            self._add_wait("sp", fin_waits, ev)
        counts = {}
        for e in ("pe", "act", "dve", "pool"):
            c = 0
            lst = []
            for o in self.streams[e]:
                if o["sig"]:
                    c += 1
                lst.append(c)
            counts[e] = lst
        block = es.enter_context(nc.Block())

        def run(engname, eobj):
            def do_wait(ev):
                if ev[0] == "eng":
                    eobj.wait_ge(sems[ev[1]], counts[ev[1]][ev[2]])
                else:
                    eobj.wait_ge(dsems[ev[1]], ev[2])
            for o in self.streams[engname]:
                for ev in o["waits"]:
                    do_wait(ev)
                if o["fn"] is None:
                    continue
                ins = o["fn"](eobj)
                if o["dma"] is not None:
                    ins.then_inc(dsems[o["dma"]], 16)
                elif o["sig"]:
                    ins.then_inc(sems[engname], 1)
            if engname == "sp":
                for ev in fin_waits:
                    do_wait(ev)

        @block.sync
        def _(e):
            run("sp", e)

        @block.tensor
        def _(e):
            run("pe", e)

        @block.scalar
        def _(e):
            run("act", e)

        @block.vector
        def _(e):
            run("dve", e)

        @block.gpsimd
        def _(e):
            run("pool", e)


class Ring:
    def __init__(self, tiles, name, keys=None):
        self.tiles = tiles
        self.keys = keys if keys is not None else [(name, j) for j in range(len(tiles))]
        self.i = 0

    def next(self):
        j = self.i % len(self.tiles)
        self.i += 1
        return self.tiles[j], self.keys[j]


def build_program(S, NSEQ, dbg=None, lvl=30, sub=99, moba=True, only_even=False, GRPN=4, bar=False):
    dbg = dbg or set()
    nc = bass.Bass("TRN2", target_bir_lowering=False)
    NT = S // 512
    NQT = S // 128
    NBLK = S // BLK
    NTOK = NSEQ * S

    def din(name, shape, dt=F32):
        return nc.dram_tensor(name, list(shape), dt, kind="ExternalInput").ap()

    x_d = din("x", [NTOK, D])
    p_d = din("p", [NTOK, PLE])
    g_pre_d = din("g_pre", [D])
    w_in_d = din("w_in", [D, IN_COLS])
    relb_d = din("rel_bias", [32, NH])
    mu_d = din("mu_shift", [RW_COLS])
    w0_d = din("w0", [B_W])
    w_up_d = din("w_up", [64, B_W])
    a0_d = din("a0", [B_W])
    a_up_d = din("a_up", [64, B_W])
    k_k_d = din("k_k", [B_W])
    k_a_d = din("k_a", [B_W])
    r_k_d = din("r_k", [B_W])
    lnw_d = din("ln_x_w", [B_W])
    lnb_d = din("ln_x_b", [B_W])
    p_a_d = din("p_a", [A_W, D])
    p_b_d = din("p_b", [B_W, D])
    w_out_d = din("w_out", [D, D])
    g_post_d = din("g_post", [D])
    w_pu_d = din("w_ple_up", [PLE, D])
    w_pg_d = din("w_ple_gate", [D, D])
    c_ident_d = din("c_ident", [128, 128])
    c_onehot_d = din("c_onehot", [33, 512])
    c_esel_d = din("c_esel", [128, 16 * 128])
    c_tri_d = din("c_tri", [64, 3 * 64])
    c_bd_d = din("c_bd", [128, 128])
    c_J_d = din("c_J", [128, 128])
    c_pen_d = din("c_pen", [16 * 128])

    out_d = nc.dram_tensor("out", [NTOK, D], F32, kind="ExternalOutput").ap()

    def scratch(name, shape, dt):
        kind = "ExternalOutput" if name in dbg else "Internal"
        return nc.dram_tensor(name, list(shape), dt, kind=kind).ap()

    qT_d = scratch("s_qT", [NSEQ, A_W, S], BF16)
    kT_d = scratch("s_kT", [NSEQ, A_W, S], BF16)
    v_d = scratch("s_v", [NSEQ, S, 2 * A_W], BF16)
    za_d = scratch("s_za", [NSEQ, A_W, S], BF16)
    rw_d = scratch("s_rw", [NSEQ, RW_COLS, S], F32)
    gt_d = scratch("s_gt", [NSEQ, 2 * D, S], BF16)
    ya_d = scratch("s_ya", [NSEQ, A_W, S], BF16)
    yb_d = scratch("s_yb", [NSEQ, B_W, S], BF16)
    fb_d = scratch("s_fb", [NH, 512], F32)

    sc = Sched()
    es = ExitStack()

    def sb(name, shape, dt=F32):
        return es.enter_context(nc.sbuf_tensor(name, list(shape), dt))

    def ps(name, shape, dt=F32):
        return es.enter_context(nc.psum_tensor(name, list(shape), dt))

    with es:
        ident_f = sb("ident_f", [128, 128])
        ident_b = sb("ident_b", [128, 128], BF16)
        sc.dma("sp", ident_f[:], c_ident_d[:, :], writes=[("ident_f",)])
        sc.op("dve", lambda e: e.tensor_copy(ident_b[:], ident_f[:]), reads=[("ident_f",)], writes=[("ident_b",)])
        psb = [ps("psb%d" % i, [128, 512]) for i in range(8)]
        psring = Ring(psb, "psb")

        vstage = sb("vstage", [64, 128])
        vc = sb("vc", [128, 64])
        VOFF = {}
        _r = 0
        sc.op("dve", lambda e: e.memset(vstage[:], 0.0), writes=[("vstage",)])
        for nm, dv, n in (("g_pre", g_pre_d, 8), ("mu", mu_d, 17), ("w0", w0_d, 4), ("a0", a0_d, 4),
                          ("k_k", k_k_d, 4), ("k_a", k_a_d, 4), ("r_k", r_k_d, 4)):
            VOFF[nm] = _r
            sc.dma("sp", vstage[_r:_r + n, :], dv.rearrange("(k p) -> k p", p=128), reads=[("vstage",)], writes=[("vstage", nm)])
            _r += n
        pt, pk = psring.next()
        sc.op("pe", lambda e, pt=pt: e.transpose(pt[:, 0:64], vstage[:], ident_f[0:64, 0:64]),
              reads=[("vstage",), ("ident_f",)] + [("vstage", nm) for nm in VOFF], writes=[pk])
        sc.op("dve", lambda e, pt=pt: e.tensor_copy(vc[:], pt[:, 0:64]), reads=[pk], writes=[("vc",)])
        gpre_c = vc[:, VOFF["g_pre"]:VOFF["g_pre"] + 8]

        kmean = sb("kmean", [128, NSEQ, 4, 16], F32)
        kmean_b = sb("kmean_b", [128, NSEQ, 4, 16], BF16)
        if True:
            es1 = ExitStack()
            with es1:
                def sb1(name, shape, dt=F32):
                    return es1.enter_context(nc.sbuf_tensor(name, list(shape), dt))
                wp = sb1("wp", [128, 8, IN_COLS], BF16)
                wst = [sb1("wst%d" % i, [128, 1568]) for i in range(2)]
                wring = Ring(wst, "wst")
                for kc in range(8):
                    for q4 in range(4):
                        t, tk = wring.next()
                        c0 = q4 * 1568
                        sc.dma("sp", t[:], w_in_d[kc * 128:(kc + 1) * 128, c0:c0 + 1568], writes=[tk])
                        eng = ("dve", "pool")[(kc * 4 + q4) % 2]
                        sc.op(eng, lambda e, t=t, kc=kc, c0=c0: e.tensor_scalar(
                            wp[:, kc, c0:c0 + 1568], t[:], gpre_c[:, kc:kc + 1], None, ALU.mult),
                            reads=[tk, ("vc",)], writes=[("wp", kc, q4)])
                wp_keys = [("wp", kc, q4) for kc in range(8) for q4 in range(4)]

                mu_c = vc[:, VOFF["mu"]:VOFF["mu"] + 17]
                carry = sb1("carry", [128, 17])
                xt = [sb1("xt%d" % i, [128, 4, D]) for i in range(2)]
                xring = Ring(xt, "xt")
                ub = [sb1("ub%d" % i, [128, D], BF16) for i in range(2)]
                ubring = Ring(ub, "ub")
                sqj = sb1("sqj", [128, D], BF16)
                ss = [sb1("ss%d" % i, [128, 4]) for i in range(2)]
                ssring = Ring(ss, "ss")
                rs = [sb1("rs%d" % i, [128, 4]) for i in range(2)]
                rsring = Ring(rs, "rs")
                uT = [sb1("uT%d" % i, [128, 8, 512], BF16) for i in range(2)]
                uTring = Ring(uT, "uT")
                ob = [sb1("ob%d" % i, [128, 512], BF16) for i in range(4)]
                obring = Ring(ob, "ob")
                vo = [sb1("vo%d" % i, [128, NH, 128], BF16) for i in range(2)]
                voring = Ring(vo, "vo")
                for i in range(2):
                    sc.op("dve", lambda e, i=i: e.memset(vo[i][:], 1.0), writes=[("vo", i), ("vo_init",)])
                cb = [sb1("cb%d" % i, [128, 513]) for i in range(3)]
                cbring = Ring(cb, "cb")
                db = [sb1("db%d" % i, [128, 512]) for i in range(2)]
                dbring = Ring(db, "db")
                shb = [sb1("shb%d" % i, [128, 512]) for i in range(3)]
                shring = Ring(shb, "shb")
                sc.op("dve", lambda e: e.memset(kmean[:], 0.0), writes=[("kmean",)])

                xloaded = {}

                def load_x(s, ti):
                    if s >= NSEQ:
                        return
                    tok0 = s * S + ti * 512
                    xtile, xk = xring.next()
                    sc.dma("sp", xtile[:], x_d[tok0:tok0 + 512, :].rearrange("(a p) d -> p a d", p=128),
                           writes=[xk])
                    xloaded[(s, ti)] = (xtile, xk)

                load_x(0, 0)
                for s in range(NSEQ if lvl >= 1 else 0):
                    sc.op("dve", lambda e: e.memset(carry[:], 0.0), writes=[("carry",)], reads=[])
                    for ti in range(NT):
                        nxt = (s, ti + 1) if ti + 1 < NT else (s + 1, 0)
                        load_x(*nxt)
                        xtile, xk = xloaded.pop((s, ti))
                        sst, ssk = ssring.next()
                        rst, rsk = rsring.next()
                        sc.op("dve", lambda e, sst=sst: e.memset(sst[:], 0.0), writes=[(ssk, a) for a in range(4)])
                        for a in range(4):
                            sc.op("act", lambda e, a=a, xtile=xtile, sst=sst: e.activation(
                                sqj[:], xtile[:, a, :], AF.Square, accum_out=sst[:, a:a + 1]),
                                reads=[xk], writes=[("sqj",), (ssk, a)])
                        sc.op("act", lambda e, sst=sst: e.activation(
                            sst[:], sst[:], AF.Sqrt, bias=float(RMS_EPS), scale=1.0 / D),
                            reads=[(ssk, a) for a in range(4)], writes=[(ssk, "q")])
                        sc.op("dve", lambda e, sst=sst, rst=rst: e.reciprocal(rst[:], sst[:]),
                              reads=[(ssk, "q")], writes=[rsk] + [(ssk, a) for a in range(4)])
                        uTt, uTk = uTring.next()
                        for a in range(4):
                            ubt, ubk = ubring.next()
                            sc.op("dve", lambda e, a=a, ubt=ubt, xtile=xtile, rst=rst: e.tensor_scalar(
                                ubt[:], xtile[:, a, :], rst[:, a:a + 1], None, ALU.mult),
                                reads=[xk, rsk], writes=[ubk])
                            pt, pk = psring.next()
                            ptb = pt[:].bitcast(BF16)

                            def tr_fn(e, ubt=ubt, ptb=ptb):
                                ins = None
                                for kc in range(8):
                                    ins = e.transpose(ptb[:, kc * 128:(kc + 1) * 128], ubt[:, kc * 128:(kc + 1) * 128], ident_b[:])
                                return ins
                            sc.op("pe", tr_fn, reads=[ubk, ("ident_b",)], writes=[pk])
                            eng = ("act", "dve")[a % 2]
                            if eng == "act":
                                sc.op("act", lambda e, a=a, uTt=uTt, ptb=ptb: e.copy(
                                    uTt[:, :, a * 128:(a + 1) * 128], ptb.rearrange("p (k t) -> p k t", k=8)),
                                    reads=[pk], writes=[(uTk, a)])
                            else:
                                sc.op("dve", lambda e, a=a, uTt=uTt, ptb=ptb: e.tensor_copy(
                                    uTt[:, :, a * 128:(a + 1) * 128], ptb.rearrange("p (k t) -> p k t", k=8)),
                                    reads=[pk], writes=[(uTk, a)])
                        uT_keys = [(uTk, a) for a in range(4)]

                        def proj(cc, pt, pk, uTt=uTt, uT_keys=uT_keys):
                            def fn(e):
                                ins = None
                                for kc in range(8):
                                    ins = e.matmul(pt[:], wp[:, kc, cc * 128:(cc + 1) * 128], uTt[:, kc, :],
                                                   start=(kc == 0), stop=(kc == 7))
                                return ins
                            sc.op("pe", fn, reads=uT_keys + wp_keys, writes=[pk])

                        for cc in range(49):
                            if 8 <= cc < 12:
                                continue
                            if lvl < 2 or (lvl == 2 and cc >= 4) or (lvl == 3 and cc >= 8) or (lvl == 4 and cc >= 16) or (lvl == 5 and cc >= 33):
                                continue
                            pt, pk = psring.next()
                            proj(cc, pt, pk)
                            tsl = slice(ti * 512, (ti + 1) * 512)
                            rows = slice((cc % 4) * 128, (cc % 4) * 128 + 128)
                            if cc < 4:
                                o, ok = obring.next()
                                sc.op("act", lambda e, o=o, pt=pt: e.mul(o[:], pt[:], 0.125), reads=[pk], writes=[ok])
                                sc.dma("pool", qT_d[s, rows, tsl], o[:], reads=[ok], writes=[("qT", s, cc, ti)])
                            elif cc < 8:
                                o, ok = obring.next()
                                for hb in range(2):
                                    sc.op("act", lambda e, o=o, pt=pt, hb=hb, s=s, cc=cc, ti=ti: e.activation(
                                        o[:, hb * 256:(hb + 1) * 256], pt[:, hb * 256:(hb + 1) * 256], AF.Copy,
                                        accum_out=kmean[:, s, cc - 4, 2 * ti + hb:2 * ti + hb + 1]),
                                        reads=[pk, ("kmean",)], writes=([ok] if hb == 0 else []) + [(ok, hb), ("kmean", s, cc - 4, ti, hb)])
                                sc.dma("pool", kT_d[s, rows, tsl], o[:], reads=[ok, (ok, 0), (ok, 1)], writes=[("kT", s, cc - 4, ti)])
                            elif cc < 16:
                                o, ok = obring.next()
                                sc.op("act", lambda e, o=o, pt=pt: e.activation(o[:], pt[:], AF.Silu), reads=[pk], writes=[ok])
                                sc.dma("pool", za_d[s, rows, tsl], o[:], reads=[ok], writes=[("za", s, cc - 12, ti)])
                            elif cc < 33:
                                j = cc - 16
                                c, ck = cbring.next()
                                sc.op("act", lambda e, c=c, pt=pt: e.copy(c[:, 1:513], pt[:]), reads=[pk], writes=[(ck, 1)])
                                sc.op("dve", lambda e, c=c, j=j: e.tensor_copy(c[:, 0:1], carry[:, j:j + 1]),
                                      reads=[("carry", j), ("carry",)], writes=[(ck, 0)])
                                sc.op("dve", lambda e, c=c, j=j: e.tensor_copy(carry[:, j:j + 1], c[:, 512:513]),
                                      reads=[(ck, 1), ("carry",)], writes=[("carry", j)])
                                dd, dk = dbring.next()
                                sc.op("dve", lambda e, c=c, dd=dd: e.tensor_tensor(dd[:], c[:, 0:512], c[:, 1:513], ALU.subtract),
                                      reads=[(ck, 0), (ck, 1)], writes=[dk])
                                sh, shk = shring.next()
                                sc.op("dve", lambda e, c=c, dd=dd, sh=sh, j=j: e.scalar_tensor_tensor(
                                    sh[:], dd[:], mu_c[:, j:j + 1], c[:, 1:513], ALU.mult, ALU.add),
                                    reads=[dk, (ck, 1), ("vc",)], writes=[shk])
                                sc.dma("pool", rw_d[s, j * 128:(j + 1) * 128, tsl], sh[:], reads=[shk], writes=[("rw", s, j, ti)])
                            else:
                                j = cc - 33
                                o, ok = obring.next()
                                sc.op("act", lambda e, o=o, pt=pt: e.activation(o[:], pt[:], AF.Sigmoid), reads=[pk], writes=[ok])
                                sc.dma("pool", gt_d[s, j * 128:(j + 1) * 128, tsl], o[:], reads=[ok], writes=[("gt", s, j, ti)])
                        for a in range(4 if lvl >= 7 else 0):
                            pt, pk = psring.next()

                            def vfn(e, a=a, pt=pt, uTt=uTt):
                                ins = None
                                for kc in range(8):
                                    ins = e.matmul(pt[:], uTt[:, kc, a * 128:(a + 1) * 128], wp[:, kc, 1024:1536],
                                                   start=(kc == 0), stop=(kc == 7))
                                return ins
                            sc.op("pe", vfn, reads=uT_keys + wp_keys, writes=[pk])
                            o, ok = voring.next()
                            sc.op("act", lambda e, o=o, pt=pt: e.copy(o[:, :, 0:64], pt[:].rearrange("p (h d) -> p h d", h=NH)),
                                  reads=[pk, ("vo_init",)], writes=[ok])
                            t0 = ti * 512 + a * 128
                            sc.dma("pool", v_d[s, t0:t0 + 128, :], o[:].rearrange("p h d -> p (h d)"), reads=[ok], writes=[("v", s, ti * 4 + a)])
                sc.op("dve", lambda e: e.tensor_scalar(kmean_b[:], kmean[:], 1.0 / BLK, None, ALU.mult),
                      reads=[("kmean",)] + [("kmean", s, c, ti, hb) for s in range(NSEQ) for c in range(4) for ti in range(NT) for hb in range(2)],
                      writes=[("kmean_b",)])

        sc.barrier()
        if lvl >= 10 and moba:
            es2 = ExitStack()
            with es2:
                def sb2(name, shape, dt=F32):
                    return es2.enter_context(nc.sbuf_tensor(name, list(shape), dt))
                psS = Ring(psb[0:4], "psb", keys=[("psb", j) for j in range(0, 4)])
                psN = Ring(psb[4:6], "psb", keys=[("psb", j) for j in range(4, 6)])
                psM = Ring(psb[6:8], "psb", keys=[("psb", j) for j in range(6, 8)])
                relb33 = sb2("relb33", [33, NH])
                b31bc = sb2("b31bc", [128, NH])
                sc.dma("sp", b31bc[:], relb_d[31, :].partition_broadcast(128), writes=[("b31bc",)])
                sc.op("dve", lambda e: e.memset(relb33[32:33, :], NEG), writes=[("relb33", 1)])
                sc.dma("sp", relb33[0:32, :], relb_d[:, :], writes=[("relb33", 0)])
                oneh = sb2("oneh", [33, 512])
                sc.dma("sp", oneh[:], c_onehot_d[:, :], writes=[("oneh",)])
                pt, pk = psM.next()
                sc.op("pe", lambda e, pt=pt: e.matmul(pt[0:8, :], relb33[:], oneh[:], start=True, stop=True),
                      reads=[("relb33", 0), ("relb33", 1), ("oneh",)], writes=[pk])
                fbs = sb2("fbs", [8, 512])
                sc.op("dve", lambda e, pt=pt: e.tensor_copy(fbs[:], pt[0:8, :]), reads=[pk], writes=[("fbs",)])
                sc.dma("sp", fb_d[:, :], fbs[:], reads=[("fbs",)], writes=[("fb_d",)])
                Jf = sb2("Jf", [128, 128])
                sc.dma("sp", Jf[:], c_J_d[:, :], writes=[("Jf",)])
                Tt = sb2("Tt", [128, NH, 2, 128], BF16)
                tfl = [sb2("tfl%d" % i, [128, 128]) for i in range(2)]
                tflr = Ring(tfl, "tfl")
                for h in range(NH):
                    for dl in range(2):
                        tf, tfk = tflr.next()
                        src = bass.AP(tensor=fb_d.tensor, offset=h * 512 + 1 + dl * 128, ap=[[1, 128], [1, 128]])
                        sc.dma("sp", tf[:], src, reads=[("fb_d",)], writes=[tfk])
                        pt, pk = psM.next()
                        sc.op("pe", lambda e, pt=pt, tf=tf: e.matmul(pt[:, 0:128], Jf[:], tf[:], start=True, stop=True),
                              reads=[tfk, ("Jf",)], writes=[pk])
                        sc.op("dve", lambda e, pt=pt, h=h, dl=dl: e.tensor_scalar(Tt[:, h, dl, :], pt[:, 0:128], b31bc[:, h:h + 1], None, ALU.subtract),
                              reads=[pk, ("b31bc",)], writes=[("Tt", h, dl)])
                Tt_keys = [("Tt", h, dl) for h in range(NH) for dl in range(2)]
                if "d_Tt" in dbg:
                    dTt = nc.dram_tensor("d_Tt", [128, NH * 2 * 128], BF16, kind="ExternalOutput").ap()
                    sc.dma("sp", dTt[:, :], Tt[:].rearrange("p h d q -> p (h d q)"), reads=Tt_keys)
                eself = sb2("eself", [128, 16 * 128])
                esel = sb2("esel", [128, 16, 128], BF16)
                sc.dma("sp", eself[:], c_esel_d[:, :], writes=[("eself",)])
                sc.op("dve", lambda e: e.tensor_copy(esel[:].rearrange("p a b -> p (a b)"), eself[:]), reads=[("eself",)], writes=[("esel",)])
                ones_b = sb2("ones_b", [128, 64], BF16)
                sc.op("dve", lambda e: e.memset(ones_b[:], 1.0), writes=[("ones_b",)])
                maskT = [sb2("maskT%d" % i, [128, NH, 128], BF16) for i in range(2)]
                for i in range(2):
                    sc.op("dve", lambda e, i=i: e.memset(maskT[i][:, :, :], 0.0), writes=[("maskT", i)])
                mring = Ring(maskT, "maskT")
                pen_sb = sb2("pen_sb", [128, 16 * 128])
                sc.op("dve", lambda e: e.memset(pen_sb[:], 0.0), writes=[("pen_sb", "z")])
                sc.dma("sp", pen_sb[0:1, :], c_pen_d.rearrange("(a n) -> a n", a=1), reads=[("pen_sb", "z")], writes=[("pen_sb", 0)])
                sc.dma("sp", pen_sb[64:65, :], c_pen_d.rearrange("(a n) -> a n", a=1), reads=[("pen_sb", "z")], writes=[("pen_sb", 1)])
                onesq_b = sb2("onesq_b", [128, 128], BF16)
                sc.op("dve", lambda e: e.memset(onesq_b[:], 1.0), writes=[("onesq_b",)])
                penb = sb2("penb", [128, 16 * 128], BF16)
                sc.op("dve", lambda e: e.tensor_copy(penb[:], pen_sb[:]), reads=[("pen_sb", 0), ("pen_sb", 1), ("pen_sb", "z")], writes=[("penb",)])
                kT_sb = sb2("kT_sb", [128, 4, S], BF16)
                qT_sb = sb2("qT_sb", [128, 4, S], BF16)
                v_sb = sb2("v_sb", [128, S // 128, 2 * A_W], BF16)
                gsb = [sb2("gsb%d" % i, [128, NH, 16]) for i in range(2)]
                gring = Ring(gsb, "gsb")
                top8 = [sb2("top8_%d" % i, [128, NH, 8]) for i in range(2)]
                t8ring = Ring(top8, "top8")
                selm = [sb2("selm%d" % i, [128, NH, 16]) for i in range(2)]
                sring = Ring(selm, "selm")
                mvb = [sb2("mvb%d" % i, [128, NH, 16], BF16) for i in range(2)]
                mvring = Ring(mvb, "mvb")
                Pb = [sb2("Pb%d" % i, [128, 512], BF16) for i in range(3)]
                Pring = Ring(Pb, "Pb")
                rden = [sb2("rden%d" % i, [64, 128]) for i in range(2)]
                rdring = Ring(rden, "rden")
                ynorm = [sb2("ynorm%d" % i, [64, 128]) for i in range(2)]
                ynring = Ring(ynorm, "ynorm")
                zat = [sb2("zat%d" % i, [64, NH, 128], BF16) for i in range(2)]
                zring = Ring(zat, "zat")
                yout = [sb2("yout%d" % i, [64, NH, 128], BF16) for i in range(2)]
                yring = Ring(yout, "yout")

                for s in range(NSEQ if lvl >= 11 else 0):
                    for c in range(4):
                        sc.dma("sp", kT_sb[:, c, :], kT_d[s, c * 128:(c + 1) * 128, :],
                               reads=[("kT", s, c, ti) for ti in range(NT)], writes=[("kT_sb", c)])
                        sc.dma("sp", qT_sb[:, c, :], qT_d[s, c * 128:(c + 1) * 128, :],
                               reads=[("qT", s, c, ti) for ti in range(NT)], writes=[("qT_sb", c)])
                    for j4 in range(0, S // 128, 4):
                        n4 = min(4, S // 128 - j4)
                        sc.dma("sp", v_sb[:, j4:j4 + n4, :], v_d[s, j4 * 128:(j4 + n4) * 128, :].rearrange("(a p) d -> p a d", p=128),
                               reads=[("v", s, j) for j in range(j4, j4 + n4)], writes=[("v_sb", j) for j in range(j4, j4 + n4)])
                    for qt in range(NQT if lvl >= 12 else 0):
                        QB = qt // 2
                        qsl = slice(qt * 128, (qt + 1) * 128)
                        mT, mTk = None, None
                        if bar:
                            sc.barrier()
                        if QB > 0:
                            pt, pk = psM.next()

                            def gfn(e, pt=pt, qsl=qsl, s=s, QB=QB):
                                ins = None
                                for h in range(NH):
                                    hp = slice((h % 2) * 64, (h % 2) * 64 + 64)
                                    ins = e.matmul(pt[:, h * 16:(h + 1) * 16], qT_sb[hp, h // 2, qsl], kmean_b[hp, s, h // 2, :],
                                                   start=True, stop=False)
                                    p0 = (h % 2) * 64
                                    ins = e.matmul(pt[:, h * 16:(h + 1) * 16], onesq_b[p0:p0 + 1, :],
                                                   penb[p0:p0 + 1, QB * 128 + h * 16:QB * 128 + (h + 1) * 16], start=False, stop=True)
                                return ins
                            sc.op("pe", gfn, reads=[("qT_sb", c) for c in range(4)] + [("kmean_b",), ("penb",), ("onesq_b",)], writes=[pk])
                            g, gk = gring.next()
                            sc.op("dve", lambda e, g=g, pt=pt: e.tensor_copy(g[:].rearrange("p h n -> p (h n)"), pt[:, 0:NH * 16]),
                                  reads=[pk], writes=[gk, (gk, 1)])
                            t8, t8k = t8ring.next()
                            for h in range(NH if sub >= 3 else 0):
                                sc.op("dve", lambda e, t8=t8, g=g, h=h: e.max(t8[:, h, :], g[:, h, :]),
                                      reads=[gk, (gk, 1)], writes=[(t8k, h)])
                            sm, smk = sring.next()
                            if sub >= 4:
                              sc.op("dve", lambda e, sm=sm, g=g, t8=t8: e.tensor_tensor(
                                sm[:], g[:], t8[:, :, 2:3].to_broadcast([128, NH, 16]), ALU.is_ge),
                                reads=[gk, (gk, 1)] + [(t8k, h) for h in range(NH)], writes=[smk])
                            mv, mvk = mvring.next()
                            if sub >= 5:
                              sc.op("dve", lambda e, mv=mv, sm=sm: e.tensor_scalar(mv[:], sm[:], -NEG, NEG, ALU.mult, ALU.add),
                                  reads=[smk], writes=[mvk])
                            pt2, pk2 = psM.next()
                            ptb2 = pt2[:].bitcast(BF16)

                            def mtr(e, mv=mv, ptb2=ptb2):
                                ins = None
                                for h in range(NH):
                                    ins = e.transpose(ptb2[0:16, h * 128:(h + 1) * 128], mv[:, h, :], ident_b[:])
                                return ins
                            if sub >= 6:
                                sc.op("pe", mtr, reads=[mvk, ("ident_b",)], writes=[pk2])
                            mT, mTk = mring.next()
                            if sub >= 7:
                                sc.op("dve", lambda e, mT=mT, ptb2=ptb2: e.tensor_copy(
                                    mT[0:16, :, :], ptb2[0:16, :].rearrange("p (h q) -> p h q", h=NH)),
                                    reads=[pk2], writes=[mTk])
                                sc.op("dve", lambda e, mT=mT, ptb2=ptb2: e.tensor_copy(
                                    mT[64:80, :, :], ptb2[0:16, :].rearrange("p (h q) -> p h q", h=NH)),
                                    reads=[pk2], writes=[(mTk, "b")])
                        if "d_gate" in dbg and qt == NQT - 1 and s == 0:
                            dg = nc.dram_tensor("d_gate", [128, NH * 16], F32, kind="ExternalOutput").ap()
                            sc.dma("sp", dg[:, :], g[:].rearrange("p h n -> p (h n)"), reads=[gk, (gk, 1)])
                            dsm = nc.dram_tensor("d_sm", [128, NH * 16], F32, kind="ExternalOutput").ap()
                            sc.dma("sp", dsm[:, :], sm[:].rearrange("p h n -> p (h n)"), reads=[smk])
                            dmt = nc.dram_tensor("d_mt", [33, NH * 128], BF16, kind="ExternalOutput").ap()
                            sc.dma("sp", dmt[:, :], mT[:].rearrange("p h n -> p (h n)"), reads=[mTk, (mTk, "c")])
                            dkm = nc.dram_tensor("d_km", [128, NSEQ * 64], F32, kind="ExternalOutput").ap()
                            sc.dma("sp", dkm[:, :], kmean[:].rearrange("p s c n -> p (s c n)"), reads=[("kmean_b",)])
                        if bar:
                            sc.barrier()
                        zt, ztk = zring.next()
                        sc.dma("sp", zt[:], za_d[s].rearrange("(h d) t -> d h t", h=NH)[:, :, qsl],
                               reads=[("za", s, c, qt // 4) for c in range(4)], writes=[ztk])
                        yo, yok = yring.next()
                        pend = []

                        def drain(keep):
                            while len(pend) > keep:
                                pend.pop(0)()
                        for h in range(NH if lvl >= 13 else 0):
                            hp = slice((h % 2) * 64, (h % 2) * 64 + 64)
                            c = h // 2
                            nd, ndk = psN.next()
                            kts = list(range(qt + 1))
                            ngroups = (len(kts) + GRPN - 1) // GRPN
                            for gi, g0 in enumerate(range(0, len(kts), GRPN)):
                                grp = kts[g0:g0 + GRPN]
                                st_, stk = psS.next()

                                def sfn(e, grp=grp, st_=st_, hp=hp, c=c, qsl=qsl, qt=qt, QB=QB, h=h, mT=mT):
                                    ins = None
                                    for j, kt in enumerate(grp):
                                        osl = st_[:, j * 128:(j + 1) * 128]
                                        extra = []
                                        n = kt // 2
                                        if n < QB:
                                            extra.append((esel[hp, n, :], mT[hp, h, :]))
                                        ins = e.matmul(osl, kT_sb[hp, c, kt * 128:(kt + 1) * 128], qT_sb[hp, c, qsl],
                                                       start=True, stop=(len(extra) == 0))
                                        for i2, (l_, r_) in enumerate(extra):
                                            ins = e.matmul(osl, l_, r_, start=False, stop=(i2 == len(extra) - 1))
                                    return ins
                                rd = [("kT_sb", c), ("qT_sb", c), ("esel",)]
                                if mTk is not None:
                                    rd += [mTk, (mTk, "b")]
                                sc.op("pe", sfn, reads=rd, writes=[stk])
                                for j, kt in enumerate(grp):
                                    if kt >= qt - 1:
                                        dl = qt - kt
                                        sc.op("dve", lambda e, st_=st_, j=j, h=h, dl=dl: e.tensor_tensor(
                                            st_[:, j * 128:(j + 1) * 128], st_[:, j * 128:(j + 1) * 128], Tt[:, h, dl, :], ALU.add),
                                            reads=[stk, ("Tt", h, dl)], writes=[stk])
                                P, Pk = Pring.next()
                                ng = len(grp)
                                sc.op("act", lambda e, P=P, st_=st_, ng=ng, h=h: e.activation(P[:, 0:ng * 128], st_[:, 0:ng * 128], AF.Exp, bias=b31bc[:, h:h + 1]),
                                      reads=[stk, ("b31bc",)], writes=[Pk])

                                def emit_pv(grp=grp, P=P, Pk=Pk, nd=nd, ndk=ndk, h=h, qt=qt, last=(gi == ngroups - 1), yo=yo, yok=yok, zt=zt, ztk=ztk):
                                    def pvfn(e):
                                        ins = None
                                        for j, kt in enumerate(grp):
                                            ins = e.matmul(nd[:, 0:128], v_sb[:, kt, h * 128:(h + 1) * 128], P[:, j * 128:(j + 1) * 128],
                                                           start=(kt == 0), stop=(kt == qt))
                                        return ins
                                    sc.op("pe", pvfn, reads=[Pk] + [("v_sb", kt) for kt in grp], writes=[ndk])
                                    if last:
                                        rdn, rdk = rdring.next()
                                        sc.op("dve", lambda e: e.reciprocal(rdn[:], nd[64:128, 0:128]), reads=[ndk], writes=[rdk])
                                        yn, ynk = ynring.next()
                                        sc.op("dve", lambda e: e.tensor_tensor(yn[:], nd[0:64, 0:128], rdn[:], ALU.mult),
                                              reads=[ndk, rdk], writes=[ynk])
                                        sc.op("pool", lambda e: e.tensor_tensor(yo[:, h, :], yn[:], zt[:, h, :], ALU.mult),
                                              reads=[ynk, ztk], writes=[(yok, h)])
                                pend.append(emit_pv)
                                drain(1)
                        drain(0)
                        sc.dma("pool", ya_d[s].rearrange("(h d) t -> d h t", h=NH)[:, :, qsl], yo[:],
                               reads=[(yok, h) for h in range(NH)], writes=[("ya", s, qt), yok])

        if lvl >= 20 and (lvl < 40 or not moba):
            zt_ = sb("zstub", [128, 512], BF16)
            sc.op("dve", lambda e: e.memset(zt_[:], 0.0), writes=[("zstub",)])
            for s in range(NSEQ):
                for ti in range(NT):
                    for c in range(4):
                        if lvl < 40:
                            sc.dma("sp", yb_d[s, c * 128:(c + 1) * 128, ti * 512:(ti + 1) * 512], zt_[:], reads=[("zstub",)],
                                   writes=[("yb", s, ti)] if c == 3 else [("yb_part", s, ti, c)])
                        if not moba:
                            sc.dma("sp", ya_d[s, c * 128:(c + 1) * 128, ti * 512:(ti + 1) * 512], zt_[:], reads=[("zstub",)],
                                   writes=[("ya", s, ti * 4 + c)])

        sc.barrier()
        if lvl >= 40:
            es4 = ExitStack()
            with es4:
                def sb4(name, shape, dt=F32):
                    return es4.enter_context(nc.sbuf_tensor(name, list(shape), dt))
                psW = Ring(psb, "psb")
                NC_ = S // 64
                NG = 4
                HS = [64, NG, 64]
                vs2 = sb4("vs2", [64, 64])
                sc.op("dve", lambda e: e.memset(vs2[:], 0.0), writes=[("vs2",)])
                VO2 = {}
                for i, (nm, dv) in enumerate((("w0", w0_d), ("a0", a0_d), ("k_k", k_k_d), ("k_a", k_a_d), ("r_k", r_k_d))):
                    VO2[nm] = i * 8
                    sc.dma("sp", vs2[i * 8:(i + 1) * 8, :], dv.rearrange("(h d) -> h d", d=64), reads=[("vs2",)], writes=[("vs2", nm)])
                pt, pk = psW.next()
                sc.op("pe", lambda e, pt=pt: e.transpose(pt[0:64, 0:64], vs2[:], ident_f[0:64, 0:64]),
                      reads=[("vs2",), ("ident_f",)] + [("vs2", nm) for nm in VO2], writes=[pk])
                vh = sb4("vh", [64, 64])
                sc.op("dve", lambda e, pt=pt: e.tensor_copy(vh[:], pt[0:64, 0:64]), reads=[pk], writes=[("vh",)])
                omk = sb4("omk", [64, NH])
                sc.op("dve", lambda e: e.tensor_scalar(omk[:], vh[:, VO2["k_a"]:VO2["k_a"] + 8], -1.0, 1.0, ALU.mult, ALU.add),
                      reads=[("vh",)], writes=[("omk",)])

                def vb(nm):
                    o = VO2[nm] + CUR["hg"] * NG
                    return vh[:, o:o + NG].rearrange("p (h o) -> p h o", o=1).to_broadcast(HS)
                wup = sb4("wup", [64, B_W])
                aup = sb4("aup", [64, B_W])
                sc.dma("sp", wup[:], w_up_d[:, :], writes=[("wup0",)])
                sc.dma("sp", aup[:], a_up_d[:, :], writes=[("aup0",)])
                wup_r = sb4("wup_r", [64, B_W], F32R)
                aup_r = sb4("aup_r", [64, B_W], F32R)
                sc.op("dve", lambda e: e.tensor_copy(wup_r[:], wup[:]), reads=[("wup0",)], writes=[("wup",)])
                sc.op("dve", lambda e: e.tensor_copy(aup_r[:], aup[:]), reads=[("aup0",)], writes=[("aup",)])
                lnw = sb4("lnw", [64, B_W])
                lnb = sb4("lnb", [64, B_W])
                sc.dma("sp", lnw[:], lnw_d.partition_broadcast(64), writes=[("lnw",)])
                sc.dma("sp", lnb[:], lnb_d.partition_broadcast(64), writes=[("lnb",)])
                tri = sb4("tri", [64, 3, 64])
                sc.dma("sp", tri[:].rearrange("p a b -> p (a b)"), c_tri_d[:, :], writes=[("tri",)])
                ones64 = sb4("ones64", [64, 64], F32R)
                ones64f = sb4("ones64f", [64, 2])
                sc.op("dve", lambda e: e.memset(ones64f[:], 1.0), writes=[("ones64f",)])
                ones_t = sb4("ones_t", [64, 64])
                sc.op("dve", lambda e: e.memset(ones_t[:], 1.0), writes=[("ones_t",)])
                sc.op("dve", lambda e: e.tensor_copy(ones64[:], ones_t[:]), reads=[("ones_t",)], writes=[("ones64",)])
                zeros_t = sb4("zeros_t", [64, 4, 64])
                sc.op("dve", lambda e: e.memset(zeros_t[:], 0.0), writes=[("zeros_t",)])
                smask = sb4("smask", HS)
                sc.op("dve", lambda e: e.memset(smask[:], 1.0), writes=[("smask",)])
                sc.op("dve", lambda e: e.memset(smask[:, :, 0:1], 0.0), reads=[("smask",)], writes=[("smask", 1)])
                identb8 = ident_f[0:64, 0:64].rearrange("p (o d) -> p o d", o=1).to_broadcast(HS)

                def trib(i):
                    return tri[:, i:i + 1, :].to_broadcast(HS)

                T_ = {}
                CUR = {"set": 0, "list": None, "hg": 0}

                class Defer:
                    def op(self, eng, fn, reads=(), writes=()):
                        CUR["list"].append(("op", eng, fn, list(reads), list(writes), {}))

                    def dma(self, eng, out, in_, reads=(), writes=(), **kw):
                        CUR["list"].append(("dma", eng, (out, in_), list(reads), list(writes), kw))
                cur = Defer()

                RNAMES = {"tw", "ad_r", "sq", "At", "Bt", "Kt", "Rt", "tm_V", "tm_bc", "tm_kc", "X", "Q0", "Q1", "P0", "P1",
                          "AakT", "ArbT", "ArkT", "Mc", "Rh"}

                def tile(name, shape=None):
                    nm = "r%d_%s" % (CUR["set"], name)
                    if nm not in T_:
                        T_[nm] = sb4(nm, shape or HS, F32R if name in RNAMES else F32)
                    return T_[nm], (nm,)
                HstA = [[sb4("Hst%d_%d" % (q, i), HS, F32R) for i in range(2)] for q in range(4)]

                def ew(eng, fn, reads, writes):
                    cur.op(eng, fn, reads=reads, writes=writes)

                def headmm(out_fn, l_fn, r_fn, reads, pk, extra=None):
                    items = []
                    for h in range(NG):
                        pairs = [(l_fn(h), r_fn(h))] + ([(a(h), b(h)) for a, b in extra] if extra else [])
                        for i, (l_, r_) in enumerate(pairs):
                            items.append((out_fn(h), l_, r_, i == 0, i == len(pairs) - 1))

                    def fn(e, items=items):
                        ins = None
                        for (o_, l_, r_, st, sp) in items:
                            ins = e.matmul(o_, l_, r_, start=st, stop=sp)
                        return ins
                    cur.op("pe", fn, reads=reads, writes=[pk])

                def flat(t):
                    return t[:].rearrange("p h t -> p (h t)")

                for s in range(NSEQ):
                    for hg in range(2):
                        sc.op("dve", lambda e, q=(s % 2) * 2 + hg: e.tensor_copy(HstA[q][0][:], zeros_t[:]), reads=[("zeros_t",)], writes=[("Hst", (s % 2) * 2 + hg, 0)])

                psSets = [Ring(psb[2 * q:2 * q + 2], "psb", keys=[("psb", j) for j in range(2 * q, 2 * q + 2)]) for q in range(4)]

                def body(s, ci, hg):
                    chain = (s % 2) * 2 + hg
                    psW = psSets[chain]
                    G0 = hg * NG
                    VB = {nm: vb(nm) for nm in VO2}
                    if True:
                        Hst = HstA[chain]
                        csl = slice(ci * 64, (ci + 1) * 64)
                        ti = ci // 8
                        hcur, hck = Hst[ci % 2], ("Hst", chain, ci % 2)
                        hnxt, hnk = Hst[(ci + 1) % 2], ("Hst", chain, (ci + 1) % 2)
                        fm = {}
                        for qi, nm in enumerate(("r", "k", "v", "z")):
                            t, tk = tile("in_" + nm)
                            cur.dma("sp", t[:], rw_d[s, qi * 512 + G0 * 64:qi * 512 + (G0 + NG) * 64, csl].rearrange("(h d) t -> d h t", h=NG),
                                   reads=[("rw", s, qi * 4 + j, ti) for j in range(4)], writes=[tk])
                            fm[nm] = (t, tk)
                        wd, wdk = tile("wd", [64, 64])
                        ad, adk = tile("ad", [64, 64])
                        cur.dma("sp", wd[:], rw_d[s, 2048:2112, csl], reads=[("rw", s, 16, ti)], writes=[wdk])
                        cur.dma("sp", ad[:], rw_d[s, 2112:2176, csl], reads=[("rw", s, 16, ti)], writes=[adk])
                        r_, rk_ = fm["r"]; k_, kk_ = fm["k"]; v_, vk_ = fm["v"]; z_, zk_ = fm["z"]
                        tw, twk = tile("tw", [64, 64])
                        adr, adrk = tile("ad_r", [64, 64])
                        ew("act", lambda e: e.copy(adr[:], ad[:]), [adk], [adrk])
                        ew("act", lambda e: e.activation(tw[:], wd[:], AF.Tanh), [wdk], [twk])
                        pW, pWk = psW.next()
                        headmm(lambda h: pW[0:64, h * 64:(h + 1) * 64], lambda h: wup_r[:, (G0 + h) * 64:(G0 + h + 1) * 64], lambda h: tw[:],
                               [twk, ("wup",)], pWk)
                        pA, pAk = psW.next()
                        headmm(lambda h: pA[0:64, h * 64:(h + 1) * 64], lambda h: aup_r[:, (G0 + h) * 64:(G0 + h + 1) * 64], lambda h: adr[:],
                               [adrk, ("aup",)], pAk)
                        pv3 = lambda p: p[0:64, 0:NG * 64].rearrange("p (h t) -> p h t", h=NG)
                        PW = NG * 64
                        lw, lwk = tile("lw")
                        ew("dve", lambda e, pW=pW: e.tensor_tensor(lw[:], pv3(pW), VB["w0"], ALU.add), [pWk, ("vh",)], [lwk])
                        ew("act", lambda e: e.activation(flat(lw), flat(lw), AF.Sigmoid), [lwk], [lwk])
                        av, avk = tile("av")
                        ew("dve", lambda e, pA=pA: e.tensor_tensor(av[:], pv3(pA), VB["a0"], ALU.add), [pAk, ("vh",)], [avk])
                        ew("act", lambda e: e.activation(flat(av), flat(av), AF.Sigmoid), [avk], [avk])
                        kr, krk = tile("kr")
                        ew("dve", lambda e: e.tensor_tensor(kr[:], k_[:], VB["k_k"], ALU.mult), [kk_, ("vh",)], [krk])
                        sq, sqk = tile("sq")
                        ew("dve", lambda e: e.tensor_tensor(flat(sq), flat(kr), flat(kr), ALU.mult), [krk], [sqk])
                        pS, pSk = psW.next()
                        cur.op("pe", lambda e, pS=pS: e.matmul(pS[0:64, 0:PW], ones64[:], flat(sq), start=True, stop=True),
                              reads=[sqk, ("ones64",)], writes=[pSk])
                        rn, rnk = tile("rn")
                        ew("act", lambda e, pS=pS: e.activation(flat(rn), pS[0:64, 0:PW], AF.Sqrt, bias=1e-24, scale=1.0), [pSk], [rnk])
                        ew("dve", lambda e: e.reciprocal(flat(rn), flat(rn)), [rnk], [rnk])
                        kkn, kknk = kr, krk
                        ew("dve", lambda e: e.tensor_tensor(flat(kkn), flat(kr), flat(rn), ALU.mult), [krk, rnk], [kknk])
                        k2, k2k = tile("k2")
                        ew("dve", lambda e: e.tensor_tensor(k2[:], av[:], VB["k_a"], ALU.mult), [avk, ("vh",)], [k2k])
                        ew("dve", lambda e: e.tensor_tensor(k2[:], k2[:], omk[:, G0:G0 + NG].rearrange("p (h o) -> p h o", o=1).to_broadcast(HS), ALU.add),
                           [k2k, ("omk",)], [k2k])
                        ew("dve", lambda e: e.tensor_tensor(flat(k2), flat(k2), flat(k_), ALU.mult), [k2k, kk_], [k2k])
                        bv, bvk = tile("bv")
                        ew("pool", lambda e: e.tensor_tensor(flat(bv), flat(kkn), flat(av), ALU.mult), [kknk, avk], [bvk])
                        cs, csk = tile("cs")
                        ew("dve", lambda e: e.tensor_tensor_scan(flat(cs), flat(smask), flat(lw), 0.0, ALU.mult, ALU.add),
                           [lwk, ("smask",), ("smask", 1)], [csk])
                        ecs, ecsk = tile("ecs")
                        ew("act", lambda e: e.activation(flat(ecs), flat(cs), AF.Exp, scale=-math.exp(-0.5)), [csk], [ecsk])
                        csx, csxk = lw, lwk
                        ew("dve", lambda e: e.tensor_tensor(flat(csx), flat(cs), flat(lw), ALU.subtract), [csk, lwk], [csxk])
                        ew("act", lambda e: e.activation(flat(csx), flat(csx), AF.Exp, scale=-math.exp(-0.5)), [csxk], [csxk])
                        encs, encsk = cs, csk
                        ew("act", lambda e: e.activation(flat(encs), flat(cs), AF.Exp, scale=math.exp(-0.5)), [csk], [encsk])
                        dte, dtek = tile("dte")
                        ew("dve", lambda e: e.tensor_tensor(dte[:], encs[:], ecs[:, :, 63:64].to_broadcast(HS), ALU.mult), [encsk, ecsk], [dtek])
                        At, Atk = tile("At")
                        ew("dve", lambda e: e.scalar_tensor_tensor(flat(At), flat(kkn), -1.0, flat(csx), ALU.mult, ALU.mult), [kknk, csxk], [Atk])
                        Bt, Btk = tile("Bt")
                        ew("dve", lambda e: e.tensor_tensor(flat(Bt), flat(bv), flat(encs), ALU.mult), [bvk, encsk], [Btk])
                        Kt, Ktk = tile("Kt")
                        ew("dve", lambda e: e.tensor_tensor(flat(Kt), flat(k2), flat(encs), ALU.mult), [k2k, encsk], [Ktk])
                        Rt, Rtk = tile("Rt")
                        ew("dve", lambda e: e.tensor_tensor(flat(Rt), flat(r_), flat(ecs), ALU.mult), [rk_, ecsk], [Rtk])
                        bc, bck = bv, bvk
                        ew("pool", lambda e: e.tensor_tensor(flat(bc), flat(bv), flat(dte), ALU.mult), [bvk, dtek], [bck])
                        kc, kck = tile("kc")
                        ew("pool", lambda e: e.tensor_tensor(flat(kc), flat(k2), flat(dte), ALU.mult), [k2k, dtek], [kck])
                        tm = {}
                        for nm, (src, srck) in (("V", (v_, vk_)), ("bc", (bc, bck)), ("kc", (kc, kck)), ("At", (At, Atk))):
                            pT_, pTk_ = psW.next()

                            def trf(e, pT_=pT_, src=src):
                                ins = None
                                for h in range(NG):
                                    ins = e.transpose(pT_[0:64, h * 64:(h + 1) * 64], src[:, h, :].bitcast(F32), ident_f[0:64, 0:64])
                                return ins
                            cur.op("pe", trf, reads=[srck, ("ident_f",)], writes=[pTk_])
                            if nm == "At":
                                X, Xk = tile("X", [64, NG, 128])
                                ew("act", lambda e, pT_=pT_: e.copy(X[:, :, 64:128], pv3(pT_)), [pTk_], [(Xk, 1)])
                            else:
                                d, dk = tile("tm_" + nm)
                                ew("act", lambda e, pT_=pT_, d=d: e.copy(flat(d), pT_[0:64, 0:PW]), [pTk_], [dk])
                                tm[nm] = (d, dk)
                        Vt, Vtk = tm["V"]; bct, bctk = tm["bc"]; kct, kctk = tm["kc"]
                        def mm_mask(name, L, Lk, Rr, Rk, mi):
                            p_, pk_ = psW.next()
                            headmm(lambda h: p_[0:64, h * 64:(h + 1) * 64], lambda h: L[:, h, :], lambda h: Rr[:, h, :], [Lk, Rk], pk_)
                            d, dk = tile(name)
                            ew("dve", lambda e, p_=p_, d=d: e.tensor_tensor(d[:], pv3(p_), trib(mi), ALU.mult), [pk_, ("tri",)], [dk])
                            return d, dk
                        Q, Qk = mm_mask("Q0", Bt, Btk, At, Atk, 0)
                        Pm, Pmk = mm_mask("P0", At, Atk, Bt, Btk, 2)
                        AakT, AakTk = mm_mask("AakT", Kt, Ktk, At, Atk, 0)
                        ArbT, ArbTk = mm_mask("ArbT", Bt, Btk, Rt, Rtk, 1)
                        ArkT, ArkTk = mm_mask("ArkT", Kt, Ktk, Rt, Rtk, 1)
                        pX, pXk = psW.next()
                        headmm(lambda h: pX[0:64, h * 64:(h + 1) * 64], lambda h: AakT[:, h, :], lambda h: Vt[:, h, :], [AakTk, Vtk], pXk)
                        ew("act", lambda e, pX=pX: e.copy(X[:, :, 0:64], pv3(pX)), [pXk], [(Xk, 0)])
                        Xkeys = [(Xk, 0), (Xk, 1)]
                        for lv in range(6):
                            pa_, pak_ = psW.next()

                            def apf(e, pa_=pa_, Q=Q):
                                ins = None
                                for h in range(NG):
                                    ins = e.matmul(pa_[0:64, h * 128:(h + 1) * 128], Q[:, h, :], X[:, h, :], start=True, stop=True)
                                return ins
                            cur.op("pe", apf, reads=[Qk] + Xkeys, writes=[pak_])
                            ew("dve", lambda e, pa_=pa_: e.tensor_tensor(X[:], X[:], pa_[0:64, 0:NG * 128].rearrange("p (h t) -> p h t", h=NG), ALU.add),
                               [pak_] + Xkeys, Xkeys)
                            if lv < 5:
                                pq_, pqk_ = psW.next()
                                headmm(lambda h, pq_=pq_: pq_[0:64, h * 64:(h + 1) * 64], lambda h, Pm=Pm: Pm[:, h, :], lambda h, Q=Q: Q[:, h, :], [Pmk, Qk], pqk_)
                                Q2, Q2k = tile("Q%d" % ((lv + 1) % 2))
                                if lv < 4:
                                    pp_, ppk_ = psW.next()
                                    headmm(lambda h, pp_=pp_: pp_[0:64, h * 64:(h + 1) * 64], lambda h, Q=Q: Q[:, h, :], lambda h, Pm=Pm: Pm[:, h, :], [Pmk, Qk], ppk_)
                                    P2, P2k = tile("P%d" % ((lv + 1) % 2))
                                    ew("act", lambda e, pp_=pp_, P2=P2: e.copy(flat(P2), pp_[0:64, 0:PW]), [ppk_], [P2k])
                                ew("act", lambda e, pq_=pq_, Q2=Q2: e.copy(flat(Q2), pq_[0:64, 0:PW]), [pqk_], [Q2k])
                                Q, Qk = Q2, Q2k
                                if lv < 4:
                                    Pm, Pmk = P2, P2k
                        U0 = lambda h: X[:, h, 0:64]
                        Ah = lambda h: X[:, h, 64:128]
                        pM, pMk = psW.next()
                        headmm(lambda h: pM[0:64, h * 64:(h + 1) * 64], Ah, lambda h: bct[:, h, :], Xkeys + [bctk], pMk)
                        Mc, Mck = tile("Mc")
                        ew("dve", lambda e: e.tensor_tensor(Mc[:], identb8, ecs[:, :, 63:64].to_broadcast(HS), ALU.mult), [ecsk, ("ident_f",)], [Mck])
                        ew("dve", lambda e, pM=pM: e.tensor_tensor(Mc[:], Mc[:], pv3(pM), ALU.add), [pMk, Mck], [Mck])
                        pG, pGk = psW.next()
                        headmm(lambda h: pG[0:64, h * 64:(h + 1) * 64], lambda h: bct[:, h, :], U0, Xkeys + [bctk, kctk, Vtk], pGk,
                               extra=[(lambda h: kct[:, h, :], lambda h: Vt[:, h, :])])
                        G, Gk = tile("G")
                        ew("act", lambda e, pG=pG: e.copy(flat(G), pG[0:64, 0:PW]), [pGk], [Gk])
                        pR, pRk = psW.next()
                        headmm(lambda h: pR[0:64, h * 64:(h + 1) * 64], Ah, lambda h: ArbT[:, h, :], Xkeys + [ArbTk], pRk)
                        Rh, Rhk = tile("Rh")
                        ew("dve", lambda e, pR=pR: e.tensor_tensor(Rh[:], Rt[:], pv3(pR), ALU.add), [pRk, Rtk], [Rhk])
                        pO, pOk = psW.next()
                        headmm(lambda h: pO[0:64, h * 64:(h + 1) * 64], lambda h: ArbT[:, h, :], U0, Xkeys + [ArbTk, ArkTk, Vtk, Rhk, hck], pOk,
                               extra=[(lambda h: ArkT[:, h, :], lambda h: Vt[:, h, :]), (lambda h: Rh[:, h, :], lambda h, hcur=hcur: hcur[:, h, :])])
                        pH, pHk = psW.next()
                        headmm(lambda h: pH[0:64, h * 64:(h + 1) * 64], lambda h: Mc[:, h, :], lambda h, hcur=hcur: hcur[:, h, :], [Mck, hck], pHk)
                        ew("dve", lambda e, pH=pH, hnxt=hnxt: e.tensor_tensor(hnxt[:], G[:], pv3(pH), ALU.add), [pHk, Gk], [hnk])
                        Ot, Otk = tile("Ot")
                        ew("act", lambda e, pO=pO: e.copy(flat(Ot), pO[0:64, 0:PW]), [pOk], [Otk])
                        st1, st1k = tile("st1", [64, NG])
                        st2, st2k = tile("st2", [64, NG])
                        junk, junkk = tile("junk", [64, 64])
                        for h in range(NG):
                            ew("act", lambda e, h=h: e.activation(junk[:], Ot[:, h, :], AF.Copy, accum_out=st1[:, h:h + 1]), [Otk], [junkk, (st1k, h)])
                            ew("act", lambda e, h=h: e.activation(junk[:], Ot[:, h, :], AF.Square, accum_out=st2[:, h:h + 1]), [Otk], [junkk, (st2k, h)])
                        st1a = [(st1k, h) for h in range(NG)]
                        st2a = [(st2k, h) for h in range(NG)]
                        ew("dve", lambda e: e.tensor_scalar(st1[:], st1[:], 1.0 / 64, None, ALU.mult), st1a, st1a)
                        msq, msqk = tile("msq", [64, NG])
                        ew("dve", lambda e: e.tensor_tensor(msq[:], st1[:], st1[:], ALU.mult), st1a, [msqk])
                        ew("dve", lambda e: e.scalar_tensor_tensor(st2[:], st2[:], 1.0 / 64, msq[:], ALU.mult, ALU.subtract), st2a + [msqk], st2a)
                        ew("act", lambda e: e.activation(st2[:], st2[:], AF.Sqrt, bias=float(GN_EPS), scale=1.0), st2a, st2a)
                        ew("dve", lambda e: e.reciprocal(st2[:], st2[:]), st2a, st2a)
                        b3 = lambda t: t[:].rearrange("p (h o) -> p h o", o=1).to_broadcast(HS)
                        ew("dve", lambda e: e.tensor_tensor(Ot[:], Ot[:], b3(st1), ALU.subtract), [Otk] + st1a, [Otk])
                        ew("dve", lambda e: e.tensor_tensor(Ot[:], Ot[:], b3(st2), ALU.mult), [Otk] + st2a, [Otk])
                        ew("dve", lambda e: e.tensor_tensor(flat(Ot), flat(Ot), lnw[:, G0 * 64:(G0 + NG) * 64], ALU.mult), [Otk, ("lnw",)], [Otk])
                        ew("dve", lambda e: e.tensor_tensor(flat(Ot), flat(Ot), lnb[:, G0 * 64:(G0 + NG) * 64], ALU.add), [Otk, ("lnb",)], [Otk])
                        rk3, rk3k = tile("rk3")
                        ew("pool", lambda e: e.tensor_tensor(flat(rk3), flat(r_), flat(k2), ALU.mult), [rk_, k2k], [rk3k])
                        ew("dve", lambda e: e.tensor_tensor(rk3[:], rk3[:], VB["r_k"], ALU.mult), [rk3k, ("vh",)], [rk3k])
                        pBn, pBnk = psW.next()
                        headmm(lambda h: pBn[0:64, h:h + 1], lambda h: rk3[:, h, :], lambda h: ones64f[:, 0:1], [rk3k, ("ones64f",)], pBnk)
                        sbn, sbnk = tile("sbn", [64, NG])
                        ew("dve", lambda e, pBn=pBn: e.tensor_copy(sbn[:], pBn[0:64, 0:NG]), [pBnk], [sbnk])
                        bon, bonk = tile("bon")
                        ew("dve", lambda e: e.tensor_tensor(bon[:], Vt[:], b3(sbn), ALU.mult), [Vtk, sbnk], [bonk])
                        ew("dve", lambda e: e.tensor_tensor(flat(Ot), flat(Ot), flat(bon), ALU.add), [Otk, bonk], [Otk])
                        pY, pYk = psW.next()

                        def tyf(e, pY=pY):
                            ins = None
                            for h in range(NG):
                                ins = e.transpose(pY[0:64, h * 64:(h + 1) * 64], Ot[:, h, :], ident_f[0:64, 0:64])
                            return ins
                        cur.op("pe", tyf, reads=[Otk, ("ident_f",)], writes=[pYk])
                        zs, zsk = tile("zs")
                        ew("act", lambda e: e.activation(flat(zs), flat(z_), AF.Silu), [zk_], [zsk])
                        ybn = "ybb%d" % CUR["set"]
                        if ybn not in T_:
                            T_[ybn] = es4.enter_context(nc.sbuf_tensor(ybn, [64, NG, 64], BF16))
                        ybb, ybbk = T_[ybn], (ybn,)
                        ew("dve", lambda e, pY=pY: e.tensor_tensor(ybb[:], zs[:], pv3(pY), ALU.mult), [pYk, zsk], [ybbk])
                        cur.dma("pool", yb_d[s, G0 * 64:(G0 + NG) * 64, :].rearrange("(h d) t -> d h t", h=NG)[:, :, csl], ybb[:], reads=[ybbk],
                               writes=[("yb_c", s, ci, hg)] + ([("yb", s, ti, hg)] if ci % 8 == 7 else []))


                def flush(lists):
                    n = max(len(l) for l in lists)
                    for i in range(n):
                        for l in lists:
                            if i < len(l):
                                kind, eng, a_, rd, wr, kw = l[i]
                                if kind == "op":
                                    sc.op(eng, a_, reads=rd, writes=wr)
                                else:
                                    sc.dma(eng, a_[0], a_[1], reads=rd, writes=wr, **kw)

                for s0 in range(0, NSEQ, 2):
                    for ci in range(NC_):
                        lists = []
                        for s in range(s0, min(s0 + 2, NSEQ)):
                            for hg in range(2):
                                CUR["set"] = (s % 2) * 2 + hg
                                CUR["hg"] = hg
                                CUR["list"] = []
                                body(s, ci, hg)
                                lists.append(CUR["list"])
                        flush(lists)

        sc.barrier()
        if lvl >= 30:
            es3 = ExitStack()
            with es3:
                def sb3(name, shape, dt=F32):
                    return es3.enter_context(nc.sbuf_tensor(name, list(shape), dt))
                psR = Ring(psb, "psb")
                wstg = [sb3("wstg%d" % i, [128, D]) for i in range(2)]
                wsr = Ring(wstg, "wstg")

                def load_w(name, dram, nk):
                    t = sb3(name, [128, nk, D], BF16)
                    for kc in range(nk):
                        st, stk = wsr.next()
                        sc.dma("sp", st[:], dram[kc * 128:(kc + 1) * 128, :], writes=[stk])
                        sc.op(("dve", "pool")[kc % 2], lambda e, st=st, kc=kc, t=t: e.tensor_copy(t[:, kc, :], st[:]),
                              reads=[stk], writes=[(name, kc)])
                    return t, [(name, kc) for kc in range(nk)]
                pa_sb, pa_k = load_w("pa_sb", p_a_d, 4)
                pb_sb, pb_k = load_w("pb_sb", p_b_d, 4)
                wo_sb, wo_k = load_w("wo_sb", w_out_d, 8)
                wg_sb, wg_k = load_w("wg_sb", w_pg_d, 8)
                wu_sb, wu_k = load_w("wu_sb", w_pu_d, 2)
                gpost = sb3("gpost", [128, D])
                sc.dma("sp", gpost[:], g_post_d.partition_broadcast(128), writes=[("gpost",)])
                yaT = [sb3("yaT%d" % i, [128, 4, 512], BF16) for i in range(2)]
                ybT = [sb3("ybT%d" % i, [128, 4, 512], BF16) for i in range(2)]
                gtT = [sb3("gtT%d" % i, [128, 16, 512], BF16) for i in range(2)]
                yar, ybr, gtr = Ring(yaT, "yaT"), Ring(ybT, "ybT"), Ring(gtT, "gtT")
                t1b = [sb3("t1b%d" % i, [128, 512]) for i in range(2)]
                t1r = Ring(t1b, "t1b")
                mgT = [sb3("mgT%d" % i, [128, 8, 512], BF16) for i in range(2)]
                mgr = Ring(mgT, "mgT")
                x3 = [sb3("x3_%d" % i, [128, D]) for i in range(2)]
                x3r = Ring(x3, "x3")
                p3 = [sb3("p3_%d" % i, [128, PLE]) for i in range(2)]
                p3r = Ring(p3, "p3")
                p3b = [sb3("p3b_%d" % i, [128, PLE], BF16) for i in range(2)]
                p3br = Ring(p3b, "p3b")
                ysb = [sb3("ysb%d" % i, [128, D]) for i in range(2)]
                ysr = Ring(ysb, "ysb")
                sq3 = sb3("sq3", [128, D], BF16)
                st3 = [sb3("st3_%d" % i, [128, 2]) for i in range(2)]
                st3r = Ring(st3, "st3")
                hsb = [sb3("hsb%d" % i, [128, D]) for i in range(2)]
                hsr = Ring(hsb, "hsb")
                hbb = [sb3("hbb%d" % i, [128, D], BF16) for i in range(2)]
                hbr = Ring(hbb, "hbb")
                hT = [sb3("hT%d" % i, [128, 8, 128], BF16) for i in range(2)]
                hTr = Ring(hT, "hT")
                pT = [sb3("pT%d" % i, [128, 2, 128], BF16) for i in range(2)]
                pTr = Ring(pT, "pT")
                sg = [sb3("sg%d" % i, [128, D]) for i in range(2)]
                sgr = Ring(sg, "sg")
                osb = [sb3("osb%d" % i, [128, D]) for i in range(2)]
                osr = Ring(osb, "osb")

                for s in range(NSEQ):
                    for ti in range(NT):
                        tsl = slice(ti * 512, (ti + 1) * 512)
                        ya, yak = yar.next()
                        yb, ybk = ybr.next()
                        gt, gtk = gtr.next()
                        sc.dma("sp", ya[:], ya_d[s, :, tsl].rearrange("(c p) t -> p c t", p=128),
                               reads=[("ya", s, qt) for qt in range(ti * 4, ti * 4 + 4)], writes=[yak])
                        sc.dma("sp", yb[:], yb_d[s, :, tsl].rearrange("(c p) t -> p c t", p=128),
                               reads=[("yb", s, ti), ("yb", s, ti, 0), ("yb", s, ti, 1)], writes=[ybk])
                        sc.dma("sp", gt[:], gt_d[s, :, tsl].rearrange("(c p) t -> p c t", p=128),
                               reads=[("gt", s, j, ti) for j in range(16)], writes=[gtk])
                        mg, mgk = mgr.next()
                        for m in range(8):
                            pA, pAk = psR.next()
                            pB, pBk = psR.next()

                            def abfn(e, pA=pA, pB=pB, m=m, ya=ya, yb=yb):
                                ins = None
                                for c in range(4):
                                    ins = e.matmul(pA[:], pa_sb[:, c, m * 128:(m + 1) * 128], ya[:, c, :], start=(c == 0), stop=(c == 3))
                                for c in range(4):
                                    ins = e.matmul(pB[:], pb_sb[:, c, m * 128:(m + 1) * 128], yb[:, c, :], start=(c == 0), stop=(c == 3))
                                return ins
                            sc.op("pe", abfn, reads=[yak, ybk] + pa_k + pb_k, writes=[pAk, pBk])
                            t1, t1k = t1r.next()
                            sc.op("dve", lambda e, t1=t1, pA=pA, gt=gt, m=m: e.tensor_tensor(t1[:], pA[:], gt[:, m, :], ALU.mult),
                                  reads=[pAk, gtk], writes=[t1k])
                            t2, t2k = t1r.next()
                            sc.op("dve", lambda e, t2=t2, pB=pB, gt=gt, m=m: e.tensor_tensor(t2[:], pB[:], gt[:, 8 + m, :], ALU.mult),
                                  reads=[pBk, gtk], writes=[t2k])
                            sc.op("dve", lambda e, mg=mg, t1=t1, t2=t2, m=m: e.tensor_tensor(mg[:, m, :], t1[:], t2[:], ALU.add),
                                  reads=[t1k, t2k], writes=[(mgk, m)])
                        mg_keys = [(mgk, m) for m in range(8)]
                        for a in range(4):
                            tok0 = s * S + ti * 512 + a * 128
                            xt3, x3k = x3r.next()
                            sc.dma("sp", xt3[:], x_d[tok0:tok0 + 128, :], writes=[x3k])
                            pt3, p3k = p3r.next()
                            sc.dma("sp", pt3[:], p_d[tok0:tok0 + 128, :], writes=[p3k])
                            yps = []
                            for half in range(2):
                                pY, pYk = psR.next()

                                def yfn(e, pY=pY, half=half, mg=mg, a=a):
                                    ins = None
                                    for m in range(8):
                                        ins = e.matmul(pY[:], mg[:, m, a * 128:(a + 1) * 128], wo_sb[:, m, half * 512:(half + 1) * 512],
                                                       start=(m == 0), stop=(m == 7))
                                    return ins
                                sc.op("pe", yfn, reads=mg_keys + wo_k, writes=[pYk])
                                yps.append((pY, pYk))
                            ys, ysk = ysr.next()
                            stt, sttk = st3r.next()
                            for half in range(2):
                                pY, pYk = yps[half]
                                sc.op("act", lambda e, ys=ys, pY=pY, half=half: e.copy(ys[:, half * 512:(half + 1) * 512], pY[:]),
                                      reads=[pYk], writes=[(ysk, half)])
                            sc.op("act", lambda e, ys=ys, stt=stt: e.activation(sq3[:], ys[:], AF.Square, accum_out=stt[:, 0:1]),
                                  reads=[(ysk, 0), (ysk, 1)], writes=[("sq3",), (sttk, 0)])
                            sc.op("act", lambda e, stt=stt: e.activation(stt[:, 0:1], stt[:, 0:1], AF.Sqrt, bias=float(RMS_EPS), scale=1.0 / D),
                                  reads=[(sttk, 0)], writes=[(sttk, 0)])
                            sc.op("dve", lambda e, stt=stt: e.reciprocal(stt[:, 1:2], stt[:, 0:1]), reads=[(sttk, 0)], writes=[(sttk, 1)])
                            hs, hsk = hsr.next()
                            sc.op("dve", lambda e, hs=hs, ys=ys, stt=stt: e.scalar_tensor_tensor(
                                hs[:], ys[:], stt[:, 1:2], gpost[:], ALU.mult, ALU.mult),
                                reads=[(ysk, 0), (ysk, 1), (sttk, 1), ("gpost",)], writes=[hsk])
                            sc.op("dve", lambda e, hs=hs, xt3=xt3: e.tensor_tensor(hs[:], hs[:], xt3[:], ALU.add),
                                  reads=[hsk, x3k], writes=[hsk])
                            hb, hbk = hbr.next()
                            sc.op("act", lambda e, hb=hb, hs=hs: e.copy(hb[:], hs[:]), reads=[hsk], writes=[hbk])
                            pb3, p3bk = p3br.next()
                            sc.op("act", lambda e, pb3=pb3, pt3=pt3: e.copy(pb3[:], pt3[:]), reads=[p3k], writes=[p3bk])
                            pTp, pTpk = psR.next()
                            ptb = pTp[:].bitcast(BF16)

                            def trfn(e, ptb=ptb, hb=hb):
                                ins = None
                                for m in range(8):
                                    ins = e.transpose(ptb[:, m * 128:(m + 1) * 128], hb[:, m * 128:(m + 1) * 128], ident_b[:])
                                return ins
                            sc.op("pe", trfn, reads=[hbk, ("ident_b",)], writes=[pTpk])
                            hTt, hTk = hTr.next()
                            sc.op("dve", lambda e, hTt=hTt, ptb=ptb: e.tensor_copy(hTt[:], ptb.rearrange("p (m t) -> p m t", m=8)),
                                  reads=[pTpk], writes=[hTk])
                            pP, pPk = psR.next()
                            ppb = pP[:].bitcast(BF16)

                            def trp(e, ppb=ppb, pb3=pb3):
                                ins = None
                                for j in range(2):
                                    ins = e.transpose(ppb[:, j * 128:(j + 1) * 128], pb3[:, j * 128:(j + 1) * 128], ident_b[:])
                                return ins
                            sc.op("pe", trp, reads=[p3bk, ("ident_b",)], writes=[pPk])
                            pTt, pTk = pTr.next()
                            sc.op("act", lambda e, pTt=pTt, ppb=ppb: e.copy(pTt[:], ppb[:, 0:256].rearrange("p (m t) -> p m t", m=2)),
                                  reads=[pPk], writes=[pTk])
                            sgt, sgk = sgr.next()
                            ot, otk = osr.next()
                            for half in range(2):
                                pG, pGk = psR.next()

                                def gfn3(e, pG=pG, half=half, hTt=hTt):
                                    ins = None
                                    for m in range(8):
                                        ins = e.matmul(pG[:], hTt[:, m, :], wg_sb[:, m, half * 512:(half + 1) * 512], start=(m == 0), stop=(m == 7))
                                    return ins
                                sc.op("pe", gfn3, reads=[hTk] + wg_k, writes=[pGk])
                                sc.op("act", lambda e, sgt=sgt, pG=pG, half=half: e.activation(sgt[:, half * 512:(half + 1) * 512], pG[:], AF.Sigmoid),
                                      reads=[pGk], writes=[(sgk, half)])
                                pE, pEk = psR.next()

                                def efn(e, pE=pE, half=half, pTt=pTt):
                                    ins = None
                                    for j in range(2):
                                        ins = e.matmul(pE[:], pTt[:, j, :], wu_sb[:, j, half * 512:(half + 1) * 512], start=(j == 0), stop=(j == 1))
                                    return ins
                                sc.op("pe", efn, reads=[pTk] + wu_k, writes=[pEk])
                                sc.op("dve", lambda e, sgt=sgt, pE=pE, half=half: e.tensor_tensor(
                                    sgt[:, half * 512:(half + 1) * 512], pE[:], sgt[:, half * 512:(half + 1) * 512], ALU.mult),
                                    reads=[pEk, (sgk, half)], writes=[(sgk, half)])
                            sc.op("dve", lambda e, ot=ot, sgt=sgt, hs=hs: e.tensor_tensor(ot[:], sgt[:], hs[:], ALU.add),
                                  reads=[(sgk, 0), (sgk, 1), hsk], writes=[otk])
                            sc.dma("pool", out_d[tok0:tok0 + 128, :], ot[:], reads=[otk], writes=[("out", tok0)], final=True)

        sc.emit(nc, es)
    return nc


def t5_bucket_np(n):
    n = np.maximum(n, 0)
    nf = np.maximum(n, 1).astype(np.float32)
    large = 16 + (np.log(nf / np.float32(16)) / np.float32(math.log(128 / 16)) * np.float32(16)).astype(np.int32)
    large = np.minimum(large, 31)
    return np.where(n < 16, n, large)


def make_consts():
    c = {}
    c["c_ident"] = np.eye(128, dtype=np.float32)
    oh = np.zeros((33, 512), np.float32)
    d = np.arange(512) - 128
    bk = t5_bucket_np(d)
    for j in range(512):
        if d[j] >= 0:
            oh[bk[j], j] = 1.0
        else:
            oh[32, j] = 1.0
    c["c_onehot"] = oh
    es_ = np.zeros((128, 16, 128), np.float32)
    for n in range(16):
        es_[n, n, :] = 1.0
        es_[64 + n, n, :] = 1.0
    c["c_esel"] = es_.reshape(128, 16 * 128)
    tri = np.zeros((64, 3, 64), np.float32)
    i = np.arange(64)
    tri[:, 0, :] = (i[:, None] < i[None, :])
    tri[:, 1, :] = (i[:, None] <= i[None, :])
    tri[:, 2, :] = (i[:, None] > i[None, :])
    c["c_tri"] = tri.reshape(64, 192)
    bd = np.zeros((128, 128), np.float32)
    bd[:64, :64] = 1.0
    bd[64:, 64:] = 1.0
    c["c_bd"] = bd
    c["c_J"] = np.ascontiguousarray(np.eye(128, dtype=np.float32)[::-1])
    pen = np.zeros((16, NH, 16), np.float32)
    for qb in range(16):
        pen[qb, :, qb:] = -30000.0
    c["c_pen"] = pen.reshape(-1)
    return c


_WNAMES = ["g_pre", "w_in", "mu_shift", "w0", "w_up", "a0", "a_up", "k_k", "k_a", "r_k", "ln_x_w", "ln_x_b",
           "p_a", "p_b", "w_out", "g_post", "w_ple_up", "w_ple_gate"]


def make_in_maps(inputs, ncores, nseq, S):
    consts = make_consts()
    maps = []
    for c in range(ncores):
        m = dict(consts)
        m["x"] = np.ascontiguousarray(inputs["x"][c * nseq:(c + 1) * nseq].reshape(nseq * S, D))
        m["p"] = np.ascontiguousarray(inputs["p"][0, c * nseq:(c + 1) * nseq].reshape(nseq * S, PLE))
        m["rel_bias"] = np.ascontiguousarray(inputs["rel_bias"])
        for n in _WNAMES:
            a = np.asarray(inputs[n])[0]
            m[n] = np.ascontiguousarray(a.reshape(-1) if n == "r_k" else a)
        maps.append(m)
    return maps


def kernel(**inputs):
    inputs = {k: np.asarray(v) for k, v in inputs.items()}
    B, S, _ = inputs["x"].shape
    nseq = B // NCORES
    nc = build_program(S, nseq, lvl=99, moba=True)
    maps = make_in_maps(inputs, NCORES, nseq, S)
    res = run_bass_kernel_spmd(nc, maps, core_ids=list(range(NCORES)))
    outs = [r["out"].reshape(nseq, S, D) for r in res.results]
    return np.concatenate(outs, axis=0).astype(np.float32)
```

```python
import math
from contextlib import ExitStack

import numpy as np
import concourse.bass as bass
import concourse.mybir as mybir
from concourse.bass_utils import run_bass_kernel_spmd

F32 = mybir.dt.float32
BF16 = mybir.dt.bfloat16
F32R = mybir.dt.float32r
ALU = mybir.AluOpType
AF = mybir.ActivationFunctionType
AX = mybir.AxisListType

D = 1024
NCORES = 8
A_W = 512
B_W = 512
HD = 64
NH = 8
IN_COLS = 6272
RW_COLS = 2176
PLE = 256
BLK = 256
RMS_EPS = 1e-6
GN_EPS = 64e-5
NEG = -30000.0


class Sched:
    ENGS = ("pe", "act", "dve", "pool", "sp")
    NDMA = 8

    def __init__(self):
        self.streams = {e: [] for e in self.ENGS}
        self.waited = {e: {} for e in self.ENGS}
        self.last_w = {}
        self.readers = {}
        self.dma_cnt = {e: 0 for e in self.ENGS}
        self.dma_val = {}
        self.final_dma = []

    def _add_wait(self, eng, waits, ev):
        if ev is None:
            return
        if ev[0] == "eng":
            _, e2, j = ev
            if e2 == "pe" and eng == "pe":
                return
            key = e2
            val = j
        else:
            _, key, val = ev
        if self.waited[eng].get(key, -1) >= val:
            return
        self.waited[eng][key] = val
        waits.append(ev)
        if ev[0] == "eng":
            self.streams[ev[1]][ev[2]]["sig"] = True

    def _deps(self, eng, reads, writes):
        evs = []
        for k in reads:
            evs.append(self.last_w.get(k))
        for k in writes:
            evs.append(self.last_w.get(k))
            evs.extend(self.readers.get(k, ()))
        best = {}
        for ev in evs:
            if ev is None:
                continue
            key = ev[1]
            if key not in best or ev[2] > best[key][2]:
                best[key] = ev
        waits = []
        for ev in best.values():
            self._add_wait(eng, waits, ev)
        return waits

    def _commit(self, ev, reads, writes):
        for k in reads:
            self.readers.setdefault(k, []).append(ev)
        for k in writes:
            self.last_w[k] = ev
            self.readers[k] = []

    def op(self, eng, fn, reads=(), writes=()):
        waits = self._deps(eng, reads, writes)
        idx = len(self.streams[eng])
        self.streams[eng].append({"fn": fn, "waits": waits, "sig": False, "dma": None})
        self._commit(("eng", eng, idx), reads, writes)

    def dma(self, eng, out, in_, reads=(), writes=(), final=False, **kw):
        slot = self.dma_cnt[eng] % self.NDMA
        self.dma_cnt[eng] += 1
        key = (eng, slot)
        prev = self.dma_val.get(key, 0)
        waits = self._deps(eng, reads, writes)
        if prev > 0:
            self._add_wait(eng, waits, ("dma", key, prev))
        val = prev + 16
        self.dma_val[key] = val
        fn = lambda e, out=out, in_=in_, kw=kw: e.dma_start(out=out, in_=in_, **kw)
        self.streams[eng].append({"fn": fn, "waits": waits, "sig": False, "dma": key})
        ev = ("dma", key, val)
        self._commit(ev, reads, writes)
        if final:
            self.final_dma.append(ev)

    def barrier(self):
        evs = []
        for e in ("pe", "act", "dve", "pool"):
            for j in range(len(self.streams[e]) - 1, -1, -1):
                if self.streams[e][j]["fn"] is not None and self.streams[e][j]["dma"] is None:
                    evs.append(("eng", e, j))
                    break
        for key, val in self.dma_val.items():
            evs.append(("dma", key, val))
        for e in self.ENGS:
            waits = []
            for ev in evs:
                if ev[0] == "eng" and ev[1] == e:
                    continue
                self._add_wait(e, waits, ev)
            self.streams[e].append({"fn": None, "waits": waits, "sig": False, "dma": None})
        self.last_w = {}
        self.readers = {}

    def emit(self, nc, es):
        sems = {e: es.enter_context(nc.semaphore("sem_" + e)) for e in ("pe", "act", "dve", "pool")}
        dsems = {}
        for (e, slot) in self.dma_val:
            dsems[(e, slot)] = es.enter_context(nc.semaphore("dsem_%s%d" % (e, slot)))
        fin_waits = []
        for ev in self.final_dma:
            self._add_wait("sp", fin_waits, ev)
        counts = {}
        for e in ("pe", "act", "dve", "pool"):
            c = 0
            lst = []
            for o in self.streams[e]:
                if o["sig"]:
                    c += 1
                lst.append(c)
            counts[e] = lst
        block = es.enter_context(nc.Block())

        def run(engname, eobj):
            def do_wait(ev):
                if ev[0] == "eng":
                    eobj.wait_ge(sems[ev[1]], counts[ev[1]][ev[2]])
                else:
                    eobj.wait_ge(dsems[ev[1]], ev[2])
            for o in self.streams[engname]:
                for ev in o["waits"]:
                    do_wait(ev)
                if o["fn"] is None:
                    continue
                ins = o["fn"](eobj)
                if o["dma"] is not None:
                    ins.then_inc(dsems[o["dma"]], 16)
                elif o["sig"]:
                    ins.then_inc(sems[engname], 1)
            if engname == "sp":
                for ev in fin_waits:
                    do_wait(ev)

        @block.sync
        def _(e):
            run("sp", e)

        @block.tensor
        def _(e):
            run("pe", e)

        @block.scalar
        def _(e):
            run("act", e)

        @block.vector
        def _(e):
            run("dve", e)

        @block.gpsimd
        def _(e):
            run("pool", e)


class Ring:
    def __init__(self, tiles, name, keys=None):
        self.tiles = tiles
        self.keys = keys if keys is not None else [(name, j) for j in range(len(tiles))]
        self.i = 0

    def next(self):
        j = self.i % len(self.tiles)
        self.i += 1
        return self.tiles[j], self.keys[j]


def build_program(S, NSEQ, dbg=None, lvl=30, sub=99, moba=True, only_even=False, GRPN=4, bar=False):
    dbg = dbg or set()
    nc = bass.Bass("TRN2", target_bir_lowering=False)
    NT = S // 512
    NQT = S // 128
    NBLK = S // BLK
    NTOK = NSEQ * S

    def din(name, shape, dt=F32):
        return nc.dram_tensor(name, list(shape), dt, kind="ExternalInput").ap()

    x_d = din("x", [NTOK, D])
    p_d = din("p", [NTOK, PLE])
    g_pre_d = din("g_pre", [D])
    w_in_d = din("w_in", [D, IN_COLS])
    relb_d = din("rel_bias", [32, NH])
    mu_d = din("mu_shift", [RW_COLS])
    w0_d = din("w0", [B_W])
    w_up_d = din("w_up", [64, B_W])
    a0_d = din("a0", [B_W])
    a_up_d = din("a_up", [64, B_W])
    k_k_d = din("k_k", [B_W])
    k_a_d = din("k_a", [B_W])
    r_k_d = din("r_k", [B_W])
    lnw_d = din("ln_x_w", [B_W])
    lnb_d = din("ln_x_b", [B_W])
    p_a_d = din("p_a", [A_W, D])
    p_b_d = din("p_b", [B_W, D])
    w_out_d = din("w_out", [D, D])
    g_post_d = din("g_post", [D])
    w_pu_d = din("w_ple_up", [PLE, D])
    w_pg_d = din("w_ple_gate", [D, D])
    c_ident_d = din("c_ident", [128, 128])
    c_onehot_d = din("c_onehot", [33, 512])
    c_esel_d = din("c_esel", [128, 16 * 128])
    c_tri_d = din("c_tri", [64, 3 * 64])
    c_bd_d = din("c_bd", [128, 128])
    c_J_d = din("c_J", [128, 128])
    c_pen_d = din("c_pen", [16 * 128])

    out_d = nc.dram_tensor("out", [NTOK, D], F32, kind="ExternalOutput").ap()

    def scratch(name, shape, dt):
        kind = "ExternalOutput" if name in dbg else "Internal"
        return nc.dram_tensor(name, list(shape), dt, kind=kind).ap()

    qT_d = scratch("s_qT", [NSEQ, A_W, S], BF16)
    kT_d = scratch("s_kT", [NSEQ, A_W, S], BF16)
    v_d = scratch("s_v", [NSEQ, S, 2 * A_W], BF16)
    za_d = scratch("s_za", [NSEQ, A_W, S], BF16)
    rw_d = scratch("s_rw", [NSEQ, RW_COLS, S], F32)
    gt_d = scratch("s_gt", [NSEQ, 2 * D, S], BF16)
    ya_d = scratch("s_ya", [NSEQ, A_W, S], BF16)
    yb_d = scratch("s_yb", [NSEQ, B_W, S], BF16)
    fb_d = scratch("s_fb", [NH, 512], F32)

    sc = Sched()
    es = ExitStack()

    def sb(name, shape, dt=F32):
        return es.enter_context(nc.sbuf_tensor(name, list(shape), dt))

    def ps(name, shape, dt=F32):
        return es.enter_context(nc.psum_tensor(name, list(shape), dt))

    with es:
        ident_f = sb("ident_f", [128, 128])
        ident_b = sb("ident_b", [128, 128], BF16)
        sc.dma("sp", ident_f[:], c_ident_d[:, :], writes=[("ident_f",)])
        sc.op("dve", lambda e: e.tensor_copy(ident_b[:], ident_f[:]), reads=[("ident_f",)], writes=[("ident_b",)])
        psb = [ps("psb%d" % i, [128, 512]) for i in range(8)]
        psring = Ring(psb, "psb")

        vstage = sb("vstage", [64, 128])
        vc = sb("vc", [128, 64])
        VOFF = {}
        _r = 0
        sc.op("dve", lambda e: e.memset(vstage[:], 0.0), writes=[("vstage",)])
        for nm, dv, n in (("g_pre", g_pre_d, 8), ("mu", mu_d, 17), ("w0", w0_d, 4), ("a0", a0_d, 4),
                          ("k_k", k_k_d, 4), ("k_a", k_a_d, 4), ("r_k", r_k_d, 4)):
            VOFF[nm] = _r
            sc.dma("sp", vstage[_r:_r + n, :], dv.rearrange("(k p) -> k p", p=128), reads=[("vstage",)], writes=[("vstage", nm)])
            _r += n
        pt, pk = psring.next()
        sc.op("pe", lambda e, pt=pt: e.transpose(pt[:, 0:64], vstage[:], ident_f[0:64, 0:64]),
              reads=[("vstage",), ("ident_f",)] + [("vstage", nm) for nm in VOFF], writes=[pk])
        sc.op("dve", lambda e, pt=pt: e.tensor_copy(vc[:], pt[:, 0:64]), reads=[pk], writes=[("vc",)])
        gpre_c = vc[:, VOFF["g_pre"]:VOFF["g_pre"] + 8]

        kmean = sb("kmean", [128, NSEQ, 4, 16], F32)
        kmean_b = sb("kmean_b", [128, NSEQ, 4, 16], BF16)
        if True:
            es1 = ExitStack()
            with es1:
                def sb1(name, shape, dt=F32):
                    return es1.enter_context(nc.sbuf_tensor(name, list(shape), dt))
                wp = sb1("wp", [128, 8, IN_COLS], BF16)
                wst = [sb1("wst%d" % i, [128, 1568]) for i in range(2)]
                wring = Ring(wst, "wst")
                for kc in range(8):
                    for q4 in range(4):
                        t, tk = wring.next()
                        c0 = q4 * 1568
                        sc.dma("sp", t[:], w_in_d[kc * 128:(kc + 1) * 128, c0:c0 + 1568], writes=[tk])
                        eng = ("dve", "pool")[(kc * 4 + q4) % 2]
                        sc.op(eng, lambda e, t=t, kc=kc, c0=c0: e.tensor_scalar(
                            wp[:, kc, c0:c0 + 1568], t[:], gpre_c[:, kc:kc + 1], None, ALU.mult),
                            reads=[tk, ("vc",)], writes=[("wp", kc, q4)])
                wp_keys = [("wp", kc, q4) for kc in range(8) for q4 in range(4)]

                mu_c = vc[:, VOFF["mu"]:VOFF["mu"] + 17]
                carry = sb1("carry", [128, 17])
                xt = [sb1("xt%d" % i, [128, 4, D]) for i in range(2)]
                xring = Ring(xt, "xt")
                ub = [sb1("ub%d" % i, [128, D], BF16) for i in range(2)]
                ubring = Ring(ub, "ub")
                sqj = sb1("sqj", [128, D], BF16)
                ss = [sb1("ss%d" % i, [128, 4]) for i in range(2)]
                ssring = Ring(ss, "ss")
                rs = [sb1("rs%d" % i, [128, 4]) for i in range(2)]
                rsring = Ring(rs, "rs")
                uT = [sb1("uT%d" % i, [128, 8, 512], BF16) for i in range(2)]
                uTring = Ring(uT, "uT")
                ob = [sb1("ob%d" % i, [128, 512], BF16) for i in range(4)]
                obring = Ring(ob, "ob")
                vo = [sb1("vo%d" % i, [128, NH, 128], BF16) for i in range(2)]
                voring = Ring(vo, "vo")
                for i in range(2):
                    sc.op("dve", lambda e, i=i: e.memset(vo[i][:], 1.0), writes=[("vo", i), ("vo_init",)])
                cb = [sb1("cb%d" % i, [128, 513]) for i in range(3)]
                cbring = Ring(cb, "cb")
                db = [sb1("db%d" % i, [128, 512]) for i in range(2)]
                dbring = Ring(db, "db")
                shb = [sb1("shb%d" % i, [128, 512]) for i in range(3)]
                shring = Ring(shb, "shb")
                sc.op("dve", lambda e: e.memset(kmean[:], 0.0), writes=[("kmean",)])

                xloaded = {}

                def load_x(s, ti):
                    if s >= NSEQ:
                        return
                    tok0 = s * S + ti * 512
                    xtile, xk = xring.next()
                    sc.dma("sp", xtile[:], x_d[tok0:tok0 + 512, :].rearrange("(a p) d -> p a d", p=128),
                           writes=[xk])
                    xloaded[(s, ti)] = (xtile, xk)

                load_x(0, 0)
                for s in range(NSEQ if lvl >= 1 else 0):
                    sc.op("dve", lambda e: e.memset(carry[:], 0.0), writes=[("carry",)], reads=[])
                    for ti in range(NT):
                        nxt = (s, ti + 1) if ti + 1 < NT else (s + 1, 0)
                        load_x(*nxt)
                        xtile, xk = xloaded.pop((s, ti))
                        sst, ssk = ssring.next()
                        rst, rsk = rsring.next()
                        sc.op("dve", lambda e, sst=sst: e.memset(sst[:], 0.0), writes=[(ssk, a) for a in range(4)])
                        for a in range(4):
                            sc.op("act", lambda e, a=a, xtile=xtile, sst=sst: e.activation(
                                sqj[:], xtile[:, a, :], AF.Square, accum_out=sst[:, a:a + 1]),
                                reads=[xk], writes=[("sqj",), (ssk, a)])
                        sc.op("act", lambda e, sst=sst: e.activation(
                            sst[:], sst[:], AF.Sqrt, bias=float(RMS_EPS), scale=1.0 / D),
                            reads=[(ssk, a) for a in range(4)], writes=[(ssk, "q")])
                        sc.op("dve", lambda e, sst=sst, rst=rst: e.reciprocal(rst[:], sst[:]),
                              reads=[(ssk, "q")], writes=[rsk] + [(ssk, a) for a in range(4)])
                        uTt, uTk = uTring.next()
                        for a in range(4):
                            ubt, ubk = ubring.next()
                            sc.op("dve", lambda e, a=a, ubt=ubt, xtile=xtile, rst=rst: e.tensor_scalar(
                                ubt[:], xtile[:, a, :], rst[:, a:a + 1], None, ALU.mult),
                                reads=[xk, rsk], writes=[ubk])
                            pt, pk = psring.next()
                            ptb = pt[:].bitcast(BF16)

                            def tr_fn(e, ubt=ubt, ptb=ptb):
                                ins = None
                                for kc in range(8):
                                    ins = e.transpose(ptb[:, kc * 128:(kc + 1) * 128], ubt[:, kc * 128:(kc + 1) * 128], ident_b[:])
                                return ins
                            sc.op("pe", tr_fn, reads=[ubk, ("ident_b",)], writes=[pk])
                            eng = ("act", "dve")[a % 2]
                            if eng == "act":
                                sc.op("act", lambda e, a=a, uTt=uTt, ptb=ptb: e.copy(
                                    uTt[:, :, a * 128:(a + 1) * 128], ptb.rearrange("p (k t) -> p k t", k=8)),
                                    reads=[pk], writes=[(uTk, a)])
                            else:
                                sc.op("dve", lambda e, a=a, uTt=uTt, ptb=ptb: e.tensor_copy(
                                    uTt[:, :, a * 128:(a + 1) * 128], ptb.rearrange("p (k t) -> p k t", k=8)),
                                    reads=[pk], writes=[(uTk, a)])
                        uT_keys = [(uTk, a) for a in range(4)]

                        def proj(cc, pt, pk, uTt=uTt, uT_keys=uT_keys):
                            def fn(e):
                                ins = None
                                for kc in range(8):
                                    ins = e.matmul(pt[:], wp[:, kc, cc * 128:(cc + 1) * 128], uTt[:, kc, :],
                                                   start=(kc == 0), stop=(kc == 7))
                                return ins
                            sc.op("pe", fn, reads=uT_keys + wp_keys, writes=[pk])

                        for cc in range(49):
                            if 8 <= cc < 12:
                                continue
                            if lvl < 2 or (lvl == 2 and cc >= 4) or (lvl == 3 and cc >= 8) or (lvl == 4 and cc >= 16) or (lvl == 5 and cc >= 33):
                                continue
                            pt, pk = psring.next()
                            proj(cc, pt, pk)
                            tsl = slice(ti * 512, (ti + 1) * 512)
                            rows = slice((cc % 4) * 128, (cc % 4) * 128 + 128)
                            if cc < 4:
                                o, ok = obring.next()
                                sc.op("act", lambda e, o=o, pt=pt: e.mul(o[:], pt[:], 0.125), reads=[pk], writes=[ok])
                                sc.dma("pool", qT_d[s, rows, tsl], o[:], reads=[ok], writes=[("qT", s, cc, ti)])
                            elif cc < 8:
                                o, ok = obring.next()
                                for hb in range(2):
                                    sc.op("act", lambda e, o=o, pt=pt, hb=hb, s=s, cc=cc, ti=ti: e.activation(
                                        o[:, hb * 256:(hb + 1) * 256], pt[:, hb * 256:(hb + 1) * 256], AF.Copy,
                                        accum_out=kmean[:, s, cc - 4, 2 * ti + hb:2 * ti + hb + 1]),
                                        reads=[pk, ("kmean",)], writes=([ok] if hb == 0 else []) + [(ok, hb), ("kmean", s, cc - 4, ti, hb)])
                                sc.dma("pool", kT_d[s, rows, tsl], o[:], reads=[ok, (ok, 0), (ok, 1)], writes=[("kT", s, cc - 4, ti)])
                            elif cc < 16:
                                o, ok = obring.next()
                                sc.op("act", lambda e, o=o, pt=pt: e.activation(o[:], pt[:], AF.Silu), reads=[pk], writes=[ok])
                                sc.dma("pool", za_d[s, rows, tsl], o[:], reads=[ok], writes=[("za", s, cc - 12, ti)])
                            elif cc < 33:
                                j = cc - 16
                                c, ck = cbring.next()
                                sc.op("act", lambda e, c=c, pt=pt: e.copy(c[:, 1:513], pt[:]), reads=[pk], writes=[(ck, 1)])
                                sc.op("dve", lambda e, c=c, j=j: e.tensor_copy(c[:, 0:1], carry[:, j:j + 1]),
                                      reads=[("carry", j), ("carry",)], writes=[(ck, 0)])
                                sc.op("dve", lambda e, c=c, j=j: e.tensor_copy(carry[:, j:j + 1], c[:, 512:513]),
                                      reads=[(ck, 1), ("carry",)], writes=[("carry", j)])
                                dd, dk = dbring.next()
                                sc.op("dve", lambda e, c=c, dd=dd: e.tensor_tensor(dd[:], c[:, 0:512], c[:, 1:513], ALU.subtract),
                                      reads=[(ck, 0), (ck, 1)], writes=[dk])
                                sh, shk = shring.next()
                                sc.op("dve", lambda e, c=c, dd=dd, sh=sh, j=j: e.scalar_tensor_tensor(
                                    sh[:], dd[:], mu_c[:, j:j + 1], c[:, 1:513], ALU.mult, ALU.add),
                                    reads=[dk, (ck, 1), ("vc",)], writes=[shk])
                                sc.dma("pool", rw_d[s, j * 128:(j + 1) * 128, tsl], sh[:], reads=[shk], writes=[("rw", s, j, ti)])
                            else:
                                j = cc - 33
                                o, ok = obring.next()
                                sc.op("act", lambda e, o=o, pt=pt: e.activation(o[:], pt[:], AF.Sigmoid), reads=[pk], writes=[ok])
                                sc.dma("pool", gt_d[s, j * 128:(j + 1) * 128, tsl], o[:], reads=[ok], writes=[("gt", s, j, ti)])
                        for a in range(4 if lvl >= 7 else 0):
                            pt, pk = psring.next()

                            def vfn(e, a=a, pt=pt, uTt=uTt):
                                ins = None
                                for kc in range(8):
                                    ins = e.matmul(pt[:], uTt[:, kc, a * 128:(a + 1) * 128], wp[:, kc, 1024:1536],
                                                   start=(kc == 0), stop=(kc == 7))
                                return ins
                            sc.op("pe", vfn, reads=uT_keys + wp_keys, writes=[pk])
                            o, ok = voring.next()
                            sc.op("act", lambda e, o=o, pt=pt: e.copy(o[:, :, 0:64], pt[:].rearrange("p (h d) -> p h d", h=NH)),
                                  reads=[pk, ("vo_init",)], writes=[ok])
                            t0 = ti * 512 + a * 128
                            sc.dma("pool", v_d[s, t0:t0 + 128, :], o[:].rearrange("p h d -> p (h d)"), reads=[ok], writes=[("v", s, ti * 4 + a)])
                sc.op("dve", lambda e: e.tensor_scalar(kmean_b[:], kmean[:], 1.0 / BLK, None, ALU.mult),
                      reads=[("kmean",)] + [("kmean", s, c, ti, hb) for s in range(NSEQ) for c in range(4) for ti in range(NT) for hb in range(2)],
                      writes=[("kmean_b",)])

        sc.barrier()
        if lvl >= 10 and moba:
            es2 = ExitStack()
            with es2:
                def sb2(name, shape, dt=F32):
                    return es2.enter_context(nc.sbuf_tensor(name, list(shape), dt))
                psS = Ring(psb[0:4], "psb", keys=[("psb", j) for j in range(0, 4)])
                psN = Ring(psb[4:6], "psb", keys=[("psb", j) for j in range(4, 6)])
                psM = Ring(psb[6:8], "psb", keys=[("psb", j) for j in range(6, 8)])
                relb33 = sb2("relb33", [33, NH])
                b31bc = sb2("b31bc", [128, NH])
                sc.dma("sp", b31bc[:], relb_d[31, :].partition_broadcast(128), writes=[("b31bc",)])
                sc.op("dve", lambda e: e.memset(relb33[32:33, :], NEG), writes=[("relb33", 1)])
                sc.dma("sp", relb33[0:32, :], relb_d[:, :], writes=[("relb33", 0)])
                oneh = sb2("oneh", [33, 512])
                sc.dma("sp", oneh[:], c_onehot_d[:, :], writes=[("oneh",)])
                pt, pk = psM.next()
                sc.op("pe", lambda e, pt=pt: e.matmul(pt[0:8, :], relb33[:], oneh[:], start=True, stop=True),
                      reads=[("relb33", 0), ("relb33", 1), ("oneh",)], writes=[pk])
                fbs = sb2("fbs", [8, 512])
                sc.op("dve", lambda e, pt=pt: e.tensor_copy(fbs[:], pt[0:8, :]), reads=[pk], writes=[("fbs",)])
                sc.dma("sp", fb_d[:, :], fbs[:], reads=[("fbs",)], writes=[("fb_d",)])
                Jf = sb2("Jf", [128, 128])
                sc.dma("sp", Jf[:], c_J_d[:, :], writes=[("Jf",)])
                Tt = sb2("Tt", [128, NH, 2, 128], BF16)
                tfl = [sb2("tfl%d" % i, [128, 128]) for i in range(2)]
                tflr = Ring(tfl, "tfl")
                for h in range(NH):
                    for dl in range(2):
                        tf, tfk = tflr.next()
                        src = bass.AP(tensor=fb_d.tensor, offset=h * 512 + 1 + dl * 128, ap=[[1, 128], [1, 128]])
                        sc.dma("sp", tf[:], src, reads=[("fb_d",)], writes=[tfk])
                        pt, pk = psM.next()
                        sc.op("pe", lambda e, pt=pt, tf=tf: e.matmul(pt[:, 0:128], Jf[:], tf[:], start=True, stop=True),
                              reads=[tfk, ("Jf",)], writes=[pk])
                        sc.op("dve", lambda e, pt=pt, h=h, dl=dl: e.tensor_scalar(Tt[:, h, dl, :], pt[:, 0:128], b31bc[:, h:h + 1], None, ALU.subtract),
                              reads=[pk, ("b31bc",)], writes=[("Tt", h, dl)])
                Tt_keys = [("Tt", h, dl) for h in range(NH) for dl in range(2)]
                if "d_Tt" in dbg:
                    dTt = nc.dram_tensor("d_Tt", [128, NH * 2 * 128], BF16, kind="ExternalOutput").ap()
                    sc.dma("sp", dTt[:, :], Tt[:].rearrange("p h d q -> p (h d q)"), reads=Tt_keys)
                eself = sb2("eself", [128, 16 * 128])
                esel = sb2("esel", [128, 16, 128], BF16)
                sc.dma("sp", eself[:], c_esel_d[:, :], writes=[("eself",)])
                sc.op("dve", lambda e: e.tensor_copy(esel[:].rearrange("p a b -> p (a b)"), eself[:]), reads=[("eself",)], writes=[("esel",)])
                ones_b = sb2("ones_b", [128, 64], BF16)
                sc.op("dve", lambda e: e.memset(ones_b[:], 1.0), writes=[("ones_b",)])
                maskT = [sb2("maskT%d" % i, [128, NH, 128], BF16) for i in range(2)]
                for i in range(2):
                    sc.op("dve", lambda e, i=i: e.memset(maskT[i][:, :, :], 0.0), writes=[("maskT", i)])
                mring = Ring(maskT, "maskT")
                pen_sb = sb2("pen_sb", [128, 16 * 128])
                sc.op("dve", lambda e: e.memset(pen_sb[:], 0.0), writes=[("pen_sb", "z")])
                sc.dma("sp", pen_sb[0:1, :], c_pen_d.rearrange("(a n) -> a n", a=1), reads=[("pen_sb", "z")], writes=[("pen_sb", 0)])
                sc.dma("sp", pen_sb[64:65, :], c_pen_d.rearrange("(a n) -> a n", a=1), reads=[("pen_sb", "z")], writes=[("pen_sb", 1)])
                onesq_b = sb2("onesq_b", [128, 128], BF16)
                sc.op("dve", lambda e: e.memset(onesq_b[:], 1.0), writes=[("onesq_b",)])
                penb = sb2("penb", [128, 16 * 128], BF16)
                sc.op("dve", lambda e: e.tensor_copy(penb[:], pen_sb[:]), reads=[("pen_sb", 0), ("pen_sb", 1), ("pen_sb", "z")], writes=[("penb",)])
                kT_sb = sb2("kT_sb", [128, 4, S], BF16)
                qT_sb = sb2("qT_sb", [128, 4, S], BF16)
                v_sb = sb2("v_sb", [128, S // 128, 2 * A_W], BF16)
                gsb = [sb2("gsb%d" % i, [128, NH, 16]) for i in range(2)]
                gring = Ring(gsb, "gsb")
                top8 = [sb2("top8_%d" % i, [128, NH, 8]) for i in range(2)]
                t8ring = Ring(top8, "top8")
                selm = [sb2("selm%d" % i, [128, NH, 16]) for i in range(2)]
                sring = Ring(selm, "selm")
                mvb = [sb2("mvb%d" % i, [128, NH, 16], BF16) for i in range(2)]
                mvring = Ring(mvb, "mvb")
                Pb = [sb2("Pb%d" % i, [128, 512], BF16) for i in range(5)]
                Pring = Ring(Pb, "Pb")
                rden = [sb2("rden%d" % i, [64, 128]) for i in range(2)]
                rdring = Ring(rden, "rden")
                ynorm = [sb2("ynorm%d" % i, [64, 128]) for i in range(2)]
                ynring = Ring(ynorm, "ynorm")
                zat = [sb2("zat%d" % i, [64, NH, 128], BF16) for i in range(2)]
                zring = Ring(zat, "zat")
                yout = [sb2("yout%d" % i, [64, NH, 128], BF16) for i in range(2)]
                yring = Ring(yout, "yout")

                for s in range(NSEQ if lvl >= 11 else 0):
                    for c in range(4):
                        sc.dma("sp", kT_sb[:, c, :], kT_d[s, c * 128:(c + 1) * 128, :],
                               reads=[("kT", s, c, ti) for ti in range(NT)], writes=[("kT_sb", c)])
                        sc.dma("sp", qT_sb[:, c, :], qT_d[s, c * 128:(c + 1) * 128, :],
                               reads=[("qT", s, c, ti) for ti in range(NT)], writes=[("qT_sb", c)])
                    for j4 in range(0, S // 128, 4):
                        n4 = min(4, S // 128 - j4)
                        sc.dma("sp", v_sb[:, j4:j4 + n4, :], v_d[s, j4 * 128:(j4 + n4) * 128, :].rearrange("(a p) d -> p a d", p=128),
                               reads=[("v", s, j) for j in range(j4, j4 + n4)], writes=[("v_sb", j) for j in range(j4, j4 + n4)])
                    for qt in range(NQT if lvl >= 12 else 0):
                        QB = qt // 2
                        qsl = slice(qt * 128, (qt + 1) * 128)
                        mT, mTk = None, None
                        if bar:
                            sc.barrier()
                        if QB > 0:
                            pt, pk = psM.next()

                            def gfn(e, pt=pt, qsl=qsl, s=s, QB=QB):
                                ins = None
                                for h in range(NH):
                                    hp = slice((h % 2) * 64, (h % 2) * 64 + 64)
                                    ins = e.matmul(pt[:, h * 16:(h + 1) * 16], qT_sb[hp, h // 2, qsl], kmean_b[hp, s, h // 2, :],
                                                   start=True, stop=False)
                                    p0 = (h % 2) * 64
                                    ins = e.matmul(pt[:, h * 16:(h + 1) * 16], onesq_b[p0:p0 + 1, :],
                                                   penb[p0:p0 + 1, QB * 128 + h * 16:QB * 128 + (h + 1) * 16], start=False, stop=True)
                                return ins
                            sc.op("pe", gfn, reads=[("qT_sb", c) for c in range(4)] + [("kmean_b",), ("penb",), ("onesq_b",)], writes=[pk])
                            g, gk = gring.next()
                            sc.op("dve", lambda e, g=g, pt=pt: e.tensor_copy(g[:].rearrange("p h n -> p (h n)"), pt[:, 0:NH * 16]),
                                  reads=[pk], writes=[gk, (gk, 1)])
                            t8, t8k = t8ring.next()
                            for h in range(NH if sub >= 3 else 0):
                                sc.op("dve", lambda e, t8=t8, g=g, h=h: e.max(t8[:, h, :], g[:, h, :]),
                                      reads=[gk, (gk, 1)], writes=[(t8k, h)])
                            sm, smk = sring.next()
                            if sub >= 4:
                              sc.op("dve", lambda e, sm=sm, g=g, t8=t8: e.tensor_tensor(
                                sm[:], g[:], t8[:, :, 2:3].to_broadcast([128, NH, 16]), ALU.is_ge),
                                reads=[gk, (gk, 1)] + [(t8k, h) for h in range(NH)], writes=[smk])
                            mv, mvk = mvring.next()
                            if sub >= 5:
                              sc.op("dve", lambda e, mv=mv, sm=sm: e.tensor_scalar(mv[:], sm[:], -NEG, NEG, ALU.mult, ALU.add),
                                  reads=[smk], writes=[mvk])
                            pt2, pk2 = psM.next()
                            ptb2 = pt2[:].bitcast(BF16)

                            def mtr(e, mv=mv, ptb2=ptb2):
                                ins = None
                                for h in range(NH):
                                    ins = e.transpose(ptb2[0:16, h * 128:(h + 1) * 128], mv[:, h, :], ident_b[:])
                                return ins
                            if sub >= 6:
                                sc.op("pe", mtr, reads=[mvk, ("ident_b",)], writes=[pk2])
                            mT, mTk = mring.next()
                            if sub >= 7:
                                sc.op("dve", lambda e, mT=mT, ptb2=ptb2: e.tensor_copy(
                                    mT[0:16, :, :], ptb2[0:16, :].rearrange("p (h q) -> p h q", h=NH)),
                                    reads=[pk2], writes=[mTk])
                                sc.op("dve", lambda e, mT=mT, ptb2=ptb2: e.tensor_copy(
                                    mT[64:80, :, :], ptb2[0:16, :].rearrange("p (h q) -> p h q", h=NH)),
                                    reads=[pk2], writes=[(mTk, "b")])
                        if "d_gate" in dbg and qt == NQT - 1 and s == 0:
                            dg = nc.dram_tensor("d_gate", [128, NH * 16], F32, kind="ExternalOutput").ap()
                            sc.dma("sp", dg[:, :], g[:].rearrange("p h n -> p (h n)"), reads=[gk, (gk, 1)])
                            dsm = nc.dram_tensor("d_sm", [128, NH * 16], F32, kind="ExternalOutput").ap()
                            sc.dma("sp", dsm[:, :], sm[:].rearrange("p h n -> p (h n)"), reads=[smk])
                            dmt = nc.dram_tensor("d_mt", [33, NH * 128], BF16, kind="ExternalOutput").ap()
                            sc.dma("sp", dmt[:, :], mT[:].rearrange("p h n -> p (h n)"), reads=[mTk, (mTk, "c")])
                            dkm = nc.dram_tensor("d_km", [128, NSEQ * 64], F32, kind="ExternalOutput").ap()
                            sc.dma("sp", dkm[:, :], kmean[:].rearrange("p s c n -> p (s c n)"), reads=[("kmean_b",)])
                        if bar:
                            sc.barrier()
                        zt, ztk = zring.next()
                        sc.dma("sp", zt[:], za_d[s].rearrange("(h d) t -> d h t", h=NH)[:, :, qsl],
                               reads=[("za", s, c, qt // 4) for c in range(4)], writes=[ztk])
                        yo, yok = yring.next()
                        pend = []

                        def drain(keep):
                            while len(pend) > keep:
                                pend.pop(0)()
                        for h in range(NH if lvl >= 13 else 0):
                            hp = slice((h % 2) * 64, (h % 2) * 64 + 64)
                            c = h // 2
                            nd, ndk = psN.next()
                            kts = list(range(qt + 1))
                            ngroups = (len(kts) + GRPN - 1) // GRPN
                            for gi, g0 in enumerate(range(0, len(kts), GRPN)):
                                grp = kts[g0:g0 + GRPN]
                                st_, stk = psS.next()

                                def sfn(e, grp=grp, st_=st_, hp=hp, c=c, qsl=qsl, qt=qt, QB=QB, h=h, mT=mT):
                                    ins = None
                                    for j, kt in enumerate(grp):
                                        osl = st_[:, j * 128:(j + 1) * 128]
                                        extra = []
                                        n = kt // 2
                                        if n < QB:
                                            extra.append((esel[hp, n, :], mT[hp, h, :]))
                                        ins = e.matmul(osl, kT_sb[hp, c, kt * 128:(kt + 1) * 128], qT_sb[hp, c, qsl],
                                                       start=True, stop=(len(extra) == 0))
                                        for i2, (l_, r_) in enumerate(extra):
                                            ins = e.matmul(osl, l_, r_, start=False, stop=(i2 == len(extra) - 1))
                                    return ins
                                rd = [("kT_sb", c), ("qT_sb", c), ("esel",)]
                                if mTk is not None:
                                    rd += [mTk, (mTk, "b")]
                                sc.op("pe", sfn, reads=rd, writes=[stk])
                                for j, kt in enumerate(grp):
                                    if kt >= qt - 1:
                                        dl = qt - kt
                                        sc.op("dve", lambda e, st_=st_, j=j, h=h, dl=dl: e.tensor_tensor(
                                            st_[:, j * 128:(j + 1) * 128], st_[:, j * 128:(j + 1) * 128], Tt[:, h, dl, :], ALU.add),
                                            reads=[stk, ("Tt", h, dl)], writes=[stk])
                                P, Pk = Pring.next()
                                ng = len(grp)
                                sc.op("act", lambda e, P=P, st_=st_, ng=ng, h=h: e.activation(P[:, 0:ng * 128], st_[:, 0:ng * 128], AF.Exp, bias=b31bc[:, h:h + 1]),
                                      reads=[stk, ("b31bc",)], writes=[Pk])

                                def emit_pv(grp=grp, P=P, Pk=Pk, nd=nd, ndk=ndk, h=h, qt=qt, last=(gi == ngroups - 1), yo=yo, yok=yok, zt=zt, ztk=ztk):
                                    def pvfn(e):
                                        ins = None
                                        for j, kt in enumerate(grp):
                                            ins = e.matmul(nd[:, 0:128], v_sb[:, kt, h * 128:(h + 1) * 128], P[:, j * 128:(j + 1) * 128],
                                                           start=(kt == 0), stop=(kt == qt))
                                        return ins
                                    sc.op("pe", pvfn, reads=[Pk] + [("v_sb", kt) for kt in grp], writes=[ndk])
                                    if last:
                                        rdn, rdk = rdring.next()
                                        sc.op("dve", lambda e: e.reciprocal(rdn[:], nd[64:128, 0:128]), reads=[ndk], writes=[rdk])
                                        yn, ynk = ynring.next()
                                        sc.op("dve", lambda e: e.tensor_tensor(yn[:], nd[0:64, 0:128], rdn[:], ALU.mult),
                                              reads=[ndk, rdk], writes=[ynk])
                                        sc.op("pool", lambda e: e.tensor_tensor(yo[:, h, :], yn[:], zt[:, h, :], ALU.mult),
                                              reads=[ynk, ztk], writes=[(yok, h)])
                                pend.append(emit_pv)
                                drain(3)
                        drain(0)
                        sc.dma("pool", ya_d[s].rearrange("(h d) t -> d h t", h=NH)[:, :, qsl], yo[:],
                               reads=[(yok, h) for h in range(NH)], writes=[("ya", s, qt), yok])

        if lvl >= 20 and (lvl < 40 or not moba):
            zt_ = sb("zstub", [128, 512], BF16)
            sc.op("dve", lambda e: e.memset(zt_[:], 0.0), writes=[("zstub",)])
            for s in range(NSEQ):
                for ti in range(NT):
                    for c in range(4):
                        if lvl < 40:
                            sc.dma("sp", yb_d[s, c * 128:(c + 1) * 128, ti * 512:(ti + 1) * 512], zt_[:], reads=[("zstub",)],
                                   writes=[("yb", s, ti)] if c == 3 else [("yb_part", s, ti, c)])
                        if not moba:
                            sc.dma("sp", ya_d[s, c * 128:(c + 1) * 128, ti * 512:(ti + 1) * 512], zt_[:], reads=[("zstub",)],
                                   writes=[("ya", s, ti * 4 + c)])

        sc.barrier()
        if lvl >= 40:
            es4 = ExitStack()
            with es4:
                def sb4(name, shape, dt=F32):
                    return es4.enter_context(nc.sbuf_tensor(name, list(shape), dt))
                psW = Ring(psb, "psb")
                NC_ = S // 64
                NG = 4
                HS = [64, NG, 64]
                vs2 = sb4("vs2", [64, 64])
                sc.op("dve", lambda e: e.memset(vs2[:], 0.0), writes=[("vs2",)])
                VO2 = {}
                for i, (nm, dv) in enumerate((("w0", w0_d), ("a0", a0_d), ("k_k", k_k_d), ("k_a", k_a_d), ("r_k", r_k_d))):
                    VO2[nm] = i * 8
                    sc.dma("sp", vs2[i * 8:(i + 1) * 8, :], dv.rearrange("(h d) -> h d", d=64), reads=[("vs2",)], writes=[("vs2", nm)])
                pt, pk = psW.next()
                sc.op("pe", lambda e, pt=pt: e.transpose(pt[0:64, 0:64], vs2[:], ident_f[0:64, 0:64]),
                      reads=[("vs2",), ("ident_f",)] + [("vs2", nm) for nm in VO2], writes=[pk])
                vh = sb4("vh", [64, 64])
                sc.op("dve", lambda e, pt=pt: e.tensor_copy(vh[:], pt[0:64, 0:64]), reads=[pk], writes=[("vh",)])
                omk = sb4("omk", [64, NH])
                sc.op("dve", lambda e: e.tensor_scalar(omk[:], vh[:, VO2["k_a"]:VO2["k_a"] + 8], -1.0, 1.0, ALU.mult, ALU.add),
                      reads=[("vh",)], writes=[("omk",)])

                def vb(nm):
                    o = VO2[nm] + CUR["hg"] * NG
                    return vh[:, o:o + NG].rearrange("p (h o) -> p h o", o=1).to_broadcast(HS)
                wup = sb4("wup", [64, B_W])
                aup = sb4("aup", [64, B_W])
                sc.dma("sp", wup[:], w_up_d[:, :], writes=[("wup0",)])
                sc.dma("sp", aup[:], a_up_d[:, :], writes=[("aup0",)])
                wup_r = sb4("wup_r", [64, B_W], F32R)
                aup_r = sb4("aup_r", [64, B_W], F32R)
                sc.op("dve", lambda e: e.tensor_copy(wup_r[:], wup[:]), reads=[("wup0",)], writes=[("wup",)])
                sc.op("dve", lambda e: e.tensor_copy(aup_r[:], aup[:]), reads=[("aup0",)], writes=[("aup",)])
                lnw = sb4("lnw", [64, B_W])
                lnb = sb4("lnb", [64, B_W])
                sc.dma("sp", lnw[:], lnw_d.partition_broadcast(64), writes=[("lnw",)])
                sc.dma("sp", lnb[:], lnb_d.partition_broadcast(64), writes=[("lnb",)])
                tri = sb4("tri", [64, 3, 64])
                sc.dma("sp", tri[:].rearrange("p a b -> p (a b)"), c_tri_d[:, :], writes=[("tri",)])
                ones64 = sb4("ones64", [64, 64], F32R)
                ones64f = sb4("ones64f", [64, 2])
                sc.op("dve", lambda e: e.memset(ones64f[:], 1.0), writes=[("ones64f",)])
                ones_t = sb4("ones_t", [64, 64])
                sc.op("dve", lambda e: e.memset(ones_t[:], 1.0), writes=[("ones_t",)])
                sc.op("dve", lambda e: e.tensor_copy(ones64[:], ones_t[:]), reads=[("ones_t",)], writes=[("ones64",)])
                zeros_t = sb4("zeros_t", [64, 4, 64])
                sc.op("dve", lambda e: e.memset(zeros_t[:], 0.0), writes=[("zeros_t",)])
                smask = sb4("smask", HS)
                sc.op("dve", lambda e: e.memset(smask[:], 1.0), writes=[("smask",)])
                sc.op("dve", lambda e: e.memset(smask[:, :, 0:1], 0.0), reads=[("smask",)], writes=[("smask", 1)])
                identb8 = ident_f[0:64, 0:64].rearrange("p (o d) -> p o d", o=1).to_broadcast(HS)

                def trib(i):
                    return tri[:, i:i + 1, :].to_broadcast(HS)

                T_ = {}
                CUR = {"set": 0, "list": None, "hg": 0}

                class Defer:
                    def op(self, eng, fn, reads=(), writes=()):
                        CUR["list"].append(("op", eng, fn, list(reads), list(writes), {}))

                    def dma(self, eng, out, in_, reads=(), writes=(), **kw):
                        CUR["list"].append(("dma", eng, (out, in_), list(reads), list(writes), kw))
                cur = Defer()

                RNAMES = {"tw", "ad_r", "sq", "At", "Bt", "Kt", "Rt", "tm_V", "tm_bc", "tm_kc", "X", "Q0", "Q1", "P0", "P1",
                          "AakT", "ArbT", "ArkT", "Mc", "Rh"}

                def tile(name, shape=None):
                    nm = "r%d_%s" % (CUR["set"], name)
                    if nm not in T_:
                        T_[nm] = sb4(nm, shape or HS, F32R if name in RNAMES else F32)
                    return T_[nm], (nm,)
                HstA = [[sb4("Hst%d_%d" % (q, i), HS, F32R) for i in range(2)] for q in range(4)]

                def ew(eng, fn, reads, writes):
                    cur.op(eng, fn, reads=reads, writes=writes)

                def headmm(out_fn, l_fn, r_fn, reads, pk, extra=None):
                    items = []
                    for h in range(NG):
                        pairs = [(l_fn(h), r_fn(h))] + ([(a(h), b(h)) for a, b in extra] if extra else [])
                        for i, (l_, r_) in enumerate(pairs):
                            items.append((out_fn(h), l_, r_, i == 0, i == len(pairs) - 1))

                    def fn(e, items=items):
                        ins = None
                        for (o_, l_, r_, st, sp) in items:
                            ins = e.matmul(o_, l_, r_, start=st, stop=sp)
                        return ins
                    cur.op("pe", fn, reads=reads, writes=[pk])

                def flat(t):
                    return t[:].rearrange("p h t -> p (h t)")

                for s in range(NSEQ):
                    for hg in range(2):
                        sc.op("dve", lambda e, q=(s % 2) * 2 + hg: e.tensor_copy(HstA[q][0][:], zeros_t[:]), reads=[("zeros_t",)], writes=[("Hst", (s % 2) * 2 + hg, 0)])

                psSets = [Ring(psb[2 * q:2 * q + 2], "psb", keys=[("psb", j) for j in range(2 * q, 2 * q + 2)]) for q in range(4)]

                def body(s, ci, hg):
                    chain = (s % 2) * 2 + hg
                    psW = psSets[chain]
                    G0 = hg * NG
                    VB = {nm: vb(nm) for nm in VO2}
                    if True:
                        Hst = HstA[chain]
                        csl = slice(ci * 64, (ci + 1) * 64)
                        ti = ci // 8
                        hcur, hck = Hst[ci % 2], ("Hst", chain, ci % 2)
                        hnxt, hnk = Hst[(ci + 1) % 2], ("Hst", chain, (ci + 1) % 2)
                        fm = {}
                        for qi, nm in enumerate(("r", "k", "v", "z")):
                            t, tk = tile("in_" + nm)
                            cur.dma("sp", t[:], rw_d[s, qi * 512 + G0 * 64:qi * 512 + (G0 + NG) * 64, csl].rearrange("(h d) t -> d h t", h=NG),
                                   reads=[("rw", s, qi * 4 + j, ti) for j in range(4)], writes=[tk])
                            fm[nm] = (t, tk)
                        wd, wdk = tile("wd", [64, 64])
                        ad, adk = tile("ad", [64, 64])
                        cur.dma("sp", wd[:], rw_d[s, 2048:2112, csl], reads=[("rw", s, 16, ti)], writes=[wdk])
                        cur.dma("sp", ad[:], rw_d[s, 2112:2176, csl], reads=[("rw", s, 16, ti)], writes=[adk])
                        r_, rk_ = fm["r"]; k_, kk_ = fm["k"]; v_, vk_ = fm["v"]; z_, zk_ = fm["z"]
                        tw, twk = tile("tw", [64, 64])
                        adr, adrk = tile("ad_r", [64, 64])
                        ew("act", lambda e: e.copy(adr[:], ad[:]), [adk], [adrk])
                        ew("act", lambda e: e.activation(tw[:], wd[:], AF.Tanh), [wdk], [twk])
                        pW, pWk = psW.next()
                        headmm(lambda h: pW[0:64, h * 64:(h + 1) * 64], lambda h: wup_r[:, (G0 + h) * 64:(G0 + h + 1) * 64], lambda h: tw[:],
                               [twk, ("wup",)], pWk)
                        pA, pAk = psW.next()
                        headmm(lambda h: pA[0:64, h * 64:(h + 1) * 64], lambda h: aup_r[:, (G0 + h) * 64:(G0 + h + 1) * 64], lambda h: adr[:],
                               [adrk, ("aup",)], pAk)
                        pv3 = lambda p: p[0:64, 0:NG * 64].rearrange("p (h t) -> p h t", h=NG)
                        PW = NG * 64
                        lw, lwk = tile("lw")
                        ew("dve", lambda e, pW=pW: e.tensor_tensor(lw[:], pv3(pW), VB["w0"], ALU.add), [pWk, ("vh",)], [lwk])
                        ew("act", lambda e: e.activation(flat(lw), flat(lw), AF.Sigmoid), [lwk], [lwk])
                        av, avk = tile("av")
                        ew("dve", lambda e, pA=pA: e.tensor_tensor(av[:], pv3(pA), VB["a0"], ALU.add), [pAk, ("vh",)], [avk])
                        ew("act", lambda e: e.activation(flat(av), flat(av), AF.Sigmoid), [avk], [avk])
                        kr, krk = tile("kr")
                        ew("dve", lambda e: e.tensor_tensor(kr[:], k_[:], VB["k_k"], ALU.mult), [kk_, ("vh",)], [krk])
                        sq, sqk = tile("sq")
                        ew("dve", lambda e: e.tensor_tensor(flat(sq), flat(kr), flat(kr), ALU.mult), [krk], [sqk])
                        pS, pSk = psW.next()
                        cur.op("pe", lambda e, pS=pS: e.matmul(pS[0:64, 0:PW], ones64[:], flat(sq), start=True, stop=True),
                              reads=[sqk, ("ones64",)], writes=[pSk])
                        rn, rnk = tile("rn")
                        ew("act", lambda e, pS=pS: e.activation(flat(rn), pS[0:64, 0:PW], AF.Sqrt, bias=1e-24, scale=1.0), [pSk], [rnk])
                        ew("dve", lambda e: e.reciprocal(flat(rn), flat(rn)), [rnk], [rnk])
                        kkn, kknk = kr, krk
                        ew("dve", lambda e: e.tensor_tensor(flat(kkn), flat(kr), flat(rn), ALU.mult), [krk, rnk], [kknk])
                        k2, k2k = tile("k2")
                        ew("dve", lambda e: e.tensor_tensor(k2[:], av[:], VB["k_a"], ALU.mult), [avk, ("vh",)], [k2k])
                        ew("dve", lambda e: e.tensor_tensor(k2[:], k2[:], omk[:, G0:G0 + NG].rearrange("p (h o) -> p h o", o=1).to_broadcast(HS), ALU.add),
                           [k2k, ("omk",)], [k2k])
                        ew("dve", lambda e: e.tensor_tensor(flat(k2), flat(k2), flat(k_), ALU.mult), [k2k, kk_], [k2k])
                        bv, bvk = tile("bv")
                        ew("pool", lambda e: e.tensor_tensor(flat(bv), flat(kkn), flat(av), ALU.mult), [kknk, avk], [bvk])
                        cs, csk = tile("cs")
                        ew("dve", lambda e: e.tensor_tensor_scan(flat(cs), flat(smask), flat(lw), 0.0, ALU.mult, ALU.add),
                           [lwk, ("smask",), ("smask", 1)], [csk])
                        ecs, ecsk = tile("ecs")
                        ew("act", lambda e: e.activation(flat(ecs), flat(cs), AF.Exp, scale=-math.exp(-0.5)), [csk], [ecsk])
                        csx, csxk = lw, lwk
                        ew("dve", lambda e: e.tensor_tensor(flat(csx), flat(cs), flat(lw), ALU.subtract), [csk, lwk], [csxk])
                        ew("act", lambda e: e.activation(flat(csx), flat(csx), AF.Exp, scale=-math.exp(-0.5)), [csxk], [csxk])
                        encs, encsk = cs, csk
                        ew("act", lambda e: e.activation(flat(encs), flat(cs), AF.Exp, scale=math.exp(-0.5)), [csk], [encsk])
                        dte, dtek = tile("dte")
                        ew("dve", lambda e: e.tensor_tensor(dte[:], encs[:], ecs[:, :, 63:64].to_broadcast(HS), ALU.mult), [encsk, ecsk], [dtek])
                        At, Atk = tile("At")
                        ew("dve", lambda e: e.scalar_tensor_tensor(flat(At), flat(kkn), -1.0, flat(csx), ALU.mult, ALU.mult), [kknk, csxk], [Atk])
                        Bt, Btk = tile("Bt")
                        ew("dve", lambda e: e.tensor_tensor(flat(Bt), flat(bv), flat(encs), ALU.mult), [bvk, encsk], [Btk])
                        Kt, Ktk = tile("Kt")
                        ew("dve", lambda e: e.tensor_tensor(flat(Kt), flat(k2), flat(encs), ALU.mult), [k2k, encsk], [Ktk])
                        Rt, Rtk = tile("Rt")
                        ew("dve", lambda e: e.tensor_tensor(flat(Rt), flat(r_), flat(ecs), ALU.mult), [rk_, ecsk], [Rtk])
                        bc, bck = bv, bvk
                        ew("pool", lambda e: e.tensor_tensor(flat(bc), flat(bv), flat(dte), ALU.mult), [bvk, dtek], [bck])
                        kc, kck = tile("kc")
                        ew("pool", lambda e: e.tensor_tensor(flat(kc), flat(k2), flat(dte), ALU.mult), [k2k, dtek], [kck])
                        tm = {}
                        for nm, (src, srck) in (("V", (v_, vk_)), ("bc", (bc, bck)), ("kc", (kc, kck)), ("At", (At, Atk))):
                            pT_, pTk_ = psW.next()

                            def trf(e, pT_=pT_, src=src):
                                ins = None
                                for h in range(NG):
                                    ins = e.transpose(pT_[0:64, h * 64:(h + 1) * 64], src[:, h, :].bitcast(F32), ident_f[0:64, 0:64])
                                return ins
                            cur.op("pe", trf, reads=[srck, ("ident_f",)], writes=[pTk_])
                            if nm == "At":
                                X, Xk = tile("X", [64, NG, 128])
                                ew("act", lambda e, pT_=pT_: e.copy(X[:, :, 64:128], pv3(pT_)), [pTk_], [(Xk, 1)])
                            else:
                                d, dk = tile("tm_" + nm)
                                ew("act", lambda e, pT_=pT_, d=d: e.copy(flat(d), pT_[0:64, 0:PW]), [pTk_], [dk])
                                tm[nm] = (d, dk)
                        Vt, Vtk = tm["V"]; bct, bctk = tm["bc"]; kct, kctk = tm["kc"]
                        def mm_mask(name, L, Lk, Rr, Rk, mi):
                            p_, pk_ = psW.next()
                            headmm(lambda h: p_[0:64, h * 64:(h + 1) * 64], lambda h: L[:, h, :], lambda h: Rr[:, h, :], [Lk, Rk], pk_)
                            d, dk = tile(name)
                            ew("dve", lambda e, p_=p_, d=d: e.tensor_tensor(d[:], pv3(p_), trib(mi), ALU.mult), [pk_, ("tri",)], [dk])
                            return d, dk
                        Q, Qk = mm_mask("Q0", Bt, Btk, At, Atk, 0)
                        Pm, Pmk = mm_mask("P0", At, Atk, Bt, Btk, 2)
                        AakT, AakTk = mm_mask("AakT", Kt, Ktk, At, Atk, 0)
                        ArbT, ArbTk = mm_mask("ArbT", Bt, Btk, Rt, Rtk, 1)
                        ArkT, ArkTk = mm_mask("ArkT", Kt, Ktk, Rt, Rtk, 1)
                        pX, pXk = psW.next()
                        headmm(lambda h: pX[0:64, h * 64:(h + 1) * 64], lambda h: AakT[:, h, :], lambda h: Vt[:, h, :], [AakTk, Vtk], pXk)
                        ew("act", lambda e, pX=pX: e.copy(X[:, :, 0:64], pv3(pX)), [pXk], [(Xk, 0)])
                        Xkeys = [(Xk, 0), (Xk, 1)]
                        for lv in range(6):
                            pa_, pak_ = psW.next()

                            def apf(e, pa_=pa_, Q=Q):
                                ins = None
                                for h in range(NG):
                                    ins = e.matmul(pa_[0:64, h * 128:(h + 1) * 128], Q[:, h, :], X[:, h, :], start=True, stop=True)
                                return ins
                            cur.op("pe", apf, reads=[Qk] + Xkeys, writes=[pak_])
                            ew("dve", lambda e, pa_=pa_: e.tensor_tensor(X[:], X[:], pa_[0:64, 0:NG * 128].rearrange("p (h t) -> p h t", h=NG), ALU.add),
                               [pak_] + Xkeys, Xkeys)
                            if lv < 5:
                                pq_, pqk_ = psW.next()
                                headmm(lambda h, pq_=pq_: pq_[0:64, h * 64:(h + 1) * 64], lambda h, Pm=Pm: Pm[:, h, :], lambda h, Q=Q: Q[:, h, :], [Pmk, Qk], pqk_)
                                Q2, Q2k = tile("Q%d" % ((lv + 1) % 2))
                                if lv < 4:
                                    pp_, ppk_ = psW.next()
                                    headmm(lambda h, pp_=pp_: pp_[0:64, h * 64:(h + 1) * 64], lambda h, Q=Q: Q[:, h, :], lambda h, Pm=Pm: Pm[:, h, :], [Pmk, Qk], ppk_)
                                    P2, P2k = tile("P%d" % ((lv + 1) % 2))
                                    ew("act", lambda e, pp_=pp_, P2=P2: e.copy(flat(P2), pp_[0:64, 0:PW]), [ppk_], [P2k])
                                ew("act", lambda e, pq_=pq_, Q2=Q2: e.copy(flat(Q2), pq_[0:64, 0:PW]), [pqk_], [Q2k])
                                Q, Qk = Q2, Q2k
                                if lv < 4:
                                    Pm, Pmk = P2, P2k
                        U0 = lambda h: X[:, h, 0:64]
                        Ah = lambda h: X[:, h, 64:128]
                        pM, pMk = psW.next()
                        headmm(lambda h: pM[0:64, h * 64:(h + 1) * 64], Ah, lambda h: bct[:, h, :], Xkeys + [bctk], pMk)
                        Mc, Mck = tile("Mc")
                        ew("dve", lambda e: e.tensor_tensor(Mc[:], identb8, ecs[:, :, 63:64].to_broadcast(HS), ALU.mult), [ecsk, ("ident_f",)], [Mck])
                        ew("dve", lambda e, pM=pM: e.tensor_tensor(Mc[:], Mc[:], pv3(pM), ALU.add), [pMk, Mck], [Mck])
                        pG, pGk = psW.next()
                        headmm(lambda h: pG[0:64, h * 64:(h + 1) * 64], lambda h: bct[:, h, :], U0, Xkeys + [bctk, kctk, Vtk], pGk,
                               extra=[(lambda h: kct[:, h, :], lambda h: Vt[:, h, :])])
                        G, Gk = tile("G")
                        ew("act", lambda e, pG=pG: e.copy(flat(G), pG[0:64, 0:PW]), [pGk], [Gk])
                        pR, pRk = psW.next()
                        headmm(lambda h: pR[0:64, h * 64:(h + 1) * 64], Ah, lambda h: ArbT[:, h, :], Xkeys + [ArbTk], pRk)
                        Rh, Rhk = tile("Rh")
                        ew("dve", lambda e, pR=pR: e.tensor_tensor(Rh[:], Rt[:], pv3(pR), ALU.add), [pRk, Rtk], [Rhk])
                        pO, pOk = psW.next()
                        headmm(lambda h: pO[0:64, h * 64:(h + 1) * 64], lambda h: ArbT[:, h, :], U0, Xkeys + [ArbTk, ArkTk, Vtk, Rhk, hck], pOk,
                               extra=[(lambda h: ArkT[:, h, :], lambda h: Vt[:, h, :]), (lambda h: Rh[:, h, :], lambda h, hcur=hcur: hcur[:, h, :])])
                        pH, pHk = psW.next()
                        headmm(lambda h: pH[0:64, h * 64:(h + 1) * 64], lambda h: Mc[:, h, :], lambda h, hcur=hcur: hcur[:, h, :], [Mck, hck], pHk)
                        ew("dve", lambda e, pH=pH, hnxt=hnxt: e.tensor_tensor(hnxt[:], G[:], pv3(pH), ALU.add), [pHk, Gk], [hnk])
                        Ot, Otk = tile("Ot")
                        ew("act", lambda e, pO=pO: e.copy(flat(Ot), pO[0:64, 0:PW]), [pOk], [Otk])
                        st1, st1k = tile("st1", [64, NG])
                        st2, st2k = tile("st2", [64, NG])
                        junk, junkk = tile("junk", [64, 64])
                        for h in range(NG):
                            ew("act", lambda e, h=h: e.activation(junk[:], Ot[:, h, :], AF.Copy, accum_out=st1[:, h:h + 1]), [Otk], [junkk, (st1k, h)])
                            ew("act", lambda e, h=h: e.activation(junk[:], Ot[:, h, :], AF.Square, accum_out=st2[:, h:h + 1]), [Otk], [junkk, (st2k, h)])
                        st1a = [(st1k, h) for h in range(NG)]
                        st2a = [(st2k, h) for h in range(NG)]
                        ew("dve", lambda e: e.tensor_scalar(st1[:], st1[:], 1.0 / 64, None, ALU.mult), st1a, st1a)
                        msq, msqk = tile("msq", [64, NG])
                        ew("dve", lambda e: e.tensor_tensor(msq[:], st1[:], st1[:], ALU.mult), st1a, [msqk])
                        ew("dve", lambda e: e.scalar_tensor_tensor(st2[:], st2[:], 1.0 / 64, msq[:], ALU.mult, ALU.subtract), st2a + [msqk], st2a)
                        ew("act", lambda e: e.activation(st2[:], st2[:], AF.Sqrt, bias=float(GN_EPS), scale=1.0), st2a, st2a)
                        ew("dve", lambda e: e.reciprocal(st2[:], st2[:]), st2a, st2a)
                        b3 = lambda t: t[:].rearrange("p (h o) -> p h o", o=1).to_broadcast(HS)
                        ew("dve", lambda e: e.tensor_tensor(Ot[:], Ot[:], b3(st1), ALU.subtract), [Otk] + st1a, [Otk])
                        ew("dve", lambda e: e.tensor_tensor(Ot[:], Ot[:], b3(st2), ALU.mult), [Otk] + st2a, [Otk])
                        ew("dve", lambda e: e.tensor_tensor(flat(Ot), flat(Ot), lnw[:, G0 * 64:(G0 + NG) * 64], ALU.mult), [Otk, ("lnw",)], [Otk])
                        ew("dve", lambda e: e.tensor_tensor(flat(Ot), flat(Ot), lnb[:, G0 * 64:(G0 + NG) * 64], ALU.add), [Otk, ("lnb",)], [Otk])
                        rk3, rk3k = tile("rk3")
                        ew("pool", lambda e: e.tensor_tensor(flat(rk3), flat(r_), flat(k2), ALU.mult), [rk_, k2k], [rk3k])
                        ew("dve", lambda e: e.tensor_tensor(rk3[:], rk3[:], VB["r_k"], ALU.mult), [rk3k, ("vh",)], [rk3k])
                        pBn, pBnk = psW.next()
                        headmm(lambda h: pBn[0:64, h:h + 1], lambda h: rk3[:, h, :], lambda h: ones64f[:, 0:1], [rk3k, ("ones64f",)], pBnk)
                        sbn, sbnk = tile("sbn", [64, NG])
                        ew("dve", lambda e, pBn=pBn: e.tensor_copy(sbn[:], pBn[0:64, 0:NG]), [pBnk], [sbnk])
                        bon, bonk = tile("bon")
                        ew("dve", lambda e: e.tensor_tensor(bon[:], Vt[:], b3(sbn), ALU.mult), [Vtk, sbnk], [bonk])
                        ew("dve", lambda e: e.tensor_tensor(flat(Ot), flat(Ot), flat(bon), ALU.add), [Otk, bonk], [Otk])
                        pY, pYk = psW.next()

                        def tyf(e, pY=pY):
                            ins = None
                            for h in range(NG):
                                ins = e.transpose(pY[0:64, h * 64:(h + 1) * 64], Ot[:, h, :], ident_f[0:64, 0:64])
                            return ins
                        cur.op("pe", tyf, reads=[Otk, ("ident_f",)], writes=[pYk])
                        zs, zsk = tile("zs")
                        ew("act", lambda e: e.activation(flat(zs), flat(z_), AF.Silu), [zk_], [zsk])
                        ybn = "ybb%d" % CUR["set"]
                        if ybn not in T_:
                            T_[ybn] = es4.enter_context(nc.sbuf_tensor(ybn, [64, NG, 64], BF16))
                        ybb, ybbk = T_[ybn], (ybn,)
                        ew("dve", lambda e, pY=pY: e.tensor_tensor(ybb[:], zs[:], pv3(pY), ALU.mult), [pYk, zsk], [ybbk])
                        cur.dma("pool", yb_d[s, G0 * 64:(G0 + NG) * 64, :].rearrange("(h d) t -> d h t", h=NG)[:, :, csl], ybb[:], reads=[ybbk],
                               writes=[("yb_c", s, ci, hg)] + ([("yb", s, ti, hg)] if ci % 8 == 7 else []))


                def flush(lists):
                    n = max(len(l) for l in lists)
                    for i in range(n):
                        for l in lists:
                            if i < len(l):
                                kind, eng, a_, rd, wr, kw = l[i]
                                if kind == "op":
                                    sc.op(eng, a_, reads=rd, writes=wr)
                                else:
                                    sc.dma(eng, a_[0], a_[1], reads=rd, writes=wr, **kw)

                for s0 in range(0, NSEQ, 2):
                    for ci in range(NC_):
                        lists = []
                        for s in range(s0, min(s0 + 2, NSEQ)):
                            for hg in range(2):
                                CUR["set"] = (s % 2) * 2 + hg
                                CUR["hg"] = hg
                                CUR["list"] = []
                                body(s, ci, hg)
                                lists.append(CUR["list"])
                        flush(lists)

        sc.barrier()
        if lvl >= 30:
            es3 = ExitStack()
            with es3:
                def sb3(name, shape, dt=F32):
                    return es3.enter_context(nc.sbuf_tensor(name, list(shape), dt))
                psR = Ring(psb, "psb")
                wstg = [sb3("wstg%d" % i, [128, D]) for i in range(2)]
                wsr = Ring(wstg, "wstg")

                def load_w(name, dram, nk):
                    t = sb3(name, [128, nk, D], BF16)
                    for kc in range(nk):
                        st, stk = wsr.next()
                        sc.dma("sp", st[:], dram[kc * 128:(kc + 1) * 128, :], writes=[stk])
                        sc.op(("dve", "pool")[kc % 2], lambda e, st=st, kc=kc, t=t: e.tensor_copy(t[:, kc, :], st[:]),
                              reads=[stk], writes=[(name, kc)])
                    return t, [(name, kc) for kc in range(nk)]
                pa_sb, pa_k = load_w("pa_sb", p_a_d, 4)
                pb_sb, pb_k = load_w("pb_sb", p_b_d, 4)
                wo_sb, wo_k = load_w("wo_sb", w_out_d, 8)
                wg_sb, wg_k = load_w("wg_sb", w_pg_d, 8)
                wu_sb, wu_k = load_w("wu_sb", w_pu_d, 2)
                gpost = sb3("gpost", [128, D])
                sc.dma("sp", gpost[:], g_post_d.partition_broadcast(128), writes=[("gpost",)])
                yaT = [sb3("yaT%d" % i, [128, 4, 512], BF16) for i in range(2)]
                ybT = [sb3("ybT%d" % i, [128, 4, 512], BF16) for i in range(2)]
                gtT = [sb3("gtT%d" % i, [128, 16, 512], BF16) for i in range(2)]
                yar, ybr, gtr = Ring(yaT, "yaT"), Ring(ybT, "ybT"), Ring(gtT, "gtT")
                t1b = [sb3("t1b%d" % i, [128, 512]) for i in range(2)]
                t1r = Ring(t1b, "t1b")
                mgT = [sb3("mgT%d" % i, [128, 8, 512], BF16) for i in range(2)]
                mgr = Ring(mgT, "mgT")
                x3 = [sb3("x3_%d" % i, [128, D]) for i in range(2)]
                x3r = Ring(x3, "x3")
                p3 = [sb3("p3_%d" % i, [128, PLE]) for i in range(2)]
                p3r = Ring(p3, "p3")
                p3b = [sb3("p3b_%d" % i, [128, PLE], BF16) for i in range(2)]
                p3br = Ring(p3b, "p3b")
                ysb = [sb3("ysb%d" % i, [128, D]) for i in range(2)]
                ysr = Ring(ysb, "ysb")
                sq3 = sb3("sq3", [128, D], BF16)
                st3 = [sb3("st3_%d" % i, [128, 2]) for i in range(2)]
                st3r = Ring(st3, "st3")
                hsb = [sb3("hsb%d" % i, [128, D]) for i in range(2)]
                hsr = Ring(hsb, "hsb")
                hbb = [sb3("hbb%d" % i, [128, D], BF16) for i in range(2)]
                hbr = Ring(hbb, "hbb")
                hT = [sb3("hT%d" % i, [128, 8, 128], BF16) for i in range(2)]
                hTr = Ring(hT, "hT")
                pT = [sb3("pT%d" % i, [128, 2, 128], BF16) for i in range(2)]
                pTr = Ring(pT, "pT")
                sg = [sb3("sg%d" % i, [128, D]) for i in range(2)]
                sgr = Ring(sg, "sg")
                osb = [sb3("osb%d" % i, [128, D]) for i in range(2)]
                osr = Ring(osb, "osb")

                for s in range(NSEQ):
                    for ti in range(NT):
                        tsl = slice(ti * 512, (ti + 1) * 512)
                        ya, yak = yar.next()
                        yb, ybk = ybr.next()
                        gt, gtk = gtr.next()
                        sc.dma("sp", ya[:], ya_d[s, :, tsl].rearrange("(c p) t -> p c t", p=128),
                               reads=[("ya", s, qt) for qt in range(ti * 4, ti * 4 + 4)], writes=[yak])
                        sc.dma("sp", yb[:], yb_d[s, :, tsl].rearrange("(c p) t -> p c t", p=128),
                               reads=[("yb", s, ti), ("yb", s, ti, 0), ("yb", s, ti, 1)], writes=[ybk])
                        sc.dma("sp", gt[:], gt_d[s, :, tsl].rearrange("(c p) t -> p c t", p=128),
                               reads=[("gt", s, j, ti) for j in range(16)], writes=[gtk])
                        mg, mgk = mgr.next()
                        for m in range(8):
                            pA, pAk = psR.next()
                            pB, pBk = psR.next()

                            def abfn(e, pA=pA, pB=pB, m=m, ya=ya, yb=yb):
                                ins = None
                                for c in range(4):
                                    ins = e.matmul(pA[:], pa_sb[:, c, m * 128:(m + 1) * 128], ya[:, c, :], start=(c == 0), stop=(c == 3))
                                for c in range(4):
                                    ins = e.matmul(pB[:], pb_sb[:, c, m * 128:(m + 1) * 128], yb[:, c, :], start=(c == 0), stop=(c == 3))
                                return ins
                            sc.op("pe", abfn, reads=[yak, ybk] + pa_k + pb_k, writes=[pAk, pBk])
                            t1, t1k = t1r.next()
                            sc.op("dve", lambda e, t1=t1, pA=pA, gt=gt, m=m: e.tensor_tensor(t1[:], pA[:], gt[:, m, :], ALU.mult),
                                  reads=[pAk, gtk], writes=[t1k])
                            t2, t2k = t1r.next()
                            sc.op("dve", lambda e, t2=t2, pB=pB, gt=gt, m=m: e.tensor_tensor(t2[:], pB[:], gt[:, 8 + m, :], ALU.mult),
                                  reads=[pBk, gtk], writes=[t2k])
                            sc.op("dve", lambda e, mg=mg, t1=t1, t2=t2, m=m: e.tensor_tensor(mg[:, m, :], t1[:], t2[:], ALU.add),
                                  reads=[t1k, t2k], writes=[(mgk, m)])
                        mg_keys = [(mgk, m) for m in range(8)]
                        for a in range(4):
                            tok0 = s * S + ti * 512 + a * 128
                            xt3, x3k = x3r.next()
                            sc.dma("sp", xt3[:], x_d[tok0:tok0 + 128, :], writes=[x3k])
                            pt3, p3k = p3r.next()
                            sc.dma("sp", pt3[:], p_d[tok0:tok0 + 128, :], writes=[p3k])
                            yps = []
                            for half in range(2):
                                pY, pYk = psR.next()

                                def yfn(e, pY=pY, half=half, mg=mg, a=a):
                                    ins = None
                                    for m in range(8):
                                        ins = e.matmul(pY[:], mg[:, m, a * 128:(a + 1) * 128], wo_sb[:, m, half * 512:(half + 1) * 512],
                                                       start=(m == 0), stop=(m == 7))
                                    return ins
                                sc.op("pe", yfn, reads=mg_keys + wo_k, writes=[pYk])
                                yps.append((pY, pYk))
                            ys, ysk = ysr.next()
                            stt, sttk = st3r.next()
                            for half in range(2):
                                pY, pYk = yps[half]
                                sc.op("act", lambda e, ys=ys, pY=pY, half=half: e.copy(ys[:, half * 512:(half + 1) * 512], pY[:]),
                                      reads=[pYk], writes=[(ysk, half)])
                            sc.op("act", lambda e, ys=ys, stt=stt: e.activation(sq3[:], ys[:], AF.Square, accum_out=stt[:, 0:1]),
                                  reads=[(ysk, 0), (ysk, 1)], writes=[("sq3",), (sttk, 0)])
                            sc.op("act", lambda e, stt=stt: e.activation(stt[:, 0:1], stt[:, 0:1], AF.Sqrt, bias=float(RMS_EPS), scale=1.0 / D),
                                  reads=[(sttk, 0)], writes=[(sttk, 0)])
                            sc.op("dve", lambda e, stt=stt: e.reciprocal(stt[:, 1:2], stt[:, 0:1]), reads=[(sttk, 0)], writes=[(sttk, 1)])
                            hs, hsk = hsr.next()
                            sc.op("dve", lambda e, hs=hs, ys=ys, stt=stt: e.scalar_tensor_tensor(
                                hs[:], ys[:], stt[:, 1:2], gpost[:], ALU.mult, ALU.mult),
                                reads=[(ysk, 0), (ysk, 1), (sttk, 1), ("gpost",)], writes=[hsk])
                            sc.op("dve", lambda e, hs=hs, xt3=xt3: e.tensor_tensor(hs[:], hs[:], xt3[:], ALU.add),
                                  reads=[hsk, x3k], writes=[hsk])
                            hb, hbk = hbr.next()
                            sc.op("act", lambda e, hb=hb, hs=hs: e.copy(hb[:], hs[:]), reads=[hsk], writes=[hbk])
                            pb3, p3bk = p3br.next()
                            sc.op("act", lambda e, pb3=pb3, pt3=pt3: e.copy(pb3[:], pt3[:]), reads=[p3k], writes=[p3bk])
                            pTp, pTpk = psR.next()
                            ptb = pTp[:].bitcast(BF16)

                            def trfn(e, ptb=ptb, hb=hb):
                                ins = None
                                for m in range(8):
                                    ins = e.transpose(ptb[:, m * 128:(m + 1) * 128], hb[:, m * 128:(m + 1) * 128], ident_b[:])
                                return ins
                            sc.op("pe", trfn, reads=[hbk, ("ident_b",)], writes=[pTpk])
                            hTt, hTk = hTr.next()
                            sc.op("dve", lambda e, hTt=hTt, ptb=ptb: e.tensor_copy(hTt[:], ptb.rearrange("p (m t) -> p m t", m=8)),
                                  reads=[pTpk], writes=[hTk])
                            pP, pPk = psR.next()
                            ppb = pP[:].bitcast(BF16)

                            def trp(e, ppb=ppb, pb3=pb3):
                                ins = None
                                for j in range(2):
                                    ins = e.transpose(ppb[:, j * 128:(j + 1) * 128], pb3[:, j * 128:(j + 1) * 128], ident_b[:])
                                return ins
                            sc.op("pe", trp, reads=[p3bk, ("ident_b",)], writes=[pPk])
                            pTt, pTk = pTr.next()
                            sc.op("act", lambda e, pTt=pTt, ppb=ppb: e.copy(pTt[:], ppb[:, 0:256].rearrange("p (m t) -> p m t", m=2)),
                                  reads=[pPk], writes=[pTk])
                            sgt, sgk = sgr.next()
                            ot, otk = osr.next()
                            for half in range(2):
                                pG, pGk = psR.next()

                                def gfn3(e, pG=pG, half=half, hTt=hTt):
                                    ins = None
                                    for m in range(8):
                                        ins = e.matmul(pG[:], hTt[:, m, :], wg_sb[:, m, half * 512:(half + 1) * 512], start=(m == 0), stop=(m == 7))
                                    return ins
                                sc.op("pe", gfn3, reads=[hTk] + wg_k, writes=[pGk])
                                sc.op("act", lambda e, sgt=sgt, pG=pG, half=half: e.activation(sgt[:, half * 512:(half + 1) * 512], pG[:], AF.Sigmoid),
                                      reads=[pGk], writes=[(sgk, half)])
                                pE, pEk = psR.next()

                                def efn(e, pE=pE, half=half, pTt=pTt):
                                    ins = None
                                    for j in range(2):
                                        ins = e.matmul(pE[:], pTt[:, j, :], wu_sb[:, j, half * 512:(half + 1) * 512], start=(j == 0), stop=(j == 1))
                                    return ins
                                sc.op("pe", efn, reads=[pTk] + wu_k, writes=[pEk])
                                sc.op("dve", lambda e, sgt=sgt, pE=pE, half=half: e.tensor_tensor(
                                    sgt[:, half * 512:(half + 1) * 512], pE[:], sgt[:, half * 512:(half + 1) * 512], ALU.mult),
                                    reads=[pEk, (sgk, half)], writes=[(sgk, half)])
                            sc.op("dve", lambda e, ot=ot, sgt=sgt, hs=hs: e.tensor_tensor(ot[:], sgt[:], hs[:], ALU.add),
                                  reads=[(sgk, 0), (sgk, 1), hsk], writes=[otk])
                            sc.dma("pool", out_d[tok0:tok0 + 128, :], ot[:], reads=[otk], writes=[("out", tok0)], final=True)

        sc.emit(nc, es)
    return nc


def t5_bucket_np(n):
    n = np.maximum(n, 0)
    nf = np.maximum(n, 1).astype(np.float32)
    large = 16 + (np.log(nf / np.float32(16)) / np.float32(math.log(128 / 16)) * np.float32(16)).astype(np.int32)
    large = np.minimum(large, 31)
    return np.where(n < 16, n, large)


def make_consts():
    c = {}
    c["c_ident"] = np.eye(128, dtype=np.float32)
    oh = np.zeros((33, 512), np.float32)
    d = np.arange(512) - 128
    bk = t5_bucket_np(d)
    for j in range(512):
        if d[j] >= 0:
            oh[bk[j], j] = 1.0
        else:
            oh[32, j] = 1.0
    c["c_onehot"] = oh
    es_ = np.zeros((128, 16, 128), np.float32)
    for n in range(16):
        es_[n, n, :] = 1.0
        es_[64 + n, n, :] = 1.0
    c["c_esel"] = es_.reshape(128, 16 * 128)
    tri = np.zeros((64, 3, 64), np.float32)
    i = np.arange(64)
    tri[:, 0, :] = (i[:, None] < i[None, :])
    tri[:, 1, :] = (i[:, None] <= i[None, :])
    tri[:, 2, :] = (i[:, None] > i[None, :])
    c["c_tri"] = tri.reshape(64, 192)
    bd = np.zeros((128, 128), np.float32)
    bd[:64, :64] = 1.0
    bd[64:, 64:] = 1.0
    c["c_bd"] = bd
    c["c_J"] = np.ascontiguousarray(np.eye(128, dtype=np.float32)[::-1])
    pen = np.zeros((16, NH, 16), np.float32)
    for qb in range(16):
        pen[qb, :, qb:] = -30000.0
    c["c_pen"] = pen.reshape(-1)
    return c


_WNAMES = ["g_pre", "w_in", "mu_shift", "w0", "w_up", "a0", "a_up", "k_k", "k_a", "r_k", "ln_x_w", "ln_x_b",
           "p_a", "p_b", "w_out", "g_post", "w_ple_up", "w_ple_gate"]


def make_in_maps(inputs, ncores, nseq, S):
    consts = make_consts()
    maps = []
    for c in range(ncores):
        m = dict(consts)
        m["x"] = np.ascontiguousarray(inputs["x"][c * nseq:(c + 1) * nseq].reshape(nseq * S, D))
        m["p"] = np.ascontiguousarray(inputs["p"][0, c * nseq:(c + 1) * nseq].reshape(nseq * S, PLE))
        m["rel_bias"] = np.ascontiguousarray(inputs["rel_bias"])
        for n in _WNAMES:
            a = np.asarray(inputs[n])[0]
            m[n] = np.ascontiguousarray(a.reshape(-1) if n == "r_k" else a)
        maps.append(m)
    return maps


def kernel(**inputs):
    inputs = {k: np.asarray(v) for k, v in inputs.items()}
    B, S, _ = inputs["x"].shape
    nseq = B // NCORES
    nc = build_program(S, nseq, lvl=99, moba=True)
    maps = make_in_maps(inputs, NCORES, nseq, S)
    res = run_bass_kernel_spmd(nc, maps, core_ids=list(range(NCORES)))
    outs = [r["out"].reshape(nseq, S, D) for r in res.results]
    return np.concatenate(outs, axis=0).astype(np.float32)
```

```python
import math
from contextlib import ExitStack

import numpy as np
import concourse.bass as bass
import concourse.mybir as mybir
from concourse.bass_utils import run_bass_kernel_spmd

F32 = mybir.dt.float32
BF16 = mybir.dt.bfloat16
F32R = mybir.dt.float32r
ALU = mybir.AluOpType
AF = mybir.ActivationFunctionType
AX = mybir.AxisListType

D = 1024
NCORES = 8
A_W = 512
B_W = 512
HD = 64
NH = 8
IN_COLS = 6272
RW_COLS = 2176
PLE = 256
BLK = 256
RMS_EPS = 1e-6
GN_EPS = 64e-5
NEG = -30000.0


class Sched:
    ENGS = ("pe", "act", "dve", "pool", "sp")
    NDMA = 8

    def __init__(self):
        self.streams = {e: [] for e in self.ENGS}
        self.waited = {e: {} for e in self.ENGS}
        self.last_w = {}
        self.readers = {}
        self.dma_cnt = {e: 0 for e in self.ENGS}
        self.dma_val = {}
        self.final_dma = []

    def _add_wait(self, eng, waits, ev):
        if ev is None:
            return
        if ev[0] == "eng":
            _, e2, j = ev
            if e2 == "pe" and eng == "pe":
                return
            key = e2
            val = j
        else:
            _, key, val = ev
        if self.waited[eng].get(key, -1) >= val:
            return
        self.waited[eng][key] = val
        waits.append(ev)
        if ev[0] == "eng":
            self.streams[ev[1]][ev[2]]["sig"] = True

    def _deps(self, eng, reads, writes):
        evs = []
        for k in reads:
            evs.append(self.last_w.get(k))
        for k in writes:
            evs.append(self.last_w.get(k))
            evs.extend(self.readers.get(k, ()))
        best = {}
        for ev in evs:
            if ev is None:
                continue
            key = ev[1]
            if key not in best or ev[2] > best[key][2]:
                best[key] = ev
        waits = []
        for ev in best.values():
            self._add_wait(eng, waits, ev)
        return waits

    def _commit(self, ev, reads, writes):
        for k in reads:
            self.readers.setdefault(k, []).append(ev)
        for k in writes:
            self.last_w[k] = ev
            self.readers[k] = []

    def op(self, eng, fn, reads=(), writes=()):
        waits = self._deps(eng, reads, writes)
        idx = len(self.streams[eng])
        self.streams[eng].append({"fn": fn, "waits": waits, "sig": False, "dma": None})
        self._commit(("eng", eng, idx), reads, writes)

    def dma(self, eng, out, in_, reads=(), writes=(), final=False, **kw):
        slot = self.dma_cnt[eng] % self.NDMA
        self.dma_cnt[eng] += 1
        key = (eng, slot)
        prev = self.dma_val.get(key, 0)
        waits = self._deps(eng, reads, writes)
        if prev > 0:
            self._add_wait(eng, waits, ("dma", key, prev))
        val = prev + 16
        self.dma_val[key] = val
        fn = lambda e, out=out, in_=in_, kw=kw: e.dma_start(out=out, in_=in_, **kw)
        self.streams[eng].append({"fn": fn, "waits": waits, "sig": False, "dma": key})
        ev = ("dma", key, val)
        self._commit(ev, reads, writes)
        if final:
            self.final_dma.append(ev)

    def barrier(self):
        evs = []
        for e in ("pe", "act", "dve", "pool"):
            for j in range(len(self.streams[e]) - 1, -1, -1):
                if self.streams[e][j]["fn"] is not None and self.streams[e][j]["dma"] is None:
                    evs.append(("eng", e, j))
                    break
        for key, val in self.dma_val.items():
            evs.append(("dma", key, val))
        for e in self.ENGS:
            waits = []
            for ev in evs:
                if ev[0] == "eng" and ev[1] == e:
                    continue
                self._add_wait(e, waits, ev)
            self.streams[e].append({"fn": None, "waits": waits, "sig": False, "dma": None})
        self.last_w = {}
        self.readers = {}

    def emit(self, nc, es):
        sems = {e: es.enter_context(nc.semaphore("sem_" + e)) for e in ("pe", "act", "dve", "pool")}
        dsems = {}
        for (e, slot) in self.dma_val:
            dsems[(e, slot)] = es.enter_context(nc.semaphore("dsem_%s%d" % (e, slot)))
        fin_waits = []
        for ev in self.final_dma:
            self._add_wait("sp", fin_waits, ev)
        counts = {}
        for e in ("pe", "act", "dve", "pool"):
            c = 0
            lst = []
            for o in self.streams[e]:
                if o["sig"]:
                    c += 1
                lst.append(c)
            counts[e] = lst
        block = es.enter_context(nc.Block())

        def run(engname, eobj):
            def do_wait(ev):
                if ev[0] == "eng":
                    eobj.wait_ge(sems[ev[1]], counts[ev[1]][ev[2]])
                else:
                    eobj.wait_ge(dsems[ev[1]], ev[2])
            for o in self.streams[engname]:
                for ev in o["waits"]:
                    do_wait(ev)
                if o["fn"] is None:
                    continue
                ins = o["fn"](eobj)
                if o["dma"] is not None:
                    ins.then_inc(dsems[o["dma"]], 16)
                elif o["sig"]:
                    ins.then_inc(sems[engname], 1)
            if engname == "sp":
                for ev in fin_waits:
                    do_wait(ev)

        @block.sync
        def _(e):
            run("sp", e)

        @block.tensor
        def _(e):
            run("pe", e)

        @block.scalar
        def _(e):
            run("act", e)

        @block.vector
        def _(e):
            run("dve", e)

        @block.gpsimd
        def _(e):
            run("pool", e)


class Ring:
    def __init__(self, tiles, name, keys=None):
        self.tiles = tiles
        self.keys = keys if keys is not None else [(name, j) for j in range(len(tiles))]
        self.i = 0

    def next(self):
        j = self.i % len(self.tiles)
        self.i += 1
        return self.tiles[j], self.keys[j]


def build_program(S, NSEQ, dbg=None, lvl=30, sub=99, moba=True, only_even=False, GRPN=4, bar=False):
    dbg = dbg or set()
    nc = bass.Bass("TRN2", target_bir_lowering=False)
    NT = S // 512
    NQT = S // 128
    NBLK = S // BLK
    NTOK = NSEQ * S

    def din(name, shape, dt=F32):
        return nc.dram_tensor(name, list(shape), dt, kind="ExternalInput").ap()

    x_d = din("x", [NTOK, D])
    p_d = din("p", [NTOK, PLE])
    g_pre_d = din("g_pre", [D])
    w_in_d = din("w_in", [D, IN_COLS])
    relb_d = din("rel_bias", [32, NH])
    mu_d = din("mu_shift", [RW_COLS])
    w0_d = din("w0", [B_W])
    w_up_d = din("w_up", [64, B_W])
    a0_d = din("a0", [B_W])
    a_up_d = din("a_up", [64, B_W])
    k_k_d = din("k_k", [B_W])
    k_a_d = din("k_a", [B_W])
    r_k_d = din("r_k", [B_W])
    lnw_d = din("ln_x_w", [B_W])
    lnb_d = din("ln_x_b", [B_W])
    p_a_d = din("p_a", [A_W, D])
    p_b_d = din("p_b", [B_W, D])
    w_out_d = din("w_out", [D, D])
    g_post_d = din("g_post", [D])
    w_pu_d = din("w_ple_up", [PLE, D])
    w_pg_d = din("w_ple_gate", [D, D])
    c_ident_d = din("c_ident", [128, 128])
    c_onehot_d = din("c_onehot", [33, 512])
    c_esel_d = din("c_esel", [128, 16 * 128])
    c_tri_d = din("c_tri", [64, 3 * 64])
    c_bd_d = din("c_bd", [128, 128])
    c_J_d = din("c_J", [128, 128])
    c_pen_d = din("c_pen", [16 * 128])

    out_d = nc.dram_tensor("out", [NTOK, D], F32, kind="ExternalOutput").ap()

    def scratch(name, shape, dt):
        kind = "ExternalOutput" if name in dbg else "Internal"
        return nc.dram_tensor(name, list(shape), dt, kind=kind).ap()

    qT_d = scratch("s_qT", [NSEQ, A_W, S], BF16)
    kT_d = scratch("s_kT", [NSEQ, A_W, S], BF16)
    v_d = scratch("s_v", [NSEQ, S, 2 * A_W], BF16)
    za_d = scratch("s_za", [NSEQ, A_W, S], BF16)
    rw_d = scratch("s_rw", [NSEQ, RW_COLS, S], F32)
    gt_d = scratch("s_gt", [NSEQ, 2 * D, S], BF16)
    ya_d = scratch("s_ya", [NSEQ, A_W, S], BF16)
    yb_d = scratch("s_yb", [NSEQ, B_W, S], BF16)
    fb_d = scratch("s_fb", [NH, 512], F32)

    sc = Sched()
    es = ExitStack()

    def sb(name, shape, dt=F32):
        return es.enter_context(nc.sbuf_tensor(name, list(shape), dt))

    def ps(name, shape, dt=F32):
        return es.enter_context(nc.psum_tensor(name, list(shape), dt))

    with es:
        ident_f = sb("ident_f", [128, 128])
        ident_b = sb("ident_b", [128, 128], BF16)
        sc.dma("sp", ident_f[:], c_ident_d[:, :], writes=[("ident_f",)])
        sc.op("dve", lambda e: e.tensor_copy(ident_b[:], ident_f[:]), reads=[("ident_f",)], writes=[("ident_b",)])
        psb = [ps("psb%d" % i, [128, 512]) for i in range(8)]
        psring = Ring(psb, "psb")

        vstage = sb("vstage", [64, 128])
        vc = sb("vc", [128, 64])
        VOFF = {}
        _r = 0
        sc.op("dve", lambda e: e.memset(vstage[:], 0.0), writes=[("vstage",)])
        for nm, dv, n in (("g_pre", g_pre_d, 8), ("mu", mu_d, 17), ("w0", w0_d, 4), ("a0", a0_d, 4),
                          ("k_k", k_k_d, 4), ("k_a", k_a_d, 4), ("r_k", r_k_d, 4)):
            VOFF[nm] = _r
            sc.dma("sp", vstage[_r:_r + n, :], dv.rearrange("(k p) -> k p", p=128), reads=[("vstage",)], writes=[("vstage", nm)])
            _r += n
        pt, pk = psring.next()
        sc.op("pe", lambda e, pt=pt: e.transpose(pt[:, 0:64], vstage[:], ident_f[0:64, 0:64]),
              reads=[("vstage",), ("ident_f",)] + [("vstage", nm) for nm in VOFF], writes=[pk])
        sc.op("dve", lambda e, pt=pt: e.tensor_copy(vc[:], pt[:, 0:64]), reads=[pk], writes=[("vc",)])
        gpre_c = vc[:, VOFF["g_pre"]:VOFF["g_pre"] + 8]

        kmean = sb("kmean", [128, NSEQ, 4, 16], F32)
        kmean_b = sb("kmean_b", [128, NSEQ, 4, 16], BF16)
        if True:
            es1 = ExitStack()
            with es1:
                def sb1(name, shape, dt=F32):
                    return es1.enter_context(nc.sbuf_tensor(name, list(shape), dt))
                wp = sb1("wp", [128, 8, IN_COLS], BF16)
                wst = [sb1("wst%d" % i, [128, 1568]) for i in range(2)]
                wring = Ring(wst, "wst")
                for kc in range(8):
                    for q4 in range(4):
                        t, tk = wring.next()
                        c0 = q4 * 1568
                        sc.dma("sp", t[:], w_in_d[kc * 128:(kc + 1) * 128, c0:c0 + 1568], writes=[tk])
                        eng = ("dve", "pool")[(kc * 4 + q4) % 2]
                        sc.op(eng, lambda e, t=t, kc=kc, c0=c0: e.tensor_scalar(
                            wp[:, kc, c0:c0 + 1568], t[:], gpre_c[:, kc:kc + 1], None, ALU.mult),
                            reads=[tk, ("vc",)], writes=[("wp", kc, q4)])
                wp_keys = [("wp", kc, q4) for kc in range(8) for q4 in range(4)]

                mu_c = vc[:, VOFF["mu"]:VOFF["mu"] + 17]
                carry = sb1("carry", [128, 17])
                xt = [sb1("xt%d" % i, [128, 4, D]) for i in range(2)]
                xring = Ring(xt, "xt")
                ub = [sb1("ub%d" % i, [128, D], BF16) for i in range(2)]
                ubring = Ring(ub, "ub")
                sqj = sb1("sqj", [128, D], BF16)
                ss = [sb1("ss%d" % i, [128, 4]) for i in range(2)]
                ssring = Ring(ss, "ss")
                rs = [sb1("rs%d" % i, [128, 4]) for i in range(2)]
                rsring = Ring(rs, "rs")
                uT = [sb1("uT%d" % i, [128, 8, 512], BF16) for i in range(2)]
                uTring = Ring(uT, "uT")
                ob = [sb1("ob%d" % i, [128, 512], BF16) for i in range(4)]
                obring = Ring(ob, "ob")
                vo = [sb1("vo%d" % i, [128, NH, 128], BF16) for i in range(2)]
                voring = Ring(vo, "vo")
                for i in range(2):
                    sc.op("dve", lambda e, i=i: e.memset(vo[i][:], 1.0), writes=[("vo", i), ("vo_init",)])
                cb = [sb1("cb%d" % i, [128, 513]) for i in range(3)]
                cbring = Ring(cb, "cb")
                db = [sb1("db%d" % i, [128, 512]) for i in range(2)]
                dbring = Ring(db, "db")
                shb = [sb1("shb%d" % i, [128, 512]) for i in range(3)]
                shring = Ring(shb, "shb")
                sc.op("dve", lambda e: e.memset(kmean[:], 0.0), writes=[("kmean",)])

                xloaded = {}

                def load_x(s, ti):
                    if s >= NSEQ:
                        return
                    tok0 = s * S + ti * 512
                    xtile, xk = xring.next()
                    sc.dma("sp", xtile[:], x_d[tok0:tok0 + 512, :].rearrange("(a p) d -> p a d", p=128),
                           writes=[xk])
                    xloaded[(s, ti)] = (xtile, xk)

                load_x(0, 0)
                for s in range(NSEQ if lvl >= 1 else 0):
                    sc.op("dve", lambda e: e.memset(carry[:], 0.0), writes=[("carry",)], reads=[])
                    for ti in range(NT):
                        nxt = (s, ti + 1) if ti + 1 < NT else (s + 1, 0)
                        load_x(*nxt)
                        xtile, xk = xloaded.pop((s, ti))
                        sst, ssk = ssring.next()
                        rst, rsk = rsring.next()
                        sc.op("dve", lambda e, sst=sst: e.memset(sst[:], 0.0), writes=[(ssk, a) for a in range(4)])
                        for a in range(4):
                            sc.op("act", lambda e, a=a, xtile=xtile, sst=sst: e.activation(
                                sqj[:], xtile[:, a, :], AF.Square, accum_out=sst[:, a:a + 1]),
                                reads=[xk], writes=[("sqj",), (ssk, a)])
                        sc.op("act", lambda e, sst=sst: e.activation(
                            sst[:], sst[:], AF.Sqrt, bias=float(RMS_EPS), scale=1.0 / D),
                            reads=[(ssk, a) for a in range(4)], writes=[(ssk, "q")])
                        sc.op("dve", lambda e, sst=sst, rst=rst: e.reciprocal(rst[:], sst[:]),
                              reads=[(ssk, "q")], writes=[rsk] + [(ssk, a) for a in range(4)])
                        uTt, uTk = uTring.next()
                        for a in range(4):
                            ubt, ubk = ubring.next()
                            sc.op("dve", lambda e, a=a, ubt=ubt, xtile=xtile, rst=rst: e.tensor_scalar(
                                ubt[:], xtile[:, a, :], rst[:, a:a + 1], None, ALU.mult),
                                reads=[xk, rsk], writes=[ubk])
                            pt, pk = psring.next()
                            ptb = pt[:].bitcast(BF16)

                            def tr_fn(e, ubt=ubt, ptb=ptb):
                                ins = None
                                for kc in range(8):
                                    ins = e.transpose(ptb[:, kc * 128:(kc + 1) * 128], ubt[:, kc * 128:(kc + 1) * 128], ident_b[:])
                                return ins
                            sc.op("pe", tr_fn, reads=[ubk, ("ident_b",)], writes=[pk])
                            eng = ("act", "dve")[a % 2]
                            if eng == "act":
                                sc.op("act", lambda e, a=a, uTt=uTt, ptb=ptb: e.copy(
                                    uTt[:, :, a * 128:(a + 1) * 128], ptb.rearrange("p (k t) -> p k t", k=8)),
                                    reads=[pk], writes=[(uTk, a)])
                            else:
                                sc.op("dve", lambda e, a=a, uTt=uTt, ptb=ptb: e.tensor_copy(
                                    uTt[:, :, a * 128:(a + 1) * 128], ptb.rearrange("p (k t) -> p k t", k=8)),
                                    reads=[pk], writes=[(uTk, a)])
                        uT_keys = [(uTk, a) for a in range(4)]

                        def proj(cc, pt, pk, uTt=uTt, uT_keys=uT_keys):
                            def fn(e):
                                ins = None
                                for kc in range(8):
                                    ins = e.matmul(pt[:], wp[:, kc, cc * 128:(cc + 1) * 128], uTt[:, kc, :],
                                                   start=(kc == 0), stop=(kc == 7))
                                return ins
                            sc.op("pe", fn, reads=uT_keys + wp_keys, writes=[pk])

                        for cc in range(49):
                            if 8 <= cc < 12:
                                continue
                            if lvl < 2 or (lvl == 2 and cc >= 4) or (lvl == 3 and cc >= 8) or (lvl == 4 and cc >= 16) or (lvl == 5 and cc >= 33):
                                continue
                            pt, pk = psring.next()
                            proj(cc, pt, pk)
                            tsl = slice(ti * 512, (ti + 1) * 512)
                            rows = slice((cc % 4) * 128, (cc % 4) * 128 + 128)
                            if cc < 4:
                                o, ok = obring.next()
                                sc.op("act", lambda e, o=o, pt=pt: e.mul(o[:], pt[:], 0.125), reads=[pk], writes=[ok])
                                sc.dma("pool", qT_d[s, rows, tsl], o[:], reads=[ok], writes=[("qT", s, cc, ti)])
                            elif cc < 8:
                                o, ok = obring.next()
                                for hb in range(2):
                                    sc.op("act", lambda e, o=o, pt=pt, hb=hb, s=s, cc=cc, ti=ti: e.activation(
                                        o[:, hb * 256:(hb + 1) * 256], pt[:, hb * 256:(hb + 1) * 256], AF.Copy,
                                        accum_out=kmean[:, s, cc - 4, 2 * ti + hb:2 * ti + hb + 1]),
                                        reads=[pk, ("kmean",)], writes=([ok] if hb == 0 else []) + [(ok, hb), ("kmean", s, cc - 4, ti, hb)])
                                sc.dma("pool", kT_d[s, rows, tsl], o[:], reads=[ok, (ok, 0), (ok, 1)], writes=[("kT", s, cc - 4, ti)])
                            elif cc < 16:
                                o, ok = obring.next()
                                sc.op("act", lambda e, o=o, pt=pt: e.activation(o[:], pt[:], AF.Silu), reads=[pk], writes=[ok])
                                sc.dma("pool", za_d[s, rows, tsl], o[:], reads=[ok], writes=[("za", s, cc - 12, ti)])
                            elif cc < 33:
                                j = cc - 16
                                c, ck = cbring.next()
                                sc.op("act", lambda e, c=c, pt=pt: e.copy(c[:, 1:513], pt[:]), reads=[pk], writes=[(ck, 1)])
                                sc.op("dve", lambda e, c=c, j=j: e.tensor_copy(c[:, 0:1], carry[:, j:j + 1]),
                                      reads=[("carry", j), ("carry",)], writes=[(ck, 0)])
                                sc.op("dve", lambda e, c=c, j=j: e.tensor_copy(carry[:, j:j + 1], c[:, 512:513]),
                                      reads=[(ck, 1), ("carry",)], writes=[("carry", j)])
                                dd, dk = dbring.next()
                                sc.op("dve", lambda e, c=c, dd=dd: e.tensor_tensor(dd[:], c[:, 0:512], c[:, 1:513], ALU.subtract),
                                      reads=[(ck, 0), (ck, 1)], writes=[dk])
                                sh, shk = shring.next()
                                sc.op("dve", lambda e, c=c, dd=dd, sh=sh, j=j: e.scalar_tensor_tensor(
                                    sh[:], dd[:], mu_c[:, j:j + 1], c[:, 1:513], ALU.mult, ALU.add),
                                    reads=[dk, (ck, 1), ("vc",)], writes=[shk])
                                sc.dma("pool", rw_d[s, j * 128:(j + 1) * 128, tsl], sh[:], reads=[shk], writes=[("rw", s, j, ti)])
                            else:
                                j = cc - 33
                                o, ok = obring.next()
                                sc.op("act", lambda e, o=o, pt=pt: e.activation(o[:], pt[:], AF.Sigmoid), reads=[pk], writes=[ok])
                                sc.dma("pool", gt_d[s, j * 128:(j + 1) * 128, tsl], o[:], reads=[ok], writes=[("gt", s, j, ti)])
                        for a in range(4 if lvl >= 7 else 0):
                            pt, pk = psring.next()

                            def vfn(e, a=a, pt=pt, uTt=uTt):
                                ins = None
                                for kc in range(8):
                                    ins = e.matmul(pt[:], uTt[:, kc, a * 128:(a + 1) * 128], wp[:, kc, 1024:1536],
                                                   start=(kc == 0), stop=(kc == 7))
                                return ins
                            sc.op("pe", vfn, reads=uT_keys + wp_keys, writes=[pk])
                            o, ok = voring.next()
                            sc.op("act", lambda e, o=o, pt=pt: e.copy(o[:, :, 0:64], pt[:].rearrange("p (h d) -> p h d", h=NH)),
                                  reads=[pk, ("vo_init",)], writes=[ok])
                            t0 = ti * 512 + a * 128
                            sc.dma("pool", v_d[s, t0:t0 + 128, :], o[:].rearrange("p h d -> p (h d)"), reads=[ok], writes=[("v", s, ti * 4 + a)])
                sc.op("dve", lambda e: e.tensor_scalar(kmean_b[:], kmean[:], 1.0 / BLK, None, ALU.mult),
                      reads=[("kmean",)] + [("kmean", s, c, ti, hb) for s in range(NSEQ) for c in range(4) for ti in range(NT) for hb in range(2)],
                      writes=[("kmean_b",)])

        sc.barrier()
        if lvl >= 10 and moba:
            es2 = ExitStack()
            with es2:
                def sb2(name, shape, dt=F32):
                    return es2.enter_context(nc.sbuf_tensor(name, list(shape), dt))
                psS = Ring(psb[0:4], "psb", keys=[("psb", j) for j in range(0, 4)])
                psN = Ring(psb[4:6], "psb", keys=[("psb", j) for j in range(4, 6)])
                psM = Ring(psb[6:8], "psb", keys=[("psb", j) for j in range(6, 8)])
                relb33 = sb2("relb33", [33, NH])
                b31bc = sb2("b31bc", [128, NH])
                sc.dma("sp", b31bc[:], relb_d[31, :].partition_broadcast(128), writes=[("b31bc",)])
                sc.op("dve", lambda e: e.memset(relb33[32:33, :], NEG), writes=[("relb33", 1)])
                sc.dma("sp", relb33[0:32, :], relb_d[:, :], writes=[("relb33", 0)])
                oneh = sb2("oneh", [33, 512])
                sc.dma("sp", oneh[:], c_onehot_d[:, :], writes=[("oneh",)])
                pt, pk = psM.next()
                sc.op("pe", lambda e, pt=pt: e.matmul(pt[0:8, :], relb33[:], oneh[:], start=True, stop=True),
                      reads=[("relb33", 0), ("relb33", 1), ("oneh",)], writes=[pk])
                fbs = sb2("fbs", [8, 512])
                sc.op("dve", lambda e, pt=pt: e.tensor_copy(fbs[:], pt[0:8, :]), reads=[pk], writes=[("fbs",)])
                sc.dma("sp", fb_d[:, :], fbs[:], reads=[("fbs",)], writes=[("fb_d",)])
                Jf = sb2("Jf", [128, 128])
                sc.dma("sp", Jf[:], c_J_d[:, :], writes=[("Jf",)])
                Tt = sb2("Tt", [128, NH, 2, 128], BF16)
                tfl = [sb2("tfl%d" % i, [128, 128]) for i in range(2)]
                tflr = Ring(tfl, "tfl")
                for h in range(NH):
                    for dl in range(2):
                        tf, tfk = tflr.next()
                        src = bass.AP(tensor=fb_d.tensor, offset=h * 512 + 1 + dl * 128, ap=[[1, 128], [1, 128]])
                        sc.dma("sp", tf[:], src, reads=[("fb_d",)], writes=[tfk])
                        pt, pk = psM.next()
                        sc.op("pe", lambda e, pt=pt, tf=tf: e.matmul(pt[:, 0:128], Jf[:], tf[:], start=True, stop=True),
                              reads=[tfk, ("Jf",)], writes=[pk])
                        sc.op("dve", lambda e, pt=pt, h=h, dl=dl: e.tensor_scalar(Tt[:, h, dl, :], pt[:, 0:128], b31bc[:, h:h + 1], None, ALU.subtract),
                              reads=[pk, ("b31bc",)], writes=[("Tt", h, dl)])
                Tt_keys = [("Tt", h, dl) for h in range(NH) for dl in range(2)]
                if "d_Tt" in dbg:
                    dTt = nc.dram_tensor("d_Tt", [128, NH * 2 * 128], BF16, kind="ExternalOutput").ap()
                    sc.dma("sp", dTt[:, :], Tt[:].rearrange("p h d q -> p (h d q)"), reads=Tt_keys)
                eself = sb2("eself", [128, 16 * 128])
                esel = sb2("esel", [128, 16, 128], BF16)
                sc.dma("sp", eself[:], c_esel_d[:, :], writes=[("eself",)])
                sc.op("dve", lambda e: e.tensor_copy(esel[:].rearrange("p a b -> p (a b)"), eself[:]), reads=[("eself",)], writes=[("esel",)])
                ones_b = sb2("ones_b", [128, 64], BF16)
                sc.op("dve", lambda e: e.memset(ones_b[:], 1.0), writes=[("ones_b",)])
                maskT = [sb2("maskT%d" % i, [128, NH, 128], BF16) for i in range(2)]
                for i in range(2):
                    sc.op("dve", lambda e, i=i: e.memset(maskT[i][:, :, :], 0.0), writes=[("maskT", i)])
                mring = Ring(maskT, "maskT")
                pen_sb = sb2("pen_sb", [128, 16 * 128])
                sc.op("dve", lambda e: e.memset(pen_sb[:], 0.0), writes=[("pen_sb", "z")])
                sc.dma("sp", pen_sb[0:1, :], c_pen_d.rearrange("(a n) -> a n", a=1), reads=[("pen_sb", "z")], writes=[("pen_sb", 0)])
                sc.dma("sp", pen_sb[64:65, :], c_pen_d.rearrange("(a n) -> a n", a=1), reads=[("pen_sb", "z")], writes=[("pen_sb", 1)])
                onesq_b = sb2("onesq_b", [128, 128], BF16)
                sc.op("dve", lambda e: e.memset(onesq_b[:], 1.0), writes=[("onesq_b",)])
                penb = sb2("penb", [128, 16 * 128], BF16)
                sc.op("dve", lambda e: e.tensor_copy(penb[:], pen_sb[:]), reads=[("pen_sb", 0), ("pen_sb", 1), ("pen_sb", "z")], writes=[("penb",)])
                kT_sb = sb2("kT_sb", [128, 4, S], BF16)
                qT_sb = sb2("qT_sb", [128, 4, S], BF16)
                v_sb = sb2("v_sb", [128, S // 128, 2 * A_W], BF16)
                gsb = [sb2("gsb%d" % i, [128, NH, 16]) for i in range(2)]
                gring = Ring(gsb, "gsb")
                top8 = [sb2("top8_%d" % i, [128, NH, 8]) for i in range(2)]
                t8ring = Ring(top8, "top8")
                selm = [sb2("selm%d" % i, [128, NH, 16]) for i in range(2)]
                sring = Ring(selm, "selm")
                mvb = [sb2("mvb%d" % i, [128, NH, 16], BF16) for i in range(2)]
                mvring = Ring(mvb, "mvb")
                Pb = [sb2("Pb%d" % i, [128, 512], BF16) for i in range(5)]
                Pring = Ring(Pb, "Pb")
                rden = [sb2("rden%d" % i, [64, 128]) for i in range(2)]
                rdring = Ring(rden, "rden")
                ynorm = [sb2("ynorm%d" % i, [64, 128]) for i in range(2)]
                ynring = Ring(ynorm, "ynorm")
                zat = [sb2("zat%d" % i, [64, NH, 128], BF16) for i in range(2)]
                zring = Ring(zat, "zat")
                yout = [sb2("yout%d" % i, [64, NH, 128], BF16) for i in range(2)]
                yring = Ring(yout, "yout")

                for s in range(NSEQ if lvl >= 11 else 0):
                    for c in range(4):
                        sc.dma("sp", kT_sb[:, c, :], kT_d[s, c * 128:(c + 1) * 128, :],
                               reads=[("kT", s, c, ti) for ti in range(NT)], writes=[("kT_sb", c)])
                        sc.dma("sp", qT_sb[:, c, :], qT_d[s, c * 128:(c + 1) * 128, :],
                               reads=[("qT", s, c, ti) for ti in range(NT)], writes=[("qT_sb", c)])
                    for j4 in range(0, S // 128, 4):
                        n4 = min(4, S // 128 - j4)
                        sc.dma("sp", v_sb[:, j4:j4 + n4, :], v_d[s, j4 * 128:(j4 + n4) * 128, :].rearrange("(a p) d -> p a d", p=128),
                               reads=[("v", s, j) for j in range(j4, j4 + n4)], writes=[("v_sb", j) for j in range(j4, j4 + n4)])
                    for qt in range(NQT if lvl >= 12 else 0):
                        QB = qt // 2
                        qsl = slice(qt * 128, (qt + 1) * 128)
                        mT, mTk = None, None
                        if bar:
                            sc.barrier()
                        if QB > 0:
                            pt, pk = psM.next()

                            def gfn(e, pt=pt, qsl=qsl, s=s, QB=QB):
                                ins = None
                                for h in range(NH):
                                    hp = slice((h % 2) * 64, (h % 2) * 64 + 64)
                                    ins = e.matmul(pt[:, h * 16:(h + 1) * 16], qT_sb[hp, h // 2, qsl], kmean_b[hp, s, h // 2, :],
                                                   start=True, stop=False)
                                    p0 = (h % 2) * 64
                                    ins = e.matmul(pt[:, h * 16:(h + 1) * 16], onesq_b[p0:p0 + 1, :],
                                                   penb[p0:p0 + 1, QB * 128 + h * 16:QB * 128 + (h + 1) * 16], start=False, stop=True)
                                return ins
                            sc.op("pe", gfn, reads=[("qT_sb", c) for c in range(4)] + [("kmean_b",), ("penb",), ("onesq_b",)], writes=[pk])
                            g, gk = gring.next()
                            sc.op("dve", lambda e, g=g, pt=pt: e.tensor_copy(g[:].rearrange("p h n -> p (h n)"), pt[:, 0:NH * 16]),
                                  reads=[pk], writes=[gk, (gk, 1)])
                            t8, t8k = t8ring.next()
                            for h in range(NH if sub >= 3 else 0):
                                sc.op("dve", lambda e, t8=t8, g=g, h=h: e.max(t8[:, h, :], g[:, h, :]),
                                      reads=[gk, (gk, 1)], writes=[(t8k, h)])
                            sm, smk = sring.next()
                            if sub >= 4:
                              sc.op("dve", lambda e, sm=sm, g=g, t8=t8: e.tensor_tensor(
                                sm[:], g[:], t8[:, :, 2:3].to_broadcast([128, NH, 16]), ALU.is_ge),
                                reads=[gk, (gk, 1)] + [(t8k, h) for h in range(NH)], writes=[smk])
                            mv, mvk = mvring.next()
                            if sub >= 5:
                              sc.op("dve", lambda e, mv=mv, sm=sm: e.tensor_scalar(mv[:], sm[:], -NEG, NEG, ALU.mult, ALU.add),
                                  reads=[smk], writes=[mvk])
                            pt2, pk2 = psM.next()
                            ptb2 = pt2[:].bitcast(BF16)

                            def mtr(e, mv=mv, ptb2=ptb2):
                                ins = None
                                for h in range(NH):
                                    ins = e.transpose(ptb2[0:16, h * 128:(h + 1) * 128], mv[:, h, :], ident_b[:])
                                return ins
                            if sub >= 6:
                                sc.op("pe", mtr, reads=[mvk, ("ident_b",)], writes=[pk2])
                            mT, mTk = mring.next()
                            if sub >= 7:
                                sc.op("dve", lambda e, mT=mT, ptb2=ptb2: e.tensor_copy(
                                    mT[0:16, :, :], ptb2[0:16, :].rearrange("p (h q) -> p h q", h=NH)),
                                    reads=[pk2], writes=[mTk])
                                sc.op("dve", lambda e, mT=mT, ptb2=ptb2: e.tensor_copy(
                                    mT[64:80, :, :], ptb2[0:16, :].rearrange("p (h q) -> p h q", h=NH)),
                                    reads=[pk2], writes=[(mTk, "b")])
                        if "d_gate" in dbg and qt == NQT - 1 and s == 0:
                            dg = nc.dram_tensor("d_gate", [128, NH * 16], F32, kind="ExternalOutput").ap()
                            sc.dma("sp", dg[:, :], g[:].rearrange("p h n -> p (h n)"), reads=[gk, (gk, 1)])
                            dsm = nc.dram_tensor("d_sm", [128, NH * 16], F32, kind="ExternalOutput").ap()
                            sc.dma("sp", dsm[:, :], sm[:].rearrange("p h n -> p (h n)"), reads=[smk])
                            dmt = nc.dram_tensor("d_mt", [33, NH * 128], BF16, kind="ExternalOutput").ap()
                            sc.dma("sp", dmt[:, :], mT[:].rearrange("p h n -> p (h n)"), reads=[mTk, (mTk, "c")])
                            dkm = nc.dram_tensor("d_km", [128, NSEQ * 64], F32, kind="ExternalOutput").ap()
                            sc.dma("sp", dkm[:, :], kmean[:].rearrange("p s c n -> p (s c n)"), reads=[("kmean_b",)])
                        if bar:
                            sc.barrier()
                        zt, ztk = zring.next()
                        sc.dma("sp", zt[:], za_d[s].rearrange("(h d) t -> d h t", h=NH)[:, :, qsl],
                               reads=[("za", s, c, qt // 4) for c in range(4)], writes=[ztk])
                        yo, yok = yring.next()
                        pend = []

                        def drain(keep):
                            while len(pend) > keep:
                                pend.pop(0)()
                        for h in range(NH if lvl >= 13 else 0):
                            hp = slice((h % 2) * 64, (h % 2) * 64 + 64)
                            c = h // 2
                            nd, ndk = psN.next()
                            kts = list(range(qt + 1))
                            ngroups = (len(kts) + GRPN - 1) // GRPN
                            for gi, g0 in enumerate(range(0, len(kts), GRPN)):
                                grp = kts[g0:g0 + GRPN]
                                st_, stk = psS.next()

                                def sfn(e, grp=grp, st_=st_, hp=hp, c=c, qsl=qsl, qt=qt, QB=QB, h=h, mT=mT):
                                    ins = None
                                    for j, kt in enumerate(grp):
                                        osl = st_[:, j * 128:(j + 1) * 128]
                                        extra = []
                                        n = kt // 2
                                        if n < QB:
                                            extra.append((esel[hp, n, :], mT[hp, h, :]))
                                        ins = e.matmul(osl, kT_sb[hp, c, kt * 128:(kt + 1) * 128], qT_sb[hp, c, qsl],
                                                       start=True, stop=(len(extra) == 0))
                                        for i2, (l_, r_) in enumerate(extra):
                                            ins = e.matmul(osl, l_, r_, start=False, stop=(i2 == len(extra) - 1))
                                    return ins
                                rd = [("kT_sb", c), ("qT_sb", c), ("esel",)]
                                if mTk is not None:
                                    rd += [mTk, (mTk, "b")]
                                sc.op("pe", sfn, reads=rd, writes=[stk])
                                for j, kt in enumerate(grp):
                                    if kt >= qt - 1:
                                        dl = qt - kt
                                        sc.op("dve", lambda e, st_=st_, j=j, h=h, dl=dl: e.tensor_tensor(
                                            st_[:, j * 128:(j + 1) * 128], st_[:, j * 128:(j + 1) * 128], Tt[:, h, dl, :], ALU.add),
                                            reads=[stk, ("Tt", h, dl)], writes=[stk])
                                P, Pk = Pring.next()
                                ng = len(grp)
                                sc.op("act", lambda e, P=P, st_=st_, ng=ng, h=h: e.activation(P[:, 0:ng * 128], st_[:, 0:ng * 128], AF.Exp, bias=b31bc[:, h:h + 1]),
                                      reads=[stk, ("b31bc",)], writes=[Pk])

                                def emit_pv(grp=grp, P=P, Pk=Pk, nd=nd, ndk=ndk, h=h, qt=qt, last=(gi == ngroups - 1), yo=yo, yok=yok, zt=zt, ztk=ztk):
                                    def pvfn(e):
                                        ins = None
                                        for j, kt in enumerate(grp):
                                            ins = e.matmul(nd[:, 0:128], v_sb[:, kt, h * 128:(h + 1) * 128], P[:, j * 128:(j + 1) * 128],
                                                           start=(kt == 0), stop=(kt == qt))
                                        return ins
                                    sc.op("pe", pvfn, reads=[Pk] + [("v_sb", kt) for kt in grp], writes=[ndk])
                                    if last:
                                        rdn, rdk = rdring.next()
                                        sc.op("dve", lambda e: e.reciprocal(rdn[:], nd[64:128, 0:128]), reads=[ndk], writes=[rdk])
                                        yn, ynk = ynring.next()
                                        sc.op("dve", lambda e: e.tensor_tensor(yn[:], nd[0:64, 0:128], rdn[:], ALU.mult),
                                              reads=[ndk, rdk], writes=[ynk])
                                        sc.op("pool", lambda e: e.tensor_tensor(yo[:, h, :], yn[:], zt[:, h, :], ALU.mult),
                                              reads=[ynk, ztk], writes=[(yok, h)])
                                pend.append(emit_pv)
                                drain(3)
                        drain(0)
                        sc.dma("pool", ya_d[s].rearrange("(h d) t -> d h t", h=NH)[:, :, qsl], yo[:],
                               reads=[(yok, h) for h in range(NH)], writes=[("ya", s, qt), yok])

        if lvl >= 20 and (lvl < 40 or not moba):
            zt_ = sb("zstub", [128, 512], BF16)
            sc.op("dve", lambda e: e.memset(zt_[:], 0.0), writes=[("zstub",)])
            for s in range(NSEQ):
                for ti in range(NT):
                    for c in range(4):
                        if lvl < 40:
                            sc.dma("sp", yb_d[s, c * 128:(c + 1) * 128, ti * 512:(ti + 1) * 512], zt_[:], reads=[("zstub",)],
                                   writes=[("yb", s, ti)] if c == 3 else [("yb_part", s, ti, c)])
                        if not moba:
                            sc.dma("sp", ya_d[s, c * 128:(c + 1) * 128, ti * 512:(ti + 1) * 512], zt_[:], reads=[("zstub",)],
                                   writes=[("ya", s, ti * 4 + c)])

        sc.barrier()
        if lvl >= 40:
            es4 = ExitStack()
            with es4:
                def sb4(name, shape, dt=F32):
                    return es4.enter_context(nc.sbuf_tensor(name, list(shape), dt))
                psW = Ring(psb, "psb")
                NC_ = S // 64
                NG = 4
                HS = [64, NG, 64]
                vs2 = sb4("vs2", [64, 64])
                sc.op("dve", lambda e: e.memset(vs2[:], 0.0), writes=[("vs2",)])
                VO2 = {}
                for i, (nm, dv) in enumerate((("w0", w0_d), ("a0", a0_d), ("k_k", k_k_d), ("k_a", k_a_d), ("r_k", r_k_d))):
                    VO2[nm] = i * 8
                    sc.dma("sp", vs2[i * 8:(i + 1) * 8, :], dv.rearrange("(h d) -> h d", d=64), reads=[("vs2",)], writes=[("vs2", nm)])
                pt, pk = psW.next()
                sc.op("pe", lambda e, pt=pt: e.transpose(pt[0:64, 0:64], vs2[:], ident_f[0:64, 0:64]),
                      reads=[("vs2",), ("ident_f",)] + [("vs2", nm) for nm in VO2], writes=[pk])
                vh = sb4("vh", [64, 64])
                sc.op("dve", lambda e, pt=pt: e.tensor_copy(vh[:], pt[0:64, 0:64]), reads=[pk], writes=[("vh",)])
                omk = sb4("omk", [64, NH])
                sc.op("dve", lambda e: e.tensor_scalar(omk[:], vh[:, VO2["k_a"]:VO2["k_a"] + 8], -1.0, 1.0, ALU.mult, ALU.add),
                      reads=[("vh",)], writes=[("omk",)])

                def vb(nm):
                    o = VO2[nm] + CUR["hg"] * NG
                    return vh[:, o:o + NG].rearrange("p (h o) -> p h o", o=1).to_broadcast(HS)
                wup = sb4("wup", [64, B_W])
                aup = sb4("aup", [64, B_W])
                sc.dma("sp", wup[:], w_up_d[:, :], writes=[("wup0",)])
                sc.dma("sp", aup[:], a_up_d[:, :], writes=[("aup0",)])
                wup_r = sb4("wup_r", [64, B_W], F32R)
                aup_r = sb4("aup_r", [64, B_W], F32R)
                sc.op("dve", lambda e: e.tensor_copy(wup_r[:], wup[:]), reads=[("wup0",)], writes=[("wup",)])
                sc.op("dve", lambda e: e.tensor_copy(aup_r[:], aup[:]), reads=[("aup0",)], writes=[("aup",)])
                lnw = sb4("lnw", [64, B_W])
                lnb = sb4("lnb", [64, B_W])
                sc.dma("sp", lnw[:], lnw_d.partition_broadcast(64), writes=[("lnw",)])
                sc.dma("sp", lnb[:], lnb_d.partition_broadcast(64), writes=[("lnb",)])
                tri = sb4("tri", [64, 3, 64])
                sc.dma("sp", tri[:].rearrange("p a b -> p (a b)"), c_tri_d[:, :], writes=[("tri",)])
                ones64 = sb4("ones64", [64, 64], F32R)
                ones64f = sb4("ones64f", [64, 2])
                sc.op("dve", lambda e: e.memset(ones64f[:], 1.0), writes=[("ones64f",)])
                ones_t = sb4("ones_t", [64, 64])
                sc.op("dve", lambda e: e.memset(ones_t[:], 1.0), writes=[("ones_t",)])
                sc.op("dve", lambda e: e.tensor_copy(ones64[:], ones_t[:]), reads=[("ones_t",)], writes=[("ones64",)])
                ident_r = sb4("ident_r", [64, 64], F32R)
                sc.op("dve", lambda e: e.tensor_copy(ident_r[:], ident_f[0:64, 0:64]), reads=[("ident_f",)], writes=[("ident_r",)])
                zeros_t = sb4("zeros_t", [64, 4, 64])
                sc.op("dve", lambda e: e.memset(zeros_t[:], 0.0), writes=[("zeros_t",)])
                smask = sb4("smask", HS)
                sc.op("dve", lambda e: e.memset(smask[:], 1.0), writes=[("smask",)])
                sc.op("dve", lambda e: e.memset(smask[:, :, 0:1], 0.0), reads=[("smask",)], writes=[("smask", 1)])
                identb8 = ident_f[0:64, 0:64].rearrange("p (o d) -> p o d", o=1).to_broadcast(HS)

                def trib(i):
                    return tri[:, i:i + 1, :].to_broadcast(HS)

                T_ = {}
                CUR = {"set": 0, "list": None, "hg": 0}

                class Defer:
                    def op(self, eng, fn, reads=(), writes=()):
                        CUR["list"].append(("op", eng, fn, list(reads), list(writes), {}))

                    def dma(self, eng, out, in_, reads=(), writes=(), **kw):
                        CUR["list"].append(("dma", eng, (out, in_), list(reads), list(writes), kw))
                cur = Defer()

                RNAMES = {"tw", "ad_r", "sq", "At", "Bt", "Kt", "Rt", "tm_V", "tm_bc", "tm_kc", "X", "Q0", "Q1", "P0", "P1",
                          "AakT", "ArbT", "ArkT", "Mc", "Rh", "Ot"}

                def tile(name, shape=None):
                    nm = "r%d_%s" % (CUR["set"], name)
                    if nm not in T_:
                        T_[nm] = sb4(nm, shape or HS, F32R if name in RNAMES else F32)
                    return T_[nm], (nm,)
                HstA = [[sb4("Hst%d_%d" % (q, i), HS, F32R) for i in range(2)] for q in range(4)]

                def ew(eng, fn, reads, writes):
                    cur.op(eng, fn, reads=reads, writes=writes)

                def headmm(out_fn, l_fn, r_fn, reads, pk, extra=None):
                    items = []
                    for h in range(NG):
                        pairs = [(l_fn(h), r_fn(h))] + ([(a(h), b(h)) for a, b in extra] if extra else [])
                        for i, (l_, r_) in enumerate(pairs):
                            items.append((out_fn(h), l_, r_, i == 0, i == len(pairs) - 1))

                    def fn(e, items=items):
                        ins = None
                        for (o_, l_, r_, st, sp) in items:
                            ins = e.matmul(o_, l_, r_, start=st, stop=sp)
                        return ins
                    cur.op("pe", fn, reads=reads, writes=[pk])

                def flat(t):
                    return t[:].rearrange("p h t -> p (h t)")

                for s in range(NSEQ):
                    for hg in range(2):
                        sc.op("dve", lambda e, q=(s % 2) * 2 + hg: e.tensor_copy(HstA[q][0][:], zeros_t[:]), reads=[("zeros_t",)], writes=[("Hst", (s % 2) * 2 + hg, 0)])

                psSets = [Ring(psb[2 * q:2 * q + 2], "psb", keys=[("psb", j) for j in range(2 * q, 2 * q + 2)]) for q in range(4)]

                def body(s, ci, hg):
                    chain = (s % 2) * 2 + hg
                    psW = psSets[chain]
                    G0 = hg * NG
                    VB = {nm: vb(nm) for nm in VO2}
                    if True:
                        Hst = HstA[chain]
                        csl = slice(ci * 64, (ci + 1) * 64)
                        ti = ci // 8
                        hcur, hck = Hst[ci % 2], ("Hst", chain, ci % 2)
                        hnxt, hnk = Hst[(ci + 1) % 2], ("Hst", chain, (ci + 1) % 2)
                        fm = {}
                        for qi, nm in enumerate(("r", "k", "v", "z")):
                            t, tk = tile("in_" + nm)
                            cur.dma("sp", t[:], rw_d[s, qi * 512 + G0 * 64:qi * 512 + (G0 + NG) * 64, csl].rearrange("(h d) t -> d h t", h=NG),
                                   reads=[("rw", s, qi * 4 + j, ti) for j in range(4)], writes=[tk])
                            fm[nm] = (t, tk)
                        wd, wdk = tile("wd", [64, 64])
                        ad, adk = tile("ad", [64, 64])
                        cur.dma("sp", wd[:], rw_d[s, 2048:2112, csl], reads=[("rw", s, 16, ti)], writes=[wdk])
                        cur.dma("sp", ad[:], rw_d[s, 2112:2176, csl], reads=[("rw", s, 16, ti)], writes=[adk])
                        r_, rk_ = fm["r"]; k_, kk_ = fm["k"]; v_, vk_ = fm["v"]; z_, zk_ = fm["z"]
                        tw, twk = tile("tw", [64, 64])
                        adr, adrk = tile("ad_r", [64, 64])
                        ew("act", lambda e: e.copy(adr[:], ad[:]), [adk], [adrk])
                        ew("act", lambda e: e.activation(tw[:], wd[:], AF.Tanh), [wdk], [twk])
                        pW, pWk = psW.next()
                        headmm(lambda h: pW[0:64, h * 64:(h + 1) * 64], lambda h: wup_r[:, (G0 + h) * 64:(G0 + h + 1) * 64], lambda h: tw[:],
                               [twk, ("wup",)], pWk)
                        pA, pAk = psW.next()
                        headmm(lambda h: pA[0:64, h * 64:(h + 1) * 64], lambda h: aup_r[:, (G0 + h) * 64:(G0 + h + 1) * 64], lambda h: adr[:],
                               [adrk, ("aup",)], pAk)
                        pv3 = lambda p: p[0:64, 0:NG * 64].rearrange("p (h t) -> p h t", h=NG)
                        PW = NG * 64
                        lw, lwk = tile("lw")
                        ew("dve", lambda e, pW=pW: e.tensor_tensor(lw[:], pv3(pW), VB["w0"], ALU.add), [pWk, ("vh",)], [lwk])
                        ew("act", lambda e: e.activation(flat(lw), flat(lw), AF.Sigmoid), [lwk], [lwk])
                        av, avk = tile("av")
                        ew("dve", lambda e, pA=pA: e.tensor_tensor(av[:], pv3(pA), VB["a0"], ALU.add), [pAk, ("vh",)], [avk])
                        ew("act", lambda e: e.activation(flat(av), flat(av), AF.Sigmoid), [avk], [avk])
                        kr, krk = tile("kr")
                        ew("dve", lambda e: e.tensor_tensor(kr[:], k_[:], VB["k_k"], ALU.mult), [kk_, ("vh",)], [krk])
                        sq, sqk = tile("sq")
                        ew("dve", lambda e: e.tensor_tensor(flat(sq), flat(kr), flat(kr), ALU.mult), [krk], [sqk])
                        pS, pSk = psW.next()
                        cur.op("pe", lambda e, pS=pS: e.matmul(pS[0:64, 0:PW], ones64[:], flat(sq), start=True, stop=True),
                              reads=[sqk, ("ones64",)], writes=[pSk])
                        rn, rnk = tile("rn")
                        ew("act", lambda e, pS=pS: e.activation(flat(rn), pS[0:64, 0:PW], AF.Sqrt, bias=1e-24, scale=1.0), [pSk], [rnk])
                        ew("dve", lambda e: e.reciprocal(flat(rn), flat(rn)), [rnk], [rnk])
                        kkn, kknk = kr, krk
                        ew("dve", lambda e: e.tensor_tensor(flat(kkn), flat(kr), flat(rn), ALU.mult), [krk, rnk], [kknk])
                        k2, k2k = tile("k2")
                        ew("dve", lambda e: e.tensor_tensor(k2[:], av[:], VB["k_a"], ALU.mult), [avk, ("vh",)], [k2k])
                        ew("dve", lambda e: e.tensor_tensor(k2[:], k2[:], omk[:, G0:G0 + NG].rearrange("p (h o) -> p h o", o=1).to_broadcast(HS), ALU.add),
                           [k2k, ("omk",)], [k2k])
                        ew("dve", lambda e: e.tensor_tensor(flat(k2), flat(k2), flat(k_), ALU.mult), [k2k, kk_], [k2k])
                        bv, bvk = tile("bv")
                        ew("pool", lambda e: e.tensor_tensor(flat(bv), flat(kkn), flat(av), ALU.mult), [kknk, avk], [bvk])
                        cs, csk = tile("cs")
                        ew("dve", lambda e: e.tensor_tensor_scan(flat(cs), flat(smask), flat(lw), 0.0, ALU.mult, ALU.add),
                           [lwk, ("smask",), ("smask", 1)], [csk])
                        ecs, ecsk = tile("ecs")
                        ew("act", lambda e: e.activation(flat(ecs), flat(cs), AF.Exp, scale=-math.exp(-0.5)), [csk], [ecsk])
                        csx, csxk = lw, lwk
                        ew("dve", lambda e: e.tensor_tensor(flat(csx), flat(cs), flat(lw), ALU.subtract), [csk, lwk], [csxk])
                        ew("act", lambda e: e.activation(flat(csx), flat(csx), AF.Exp, scale=-math.exp(-0.5)), [csxk], [csxk])
                        encs, encsk = cs, csk
                        ew("act", lambda e: e.activation(flat(encs), flat(cs), AF.Exp, scale=math.exp(-0.5)), [csk], [encsk])
                        dte, dtek = tile("dte")
                        ew("dve", lambda e: e.tensor_tensor(dte[:], encs[:], ecs[:, :, 63:64].to_broadcast(HS), ALU.mult), [encsk, ecsk], [dtek])
                        At, Atk = tile("At")
                        ew("dve", lambda e: e.scalar_tensor_tensor(flat(At), flat(kkn), -1.0, flat(csx), ALU.mult, ALU.mult), [kknk, csxk], [Atk])
                        Bt, Btk = tile("Bt")
                        ew("dve", lambda e: e.tensor_tensor(flat(Bt), flat(bv), flat(encs), ALU.mult), [bvk, encsk], [Btk])
                        Kt, Ktk = tile("Kt")
                        ew("dve", lambda e: e.tensor_tensor(flat(Kt), flat(k2), flat(encs), ALU.mult), [k2k, encsk], [Ktk])
                        Rt, Rtk = tile("Rt")
                        ew("dve", lambda e: e.tensor_tensor(flat(Rt), flat(r_), flat(ecs), ALU.mult), [rk_, ecsk], [Rtk])
                        bc, bck = bv, bvk
                        ew("pool", lambda e: e.tensor_tensor(flat(bc), flat(bv), flat(dte), ALU.mult), [bvk, dtek], [bck])
                        kc, kck = tile("kc")
                        ew("pool", lambda e: e.tensor_tensor(flat(kc), flat(k2), flat(dte), ALU.mult), [k2k, dtek], [kck])
                        tm = {}
                        for nm, (src, srck) in (("V", (v_, vk_)), ("bc", (bc, bck)), ("kc", (kc, kck)), ("At", (At, Atk))):
                            pT_, pTk_ = psW.next()

                            def trf(e, pT_=pT_, src=src, fast=(nm == "At")):
                                ins = None
                                for h in range(NG):
                                    if fast:
                                        ins = e.transpose(pT_[0:64, h * 64:(h + 1) * 64].bitcast(F32R), src[:, h, :], ident_r[:])
                                    else:
                                        ins = e.transpose(pT_[0:64, h * 64:(h + 1) * 64], src[:, h, :].bitcast(F32), ident_f[0:64, 0:64])
                                return ins
                            cur.op("pe", trf, reads=[srck, ("ident_f",), ("ident_r",)], writes=[pTk_])
                            if nm == "At":
                                X, Xk = tile("X", [64, NG, 128])
                                ew("act", lambda e, pT_=pT_: e.copy(X[:, :, 64:128], pv3(pT_)), [pTk_], [(Xk, 1)])
                            else:
                                d, dk = tile("tm_" + nm)
                                ew("act", lambda e, pT_=pT_, d=d: e.copy(flat(d), pT_[0:64, 0:PW]), [pTk_], [dk])
                                tm[nm] = (d, dk)
                        Vt, Vtk = tm["V"]; bct, bctk = tm["bc"]; kct, kctk = tm["kc"]
                        def mm_mask(name, L, Lk, Rr, Rk, mi):
                            p_, pk_ = psW.next()
                            headmm(lambda h: p_[0:64, h * 64:(h + 1) * 64], lambda h: L[:, h, :], lambda h: Rr[:, h, :], [Lk, Rk], pk_)
                            d, dk = tile(name)
                            ew("dve", lambda e, p_=p_, d=d: e.tensor_tensor(d[:], pv3(p_), trib(mi), ALU.mult), [pk_, ("tri",)], [dk])
                            return d, dk
                        Q, Qk = mm_mask("Q0", Bt, Btk, At, Atk, 0)
                        Pm, Pmk = mm_mask("P0", At, Atk, Bt, Btk, 2)
                        AakT, AakTk = mm_mask("AakT", Kt, Ktk, At, Atk, 0)
                        ArbT, ArbTk = mm_mask("ArbT", Bt, Btk, Rt, Rtk, 1)
                        ArkT, ArkTk = mm_mask("ArkT", Kt, Ktk, Rt, Rtk, 1)
                        pX, pXk = psW.next()
                        headmm(lambda h: pX[0:64, h * 64:(h + 1) * 64], lambda h: AakT[:, h, :], lambda h: Vt[:, h, :], [AakTk, Vtk], pXk)
                        ew("act", lambda e, pX=pX: e.copy(X[:, :, 0:64], pv3(pX)), [pXk], [(Xk, 0)])
                        Xkeys = [(Xk, 0), (Xk, 1)]
                        for lv in range(6):
                            pa_, pak_ = psW.next()

                            def apf(e, pa_=pa_, Q=Q):
                                ins = None
                                for h in range(NG):
                                    ins = e.matmul(pa_[0:64, h * 128:(h + 1) * 128], Q[:, h, :], X[:, h, :], start=True, stop=True)
                                return ins
                            cur.op("pe", apf, reads=[Qk] + Xkeys, writes=[pak_])
                            ew("dve", lambda e, pa_=pa_: e.tensor_tensor(X[:], X[:], pa_[0:64, 0:NG * 128].rearrange("p (h t) -> p h t", h=NG), ALU.add),
                               [pak_] + Xkeys, Xkeys)
                            if lv < 5:
                                pq_, pqk_ = psW.next()
                                headmm(lambda h, pq_=pq_: pq_[0:64, h * 64:(h + 1) * 64], lambda h, Pm=Pm: Pm[:, h, :], lambda h, Q=Q: Q[:, h, :], [Pmk, Qk], pqk_)
                                Q2, Q2k = tile("Q%d" % ((lv + 1) % 2))
                                if lv < 4:
                                    pp_, ppk_ = psW.next()
                                    headmm(lambda h, pp_=pp_: pp_[0:64, h * 64:(h + 1) * 64], lambda h, Q=Q: Q[:, h, :], lambda h, Pm=Pm: Pm[:, h, :], [Pmk, Qk], ppk_)
                                    P2, P2k = tile("P%d" % ((lv + 1) % 2))
                                    ew("act", lambda e, pp_=pp_, P2=P2: e.copy(flat(P2), pp_[0:64, 0:PW]), [ppk_], [P2k])
                                ew("act", lambda e, pq_=pq_, Q2=Q2: e.copy(flat(Q2), pq_[0:64, 0:PW]), [pqk_], [Q2k])
                                Q, Qk = Q2, Q2k
                                if lv < 4:
                                    Pm, Pmk = P2, P2k
                        U0 = lambda h: X[:, h, 0:64]
                        Ah = lambda h: X[:, h, 64:128]
                        pM, pMk = psW.next()
                        headmm(lambda h: pM[0:64, h * 64:(h + 1) * 64], Ah, lambda h: bct[:, h, :], Xkeys + [bctk], pMk)
                        Mc, Mck = tile("Mc")
                        ew("dve", lambda e: e.tensor_tensor(Mc[:], identb8, ecs[:, :, 63:64].to_broadcast(HS), ALU.mult), [ecsk, ("ident_f",)], [Mck])
                        ew("dve", lambda e, pM=pM: e.tensor_tensor(Mc[:], Mc[:], pv3(pM), ALU.add), [pMk, Mck], [Mck])
                        pG, pGk = psW.next()
                        headmm(lambda h: pG[0:64, h * 64:(h + 1) * 64], lambda h: bct[:, h, :], U0, Xkeys + [bctk, kctk, Vtk], pGk,
                               extra=[(lambda h: kct[:, h, :], lambda h: Vt[:, h, :])])
                        G, Gk = tile("G")
                        ew("act", lambda e, pG=pG: e.copy(flat(G), pG[0:64, 0:PW]), [pGk], [Gk])
                        pR, pRk = psW.next()
                        headmm(lambda h: pR[0:64, h * 64:(h + 1) * 64], Ah, lambda h: ArbT[:, h, :], Xkeys + [ArbTk], pRk)
                        Rh, Rhk = tile("Rh")
                        ew("dve", lambda e, pR=pR: e.tensor_tensor(Rh[:], Rt[:], pv3(pR), ALU.add), [pRk, Rtk], [Rhk])
                        pO, pOk = psW.next()
                        headmm(lambda h: pO[0:64, h * 64:(h + 1) * 64], lambda h: ArbT[:, h, :], U0, Xkeys + [ArbTk, ArkTk, Vtk, Rhk, hck], pOk,
                               extra=[(lambda h: ArkT[:, h, :], lambda h: Vt[:, h, :]), (lambda h: Rh[:, h, :], lambda h, hcur=hcur: hcur[:, h, :])])
                        pH, pHk = psW.next()
                        headmm(lambda h: pH[0:64, h * 64:(h + 1) * 64], lambda h: Mc[:, h, :], lambda h, hcur=hcur: hcur[:, h, :], [Mck, hck], pHk)
                        ew("dve", lambda e, pH=pH, hnxt=hnxt: e.tensor_tensor(hnxt[:], G[:], pv3(pH), ALU.add), [pHk, Gk], [hnk])
                        Ot, Otk = tile("Ot")
                        ew("act", lambda e, pO=pO: e.copy(flat(Ot), pO[0:64, 0:PW]), [pOk], [Otk])
                        st1, st1k = tile("st1", [64, NG])
                        st2, st2k = tile("st2", [64, NG])
                        junk, junkk = tile("junk", [64, 64])
                        for h in range(NG):
                            ew("act", lambda e, h=h: e.activation(junk[:], Ot[:, h, :], AF.Copy, accum_out=st1[:, h:h + 1]), [Otk], [junkk, (st1k, h)])
                            ew("act", lambda e, h=h: e.activation(junk[:], Ot[:, h, :], AF.Square, accum_out=st2[:, h:h + 1]), [Otk], [junkk, (st2k, h)])
                        st1a = [(st1k, h) for h in range(NG)]
                        st2a = [(st2k, h) for h in range(NG)]
                        ew("dve", lambda e: e.tensor_scalar(st1[:], st1[:], 1.0 / 64, None, ALU.mult), st1a, st1a)
                        msq, msqk = tile("msq", [64, NG])
                        ew("dve", lambda e: e.tensor_tensor(msq[:], st1[:], st1[:], ALU.mult), st1a, [msqk])
                        ew("dve", lambda e: e.scalar_tensor_tensor(st2[:], st2[:], 1.0 / 64, msq[:], ALU.mult, ALU.subtract), st2a + [msqk], st2a)
                        ew("act", lambda e: e.activation(st2[:], st2[:], AF.Sqrt, bias=float(GN_EPS), scale=1.0), st2a, st2a)
                        ew("dve", lambda e: e.reciprocal(st2[:], st2[:]), st2a, st2a)
                        b3 = lambda t: t[:].rearrange("p (h o) -> p h o", o=1).to_broadcast(HS)
                        ew("dve", lambda e: e.tensor_tensor(Ot[:], Ot[:], b3(st1), ALU.subtract), [Otk] + st1a, [Otk])
                        ew("dve", lambda e: e.tensor_tensor(Ot[:], Ot[:], b3(st2), ALU.mult), [Otk] + st2a, [Otk])
                        ew("dve", lambda e: e.tensor_tensor(flat(Ot), flat(Ot), lnw[:, G0 * 64:(G0 + NG) * 64], ALU.mult), [Otk, ("lnw",)], [Otk])
                        ew("dve", lambda e: e.tensor_tensor(flat(Ot), flat(Ot), lnb[:, G0 * 64:(G0 + NG) * 64], ALU.add), [Otk, ("lnb",)], [Otk])
                        rk3, rk3k = tile("rk3")
                        ew("pool", lambda e: e.tensor_tensor(flat(rk3), flat(r_), flat(k2), ALU.mult), [rk_, k2k], [rk3k])
                        ew("dve", lambda e: e.tensor_tensor(rk3[:], rk3[:], VB["r_k"], ALU.mult), [rk3k, ("vh",)], [rk3k])
                        pBn, pBnk = psW.next()
                        headmm(lambda h: pBn[0:64, h:h + 1], lambda h: rk3[:, h, :], lambda h: ones64f[:, 0:1], [rk3k, ("ones64f",)], pBnk)
                        sbn, sbnk = tile("sbn", [64, NG])
                        ew("dve", lambda e, pBn=pBn: e.tensor_copy(sbn[:], pBn[0:64, 0:NG]), [pBnk], [sbnk])
                        bon, bonk = tile("bon")
                        ew("dve", lambda e: e.tensor_tensor(bon[:], Vt[:], b3(sbn), ALU.mult), [Vtk, sbnk], [bonk])
                        ew("dve", lambda e: e.tensor_tensor(flat(Ot), flat(Ot), flat(bon), ALU.add), [Otk, bonk], [Otk])
                        pY, pYk = psW.next()

                        def tyf(e, pY=pY):
                            ins = None
                            for h in range(NG):
                                ins = e.transpose(pY[0:64, h * 64:(h + 1) * 64].bitcast(F32R), Ot[:, h, :], ident_r[:])
                            return ins
                        cur.op("pe", tyf, reads=[Otk, ("ident_r",)], writes=[pYk])
                        zs, zsk = tile("zs")
                        ew("act", lambda e: e.activation(flat(zs), flat(z_), AF.Silu), [zk_], [zsk])
                        ybn = "ybb%d" % CUR["set"]
                        if ybn not in T_:
                            T_[ybn] = es4.enter_context(nc.sbuf_tensor(ybn, [64, NG, 64], BF16))
                        ybb, ybbk = T_[ybn], (ybn,)
                        ew("dve", lambda e, pY=pY: e.tensor_tensor(ybb[:], zs[:], pv3(pY), ALU.mult), [pYk, zsk], [ybbk])
                        cur.dma("pool", yb_d[s, G0 * 64:(G0 + NG) * 64, :].rearrange("(h d) t -> d h t", h=NG)[:, :, csl], ybb[:], reads=[ybbk],
                               writes=[("yb_c", s, ci, hg)] + ([("yb", s, ti, hg)] if ci % 8 == 7 else []))


                def flush(lists):
                    n = max(len(l) for l in lists)
                    for i in range(n):
                        for l in lists:
                            if i < len(l):
                                kind, eng, a_, rd, wr, kw = l[i]
                                if kind == "op":
                                    sc.op(eng, a_, reads=rd, writes=wr)
                                else:
                                    sc.dma(eng, a_[0], a_[1], reads=rd, writes=wr, **kw)

                for s0 in range(0, NSEQ, 2):
                    for ci in range(NC_):
                        lists = []
                        for s in range(s0, min(s0 + 2, NSEQ)):
                            for hg in range(2):
                                CUR["set"] = (s % 2) * 2 + hg
                                CUR["hg"] = hg
                                CUR["list"] = []
                                body(s, ci, hg)
                                lists.append(CUR["list"])
                        flush(lists)

        sc.barrier()
        if lvl >= 30:
            es3 = ExitStack()
            with es3:
                def sb3(name, shape, dt=F32):
                    return es3.enter_context(nc.sbuf_tensor(name, list(shape), dt))
                psR = Ring(psb, "psb")
                wstg = [sb3("wstg%d" % i, [128, D]) for i in range(2)]
                wsr = Ring(wstg, "wstg")

                def load_w(name, dram, nk):
                    t = sb3(name, [128, nk, D], BF16)
                    for kc in range(nk):
                        st, stk = wsr.next()
                        sc.dma("sp", st[:], dram[kc * 128:(kc + 1) * 128, :], writes=[stk])
                        sc.op(("dve", "pool")[kc % 2], lambda e, st=st, kc=kc, t=t: e.tensor_copy(t[:, kc, :], st[:]),
                              reads=[stk], writes=[(name, kc)])
                    return t, [(name, kc) for kc in range(nk)]
                pa_sb, pa_k = load_w("pa_sb", p_a_d, 4)
                pb_sb, pb_k = load_w("pb_sb", p_b_d, 4)
                wo_sb, wo_k = load_w("wo_sb", w_out_d, 8)
                wg_sb, wg_k = load_w("wg_sb", w_pg_d, 8)
                wu_sb, wu_k = load_w("wu_sb", w_pu_d, 2)
                gpost = sb3("gpost", [128, D])
                sc.dma("sp", gpost[:], g_post_d.partition_broadcast(128), writes=[("gpost",)])
                yaT = [sb3("yaT%d" % i, [128, 4, 512], BF16) for i in range(2)]
                ybT = [sb3("ybT%d" % i, [128, 4, 512], BF16) for i in range(2)]
                gtT = [sb3("gtT%d" % i, [128, 16, 512], BF16) for i in range(2)]
                yar, ybr, gtr = Ring(yaT, "yaT"), Ring(ybT, "ybT"), Ring(gtT, "gtT")
                t1b = [sb3("t1b%d" % i, [128, 512]) for i in range(2)]
                t1r = Ring(t1b, "t1b")
                mgT = [sb3("mgT%d" % i, [128, 8, 512], BF16) for i in range(2)]
                mgr = Ring(mgT, "mgT")
                x3 = [sb3("x3_%d" % i, [128, D]) for i in range(2)]
                x3r = Ring(x3, "x3")
                p3 = [sb3("p3_%d" % i, [128, PLE]) for i in range(2)]
                p3r = Ring(p3, "p3")
                p3b = [sb3("p3b_%d" % i, [128, PLE], BF16) for i in range(2)]
                p3br = Ring(p3b, "p3b")
                ysb = [sb3("ysb%d" % i, [128, D]) for i in range(2)]
                ysr = Ring(ysb, "ysb")
                sq3 = sb3("sq3", [128, D], BF16)
                st3 = [sb3("st3_%d" % i, [128, 2]) for i in range(2)]
                st3r = Ring(st3, "st3")
                hsb = [sb3("hsb%d" % i, [128, D]) for i in range(2)]
                hsr = Ring(hsb, "hsb")
                hbb = [sb3("hbb%d" % i, [128, D], BF16) for i in range(2)]
                hbr = Ring(hbb, "hbb")
                hT = [sb3("hT%d" % i, [128, 8, 128], BF16) for i in range(2)]
                hTr = Ring(hT, "hT")
                pT = [sb3("pT%d" % i, [128, 2, 128], BF16) for i in range(2)]
                pTr = Ring(pT, "pT")
                sg = [sb3("sg%d" % i, [128, D]) for i in range(2)]
                sgr = Ring(sg, "sg")
                osb = [sb3("osb%d" % i, [128, D]) for i in range(2)]
                osr = Ring(osb, "osb")

                for s in range(NSEQ):
                    for ti in range(NT):
                        tsl = slice(ti * 512, (ti + 1) * 512)
                        ya, yak = yar.next()
                        yb, ybk = ybr.next()
                        gt, gtk = gtr.next()
                        sc.dma("sp", ya[:], ya_d[s, :, tsl].rearrange("(c p) t -> p c t", p=128),
                               reads=[("ya", s, qt) for qt in range(ti * 4, ti * 4 + 4)], writes=[yak])
                        sc.dma("sp", yb[:], yb_d[s, :, tsl].rearrange("(c p) t -> p c t", p=128),
                               reads=[("yb", s, ti), ("yb", s, ti, 0), ("yb", s, ti, 1)], writes=[ybk])
                        sc.dma("sp", gt[:], gt_d[s, :, tsl].rearrange("(c p) t -> p c t", p=128),
                               reads=[("gt", s, j, ti) for j in range(16)], writes=[gtk])
                        mg, mgk = mgr.next()
                        for m in range(8):
                            pA, pAk = psR.next()
                            pB, pBk = psR.next()

                            def abfn(e, pA=pA, pB=pB, m=m, ya=ya, yb=yb):
                                ins = None
                                for c in range(4):
                                    ins = e.matmul(pA[:], pa_sb[:, c, m * 128:(m + 1) * 128], ya[:, c, :], start=(c == 0), stop=(c == 3))
                                for c in range(4):
                                    ins = e.matmul(pB[:], pb_sb[:, c, m * 128:(m + 1) * 128], yb[:, c, :], start=(c == 0), stop=(c == 3))
                                return ins
                            sc.op("pe", abfn, reads=[yak, ybk] + pa_k + pb_k, writes=[pAk, pBk])
                            t1, t1k = t1r.next()
                            sc.op("dve", lambda e, t1=t1, pA=pA, gt=gt, m=m: e.tensor_tensor(t1[:], pA[:], gt[:, m, :], ALU.mult),
                                  reads=[pAk, gtk], writes=[t1k])
                            t2, t2k = t1r.next()
                            sc.op("dve", lambda e, t2=t2, pB=pB, gt=gt, m=m: e.tensor_tensor(t2[:], pB[:], gt[:, 8 + m, :], ALU.mult),
                                  reads=[pBk, gtk], writes=[t2k])
                            sc.op("dve", lambda e, mg=mg, t1=t1, t2=t2, m=m: e.tensor_tensor(mg[:, m, :], t1[:], t2[:], ALU.add),
                                  reads=[t1k, t2k], writes=[(mgk, m)])
                        mg_keys = [(mgk, m) for m in range(8)]
                        for a in range(4):
                            tok0 = s * S + ti * 512 + a * 128
                            xt3, x3k = x3r.next()
                            sc.dma("sp", xt3[:], x_d[tok0:tok0 + 128, :], writes=[x3k])
                            pt3, p3k = p3r.next()
                            sc.dma("sp", pt3[:], p_d[tok0:tok0 + 128, :], writes=[p3k])
                            yps = []
                            for half in range(2):
                                pY, pYk = psR.next()

                                def yfn(e, pY=pY, half=half, mg=mg, a=a):
                                    ins = None
                                    for m in range(8):
                                        ins = e.matmul(pY[:], mg[:, m, a * 128:(a + 1) * 128], wo_sb[:, m, half * 512:(half + 1) * 512],
                                                       start=(m == 0), stop=(m == 7))
                                    return ins
                                sc.op("pe", yfn, reads=mg_keys + wo_k, writes=[pYk])
                                yps.append((pY, pYk))
                            ys, ysk = ysr.next()
                            stt, sttk = st3r.next()
                            for half in range(2):
                                pY, pYk = yps[half]
                                sc.op("act", lambda e, ys=ys, pY=pY, half=half: e.copy(ys[:, half * 512:(half + 1) * 512], pY[:]),
                                      reads=[pYk], writes=[(ysk, half)])
                            sc.op("act", lambda e, ys=ys, stt=stt: e.activation(sq3[:], ys[:], AF.Square, accum_out=stt[:, 0:1]),
                                  reads=[(ysk, 0), (ysk, 1)], writes=[("sq3",), (sttk, 0)])
                            sc.op("act", lambda e, stt=stt: e.activation(stt[:, 0:1], stt[:, 0:1], AF.Sqrt, bias=float(RMS_EPS), scale=1.0 / D),
                                  reads=[(sttk, 0)], writes=[(sttk, 0)])
                            sc.op("dve", lambda e, stt=stt: e.reciprocal(stt[:, 1:2], stt[:, 0:1]), reads=[(sttk, 0)], writes=[(sttk, 1)])
                            hs, hsk = hsr.next()
                            sc.op("dve", lambda e, hs=hs, ys=ys, stt=stt: e.scalar_tensor_tensor(
                                hs[:], ys[:], stt[:, 1:2], gpost[:], ALU.mult, ALU.mult),
                                reads=[(ysk, 0), (ysk, 1), (sttk, 1), ("gpost",)], writes=[hsk])
                            sc.op("dve", lambda e, hs=hs, xt3=xt3: e.tensor_tensor(hs[:], hs[:], xt3[:], ALU.add),
                                  reads=[hsk, x3k], writes=[hsk])
                            hb, hbk = hbr.next()
                            sc.op("act", lambda e, hb=hb, hs=hs: e.copy(hb[:], hs[:]), reads=[hsk], writes=[hbk])
                            pb3, p3bk = p3br.next()
                            sc.op("act", lambda e, pb3=pb3, pt3=pt3: e.copy(pb3[:], pt3[:]), reads=[p3k], writes=[p3bk])
                            pTp, pTpk = psR.next()
                            ptb = pTp[:].bitcast(BF16)

                            def trfn(e, ptb=ptb, hb=hb):
                                ins = None
                                for m in range(8):
                                    ins = e.transpose(ptb[:, m * 128:(m + 1) * 128], hb[:, m * 128:(m + 1) * 128], ident_b[:])
                                return ins
                            sc.op("pe", trfn, reads=[hbk, ("ident_b",)], writes=[pTpk])
                            hTt, hTk = hTr.next()
                            sc.op("dve", lambda e, hTt=hTt, ptb=ptb: e.tensor_copy(hTt[:], ptb.rearrange("p (m t) -> p m t", m=8)),
                                  reads=[pTpk], writes=[hTk])
                            pP, pPk = psR.next()
                            ppb = pP[:].bitcast(BF16)

                            def trp(e, ppb=ppb, pb3=pb3):
                                ins = None
                                for j in range(2):
                                    ins = e.transpose(ppb[:, j * 128:(j + 1) * 128], pb3[:, j * 128:(j + 1) * 128], ident_b[:])
                                return ins
                            sc.op("pe", trp, reads=[p3bk, ("ident_b",)], writes=[pPk])
                            pTt, pTk = pTr.next()
                            sc.op("act", lambda e, pTt=pTt, ppb=ppb: e.copy(pTt[:], ppb[:, 0:256].rearrange("p (m t) -> p m t", m=2)),
                                  reads=[pPk], writes=[pTk])
                            sgt, sgk = sgr.next()
                            ot, otk = osr.next()
                            for half in range(2):
                                pG, pGk = psR.next()

                                def gfn3(e, pG=pG, half=half, hTt=hTt):
                                    ins = None
                                    for m in range(8):
                                        ins = e.matmul(pG[:], hTt[:, m, :], wg_sb[:, m, half * 512:(half + 1) * 512], start=(m == 0), stop=(m == 7))
                                    return ins
                                sc.op("pe", gfn3, reads=[hTk] + wg_k, writes=[pGk])
                                sc.op("act", lambda e, sgt=sgt, pG=pG, half=half: e.activation(sgt[:, half * 512:(half + 1) * 512], pG[:], AF.Sigmoid),
                                      reads=[pGk], writes=[(sgk, half)])
                                pE, pEk = psR.next()

                                def efn(e, pE=pE, half=half, pTt=pTt):
                                    ins = None
                                    for j in range(2):
                                        ins = e.matmul(pE[:], pTt[:, j, :], wu_sb[:, j, half * 512:(half + 1) * 512], start=(j == 0), stop=(j == 1))
                                    return ins
                                sc.op("pe", efn, reads=[pTk] + wu_k, writes=[pEk])
                                sc.op("dve", lambda e, sgt=sgt, pE=pE, half=half: e.tensor_tensor(
                                    sgt[:, half * 512:(half + 1) * 512], pE[:], sgt[:, half * 512:(half + 1) * 512], ALU.mult),
                                    reads=[pEk, (sgk, half)], writes=[(sgk, half)])
                            sc.op("dve", lambda e, ot=ot, sgt=sgt, hs=hs: e.tensor_tensor(ot[:], sgt[:], hs[:], ALU.add),
                                  reads=[(sgk, 0), (sgk, 1), hsk], writes=[otk])
                            sc.dma("pool", out_d[tok0:tok0 + 128, :], ot[:], reads=[otk], writes=[("out", tok0)], final=True)

        sc.emit(nc, es)
    return nc


def t5_bucket_np(n):
    n = np.maximum(n, 0)
    nf = np.maximum(n, 1).astype(np.float32)
    large = 16 + (np.log(nf / np.float32(16)) / np.float32(math.log(128 / 16)) * np.float32(16)).astype(np.int32)
    large = np.minimum(large, 31)
    return np.where(n < 16, n, large)


def make_consts():
    c = {}
    c["c_ident"] = np.eye(128, dtype=np.float32)
    oh = np.zeros((33, 512), np.float32)
    d = np.arange(512) - 128
    bk = t5_bucket_np(d)
    for j in range(512):
        if d[j] >= 0:
            oh[bk[j], j] = 1.0
        else:
            oh[32, j] = 1.0
    c["c_onehot"] = oh
    es_ = np.zeros((128, 16, 128), np.float32)
    for n in range(16):
        es_[n, n, :] = 1.0
        es_[64 + n, n, :] = 1.0
    c["c_esel"] = es_.reshape(128, 16 * 128)
    tri = np.zeros((64, 3, 64), np.float32)
    i = np.arange(64)
    tri[:, 0, :] = (i[:, None] < i[None, :])
    tri[:, 1, :] = (i[:, None] <= i[None, :])
    tri[:, 2, :] = (i[:, None] > i[None, :])
    c["c_tri"] = tri.reshape(64, 192)
    bd = np.zeros((128, 128), np.float32)
    bd[:64, :64] = 1.0
    bd[64:, 64:] = 1.0
    c["c_bd"] = bd
    c["c_J"] = np.ascontiguousarray(np.eye(128, dtype=np.float32)[::-1])
    pen = np.zeros((16, NH, 16), np.float32)
    for qb in range(16):
        pen[qb, :, qb:] = -30000.0
    c["c_pen"] = pen.reshape(-1)
    return c


_WNAMES = ["g_pre", "w_in", "mu_shift", "w0", "w_up", "a0", "a_up", "k_k", "k_a", "r_k", "ln_x_w", "ln_x_b",
           "p_a", "p_b", "w_out", "g_post", "w_ple_up", "w_ple_gate"]


def make_in_maps(inputs, ncores, nseq, S):
    consts = make_consts()
    maps = []
    for c in range(ncores):
        m = dict(consts)
        m["x"] = np.ascontiguousarray(inputs["x"][c * nseq:(c + 1) * nseq].reshape(nseq * S, D))
        m["p"] = np.ascontiguousarray(inputs["p"][0, c * nseq:(c + 1) * nseq].reshape(nseq * S, PLE))
        m["rel_bias"] = np.ascontiguousarray(inputs["rel_bias"])
        for n in _WNAMES:
            a = np.asarray(inputs[n])[0]
            m[n] = np.ascontiguousarray(a.reshape(-1) if n == "r_k" else a)
        maps.append(m)
    return maps


def kernel(**inputs):
    inputs = {k: np.asarray(v) for k, v in inputs.items()}
    B, S, _ = inputs["x"].shape
    nseq = B // NCORES
    nc = build_program(S, nseq, lvl=99, moba=True)
    maps = make_in_maps(inputs, NCORES, nseq, S)
    res = run_bass_kernel_spmd(nc, maps, core_ids=list(range(NCORES)))
    outs = [r["out"].reshape(nseq, S, D) for r in res.results]
    return np.concatenate(outs, axis=0).astype(np.float32)
```
